# Optimizing a Trainium2 kernel written in Bass

```python
import jax
import jax.numpy as jnp
from jax import lax
import numpy as np

D_MODEL = 1024
BATCH = 4
SEQ = 8192
DEPTH = 4

HEAD_DIM = 64
ROT_DIM = HEAD_DIM // 4
ROPE_THETA = 500000.0
BLOCK = 128
NORM_EPS = 1e-5

A_Q_HEADS = 12
A_KV_HEADS = 3
A_GROUP = A_Q_HEADS // A_KV_HEADS
A_WINDOW = 128

B_PATTERNS = ((128, 1), (512, 4), (2048, 16))
B_N_PAT = len(B_PATTERNS)
B_HEADS = 4

A_Q_W = A_Q_HEADS * HEAD_DIM
A_KV_W = A_KV_HEADS * HEAD_DIM
B_W = B_N_PAT * B_HEADS * HEAD_DIM
IN_W = A_Q_W + 2 * A_KV_W + 3 * B_W
IN_SPLITS = (A_Q_W, A_Q_W + A_KV_W, A_Q_W + 2 * A_KV_W, A_Q_W + 2 * A_KV_W + B_W, A_Q_W + 2 * A_KV_W + 2 * B_W)
MIX_W = A_Q_W + B_HEADS * HEAD_DIM

RWKV_HEADS = D_MODEL // HEAD_DIM
DECAY_LORA = max(32, int(round(1.8 * D_MODEL ** 0.5 / 32)) * 32)
AAA_LORA = max(32, int(round(1.8 * D_MODEL ** 0.5 / 32)) * 32)
MV_LORA = max(32, int(round(1.3 * D_MODEL ** 0.5 / 32)) * 32)
GATE_LORA = max(32, int(round(0.6 * D_MODEL ** 0.8 / 32)) * 32)
LNX_EPS = 64e-5

D_FF = ((8 * D_MODEL + 3 * 256 - 1) // (3 * 256)) * 256

N_ATTN_LAYERS = (DEPTH + 1) // 2
N_RWKV_LAYERS = DEPTH // 2
MAX_POS_OFFSET = 4096

kernel_name = 'hybrid_swa_dilated_rwkv7_trunk'


def rms_norm(x, g):
    xf = x.astype(jnp.float32)
    y = xf * lax.rsqrt(jnp.mean(xf * xf, axis=-1, keepdims=True) + NORM_EPS) * g
    return y.astype(x.dtype)


def rope_partial(x, positions):
    half = ROT_DIM // 2
    inv_freq = jnp.power(ROPE_THETA, -2.0 * jnp.arange(half, dtype=jnp.float32) / ROT_DIM)
    ang = positions.astype(jnp.float32)[..., None] * inv_freq
    cos = jnp.cos(ang)[:, :, None, :]
    sin = jnp.sin(ang)[:, :, None, :]
    xf = x.astype(jnp.float32)
    x1, x2, xp = xf[..., :half], xf[..., half:ROT_DIM], xf[..., ROT_DIM:]
    out = jnp.concatenate([x1 * cos - x2 * sin, x2 * cos + x1 * sin, xp], axis=-1)
    return out.astype(x.dtype)


def banded_attention(q, k, v, max_dist, sink=None):
    n, t, hk, g, d = q.shape
    nb = -(-t // BLOCK)
    tp = nb * BLOCK
    pad = tp - t
    if pad:
        q = jnp.pad(q, ((0, 0), (0, pad), (0, 0), (0, 0), (0, 0)))
        k = jnp.pad(k, ((0, 0), (0, pad), (0, 0), (0, 0)))
        v = jnp.pad(v, ((0, 0), (0, pad), (0, 0), (0, 0)))
    qb = q.reshape(n, nb, BLOCK, hk, g, d).astype(jnp.float32)

    def with_prev(z):
        zb = z.reshape(n, nb, BLOCK, hk, d).astype(jnp.float32)
        prev = jnp.pad(zb[:, :-1], ((0, 0), (1, 0), (0, 0), (0, 0), (0, 0)))
        return jnp.concatenate([prev, zb], axis=2)

    kb, vb = with_prev(k), with_prev(v)
    s = jnp.einsum('nbqhgd,nbkhd->nbhgqk', qb, kb) * (d ** -0.5)
    blk = jnp.arange(nb)[:, None, None]
    qi = jnp.arange(BLOCK)[None, :, None]
    kj = jnp.arange(2 * BLOCK)[None, None, :]
    dist = BLOCK + qi - kj
    valid = (dist >= 0) & (dist <= max_dist) & (blk * BLOCK + kj - BLOCK >= 0)
    s = jnp.where(valid[None, :, None, None], s, -jnp.inf)
    m = jnp.max(s, axis=-1)
    if sink is not None:
        sk = sink.astype(jnp.float32).reshape(1, 1, hk, g, 1)
        m = jnp.maximum(m, sk)
    p = jnp.exp(s - m[..., None])
    den = jnp.sum(p, axis=-1)
    if sink is not None:
        den = den + jnp.exp(sk - m)
    o = jnp.einsum('nbhgqk,nbkhd->nbqhgd', p, vb) / jnp.moveaxis(den, -1, 2)[..., None]
    lse = jnp.moveaxis(m + jnp.log(den), -1, 2).reshape(n, tp, hk, g)[:, :t]
    o = o.reshape(n, tp, hk, g, d)[:, :t].astype(q.dtype)
    return o, lse


def dilated_attention(q, k, v, window, dil):
    bsz, s, h, d = q.shape
    length = s // dil

    def to_sub(z):
        return z.reshape(bsz, length, dil, h, d).transpose(0, 2, 1, 3, 4).reshape(bsz * dil, length, h, d)

    o, lse = banded_attention(to_sub(q)[:, :, :, None], to_sub(k), to_sub(v), window // dil)
    o = o[:, :, :, 0].reshape(bsz, dil, length, h, d).transpose(0, 2, 1, 3, 4).reshape(bsz, s, h, d)
    lse = lse[..., 0].reshape(bsz, dil, length, h).transpose(0, 2, 1, 3).reshape(bsz, s, h)
    return o, lse


def parallel_attention(h, positions, w_in, b_in, sinks, w_out):
    bsz, s, _ = h.shape
    proj = h @ w_in + b_in
    qa, ka, va, qb, kb, vb = jnp.split(proj, IN_SPLITS, axis=-1)
    qa = rope_partial(qa.reshape(bsz, s, A_Q_HEADS, HEAD_DIM), positions).reshape(bsz, s, A_KV_HEADS, A_GROUP, HEAD_DIM)
    ka = rope_partial(ka.reshape(bsz, s, A_KV_HEADS, HEAD_DIM), positions)
    va = va.reshape(bsz, s, A_KV_HEADS, HEAD_DIM)
    out_a, _ = banded_attention(qa, ka, va, A_WINDOW - 1, sinks)
    out_a = out_a.reshape(bsz, s, A_Q_W)
    nbh = B_N_PAT * B_HEADS
    qb = rope_partial(qb.reshape(bsz, s, nbh, HEAD_DIM), positions).reshape(bsz, s, B_N_PAT, B_HEADS, HEAD_DIM)
    kb = rope_partial(kb.reshape(bsz, s, nbh, HEAD_DIM), positions).reshape(bsz, s, B_N_PAT, B_HEADS, HEAD_DIM)
    vb = vb.reshape(bsz, s, B_N_PAT, B_HEADS, HEAD_DIM)
    outs, lses = [], []
    for gi, (win, dil) in enumerate(B_PATTERNS):
        o, l = dilated_attention(qb[:, :, gi], kb[:, :, gi], vb[:, :, gi], win, dil)
        outs.append(o.astype(jnp.float32))
        lses.append(l)
    wts = jax.nn.softmax(jnp.stack(lses), axis=0)
    out_b = jnp.sum(wts[..., None] * jnp.stack(outs), axis=0).astype(h.dtype).reshape(bsz, s, B_HEADS * HEAD_DIM)
    return jnp.concatenate([out_a, out_b], axis=-1) @ w_out


def wkv7_scan(r, decay, k, v, a_vec, b_vec):
    bsz, _, nh, n = r.shape
    xs = tuple(jnp.moveaxis(z, 1, 0) for z in (r, decay, k, v, a_vec, b_vec))

    def step(state, inp):
        r_t, w_t, k_t, v_t, a_t, b_t = inp
        sa = jnp.einsum('bhvk,bhk->bhv', state, a_t)
        state = state * w_t[:, :, None, :] + sa[..., None] * b_t[:, :, None, :] + v_t[..., None] * k_t[:, :, None, :]
        return state, jnp.einsum('bhvk,bhk->bhv', state, r_t)

    state0 = jnp.zeros((bsz, nh, n, n), jnp.float32)
    _, ys = lax.scan(step, state0, xs)
    return jnp.moveaxis(ys, 0, 1)


def rwkv_time_mix(h, v_first, v_lora, mu, w_rkv, w0, w1, w2, a0, a1, a2, g1, g2, k_k, k_a, r_k, lnx_w, lnx_b, w_o):
    bsz, s, dm = h.shape
    xx = jnp.pad(h[:, :-1], ((0, 0), (1, 0), (0, 0))) - h
    xr, xw, xk, xv, xa, xg = [h + xx * mu[i] for i in range(6)]
    r = xr @ w_rkv[0]
    k = xk @ w_rkv[1]
    v = xv @ w_rkv[2]
    w = -jax.nn.softplus(-(w0 + jnp.tanh(xw @ w1) @ w2)) - 0.5
    if v_lora is None:
        v_first = v
    else:
        v0, v1, v2 = v_lora
        v = v + (v_first - v) * jax.nn.sigmoid(v0 + (xv @ v1) @ v2)
    a = jax.nn.sigmoid(a0 + (xa @ a1) @ a2)
    g = jax.nn.sigmoid(xg @ g1) @ g2

    def heads(z):
        return z.reshape(bsz, s, RWKV_HEADS, HEAD_DIM).astype(jnp.float32)

    kk = heads(k * k_k)
    kk = kk / jnp.maximum(jnp.sqrt(jnp.sum(kk * kk, axis=-1, keepdims=True)), 1e-12)
    k = k * (1.0 + (a - 1.0) * k_a)
    rh, kh, vh, ah = heads(r), heads(k), heads(v), heads(a)
    decay = jnp.exp(-jnp.exp(heads(w)))
    y = wkv7_scan(rh, decay, kh, vh, -kk, kk * ah)
    mean = jnp.mean(y, axis=-1, keepdims=True)
    var = jnp.mean(jnp.square(y - mean), axis=-1, keepdims=True)
    y = ((y - mean) * lax.rsqrt(var + LNX_EPS)).reshape(bsz, s, dm) * lnx_w + lnx_b
    bonus = jnp.sum(rh * kh * r_k, axis=-1, keepdims=True) * vh
    y = y + bonus.reshape(bsz, s, dm)
    out = (y.astype(h.dtype) * g) @ w_o
    return out, v_first


def swiglu(h, w_gate, w_up, w_down):
    return (jax.nn.silu(h @ w_gate) * (h @ w_up)) @ w_down


def setup_inputs(seed: int = 0) -> dict:
    key = jax.random.key(seed)
    keys = list(jax.random.split(key, 40))

    def nrm(shape, scale):
        return scale * jax.random.normal(keys.pop(), shape, jnp.float32)

    def gain(shape):
        return 1.0 + nrm(shape, 0.02)

    def unif(shape, lo, hi):
        return jax.random.uniform(keys.pop(), shape, jnp.float32, lo, hi)

    na, nr, d = N_ATTN_LAYERS, N_RWKV_LAYERS, D_MODEL
    x = nrm((BATCH, SEQ, d), 1.0)
    start = jax.random.randint(keys.pop(), (BATCH, 1), 0, MAX_POS_OFFSET, dtype=jnp.int32)
    positions = start + jnp.arange(SEQ, dtype=jnp.int32)[None, :]
    return {
        'x': x,
        'positions': positions,
        'norm_mix': gain((DEPTH, d)),
        'norm_ffn': gain((DEPTH, d)),
        'norm_final': gain((d,)),
        'attn_w_in': nrm((na, d, IN_W), d ** -0.5),
        'attn_b_in': nrm((na, IN_W), 0.02),
        'attn_sinks': nrm((na, A_Q_HEADS), 0.5),
        'attn_w_out': nrm((na, MIX_W, d), MIX_W ** -0.5),
        'rwkv_mu': unif((nr, 6, d), 0.0, 1.0),
        'rwkv_w_rkv': nrm((nr, 3, d, d), d ** -0.5),
        'rwkv_w0': unif((nr, d), -6.0, -1.0),
        'rwkv_w1': nrm((nr, d, DECAY_LORA), d ** -0.5),
        'rwkv_w2': nrm((nr, DECAY_LORA, d), 0.1 * DECAY_LORA ** -0.5),
        'rwkv_a0': nrm((nr, d), 0.1),
        'rwkv_a1': nrm((nr, d, AAA_LORA), d ** -0.5),
        'rwkv_a2': nrm((nr, AAA_LORA, d), 0.1 * AAA_LORA ** -0.5),
        'rwkv_g1': nrm((nr, d, GATE_LORA), d ** -0.5),
        'rwkv_g2': nrm((nr, GATE_LORA, d), GATE_LORA ** -0.5),
        'rwkv_k_k': 0.85 + nrm((nr, d), 0.02),
        'rwkv_k_a': gain((nr, d)),
        'rwkv_r_k': nrm((nr, RWKV_HEADS, HEAD_DIM), 0.1),
        'rwkv_lnx_w': gain((nr, d)),
        'rwkv_lnx_b': nrm((nr, d), 0.02),
        'rwkv_w_o': nrm((nr, d, d), d ** -0.5),
        'rwkv_v0': nrm((nr - 1, d), 0.1),
        'rwkv_v1': nrm((nr - 1, d, MV_LORA), d ** -0.5),
        'rwkv_v2': nrm((nr - 1, MV_LORA, d), 0.1 * MV_LORA ** -0.5),
        'ffn_w_gate': nrm((DEPTH, d, D_FF), d ** -0.5),
        'ffn_w_up': nrm((DEPTH, d, D_FF), d ** -0.5),
        'ffn_w_down': nrm((DEPTH, D_FF, d), D_FF ** -0.5),
    }


def reference(x, positions, norm_mix, norm_ffn, norm_final, attn_w_in, attn_b_in, attn_sinks, attn_w_out,
              rwkv_mu, rwkv_w_rkv, rwkv_w0, rwkv_w1, rwkv_w2, rwkv_a0, rwkv_a1, rwkv_a2, rwkv_g1, rwkv_g2,
              rwkv_k_k, rwkv_k_a, rwkv_r_k, rwkv_lnx_w, rwkv_lnx_b, rwkv_w_o, rwkv_v0, rwkv_v1, rwkv_v2,
              ffn_w_gate, ffn_w_up, ffn_w_down):
    v_first = None
    for layer in range(DEPTH):
        h = rms_norm(x, norm_mix[layer])
        i = layer // 2
        if layer % 2 == 0:
            mix = parallel_attention(h, positions, attn_w_in[i], attn_b_in[i], attn_sinks[i], attn_w_out[i])
        else:
            v_lora = None if i == 0 else (rwkv_v0[i - 1], rwkv_v1[i - 1], rwkv_v2[i - 1])
            mix, v_first = rwkv_time_mix(h, v_first, v_lora, rwkv_mu[i], rwkv_w_rkv[i], rwkv_w0[i], rwkv_w1[i],
                                         rwkv_w2[i], rwkv_a0[i], rwkv_a1[i], rwkv_a2[i], rwkv_g1[i], rwkv_g2[i],
                                         rwkv_k_k[i], rwkv_k_a[i], rwkv_r_k[i], rwkv_lnx_w[i], rwkv_lnx_b[i],
                                         rwkv_w_o[i])
        x = x + mix
        h = rms_norm(x, norm_ffn[layer])
        x = x + swiglu(h, ffn_w_gate[layer], ffn_w_up[layer], ffn_w_down[layer])
    return rms_norm(x, norm_final)
```

```python
import numpy as np
import concourse.bass as bass
import concourse.mybir as mybir
from contextlib import ExitStack
from concourse.bass_utils import run_bass_kernel_spmd

F32 = mybir.dt.float32
BF16 = mybir.dt.bfloat16
I32 = mybir.dt.int32
ALU = mybir.AluOpType
AF = mybir.ActivationFunctionType
AX = mybir.AxisListType

ENGS = ("tensor", "vector", "scalar", "gpsimd", "sync")


class _Op:
    __slots__ = ("eng", "fn", "deps", "idx", "dma_key", "signal", "semval", "waits")

    def __init__(self, eng, fn, dma_key):
        self.eng = eng
        self.fn = fn
        self.deps = []
        self.dma_key = dma_key
        self.signal = False
        self.semval = None
        self.waits = []


class Prog:
    def __init__(self, nc):
        self.nc = nc
        self.ops = {e: [] for e in ENGS}
        self.last_w = {}
        self.readers = {}
        self.all_ops = []
        self.bar = {e: None for e in ENGS}
        self.last_dma = {}

    def barrier(self):
        lasts = [self.ops[e][-1] for e in ENGS if self.ops[e]] + list(self.last_dma.values())
        for e in ENGS:
            self.bar[e] = list(lasts) + (self.bar[e] or [])
        self.last_w = {}
        self.readers = {}

    def op(self, eng, fn, reads=(), writes=(), dma_key=None):
        o = _Op(eng, fn, dma_key)
        deps = []
        for r in reads:
            w = self.last_w.get(r)
            if w is not None:
                deps.append(w)
        for w_ in writes:
            w = self.last_w.get(w_)
            if w is not None:
                deps.append(w)
            deps.extend(self.readers.get(w_, ()))
        if self.bar[eng] is not None:
            deps.extend(self.bar[eng])
            self.bar[eng] = None
        if dma_key is not None:
            self.last_dma[dma_key] = o
        o.deps = deps
        o.idx = len(self.ops[eng])
        self.ops[eng].append(o)
        self.all_ops.append(o)
        for r in reads:
            self.readers.setdefault(r, []).append(o)
        for w_ in writes:
            self.last_w[w_] = o
            self.readers[w_] = []
        return o

    def dma(self, eng, out, in_, reads, writes, key, **kw):
        def fn(e, out=out, in_=in_):
            return e.dma_start(out=out, in_=in_, **kw)
        return self.op(eng, fn, reads, writes, dma_key=key)

    def emit(self, extra_final_wait=True):
        nc = self.nc
        dma_count = {}
        for o in self.all_ops:
            if o.dma_key is not None:
                dma_count[o.dma_key] = dma_count.get(o.dma_key, 0) + 1
                o.semval = 16 * dma_count[o.dma_key]
        seen = {e: {} for e in ENGS}
        for e in ENGS:
            for o in self.ops[e]:
                need = {}
                for d in o.deps:
                    if d.dma_key is not None:
                        k = ("D", d.dma_key)
                        v = d.semval
                    else:
                        if d.eng == e and (d.idx >= o.idx or e == "tensor"):
                            continue
                        k = ("E", d.eng)
                        v = d.idx + 1
                    if v > need.get(k, (0, None))[0]:
                        need[k] = (v, d)
                for k, (v, d) in need.items():
                    if seen[e].get(k, 0) >= v:
                        continue
                    seen[e][k] = v
                    o.waits.append(d)
                    d.signal = True
        for e in ENGS:
            c = 0
            for o in self.ops[e]:
                if o.dma_key is None and o.signal:
                    c += 1
                    o.semval = c
        keys = sorted(dma_count.keys())
        with ExitStack() as st:
            if getattr(self, "persist_sems", False):
                _PH[0] += 1
                esem = {e: nc.alloc_semaphore(name=f"s{_PH[0]}_" + e) for e in ENGS}
                dsem = {k: nc.alloc_semaphore(name=f"d{_PH[0]}_" + str(k)) for k in keys}
            else:
                esem = {e: st.enter_context(nc.semaphore("s_" + e)) for e in ENGS}
                dsem = {k: st.enter_context(nc.semaphore("d_" + str(k))) for k in keys}
            block = st.enter_context(nc.Block())

            def mk(e):
                def body(eng):
                    for o in self.ops[e]:
                        for d in o.waits:
                            if d.dma_key is not None:
                                eng.wait_ge(dsem[d.dma_key], d.semval)
                            else:
                                eng.wait_ge(esem[d.eng], d.semval)
                        ins = o.fn(eng)
                        if ins is None:
                            continue
                        if o.dma_key is not None:
                            ins.then_inc(dsem[o.dma_key], 16)
                        elif o.signal:
                            ins.then_inc(esem[e], 1)
                return body

            for e in ENGS:
                if self.ops[e]:
                    getattr(block, e)(mk(e))
        return {e: len(self.ops[e]) for e in ENGS}


D = 1024
DFF = 2816
EPS = 1e-5
NH_ROT = 39
NV = 15
INW = 3456
PI = float(np.pi)
ROPE_THETA = 500000.0


_PH = [0]


def _alloc(nc, st):
    _PH[0] += 1
    pre = f"p{_PH[0]}_"
    sb = lambda name, shape, dt: st.enter_context(nc.sbuf_tensor(pre + name, shape, dt))
    ps = lambda name, shape, dt: st.enter_context(nc.psum_tensor(pre + name, shape, dt))
    return sb, ps


def emit_consts(P, nc, sb):
    c = {}
    c["identf"] = sb("identf", [128, 128], F32)
    c["ident"] = sb("ident", [128, 128], BF16)
    c["epsb"] = sb("epsb", [128, 1], F32)
    identf, ident, epsb = c["identf"], c["ident"], c["epsb"]
    P.op("gpsimd", lambda e: e.memset(identf[:], 0.0), [], ["identf"])
    P.op("gpsimd", lambda e: e.affine_select(out=identf[:], in_=identf[:], pattern=[[-1, 128]], compare_op=ALU.not_equal,
                                             fill=1.0, base=0, channel_multiplier=1), ["identf"], ["identf"])
    P.op("vector", lambda e: e.tensor_copy(out=ident[:], in_=identf[:]), ["identf"], ["ident"])
    P.op("vector", lambda e: e.memset(epsb[:], EPS), [], ["epsb"])
    return c


def emit_norm_T(P, c, x, xn, ss, rstd, hb, ptr, hT_dst, hT_name, b, eps_name="epsb"):
    epsb, ident = c["epsb"], c["ident"]
    P.op("scalar", lambda e: e.activation(out=hb[:], in_=x[:], func=AF.Square, accum_out=ss[:]), [xn], [f"hb{b}", f"ss{b}"])
    P.op("scalar", lambda e: e.activation(out=rstd[:], in_=ss[:], func=AF.Sqrt, bias=epsb[:], scale=1.0 / D), [f"ss{b}", "epsb"], [f"rstd{b}"])
    P.op("vector", lambda e: e.reciprocal(out=rstd[:], in_=rstd[:]), [f"rstd{b}"], [f"rstd{b}"])
    P.op("vector", lambda e: e.tensor_scalar(out=hb[:], in0=x[:], scalar1=rstd[:], scalar2=None, op0=ALU.mult), [xn, f"rstd{b}"], [f"hb{b}"])
    for cc in range(8):
        P.op("tensor", lambda e, cc=cc: e.transpose(out=ptr[:, cc, :], in_=hb[:, cc * 128:(cc + 1) * 128], identity=ident[:]),
             [f"hb{b}", "ident"], ["ptr"])
    P.op("scalar", lambda e: e.copy(out=hT_dst, in_=ptr[:]), ["ptr"], [hT_name])


def load_weight_bf16(P, nc, w_rows, wb_dst_fn, ncols, stg, gcol_fn, names_fn, k0=0, piece=1408):
    k = k0
    for c, wr in enumerate(w_rows):
        for h0 in range(0, ncols, piece):
            h1 = min(ncols, h0 + piece)
            s = stg[k % 2]
            sn = f"stg{k % 2}"
            P.dma("sync", s[:, 0:h1 - h0], wr[:, h0:h1], [], [sn], sn)
            eng = "vector" if k % 2 == 0 else "gpsimd"
            g = gcol_fn(c) if gcol_fn is not None else None
            if g is not None:
                P.op(eng, lambda e, s=s, c=c, h0=h0, h1=h1, g=g: e.tensor_scalar(out=wb_dst_fn(c)[:, h0:h1], in0=s[:, 0:h1 - h0], scalar1=g, scalar2=None, op0=ALU.mult),
                     [sn, "gcol"], [names_fn(c)])
            else:
                P.op(eng, lambda e, s=s, c=c, h0=h0, h1=h1: e.tensor_copy(out=wb_dst_fn(c)[:, h0:h1], in_=s[:, 0:h1 - h0]), [sn], [names_fn(c)])
            k += 1
    return k


def emit_ffn(P, nc, st, x_in, gvec, wg, wu, wd, x_out, T, TG=256, final_g=None):
    NJ = TG // 128
    NF = DFF // 128
    sb, ps = _alloc(nc, st)
    c = emit_consts(P, nc, sb)
    wgb = sb("wgb", [128, 8, DFF], BF16)
    wub = sb("wub", [128, 8, DFF], BF16)
    wdb = sb("wdb", [128, NF, D], BF16)
    stg = [sb(f"stg{i}", [128, 1408], F32) for i in range(2)]
    gcol = sb("gcol", [128, 8], F32)
    P.dma("sync", gcol[:], gvec.rearrange("(c p) -> p c", p=128), [], ["gcol"], "gcol", allow_slow_non_contiguous=True)
    k = load_weight_bf16(P, nc, [wg[cc * 128:(cc + 1) * 128, :] for cc in range(8)], lambda cc: wgb[:, cc, :], DFF, stg,
                         lambda cc: gcol[:, cc:cc + 1], lambda cc: f"wgb{cc}")
    k = load_weight_bf16(P, nc, [wu[cc * 128:(cc + 1) * 128, :] for cc in range(8)], lambda cc: wub[:, cc, :], DFF, stg,
                         lambda cc: gcol[:, cc:cc + 1], lambda cc: f"wub{cc}", k0=k)
    k = load_weight_bf16(P, nc, [wd[f * 128:(f + 1) * 128, :] for f in range(NF)], lambda f: wdb[:, f, :], D, stg,
                         None, lambda f: f"wdb{f}", k0=k)
    if final_g is not None:
        gfin = sb("gfin", [128, D], F32)
        P.dma("sync", gfin[:], final_g.partition_broadcast(128), [], ["gfin"], "gfin")

    xt = [sb(f"xt{j}", [128, D], F32) for j in range(2 * NJ)]
    hb = [sb(f"hb{j}", [128, D], BF16) for j in range(2)]
    ss = [sb(f"ss{j}", [128, 1], F32) for j in range(2)]
    rstd = [sb(f"rstd{j}", [128, 1], F32) for j in range(2)]
    hT = [sb(f"hT{i}", [128, 8, TG], BF16) for i in range(2)]
    act = sb("act", [128, NF, TG], BF16)
    sg = [sb(f"sg{i}", [128, TG], F32) for i in range(2)]
    ptr = ps("ptr", [128, 8, 128], BF16)
    pg = [ps(f"pg{i}", [128, 512], F32) for i in range(2)]
    pu = [ps(f"pu{i}", [128, 512], F32) for i in range(2)]
    po = [ps(f"po{i}", [128, 512], F32) for i in range(2)]

    ng = T // TG
    cnt = fcnt = ocnt = 0
    outs = []
    for g in range(ng):
        hTg = hT[g % 2]
        hTn = f"hT{g % 2}"
        for j in range(NJ):
            xi = (g % 2) * NJ + j
            x, xn = xt[xi], f"xt{xi}"
            t0 = g * TG + j * 128
            P.dma("sync", x[:], x_in[t0:t0 + 128, :], [], [xn], xn)
            b = cnt % 2
            cnt += 1
            emit_norm_T(P, c, x, xn, ss[b], rstd[b], hb[b], ptr, hTg[:, :, j * 128:(j + 1) * 128], hTn + f"_{j}", b)
        hreads = [hTn + f"_{j}" for j in range(NJ)]
        for f in range(NF):
            b = fcnt % 2
            fcnt += 1
            for cc in range(8):
                P.op("tensor", lambda e, b=b, cc=cc, f=f, hTg=hTg: e.matmul(out=pg[b][:, 0:TG], lhsT=wgb[:, cc, f * 128:(f + 1) * 128], rhs=hTg[:, cc, :], start=(cc == 0), stop=(cc == 7)),
                     hreads + [f"wgb{cc}"], [f"pg{b}"])
            for cc in range(8):
                P.op("tensor", lambda e, b=b, cc=cc, f=f, hTg=hTg: e.matmul(out=pu[b][:, 0:TG], lhsT=wub[:, cc, f * 128:(f + 1) * 128], rhs=hTg[:, cc, :], start=(cc == 0), stop=(cc == 7)),
                     hreads + [f"wub{cc}"], [f"pu{b}"])
            P.op("scalar", lambda e, b=b: e.activation(out=sg[b][:], in_=pg[b][:, 0:TG], func=AF.Silu), [f"pg{b}"], [f"sg{b}"])
            P.op("vector", lambda e, b=b, f=f: e.tensor_tensor(out=act[:, f, :], in0=sg[b][:], in1=pu[b][:, 0:TG], op=ALU.mult), [f"sg{b}", f"pu{b}"], [f"act{f}"])
        for j in range(NJ):
            xi = (g % 2) * NJ + j
            x, xn = xt[xi], f"xt{xi}"
            t0 = g * TG + j * 128
            for n in range(2):
                b = ocnt % 2
                ocnt += 1
                for f in range(NF):
                    P.op("tensor", lambda e, b=b, f=f, j=j, n=n: e.matmul(out=po[b][:], lhsT=act[:, f, j * 128:(j + 1) * 128], rhs=wdb[:, f, n * 512:(n + 1) * 512], start=(f == 0), stop=(f == NF - 1)),
                         [f"act{f}", f"wdb{f}"], [f"po{b}"])
                P.op("vector", lambda e, b=b, x=x, n=n: e.tensor_tensor(out=x[:, n * 512:(n + 1) * 512], in0=po[b][:], in1=x[:, n * 512:(n + 1) * 512], op=ALU.add), [f"po{b}", xn], [xn])
            if final_g is not None:
                b = cnt % 2
                cnt += 1
                P.op("scalar", lambda e, x=x, b=b: e.activation(out=hb[b][:], in_=x[:], func=AF.Square, accum_out=ss[b][:]), [xn], [f"hb{b}", f"ss{b}"])
                P.op("scalar", lambda e, b=b: e.activation(out=rstd[b][:], in_=ss[b][:], func=AF.Sqrt, bias=c["epsb"][:], scale=1.0 / D), [f"ss{b}", "epsb"], [f"rstd{b}"])
                P.op("vector", lambda e, b=b: e.reciprocal(out=rstd[b][:], in_=rstd[b][:]), [f"rstd{b}"], [f"rstd{b}"])
                P.op("vector", lambda e, x=x, b=b: e.scalar_tensor_tensor(out=x[:], in0=x[:], scalar=rstd[b][:], in1=gfin[:], op0=ALU.mult, op1=ALU.mult), [xn, f"rstd{b}", "gfin"], [xn])
            P.dma("gpsimd", x_out[t0:t0 + 128, :], x[:], [xn], [f"out_{t0}"], xn + "o")
            outs.append(f"out_{t0}")
    P.op("sync", lambda e: None, outs, [])
    P.op("gpsimd", lambda e: None, outs, [])


def emit_attn_proj(P, nc, st, x_in, pos, gvec, w_in, b_in, qkT_out, v_out, T):
    sb, ps = _alloc(nc, st)
    c = emit_consts(P, nc, sb)
    NT = T // 128
    winb = sb("winb", [128, 8, INW], BF16)
    stg = [sb(f"stg{i}", [128, 1728], F32) for i in range(2)]
    gcol = sb("gcol", [128, 8], F32)
    P.dma("sync", gcol[:], gvec.rearrange("(c p) -> p c", p=128), [], ["gcol"], "gcol", allow_slow_non_contiguous=True)
    load_weight_bf16(P, nc, [w_in[cc * 128:(cc + 1) * 128, :] for cc in range(8)], lambda cc: winb[:, cc, :], INW, stg,
                     lambda cc: gcol[:, cc:cc + 1], lambda cc: f"winb{cc}", piece=1728)
    biasb = sb("biasb", [1, INW], BF16)
    ones1 = sb("ones1", [1, 128], BF16)
    b2 = b_in.rearrange("(o n) -> o n", o=1)
    for hi in range(2):
        P.dma("sync", stg[hi][0:1, :], b2[:, hi * 1728:(hi + 1) * 1728], [], [f"stg{hi}"], f"stg{hi}")
        P.op("vector", lambda e, hi=hi: e.tensor_copy(out=biasb[:, hi * 1728:(hi + 1) * 1728], in_=stg[hi][0:1, :]), [f"stg{hi}"], ["biasb"])
    P.op("vector", lambda e: e.memset(ones1[:], 1.0), [], ["ones1"])
    posi = sb("posi", [128, NT], I32)
    posf = sb("posf", [128, NT], F32)
    ftab = sb("ftab", [128, 32], F32)
    otab = sb("otab", [128, 32], F32)
    ang = sb("ang", [128, NT, 32], F32)
    cs = sb("cs", [128, NT, 32], F32)
    qkv = [sb(f"qkv{i}", [128, INW], F32) for i in range(2)]
    ang2 = qkv[0][:, 0:NT * 32].rearrange("p (n f) -> p n f", f=32)
    angi = qkv[1][:, 0:NT * 32].bitcast(I32).rearrange("p (n f) -> p n f", f=32)
    P.dma("sync", posi[:], pos.rearrange("(n p) -> p n", p=128), [], ["posi"], "posi", allow_slow_non_contiguous=True)
    P.op("vector", lambda e: e.tensor_copy(out=posf[:], in_=posi[:]), ["posi"], ["posf"])
    ft3 = ftab[:].rearrange("p (a i) -> p a i", i=8)
    for i in range(8):
        fi = float(np.power(np.float32(ROPE_THETA), np.float32(-2.0 * i / 16.0)))
        P.op("gpsimd", lambda e, i=i, fi=fi: e.memset(ft3[:, :, i], fi), [], ["ftab"])
    for a, off in enumerate((PI / 2, PI / 2, PI, 0.0)):
        P.op("gpsimd", lambda e, a=a, off=off: e.memset(otab[:, a * 8:(a + 1) * 8], off), [], ["otab"])
    P.op("vector", lambda e: e.tensor_tensor(out=ang[:], in0=ftab[:].unsqueeze(1).broadcast_to([128, NT, 32]),
                                             in1=posf[:].unsqueeze(2).broadcast_to([128, NT, 32]), op=ALU.mult), ["ftab", "posf"], ["ang"])
    P.op("vector", lambda e: e.tensor_tensor(out=ang[:], in0=ang[:], in1=otab[:].unsqueeze(1).broadcast_to([128, NT, 32]), op=ALU.add), ["ang", "otab"], ["ang"])
    P.op("vector", lambda e: e.tensor_scalar(out=angi[:], in0=ang[:], scalar1=1.0 / (2 * PI), scalar2=None, op0=ALU.mult), ["ang"], ["angi"])
    P.op("vector", lambda e: e.tensor_copy(out=ang2[:], in_=angi[:]), ["angi"], ["ang2"])
    P.op("vector", lambda e: e.scalar_tensor_tensor(out=ang[:], in0=ang2[:], scalar=-2 * PI, in1=ang[:], op0=ALU.mult, op1=ALU.add), ["ang2", "ang"], ["ang"])
    P.op("vector", lambda e: e.tensor_scalar(out=ang2[:], in0=ang[:], scalar1=PI, scalar2=-2 * PI, op0=ALU.is_gt, op1=ALU.mult), ["ang"], ["ang2"])
    P.op("vector", lambda e: e.tensor_tensor(out=ang[:], in0=ang[:], in1=ang2[:], op=ALU.add), ["ang", "ang2"], ["ang"])
    P.op("vector", lambda e: e.tensor_scalar(out=ang2[:], in0=ang[:], scalar1=-PI, scalar2=2 * PI, op0=ALU.is_lt, op1=ALU.mult), ["ang"], ["ang2"])
    P.op("vector", lambda e: e.tensor_tensor(out=ang[:], in0=ang[:], in1=ang2[:], op=ALU.add), ["ang", "ang2"], ["ang"])
    P.op("scalar", lambda e: e.activation(out=cs[:], in_=ang[:], func=AF.Sin), ["ang"], ["cs"])

    xt = [sb(f"xt{j}", [128, D], F32) for j in range(2)]
    hb = [sb(f"hb{j}", [128, D], BF16) for j in range(2)]
    ss = [sb(f"ss{j}", [128, 1], F32) for j in range(2)]
    rstd = [sb(f"rstd{j}", [128, 1], F32) for j in range(2)]
    hT = [sb(f"hT{i}", [128, 8, 128], BF16) for i in range(2)]
    P.barrier()
    tA = sb("tA", [128, NH_ROT, 16], F32)
    tB = sb("tB", [128, NH_ROT, 16], F32)
    qkb = [sb(f"qkb{i}", [128, NH_ROT * 64], BF16) for i in range(2)]
    vb = [sb(f"vb{i}", [128, NV * 64], BF16) for i in range(2)]
    qkTs = [sb(f"qkTs{i}", [64, NH_ROT, 256], BF16) for i in range(2)]
    ptr = ps("ptr", [128, 8, 128], BF16)
    pp = [ps(f"pp{i}", [128, 512], F32) for i in range(2)]
    ptq = [ps(f"ptq{i}", [64, 8, 128], BF16) for i in range(2)]
    ident = c["ident"]
    pcnt = 0
    qcnt = 0
    outs = []
    for t in range(NT):
        b = t % 2
        x, xn = xt[b], f"xt{b}"
        t0 = t * 128
        P.dma("sync", x[:], x_in[t0:t0 + 128, :], [], [xn], xn)
        emit_norm_T(P, c, x, xn, ss[b], rstd[b], hb[b], ptr, hT[b][:], f"hT{b}", b)
        for n0 in range(0, INW, 512):
            n1 = min(INW, n0 + 512)
            pb = pcnt % 2
            pcnt += 1
            for cc in range(8):
                P.op("tensor", lambda e, pb=pb, cc=cc, n0=n0, n1=n1, b=b: e.matmul(out=pp[pb][:, 0:n1 - n0], lhsT=hT[b][:, cc, :], rhs=winb[:, cc, n0:n1], start=(cc == 0), stop=False),
                     [f"hT{b}", f"winb{cc}"], [f"pp{pb}"])
            P.op("tensor", lambda e, pb=pb, n0=n0, n1=n1: e.matmul(out=pp[pb][:, 0:n1 - n0], lhsT=ones1[:], rhs=biasb[:, n0:n1], start=False, stop=True),
                 ["ones1", "biasb"], [f"pp{pb}"])
            eng = "scalar" if (pcnt % 2 == 0) else "vector"
            if eng == "scalar":
                P.op("scalar", lambda e, pb=pb, n0=n0, n1=n1, b=b: e.copy(out=qkv[b][:, n0:n1], in_=pp[pb][:, 0:n1 - n0]), [f"pp{pb}"], [f"qkv{b}"])
            else:
                P.op("vector", lambda e, pb=pb, n0=n0, n1=n1, b=b: e.tensor_copy(out=qkv[b][:, n0:n1], in_=pp[pb][:, 0:n1 - n0]), [f"pp{pb}"], [f"qkv{b}"])
        Q3 = qkv[b][:, 0:NH_ROT * 64].rearrange("p (h d) -> p h d", d=64)
        O3 = qkb[b][:].rearrange("p (h d) -> p h d", d=64)
        cst = cs[:, t, :]
        P.op("gpsimd", lambda e, Q3=Q3, cst=cst: e.tensor_tensor(out=tA[:], in0=Q3[:, :, 0:16], in1=cst[:, 0:16].unsqueeze(1).broadcast_to([128, NH_ROT, 16]), op=ALU.mult), [f"qkv{b}", "cs"], ["tA"])
        P.op("vector", lambda e, Q3=Q3, cst=cst: e.tensor_tensor(out=tB[:, :, 0:8], in0=Q3[:, :, 8:16], in1=cst[:, 16:24].unsqueeze(1).broadcast_to([128, NH_ROT, 8]), op=ALU.mult), [f"qkv{b}", "cs"], ["tB0"])
        P.op("vector", lambda e, Q3=Q3, cst=cst: e.tensor_tensor(out=tB[:, :, 8:16], in0=Q3[:, :, 0:8], in1=cst[:, 24:32].unsqueeze(1).broadcast_to([128, NH_ROT, 8]), op=ALU.mult), [f"qkv{b}", "cs"], ["tB1"])
        P.op("vector", lambda e, O3=O3: e.tensor_tensor(out=O3[:, :, 0:16], in0=tA[:], in1=tB[:], op=ALU.add), ["tA", "tB0", "tB1"], [f"qkb{b}r"])
        P.op("scalar", lambda e, O3=O3, Q3=Q3: e.copy(out=O3[:, :, 16:64], in_=Q3[:, :, 16:64]), [f"qkv{b}"], [f"qkb{b}c"])
        P.op("gpsimd", lambda e, b=b: e.tensor_copy(out=vb[b][:], in_=qkv[b][:, NH_ROT * 64:INW]), [f"qkv{b}"], [f"vb{b}"])
        P.dma("gpsimd", v_out[t0:t0 + 128, :], vb[b][:], [f"vb{b}"], [f"vout{t}"], f"vb{b}o")
        outs.append(f"vout{t}")
        sbi = (t // 2) % 2
        so = (t % 2) * 128
        for h0 in range(0, NH_ROT, 8):
            h1 = min(NH_ROT, h0 + 8)
            qb_ = qcnt % 2
            qcnt += 1
            for h in range(h0, h1):
                P.op("tensor", lambda e, qb_=qb_, h=h, h0=h0, b=b: e.transpose(out=ptq[qb_][:, h - h0, :], in_=qkb[b][:, h * 64:(h + 1) * 64], identity=ident[:]),
                     [f"qkb{b}r", f"qkb{b}c", "ident"], [f"ptq{qb_}"])
            eng = "scalar" if (qcnt % 2 == 0) else "vector"
            if eng == "scalar":
                P.op("scalar", lambda e, qb_=qb_, h0=h0, h1=h1, sbi=sbi, so=so: e.copy(out=qkTs[sbi][:, h0:h1, so:so + 128], in_=ptq[qb_][:, 0:h1 - h0, :]), [f"ptq{qb_}"], [f"qkTs{sbi}_{t % 2}_{h0}"])
            else:
                P.op("vector", lambda e, qb_=qb_, h0=h0, h1=h1, sbi=sbi, so=so: e.tensor_copy(out=qkTs[sbi][:, h0:h1, so:so + 128], in_=ptq[qb_][:, 0:h1 - h0, :]), [f"ptq{qb_}"], [f"qkTs{sbi}_{t % 2}_{h0}"])
        if t % 2 == 1:
            tt0 = (t - 1) * 128
            rd = [f"qkTs{sbi}_{u}_{h0}" for u in range(2) for h0 in range(0, NH_ROT, 8)]
            for qi, (ha, hb_) in enumerate(((0, 13), (13, 26), (26, 39))):
                P.dma("sync", qkT_out[ha:hb_, :, tt0:tt0 + 256].rearrange("h d t -> d h t"), qkTs[sbi][:, ha:hb_, :], rd, [f"qout{t}_{qi}"], f"qkTs{sbi}o{qi}")
                outs.append(f"qout{t}_{qi}")
    P.op("sync", lambda e: None, outs, [])
    P.op("gpsimd", lambda e: None, outs, [])


def emit_attn_core(P, nc, st, qkT, v, sinks, oT_out, S):
    sb, ps = _alloc(nc, st)
    CH = 2048
    NCH = S // CH
    ones64 = sb("ones64", [128, 64], BF16)
    mfull = sb("mfull", [128, 4, 128], F32)
    m_own = sb("m_own", [128, 4, 128], BF16)
    m_pA = sb("m_pA", [128, 4, 128], BF16)
    m_pB = sb("m_pB", [128, 4, 128], BF16)
    P.op("vector", lambda e: e.memset(ones64[:], 1.0), [], ["ones64"])
    for (m, cm, pat, cmp_, nm) in ((m_own, -1, 1, ALU.is_ge, "m_own"), (m_pA, 1, -1, ALU.is_gt, "m_pA"), (m_pB, 1, -1, ALU.is_ge, "m_pB")):
        P.op("gpsimd", lambda e: e.memset(mfull[:], 1.0), [], ["mfull"])
        P.op("gpsimd", lambda e, cm=cm, pat=pat, cmp_=cmp_: e.affine_select(out=mfull[:], in_=mfull[:], pattern=[[0, 4], [pat, 128]], compare_op=cmp_,
                                                                           fill=0.0, base=0, channel_multiplier=cm), ["mfull"], ["mfull"])
        P.op("gpsimd", lambda e, m=m: e.tensor_copy(out=m[:], in_=mfull[:]), ["mfull"], [nm])
    sk = sb("sk", [64, 12], F32)
    esink = sb("esink", [64, 12, 128], F32)
    P.dma("sync", sk[:], sinks.partition_broadcast(64), [], ["sk"], "sk")
    P.op("scalar", lambda e: e.activation(out=sk[:], in_=sk[:], func=AF.Exp), ["sk"], ["sk"])
    P.op("vector", lambda e: e.memset(esink[:], 0.0), [], ["esink"])
    for h in range(12):
        P.op("vector", lambda e, h=h: e.tensor_scalar(out=esink[:, h, :], in0=esink[:, h, :], scalar1=sk[:, h:h + 1], scalar2=None, op0=ALU.add), ["esink", "sk"], ["esink"])

    qt = sb("qt", [64, 4, CH], BF16)
    kt = sb("kt", [64, 4, 2 * CH], BF16)
    vt = sb("vt", [128, 32, 256], BF16)
    vA = sb("vA", [128, 17, 64], BF16)
    accn = sb("accn", [64, 4, CH], F32)
    accd = sb("accd", [64, 4, CH], F32)
    pT = [sb(f"pT{i}", [128, 4, 128], BF16) for i in range(4)]
    tmp = [sb(f"tmp{i}", [64, 4, 128], F32) for i in range(2)]
    oA = [sb(f"oA{i}", [64, 4, 128], BF16) for i in range(2)]
    oB = [sb(f"oB{i}", [64, 4, 512], BF16) for i in range(2)]
    rdB = sb("rdB", [64, 4, 512], F32)
    psc = [ps(f"psc{i}", [128, 4, 128], F32) for i in range(4)]
    pnum = [ps(f"pnum{i}", [64, 4, 128], F32) for i in range(2)]
    pden = [ps(f"pden{i}", [64, 4, 128], F32) for i in range(2)]
    outs = []
    ucnt = 0

    def unit(kind, g, c, first, score_fn, pv_fn, out_fn):
        nonlocal ucnt
        u = ucnt % 2
        ucnt += 1
        kbs = (0,) if first else (0, 1)
        for kb in kbs:
            si = 2 * u + kb
            score_fn(kb, psc[si], f"psc{si}")
            P.op("scalar", lambda e, si=si: e.activation(out=pT[si][:], in_=psc[si][:], func=AF.Exp, scale=0.125), [f"psc{si}"], [f"pT{si}"])
            m, mn = (m_own, "m_own") if kb == 0 else ((m_pA, "m_pA") if kind == "A" else (m_pB, "m_pB"))
            P.op("gpsimd", lambda e, si=si, m=m: e.tensor_tensor(out=pT[si][:], in0=pT[si][:], in1=m[:], op=ALU.mult), [f"pT{si}", mn], [f"pT{si}"])
        pv_fn(kbs, [pT[2 * u + kb] for kb in kbs], [f"pT{2 * u + kb}" for kb in kbs], pnum[u], f"pnum{u}")
        for i, kb in enumerate(kbs):
            P.op("tensor", lambda e, u=u, kb=kb, i=i: e.matmul(out=pden[u][:], lhsT=ones64[:], rhs=pT[2 * u + kb][:], start=(i == 0), stop=(i == len(kbs) - 1)),
                 [f"pT{2 * u + kb}", "ones64"], [f"pden{u}"])
        out_fn(u)

    for c in range(NCH):
        c0 = c * CH
        for g in range(3):
            P.dma("sync", qt[:], qkT[4 * g:4 * g + 4, :, c0:c0 + CH].rearrange("h d t -> d h t"), [], ["qt"], "qt")
            if c == 0:
                P.dma("sync", kt[:, 0, CH:2 * CH], qkT[12 + g, :, c0:c0 + CH], [], ["kt"], "kt")
                for n0 in (0, 8):
                    P.dma("sync", vA[:, 1 + n0:9 + n0, :], v[c0 + n0 * 128:c0 + (n0 + 8) * 128, g * 64:(g + 1) * 64].rearrange("(n p) d -> p n d", p=128), [], ["vA"], f"vA{n0}")
            else:
                P.dma("sync", kt[:, 0, CH - 128:2 * CH], qkT[12 + g, :, c0 - 128:c0 + CH], [], ["kt"], "kt")
                for n0, n1 in ((0, 9), (9, 17)):
                    P.dma("sync", vA[:, n0:n1, :], v[c0 - 128 + n0 * 128:c0 - 128 + n1 * 128, g * 64:(g + 1) * 64].rearrange("(n p) d -> p n d", p=128), [], ["vA"], f"vA{n0}")
            for i in range(16):
                first = (c == 0 and i == 0)

                def score_fn(kb, pt, pn, i=i):
                    k0 = CH + (i - kb) * 128
                    P.op("tensor", lambda e: e.matmul(out=pt[:], lhsT=kt[:, 0, k0:k0 + 128], rhs=qt[:, :, i * 128:(i + 1) * 128], start=True, stop=True),
                         ["qt", "kt"], [pn])

                def pv_fn(kbs, pts, ptn, pn_, pnn, i=i):
                    for ii, kb in enumerate(kbs):
                        P.op("tensor", lambda e, ii=ii, kb=kb: e.matmul(out=pn_[:], lhsT=vA[:, i + 1 - kb, :], rhs=pts[ii][:], start=(ii == 0), stop=(ii == len(kbs) - 1)),
                             [ptn[ii], "vA"], [pnn])

                def out_fn(u, i=i, g=g, c0=c0):
                    P.op("vector", lambda e: e.tensor_tensor(out=tmp[u][:], in0=pden[u][:], in1=esink[:, 4 * g:4 * g + 4, :], op=ALU.add), [f"pden{u}", "esink"], [f"tmp{u}"])
                    P.op("vector", lambda e: e.reciprocal(out=tmp[u][:], in_=tmp[u][:]), [f"tmp{u}"], [f"tmp{u}"])
                    P.op("vector", lambda e: e.tensor_tensor(out=oA[u][:], in0=pnum[u][:], in1=tmp[u][:], op=ALU.mult), [f"pnum{u}", f"tmp{u}"], [f"oA{u}"])
                    nm = f"oo_{c0}_{g}_{i}"
                    P.dma("gpsimd", oT_out[4 * g:4 * g + 4, :, c0 + i * 128:c0 + (i + 1) * 128].rearrange("h d t -> d h t"), oA[u][:], [f"oA{u}"], [nm], f"oA{u}o")
                    outs.append(nm)

                unit("A", g, c, first, score_fn, pv_fn, out_fn)
        for p, d in enumerate((1, 4, 16)):
            halo = 128 * d
            NIq = CH // (128 * d)
            P.dma("sync", qt[:], qkT[15 + 4 * p:19 + 4 * p, :, c0:c0 + CH].rearrange("h d t -> d h t"), [], ["qt"], "qt")
            lo = 0 if c == 0 else halo
            P.dma("sync", kt[:, :, CH - lo:2 * CH], qkT[27 + 4 * p:31 + 4 * p, :, c0 - lo:c0 + CH].rearrange("h d t -> d h t"), [], ["kt"], "kt")
            vt4 = vt[:, 0:(NIq + 1) * d, :].rearrange("j (I r) f -> j I r f", r=d)
            fcol = 192 + p * 256
            for I in range(0 if c > 0 else 1, NIq + 1):
                tok0 = c0 - halo + I * 128 * d
                for r0 in range(0, d, 8):
                    r1 = min(d, r0 + 8)
                    P.dma("sync", vt4[:, I, r0:r1, :], v[tok0:tok0 + 128 * d, fcol:fcol + 256].rearrange("(j r) f -> j r f", r=d)[:, r0:r1, :], [], [f"vt"], f"vt{(I + r0 // 8) % 8}")
            qv = [qt[:, h, :].rearrange("p (I j r) -> p I j r", j=128, r=d) for h in range(4)]
            kv = [kt[:, h, CH - halo:2 * CH].rearrange("p (I j r) -> p I j r", j=128, r=d) for h in range(4)]
            accn4 = [accn[:, h, :].rearrange("p (I j r) -> p I j r", j=128, r=d) for h in range(4)]
            accd4 = [accd[:, h, :].rearrange("p (I j r) -> p I j r", j=128, r=d) for h in range(4)]
            for Iq in range(NIq):
                for r in range(d):
                    first = (c == 0 and Iq == 0)

                    def score_fn(kb, pt, pn, Iq=Iq, r=r, kv=kv, qv=qv):
                        for h in range(4):
                            P.op("tensor", lambda e, h=h: e.matmul(out=pt[:, h, :], lhsT=kv[h][:, Iq + 1 - kb, :, r], rhs=qv[h][:, Iq, :, r], start=True, stop=True),
                                 ["qt", "kt"], [pn])

                    def pv_fn(kbs, pts, ptn, pn_, pnn, Iq=Iq, r=r, vt4=vt4):
                        for h in range(4):
                            for ii, kb in enumerate(kbs):
                                P.op("tensor", lambda e, h=h, ii=ii, kb=kb: e.matmul(out=pn_[:, h, :], lhsT=vt4[:, Iq + 1 - kb, r, h * 64:(h + 1) * 64], rhs=pts[ii][:, h, :],
                                                                                     start=(ii == 0), stop=(ii == len(kbs) - 1)),
                                     [ptn[ii], "vt"], [pnn])

                    def out_fn(u, Iq=Iq, r=r, p=p, accn4=accn4, accd4=accd4):
                        for h in range(4):
                            if p == 0:
                                P.op("scalar", lambda e, h=h: e.copy(out=accn4[h][:, Iq, :, r], in_=pnum[u][:, h, :]), [f"pnum{u}"], ["accn"])
                                P.op("vector", lambda e, h=h: e.tensor_copy(out=accd4[h][:, Iq, :, r], in_=pden[u][:, h, :]), [f"pden{u}"], ["accd"])
                            else:
                                P.op("vector", lambda e, h=h: e.tensor_tensor(out=accn4[h][:, Iq, :, r], in0=pnum[u][:, h, :], in1=accn4[h][:, Iq, :, r], op=ALU.add), [f"pnum{u}", "accn"], ["accn"])
                                P.op("vector", lambda e, h=h: e.tensor_tensor(out=accd4[h][:, Iq, :, r], in0=pden[u][:, h, :], in1=accd4[h][:, Iq, :, r], op=ALU.add), [f"pden{u}", "accd"], ["accd"])

                    unit("B", p, c, first, score_fn, pv_fn, out_fn)
        for q in range(CH // 512):
            u = q % 2
            P.op("vector", lambda e, q=q: e.reciprocal(out=rdB[:], in_=accd[:, :, q * 512:(q + 1) * 512]), ["accd"], ["rdB"])
            P.op("vector", lambda e, q=q, u=u: e.tensor_tensor(out=oB[u][:], in0=accn[:, :, q * 512:(q + 1) * 512], in1=rdB[:], op=ALU.mult), ["accn", "rdB"], [f"oB{u}"])
            nm = f"oob_{c0}_{q}"
            P.dma("gpsimd", oT_out[12:16, :, c0 + q * 512:c0 + (q + 1) * 512].rearrange("h d t -> d h t"), oB[u][:], [f"oB{u}"], [nm], f"oB{u}o")
            outs.append(nm)
    P.op("sync", lambda e: None, outs, [])
    P.op("gpsimd", lambda e: None, outs, [])


def emit_head_outproj(P, nc, st, x_in, oT, w_out, x_out, T, nheads=16):
    sb, ps = _alloc(nc, st)
    wob = sb("wob", [64, nheads, D], BF16)
    stg = [sb(f"stg{i}", [64, D], F32) for i in range(2)]
    w3 = w_out.rearrange("(h d) n -> d h n", d=64)
    for h in range(nheads):
        s, sn = stg[h % 2], f"stg{h % 2}"
        P.dma("sync", s[:], w3[:, h, :], [], [sn], sn)
        P.op("vector" if h % 2 == 0 else "gpsimd", lambda e, s=s, h=h: e.tensor_copy(out=wob[:, h, :], in_=s[:]), [sn], [f"wob{h}"])
    TG = 512
    oTt = [sb(f"oTt{i}", [64, nheads, TG], BF16) for i in range(2)]
    xt = [sb(f"xt{i}", [128, D], F32) for i in range(4)]
    po = [ps(f"po{i}", [128, 512], F32) for i in range(4)]
    outs = []
    cnt = 0
    ocnt = 0
    for g in range(T // TG):
        ob = g % 2
        for hh in range(0, nheads, 4):
            P.dma("sync", oTt[ob][:, hh:hh + 4, :], oT[hh:hh + 4, :, g * TG:(g + 1) * TG].rearrange("h d t -> d h t"), [], [f"oTt{ob}_{hh}"], f"oTt{ob}_{hh}")
        ords = [f"oTt{ob}_{hh}" for hh in range(0, nheads, 4)]
        for j in range(TG // 128):
            xi = cnt % 4
            cnt += 1
            x, xn = xt[xi], f"xt{xi}"
            t0 = g * TG + j * 128
            P.dma("sync", x[:], x_in[t0:t0 + 128, :], [], [xn], xn)
            for n in range(2):
                b = ocnt % 4
                ocnt += 1
                for h in range(nheads):
                    P.op("tensor", lambda e, b=b, h=h, j=j, n=n, ob=ob: e.matmul(out=po[b][:], lhsT=oTt[ob][:, h, j * 128:(j + 1) * 128], rhs=wob[:, h, n * 512:(n + 1) * 512],
                                                                              start=(h == 0), stop=(h == nheads - 1)), ords + [f"wob{h}"], [f"po{b}"])
                P.op("vector", lambda e, b=b, x=x, n=n: e.tensor_tensor(out=x[:, n * 512:(n + 1) * 512], in0=po[b][:], in1=x[:, n * 512:(n + 1) * 512], op=ALU.add), [f"po{b}", xn], [xn])
            P.dma("gpsimd", x_out[t0:t0 + 128, :], x[:], [xn], [f"out_{t0}"], xn + "o")
            outs.append(f"out_{t0}")
    P.op("sync", lambda e: None, outs, [])
    P.op("gpsimd", lambda e: None, outs, [])


NCORES = 8
_PERM = np.concatenate([np.arange(0, 768), np.arange(768, 960), np.arange(1152, 1920), np.arange(1920, 2688),
                        np.arange(960, 1152), np.arange(2688, 3456)])


def _run(nc, in_maps):
    res = run_bass_kernel_spmd(nc, in_maps, core_ids=list(range(NCORES)))
    return res.results


def _dram(nc, name, shape, dt, kind):
    return nc.dram_tensor(name, list(shape), dt, kind=kind).ap()


def launch_attn_proj(x, pos, gvec, w_in, b_in):
    N = x.shape[0]
    T = N // NCORES
    nc = bass.Bass("TRN2", target_bir_lowering=False)
    xa = _dram(nc, "x", [T, D], F32, "ExternalInput")
    pa = _dram(nc, "pos", [T], I32, "ExternalInput")
    ga = _dram(nc, "g", [D], F32, "ExternalInput")
    wa = _dram(nc, "w_in", [D, INW], F32, "ExternalInput")
    ba = _dram(nc, "b_in", [INW], F32, "ExternalInput")
    qo = _dram(nc, "qkT", [NH_ROT, 64, T], BF16, "ExternalOutput")
    vo = _dram(nc, "v", [T, NV * 64], BF16, "ExternalOutput")
    P = Prog(nc)
    with ExitStack() as st:
        emit_attn_proj(P, nc, st, xa, pa, ga, wa, ba, qo, vo, T)
        P.emit()
    wp = np.ascontiguousarray(w_in[:, _PERM])
    bp = np.ascontiguousarray(b_in[_PERM])
    maps = [{"x": x[i * T:(i + 1) * T], "pos": pos[i * T:(i + 1) * T], "g": gvec, "w_in": wp, "b_in": bp} for i in range(NCORES)]
    r = _run(nc, maps)
    return [ri["qkT"] for ri in r], [ri["v"] for ri in r]


def launch_attn_core(qkT_seq, v_seq, sinks, S):
    nc = bass.Bass("TRN2", target_bir_lowering=False)
    qa = _dram(nc, "qkT", [NH_ROT, 64, S], BF16, "ExternalInput")
    va = _dram(nc, "v", [S, NV * 64], BF16, "ExternalInput")
    sa = _dram(nc, "sinks", [12], F32, "ExternalInput")
    oo = _dram(nc, "oT", [16, 64, S], BF16, "ExternalOutput")
    P = Prog(nc)
    with ExitStack() as st:
        emit_attn_core(P, nc, st, qa, va, sa, oo, S)
        P.emit()
    nb = len(qkT_seq)
    maps = [{"qkT": qkT_seq[i % nb], "v": v_seq[i % nb], "sinks": sinks} for i in range(NCORES)]
    r = _run(nc, maps)
    return [r[i]["oT"] for i in range(nb)]


def launch_mix_ffn(x, oT_cores, w_out, gffn, wg, wu, wd, final_g=None):
    N = x.shape[0]
    T = N // NCORES
    nc = bass.Bass("TRN2", target_bir_lowering=False)
    xa = _dram(nc, "x", [T, D], F32, "ExternalInput")
    oa = _dram(nc, "oT", [16, 64, T], BF16, "ExternalInput")
    woa = _dram(nc, "w_out", [D, D], F32, "ExternalInput")
    ga = _dram(nc, "g", [D], F32, "ExternalInput")
    wga = _dram(nc, "wg", [D, DFF], F32, "ExternalInput")
    wua = _dram(nc, "wu", [D, DFF], F32, "ExternalInput")
    wda = _dram(nc, "wd", [DFF, D], F32, "ExternalInput")
    ya = _dram(nc, "y", [T, D], F32, "ExternalOutput")
    fa = _dram(nc, "gfin", [D], F32, "ExternalInput") if final_g is not None else None
    P1 = Prog(nc)
    with ExitStack() as st:
        emit_head_outproj(P1, nc, st, xa, oa, woa, ya, T)
        P1.emit()
    P2 = Prog(nc)
    with ExitStack() as st:
        emit_ffn(P2, nc, st, ya, ga, wga, wua, wda, ya, T, final_g=fa)
        P2.emit()
    maps = []
    for i in range(NCORES):
        m = {"x": x[i * T:(i + 1) * T], "oT": oT_cores[i], "w_out": w_out, "g": gffn, "wg": wg, "wu": wu, "wd": wd}
        if final_g is not None:
            m["gfin"] = final_g
        maps.append(m)
    r = _run(nc, maps)
    return np.concatenate([ri["y"] for ri in r], axis=0)


def attn_layer(x, pos, B, S, g_mix, w_in, b_in, sinks, w_out, g_ffn, wg, wu, wd, final_g=None):
    N = B * S
    T = N // NCORES
    qkT_c, v_c = launch_attn_proj(x, pos, g_mix, w_in, b_in)
    per = S // T
    qkT_seq = [np.concatenate(qkT_c[b * per:(b + 1) * per], axis=2) for b in range(B)]
    v_seq = [np.concatenate(v_c[b * per:(b + 1) * per], axis=0) for b in range(B)]
    oT_seq = launch_attn_core(qkT_seq, v_seq, sinks, S)
    oT_cores = [np.ascontiguousarray(oT_seq[i // per][:, :, (i % per) * T:((i % per) + 1) * T]) for i in range(NCORES)]
    return launch_mix_ffn(x, oT_cores, w_out, g_ffn, wg, wu, wd, final_g=final_g)


WSC = float(np.exp(-0.5))


def emit_rwkv_proj(P, nc, st, x_in, xh_in, gvec, pr, QG_out, PH_out, vb_out, gate_out, bonus_out, vfirst_io, T, has_vlora):
    sb, ps = _alloc(nc, st)
    c = emit_consts(P, nc, sb)
    ident, identf = c["ident"], c["identf"]
    NT = T // 128
    Et = sb("Et", [128, D], F32)
    t1 = sb("t1", [128, D], F32)
    stg = [Et, t1]
    gcol = sb("gcol", [128, 8], F32)
    P.dma("sync", gcol[:], gvec.rearrange("(c p) -> p c", p=128), [], ["gcol"], "gcol", allow_slow_non_contiguous=True)
    wrkv = [sb(f"w{nm}", [128, 8, D], BF16) for nm in "rkv"]
    k = 0
    for i in range(3):
        k = load_weight_bf16(P, nc, [pr["w_rkv"][i, cc * 128:(cc + 1) * 128, :] for cc in range(8)], lambda cc, i=i: wrkv[i][:, cc, :], D, stg,
                             lambda cc: gcol[:, cc:cc + 1], lambda cc, i=i: f"wrkv{i}_{cc}", k0=k, piece=1024)
    LW = {"w1": 64, "a1": 64, "g1": 160}
    if has_vlora:
        LW["v1"] = 32
    l1 = {}
    for nm, wd_ in LW.items():
        l1[nm] = sb("l1" + nm, [128, 8, wd_], BF16)
        k = load_weight_bf16(P, nc, [pr[nm][cc * 128:(cc + 1) * 128, :] for cc in range(8)], lambda cc, nm=nm: l1[nm][:, cc, :], wd_, stg,
                             lambda cc: gcol[:, cc:cc + 1], lambda cc, nm=nm: f"l1{nm}_{cc}", k0=k, piece=1024)
    l2 = {}
    for nm, rows in (("w2", 64), ("a2", 64), ("g2a", 128), ("g2b", 32)) + ((("v2", 32),) if has_vlora else ()):
        l2[nm] = sb("l2" + nm, [rows, D], BF16)
        src = pr["g2"][0:128, :] if nm == "g2a" else (pr["g2"][128:160, :] if nm == "g2b" else pr[nm])
        s, sn = stg[k % 2], f"stg{k % 2}"
        P.dma("sync", s[0:rows, :], src, [], [sn], sn)
        P.op("vector", lambda e, s=s, nm=nm, rows=rows: e.tensor_copy(out=l2[nm][:], in_=s[0:rows, :]), [sn], ["l2" + nm])
        k += 1
    bc = {}
    for nm in ("w0", "a0", "k_k", "k_a", "r_k") + (("v0",) if has_vlora else ()):
        bc[nm] = sb("bc" + nm, [128, D], F32)
        P.dma("sync", bc[nm][:], pr[nm].partition_broadcast(128), [], ["bc" + nm], "bc" + nm)
    mucol = sb("mucol", [128, 6, 8], F32)
    for i in range(6):
        P.dma("sync", mucol[:, i, :], pr["mu"][i].rearrange("(c p) -> p c", p=128), [], ["mucol"], f"mucol{i}", allow_slow_non_contiguous=True)
    ltri = sb("ltri", [128, 128], F32)
    sel0 = sb("sel0", [128, 128], F32)
    sel1 = sb("sel1", [128, 128], F32)
    mlo = sb("mlo", [128, 128], F32)
    mup = sb("mup", [128, 128], F32)
    mle = sb("mle", [128, 64], F32)
    addI = sb("addI", [128, 128], BF16)
    addI2 = sb("addI2", [128, 128], BF16)

    def gp(fn, r, w):
        P.op("gpsimd", fn, r, w)
    for tname, tl in (("ltri", ltri), ("mlo", mlo), ("mup", mup), ("sel0", sel0), ("sel1", sel1)):
        gp(lambda e, tl=tl: e.memset(tl[:], 1.0), [], [tname])
    for tname, tl in (("ltri", ltri), ("mlo", mlo), ("mup", mup)):
        gp(lambda e, tl=tl: e.memset(tl[0:64, 64:128], 0.0), [tname], [tname])
        gp(lambda e, tl=tl: e.memset(tl[64:128, 0:64], 0.0), [tname], [tname])
    gp(lambda e: e.affine_select(out=ltri[:], in_=ltri[:], pattern=[[1, 128]], compare_op=ALU.is_ge, fill=0.0, base=0, channel_multiplier=-1), ["ltri"], ["ltri"])
    gp(lambda e: e.affine_select(out=mup[:], in_=mup[:], pattern=[[1, 128]], compare_op=ALU.is_gt, fill=0.0, base=0, channel_multiplier=-1), ["mup"], ["mup"])
    gp(lambda e: e.affine_select(out=mlo[:], in_=mlo[:], pattern=[[-1, 128]], compare_op=ALU.is_gt, fill=0.0, base=0, channel_multiplier=1), ["mlo"], ["mlo"])
    gp(lambda e: e.affine_select(out=sel0[:], in_=sel0[:], pattern=[[0, 128]], compare_op=ALU.is_equal, fill=0.0, base=-63, channel_multiplier=1), ["sel0"], ["sel0"])
    gp(lambda e: e.affine_select(out=sel1[:], in_=sel1[:], pattern=[[0, 128]], compare_op=ALU.is_equal, fill=0.0, base=-127, channel_multiplier=1), ["sel1"], ["sel1"])
    gp(lambda e: e.tensor_copy(out=mle[0:64, :], in_=ltri[0:64, 0:64]), ["ltri"], ["mle"])
    gp(lambda e: e.tensor_copy(out=mle[64:128, :], in_=ltri[64:128, 64:128]), ["ltri"], ["mle"])
    P.op("vector", lambda e: e.memset(addI[:], 0.0), [], ["addI"])
    P.op("vector", lambda e: e.memset(addI2[:], 0.0), [], ["addI2"])
    for tcs in (slice(0, 64), slice(64, 128)):
        P.op("vector", lambda e, tcs=tcs: e.tensor_copy(out=addI[tcs, 0:64], in_=identf[tcs, tcs]), ["addI", "identf"], ["addI"])
        P.op("vector", lambda e, tcs=tcs: e.tensor_copy(out=addI2[tcs, 64:128], in_=identf[tcs, tcs]), ["addI2", "identf"], ["addI2"])

    xt = [sb(f"xt{i}", [128, D], F32) for i in range(2)]
    hb = [sb(f"hb{i}", [128, D], BF16) for i in range(2)]
    ss = [sb(f"ss{i}", [128, 1], F32) for i in range(2)]
    rstd = [sb(f"rstd{i}", [128, 1], F32) for i in range(2)]
    hT = [sb(f"hT{i}", [128, 8, 129], BF16) for i in range(2)]
    xxT = sb("xxT", [128, 8, 128], F32)
    mixT = [sb(f"mixT{i}", [128, 8, 128], BF16) for i in range(6)]
    rt = sb("rt", [128, D], F32)
    kt_ = sb("kt_", [128, D], F32)
    vt_ = sb("vt_", [128, D], F32)
    sgm = sb("sgm", [128, D], F32)
    asg = sb("asg", [128, D], F32)
    kkn = sb("kkn", [128, D], F32)
    t2 = sb("t2", [128, D], F32)
    ge = [sb(f"ge{i}", [128, D], F32) for i in range(2)]
    nrm = sb("nrm", [128, 16], F32)
    bsc = sb("bsc", [128, 16], F32)
    tmA = sb("tmA", [128, D], BF16)
    tmB = sb("tmB", [128, D], BF16)
    tmK = sb("tmK", [128, D], BF16)
    tmR = sb("tmR", [128, D], BF16)
    vbf = sb("vbf", [128, D], BF16)
    fm = {nm: sb("fm" + nm, [64, 16, 128], BF16) for nm in "ABKR"}
    lo1 = {nm: sb("lo1" + nm, [wd_ if wd_ <= 128 else 128, 128], BF16) for nm, wd_ in LW.items()}
    lo1g2 = sb("lo1g2", [32, 128], BF16)
    bd = {nm: [sb(f"bd{nm}{i}", [128, 128], BF16) for i in range(2)] for nm in ("N", "NT")}
    bdNT = [sb(f"bdNTj{j}", [128, 128], BF16) for j in range(6)]
    bdN = [sb(f"bdNj{j}", [128, 128], BF16) for j in range(5)]
    bdAk = sb("bdAk", [128, 128], BF16)
    Z = [sb(f"Z{i}", [128, 128], BF16) for i in range(2)]
    ADD = sb("ADD", [128, 128], BF16)
    QGs = [sb(f"QGs{i}", [128, 16, 128], BF16) for i in range(1)] * 2
    PHs = [sb(f"PHs{i}", [128, 16, 128], BF16) for i in range(1)] * 2
    ptr = ps("ptr", [128, 8, 128], BF16)
    pA = ps("pA", [128, 512], F32)
    pB = ps("pB", [128, 512], F32)
    pc = [pA, pB]
    pl = ps("pl", [128, 4, 128], F32)
    pg_ = ps("pg_", [128, 4, 128], F32)
    pwt = [ps(f"pw{i}", [128, 512], F32) for i in range(3)]
    P.op("vector", lambda e: e.memset(pg_[:], 0.0), [], ["pg"])
    P.barrier()
    P.dma("sync", xt[1][:], xh_in, [], ["xt1"], "xt1")
    emit_norm_T(P, c, xt[1], "xt1", ss[1], rstd[1], hb[1], ptr, hT[0][:, :, 1:129], "hT0", 1)
    P.op("gpsimd", lambda e: e.tensor_copy(out=hT[1][:, :, 128:129], in_=hT[0][:, :, 1:2]), ["hT0"], ["hT1"])
    import os
    STOP = int(os.environ.get('R1_STOP', '9'))
    outs = []
    pwc = 0
    for n in range(NT if STOP > 1 else 0):
        b = n % 2
        x, xn = xt[b], f"xt{b}"
        t0 = n * 128
        P.dma("sync", x[:], x_in[t0:t0 + 128, :], [], [xn], xn)
        emit_norm_T(P, c, x, xn, ss[b], rstd[b], hb[b], ptr, hT[b][:, :, 1:129], f"hT{b}", b)
        P.op("gpsimd", lambda e, b=b: e.tensor_copy(out=hT[b][:, :, 0:1], in_=hT[1 - b][:, :, 128:129]), [f"hT{1 - b}"], [f"hT{b}h"])
        hr = [f"hT{b}", f"hT{b}h"]
        P.op("vector", lambda e, b=b: e.tensor_tensor(out=xxT[:], in0=hT[b][:, :, 0:128], in1=hT[b][:, :, 1:129], op=ALU.subtract), hr, ["xxT"])
        for i in range(6):
            for cc in range(8):
                P.op("vector", lambda e, i=i, cc=cc, b=b: e.scalar_tensor_tensor(out=mixT[i][:, cc, :], in0=xxT[:, cc, :], scalar=mucol[:, i, cc:cc + 1], in1=hT[b][:, cc, 1:129],
                                                                               op0=ALU.mult, op1=ALU.add), ["xxT", "mucol"] + hr, [f"mixT{i}"])
        slot = {"w1": 0, "a1": 1, "g1": 2, "v1": 3}
        msrc = {"w1": 1, "a1": 4, "g1": 5, "v1": 3}
        for nm, wd_ in LW.items():
            s_ = slot[nm]
            w0_ = min(wd_, 128)
            for cc in range(8):
                P.op("tensor", lambda e, nm=nm, cc=cc, s_=s_, w0_=w0_: e.matmul(out=pl[0:w0_, s_, :], lhsT=l1[nm][:, cc, 0:w0_], rhs=mixT[msrc[nm]][:, cc, :], start=(cc == 0), stop=(cc == 7)),
                     [f"mixT{msrc[nm]}", f"l1{nm}_{cc}"], ["pl"])
        for cc in range(8):
            P.op("tensor", lambda e, cc=cc: e.matmul(out=pB[0:32, 0:128], lhsT=l1["g1"][:, cc, 128:160], rhs=mixT[5][:, cc, :], start=(cc == 0), stop=(cc == 7)),
                 ["mixT5", f"l1g1_{cc}"], ["pB"])
        P.op("scalar", lambda e: e.activation(out=lo1["w1"][:], in_=pl[0:64, 0, :], func=AF.Tanh), ["pl"], ["lo1w1"])
        P.op("scalar", lambda e: e.copy(out=lo1["a1"][:], in_=pl[0:64, 1, :]), ["pl"], ["lo1a1"])
        P.op("scalar", lambda e: e.activation(out=lo1["g1"][:], in_=pl[:, 2, :], func=AF.Sigmoid), ["pl"], ["lo1g1"])
        P.op("scalar", lambda e: e.activation(out=lo1g2[:], in_=pB[0:32, 0:128], func=AF.Sigmoid), ["pB"], ["lo1g2"])
        if has_vlora:
            P.op("scalar", lambda e: e.copy(out=lo1["v1"][:], in_=pl[0:32, 3, :]), ["pl"], ["lo1v1"])
        if STOP <= 2:
            continue

        def proj(dst_fn, lhs_list, reads):
            for half, pp_, pn in ((0, pA, "pA"), (1, pB, "pB")):
                nk = len(lhs_list)
                for ki, (lh, rh, rn) in enumerate(lhs_list):
                    P.op("tensor", lambda e, lh=lh, rh=rh, half=half, pp_=pp_, ki=ki, nk=nk: e.matmul(out=pp_[:], lhsT=lh, rhs=rh[:, half * 512:(half + 1) * 512], start=(ki == 0), stop=(ki == nk - 1)),
                         reads + [rn], [pn])
                dst_fn(half, pp_, pn)

        H = lambda half: slice(half * 512, (half + 1) * 512)
        for (mi, wi, dst, dn) in ((0, 0, rt, "rt"), (2, 1, kt_, "kt_"), (3, 2, vt_, "vt_")):
            proj(lambda half, pp_, pn, dst=dst, dn=dn: P.op("scalar", lambda e: e.copy(out=dst[:, H(half)], in_=pp_[:]), [pn], [dn + str(half)]),
                 [(mixT[mi][:, cc, :], wrkv[wi][:, cc, :], f"wrkv{wi}_{cc}") for cc in range(8)], [f"mixT{mi}"])
        RT, KT, VT = ["rt0", "rt1"], ["kt_0", "kt_1"], ["vt_0", "vt_1"]
        def ev_sig(dst, dn, bcn):
            def f(half, pp_, pn):
                P.op("vector", lambda e: e.tensor_tensor(out=dst[:, H(half)], in0=pp_[:], in1=bc[bcn][:, H(half)], op=ALU.add), [pn, "bc" + bcn], [dn + str(half)])
                P.op("scalar", lambda e: e.activation(out=dst[:, H(half)], in_=dst[:, H(half)], func=AF.Sigmoid), [dn + str(half)], [dn + str(half)])
            return f
        proj(ev_sig(sgm, "sgm", "w0"), [(lo1["w1"][:], l2["w2"], "l2w2")], ["lo1w1"])
        proj(ev_sig(asg, "asg", "a0"), [(lo1["a1"][:], l2["a2"], "l2a2")], ["lo1a1"])
        SG, AS = ["sgm0", "sgm1"], ["asg0", "asg1"]
        proj(lambda half, pp_, pn: P.op("scalar", lambda e: e.copy(out=t1[:, H(half)], in_=pp_[:]), [pn], ["t1"]),
             [(lo1["g1"][:], l2["g2a"], "l2g2a"), (lo1g2[:], l2["g2b"], "l2g2b")], ["lo1g1", "lo1g2"])
        P.dma("gpsimd", gate_out[t0:t0 + 128, :], t1[:], ["t1"], [f"go{n}"], "t1o")
        outs.append(f"go{n}")
        if has_vlora:
            proj(ev_sig(Et, "Et", "v0"), [(lo1["v1"][:], l2["v2"], "l2v2")], ["lo1v1"])
            P.dma("sync", t2[:], vfirst_io[t0:t0 + 128, :], [], ["t2"], "t2i")
            P.op("gpsimd", lambda e: e.tensor_tensor(out=t2[:], in0=t2[:], in1=vt_[:], op=ALU.subtract), ["t2"] + VT, ["t2"])
            P.op("gpsimd", lambda e: e.tensor_tensor(out=t2[:], in0=t2[:], in1=Et[:], op=ALU.mult), ["t2", "Et0", "Et1"], ["t2"])
            P.op("gpsimd", lambda e: e.tensor_tensor(out=vt_[:], in0=vt_[:], in1=t2[:], op=ALU.add), ["t2"] + VT, VT)
        else:
            P.dma("gpsimd", vfirst_io[t0:t0 + 128, :], vt_[:], VT, [f"vfo{n}"], "vfo")
            outs.append(f"vfo{n}")
        P.op("gpsimd", lambda e: e.tensor_copy(out=vbf[:], in_=vt_[:]), VT, ["vbf"])
        P.dma("gpsimd", vb_out[t0:t0 + 128, :], vbf[:], ["vbf"], [f"vbo{n}"], "vbfo")
        outs.append(f"vbo{n}")
        kkn3 = kkn[:].rearrange("p (h d) -> p h d", d=64)
        t13 = t1[:].rearrange("p (h d) -> p h d", d=64)
        t23 = t2[:].rearrange("p (h d) -> p h d", d=64)
        P.op("vector", lambda e: e.tensor_tensor(out=kkn[:], in0=kt_[:], in1=bc["k_k"][:], op=ALU.mult), KT + ["bck_k"], ["kkn"])
        P.op("gpsimd", lambda e: e.tensor_tensor(out=t2[:], in0=kkn[:], in1=kkn[:], op=ALU.mult), ["kkn"], ["t2"])
        P.op("vector", lambda e: e.tensor_reduce(out=nrm[:], in_=t23, axis=AX.X, op=ALU.add), ["t2"], ["nrm"])
        P.op("scalar", lambda e: e.activation(out=nrm[:], in_=nrm[:], func=AF.Sqrt), ["nrm"], ["nrm"])
        P.op("vector", lambda e: e.tensor_scalar(out=nrm[:], in0=nrm[:], scalar1=1e-12, scalar2=None, op0=ALU.max), ["nrm"], ["nrm"])
        P.op("vector", lambda e: e.reciprocal(out=nrm[:], in_=nrm[:]), ["nrm"], ["nrm"])
        P.op("vector", lambda e: e.tensor_tensor(out=kkn3, in0=kkn3, in1=nrm[:].unsqueeze(2).broadcast_to([128, 16, 64]), op=ALU.mult), ["kkn", "nrm"], ["kkn"])
        P.op("vector", lambda e: e.scalar_tensor_tensor(out=t2[:], in0=asg[:], scalar=-1.0, in1=bc["k_a"][:], op0=ALU.add, op1=ALU.mult), AS + ["bck_a", "t2"], ["t2"])
        P.op("vector", lambda e: e.scalar_tensor_tensor(out=kt_[:], in0=t2[:], scalar=1.0, in1=kt_[:], op0=ALU.add, op1=ALU.mult), ["t2"] + KT, KT)
        P.op("gpsimd", lambda e: e.tensor_tensor(out=t2[:], in0=rt[:], in1=kt_[:], op=ALU.mult), RT + KT + ["t2"], ["t2"])
        P.op("gpsimd", lambda e: e.tensor_tensor(out=t2[:], in0=t2[:], in1=bc["r_k"][:], op=ALU.mult), ["t2", "bcr_k"], ["t2"])
        P.op("vector", lambda e: e.tensor_reduce(out=bsc[:], in_=t23, axis=AX.X, op=ALU.add), ["t2"], ["bsc"])
        P.op("vector", lambda e: e.tensor_tensor(out=t23, in0=vt_[:].rearrange("p (h d) -> p h d", d=64), in1=bsc[:].unsqueeze(2).broadcast_to([128, 16, 64]), op=ALU.mult), VT + ["bsc", "t2"], ["t2"])
        P.dma("gpsimd", bonus_out[t0:t0 + 128, :], t2[:], ["t2"], [f"bo{n}"], "t2o")
        outs.append(f"bo{n}")
        for half in range(2):
            P.op("tensor", lambda e, half=half: e.matmul(out=pc[half][:], lhsT=ltri[:], rhs=sgm[:, H(half)], start=True, stop=True), [SG[half], "ltri"], [("pA", "pB")[half]])
        for half in range(2):
            P.op("scalar", lambda e, half=half: e.activation(out=Et[:, H(half)], in_=pc[half][:], func=AF.Exp, scale=-WSC), [("pA", "pB")[half]], [f"Et{half}"])
        P.op("vector", lambda e: e.tensor_tensor(out=tmR[:], in0=rt[:], in1=Et[:], op=ALU.mult), RT + ["Et0", "Et1"], ["tmR"])
        for half in range(2):
            P.op("scalar", lambda e, half=half: e.activation(out=Et[:, H(half)], in_=pc[half][:], func=AF.Exp, scale=WSC), [("pA", "pB")[half], "tmR"], [f"Et{half}"])
        P.op("vector", lambda e: e.tensor_tensor(out=tmK[:], in0=kt_[:], in1=Et[:], op=ALU.mult), KT + ["Et0", "Et1"], ["tmK"])
        P.op("gpsimd", lambda e: e.tensor_tensor(out=t1[:], in0=kkn[:], in1=asg[:], op=ALU.mult), ["kkn", "t1"] + AS, ["t1"])
        P.op("vector", lambda e: e.tensor_tensor(out=tmB[:], in0=t1[:], in1=Et[:], op=ALU.mult), ["t1", "Et0", "Et1"], ["tmB"])
        for half in range(2):
            P.op("vector", lambda e, half=half: e.tensor_tensor(out=t1[:, H(half)], in0=pc[half][:], in1=sgm[:, H(half)], op=ALU.subtract), [("pA", "pB")[half], SG[half], "t1", "tmB"], ["t1"])
        P.op("scalar", lambda e: e.activation(out=Et[:], in_=t1[:], func=AF.Exp, scale=-WSC), ["t1", "tmK", "tmB", "Et0", "Et1"], ["Et0", "Et1"])
        P.op("vector", lambda e: e.scalar_tensor_tensor(out=tmA[:], in0=kkn[:], scalar=-1.0, in1=Et[:], op0=ALU.mult, op1=ALU.mult), ["kkn", "Et0", "Et1"], ["tmA"])
        for half in range(2):
            P.op("scalar", lambda e, half=half: e.copy(out=t1[:, H(half)], in_=pc[half][:]), [("pA", "pB")[half], "t1"], ["t1"])
        for s_, sel in ((0, sel0), (1, sel1)):
            for half in range(2):
                P.op("tensor", lambda e, half=half, sel=sel: e.matmul(out=pc[half][:], lhsT=sel[:], rhs=t1[:, H(half)], start=True, stop=True), ["t1", "sel0", "sel1"], [("pA", "pB")[half]])
                P.op("scalar", lambda e, half=half, s_=s_: e.activation(out=ge[s_][:, H(half)], in_=pc[half][:], func=AF.Exp, scale=-WSC), [("pA", "pB")[half]], [f"ge{s_}"])
        if STOP <= 3:
            continue
        for nm, src, sn in (("A", tmA, "tmA"), ("B", tmB, "tmB"), ("K", tmK, "tmK"), ("R", tmR, "tmR")):
            for h0 in (0, 8):
                for hh in range(8):
                    P.op("tensor", lambda e, hh=hh, h0=h0, src=src: e.transpose(out=ptr[0:64, hh, :], in_=src[:, (h0 + hh) * 64:(h0 + hh + 1) * 64], identity=ident[:]), [sn, "ident"], ["ptr"])
                if nm in "AK":
                    P.op("scalar", lambda e, nm=nm, h0=h0: e.copy(out=fm[nm][:, h0:h0 + 8, :], in_=ptr[0:64, :, :]), ["ptr"], ["fm" + nm])
                else:
                    P.op("vector", lambda e, nm=nm, h0=h0: e.tensor_copy(out=fm[nm][:, h0:h0 + 8, :], in_=ptr[0:64, :, :]), ["ptr"], ["fm" + nm])
        ob = 0
        for h in range(16):
            hc = slice(h * 64, (h + 1) * 64)
            fA, fB, fK, fR = (fm[q][:, h, :] for q in "ABKR")
            for cc in range(2):
                tc = slice(cc * 64, (cc + 1) * 64)
                for (slot_, lh, rh, rn1, rn2) in ((0, fA, fB, "fmA", "fmB"), (1, fA, fK, "fmA", "fmK"), (2, fB, fA, "fmB", "fmA")):
                    P.op("tensor", lambda e, slot_=slot_, lh=lh, rh=rh, tc=tc: e.matmul(out=pg_[tc, slot_, tc], lhsT=lh[:, tc], rhs=rh[:, tc], start=True, stop=True), [rn1, rn2], ["pg"])
                P.op("tensor", lambda e, tc=tc, fB=fB, fR=fR: e.matmul(out=pg_[tc, 3, 0:64], lhsT=fB[:, tc], rhs=fR[:, tc], start=True, stop=True), ["fmB", "fmR"], ["pg"])
                P.op("tensor", lambda e, tc=tc, fK=fK, fR=fR: e.matmul(out=pg_[tc, 3, 64:128], lhsT=fK[:, tc], rhs=fR[:, tc], start=True, stop=True), ["fmK", "fmR"], ["pg"])
            P.op("vector", lambda e: e.tensor_tensor(out=bdNT[0][:], in0=pg_[:, 0, :], in1=mlo[:], op=ALU.mult), ["pg", "mlo"], ["bdNT0"])
            P.op("vector", lambda e: e.tensor_tensor(out=bdAk[:], in0=pg_[:, 1, :], in1=mlo[:], op=ALU.mult), ["pg", "mlo"], ["bdAk"])
            P.op("vector", lambda e: e.tensor_tensor(out=bdN[0][:], in0=pg_[:, 2, :], in1=mup[:], op=ALU.mult), ["pg", "mup"], ["bdN0"])
            P.op("vector", lambda e: e.tensor_tensor(out=Z[0][:, 0:64], in0=pg_[:, 3, 0:64], in1=mle[:], op=ALU.mult), ["pg", "mle"], ["Z0a"])
            P.op("vector", lambda e: e.tensor_tensor(out=ADD[:, 0:64], in0=pg_[:, 3, 64:128], in1=mle[:], op=ALU.mult), ["pg", "mle"], ["ADDa"])
            P.op("gpsimd", lambda e, hc=hc: e.tensor_copy(out=Z[0][:, 64:128], in_=tmB[:, hc]), ["tmB"], ["Z0b"])
            P.op("gpsimd", lambda e, hc=hc: e.tensor_copy(out=ADD[:, 64:128], in_=tmK[:, hc]), ["tmK"], ["ADDb"])
            if STOP <= 4:
                continue

            def nxt():
                nonlocal pwc
                k_ = pwc % 3
                pwc += 1
                return pwt[k_], f"pw{k_}"
            for j in range(5):
                pa_, pan = nxt()
                P.op("tensor", lambda e, j=j, pa_=pa_: e.matmul(out=pa_[:, 0:128], lhsT=bdN[j][:], rhs=bdNT[j][:], start=True, stop=True), [f"bdN{j}", f"bdNT{j}"], [pan])
                P.op("scalar", lambda e, j=j, pa_=pa_: e.copy(out=bdNT[j + 1][:], in_=pa_[:, 0:128]), [pan], [f"bdNT{j + 1}"])
                if j < 4:
                    pb2, pbn = nxt()
                    P.op("tensor", lambda e, j=j, pb2=pb2: e.matmul(out=pb2[:, 0:128], lhsT=bdNT[j][:], rhs=bdN[j][:], start=True, stop=True), [f"bdN{j}", f"bdNT{j}"], [pbn])
                    P.op("vector", lambda e, j=j, pb2=pb2: e.tensor_copy(out=bdN[j + 1][:], in_=pb2[:, 0:128]), [pbn], [f"bdN{j + 1}"])
            if STOP <= 5:
                continue
            zi = 0
            zr = ["Z0a", "Z0b"]
            for j in range(5, -1, -1):
                pa_, pan = nxt()
                P.op("tensor", lambda e, zi=zi, pa_=pa_: e.matmul(out=pa_[:, 0:128], lhsT=ident[:], rhs=Z[zi][:], start=True, stop=False), zr + ["ident"], [pan])
                P.op("tensor", lambda e, zi=zi, pa_=pa_, j=j: e.matmul(out=pa_[:, 0:128], lhsT=bdNT[j][:], rhs=Z[zi][:], start=False, stop=True), zr + [f"bdNT{j}"], [pan])
                zi = 1 - zi
                znew = [f"Z{zi}a", f"Z{zi}b"]
                if j % 2:
                    P.op("scalar", lambda e, zi=zi, pa_=pa_: e.copy(out=Z[zi][:], in_=pa_[:, 0:128]), [pan], znew)
                else:
                    P.op("vector", lambda e, zi=zi, pa_=pa_: e.tensor_copy(out=Z[zi][:], in_=pa_[:, 0:128]), [pan], znew)
                zr = znew
            Zf = Z[zi]
            if STOP <= 6:
                continue
            SUB = os.environ.get('R1_SUB', '012')
            for cc in [c_ for c_ in range(2) if str(c_) in SUB]:
                tc = slice(cc * 64, (cc + 1) * 64)
                pbk, pnm = nxt()
                P.op("tensor", lambda e, tc=tc, hc=hc, Zf=Zf, pbk=pbk: e.matmul(out=pbk[tc, 0:128], lhsT=tmA[tc, hc], rhs=Zf[tc, :], start=True, stop=False), zr + ["tmA"], [pnm])
                P.op("tensor", lambda e, tc=tc, hc=hc, pbk=pbk: e.matmul(out=pbk[tc, 0:128], lhsT=tmR[tc, hc], rhs=addI[tc, :], start=False, stop=False), ["tmR", "addI"], [pnm])
                P.op("tensor", lambda e, tc=tc, pbk=pbk: e.matmul(out=pbk[tc, 0:128], lhsT=ident[tc, tc], rhs=addI2[tc, :], start=False, stop=True), ["ident", "addI2"], [pnm])
                P.op("vector", lambda e, cc=cc, h=h, ob=ob, tc=tc, pbk=pbk: e.tensor_copy(out=QGs[ob][tc, h, 0:64], in_=pbk[tc, 0:64]), [pnm], [f"QGs{ob}_{h}_{cc}q"])
                P.op("vector", lambda e, cc=cc, h=h, ob=ob, hc=hc, tc=tc, pbk=pbk: e.tensor_tensor(out=QGs[ob][tc, h, 64:128], in0=pbk[tc, 64:128], in1=ge[cc][tc, hc], op=ALU.mult),
                     [pnm, f"ge{cc}"], [f"QGs{ob}_{h}_{cc}g"])
            if '2' not in SUB:
                continue
            pa_, pan = nxt()
            P.op("tensor", lambda e, pa_=pa_, Zf=Zf: e.matmul(out=pa_[:, 0:128], lhsT=bdAk[:], rhs=Zf[:], start=True, stop=False), zr + ["bdAk"], [pan])
            P.op("tensor", lambda e, pa_=pa_: e.matmul(out=pa_[:, 0:128], lhsT=ident[:], rhs=ADD[:], start=False, stop=True), ["ADDa", "ADDb", "ident"], [pan])
            V_ = os.environ.get('R1_V', 'abc')
            if 'a' in V_:
                P.op("vector", lambda e, pa_=pa_, h=h, ob=ob: e.tensor_copy(out=PHs[ob][:, h, 0:64], in_=pa_[:, 0:64]), [pan], [f"PHs{ob}_{h}p"])
            for cc in range(2):
                tc = slice(cc * 64, (cc + 1) * 64)
                if 'bc'[cc] not in V_:
                    continue
                P.op("vector", lambda e, pa_=pa_, h=h, ob=ob, tc=tc, cc=cc, hc=hc: e.tensor_tensor(out=PHs[ob][tc, h, 64:128], in0=pa_[tc, 64:128], in1=ge[cc][tc, hc], op=ALU.mult),
                     [pan, f"ge{cc}"], [f"PHs{ob}_{h}h{cc}"])
        if STOP <= 7:
            continue
        qrd = [f"QGs{ob}_{h}_{cc}{q}" for h in range(16) for cc in range(2) for q in "qg"]
        prd = [f"PHs{ob}_{h}{q}" for h in range(16) for q in ("p", "h0", "h1")]
        for hh in range(0, 16, 4):
            P.dma("sync", QG_out[n, hh:hh + 4].rearrange("h i c -> i h c"), QGs[ob][:, hh:hh + 4, :], qrd, [f"qgo{n}_{hh}"], f"QGs{ob}o{hh}")
            P.dma("sync", PH_out[n, hh:hh + 4].rearrange("h i c -> i h c"), PHs[ob][:, hh:hh + 4, :], prd, [f"pho{n}_{hh}"], f"PHs{ob}o{hh}")
            outs += [f"qgo{n}_{hh}", f"pho{n}_{hh}"]
    P.op("sync", lambda e: None, outs, [])
    P.op("gpsimd", lambda e: None, outs, [])


def emit_rwkv_scan(P, nc, st, QG, PH, V, y_out, NTs):
    sb, ps = _alloc(nc, st)
    NB = 3
    qg = [sb(f"qg{i}", [128, 8, 128], BF16) for i in range(NB)]
    ph = [sb(f"ph{i}", [128, 8, 128], BF16) for i in range(NB)]
    vv = [sb(f"vv{i}", [128, 512], BF16) for i in range(NB)]
    S = sb("S", [128, 8, 64], BF16)
    yt = [sb(f"yt{i}", [128, 512], F32) for i in range(2)]
    pS = [ps(f"pS{i}", [128, 8, 64], F32) for i in range(2)]
    pY = [ps(f"pY{i}", [128, 8, 64], F32) for i in range(4)]
    P.op("vector", lambda e: e.memset(S[:], 0.0), [], ["S0", "S1"])
    outs = []
    for n in range(NTs):
        b = n % NB
        P.dma("sync", qg[b][:], QG[n].rearrange("h i c -> i h c"), [], [f"qg{b}"], f"qg{b}")
        P.dma("sync", ph[b][:], PH[n].rearrange("h i c -> i h c"), [], [f"ph{b}"], f"ph{b}")
        P.dma("sync", vv[b][:], V[n * 128:(n + 1) * 128, :], [], [f"vv{b}"], f"vv{b}")
        yb = n % 2
        for cc in range(2):
            tc = slice(cc * 64, (cc + 1) * 64)
            to = slice((1 - cc) * 64, (2 - cc) * 64)
            pSb = pS[cc]
            rds = [f"qg{b}", f"ph{b}", f"vv{b}", f"S{cc}"]
            for h in range(8):
                P.op("tensor", lambda e, h=h, b=b, tc=tc, yb=yb, cc=cc: e.matmul(out=pY[2 * yb + cc][tc, h, :], lhsT=qg[b][tc, h, 0:64], rhs=S[tc, h, :], start=True, stop=False), rds, [f"pY{2 * yb + cc}"])
                P.op("tensor", lambda e, h=h, b=b, tc=tc, yb=yb, cc=cc: e.matmul(out=pY[2 * yb + cc][tc, h, :], lhsT=ph[b][tc, h, 0:64], rhs=vv[b][tc, h * 64:(h + 1) * 64], start=False, stop=True), rds, [f"pY{2 * yb + cc}"])
                P.op("tensor", lambda e, h=h, b=b, tc=tc, to=to, pSb=pSb: e.matmul(out=pSb[to, h, :], lhsT=qg[b][tc, h, 64:128], rhs=S[tc, h, :], start=True, stop=False), rds, [f"pS{cc}"])
                P.op("tensor", lambda e, h=h, b=b, tc=tc, to=to, pSb=pSb: e.matmul(out=pSb[to, h, :], lhsT=ph[b][tc, h, 64:128], rhs=vv[b][tc, h * 64:(h + 1) * 64], start=False, stop=True), rds, [f"pS{cc}"])
            P.op("scalar", lambda e, to=to, pSb=pSb: e.copy(out=S[to, :, :], in_=pSb[to, :, :]), [f"pS{cc}"], [f"S{1 - cc}"])
        for cc in range(2):
            tc = slice(cc * 64, (cc + 1) * 64)
            P.op("vector", lambda e, yb=yb, cc=cc, tc=tc: e.tensor_copy(out=yt[yb][tc, :], in_=pY[2 * yb + cc][tc, :, :].rearrange("p h d -> p (h d)")), [f"pY{2 * yb + cc}"], [f"yt{yb}"])
        P.dma("gpsimd", y_out[n * 128:(n + 1) * 128, :], yt[yb][:], [f"yt{yb}"], [f"yo{n}"], f"yt{yb}o")
        outs.append(f"yo{n}")
    P.op("sync", lambda e: None, outs, [])
    P.op("gpsimd", lambda e: None, outs, [])


LNX_EPS = 64e-5


def emit_rwkv_out(P, nc, st, x_in, y_in, gate_in, bonus_in, lnw, lnb, w_o, x_out, T):
    sb, ps = _alloc(nc, st)
    c = emit_consts(P, nc, sb)
    ident = c["ident"]
    wob = sb("wob", [128, 8, D], BF16)
    stg = [sb(f"stg{i}", [128, D], F32) for i in range(2)]
    load_weight_bf16(P, nc, [w_o[cc * 128:(cc + 1) * 128, :] for cc in range(8)], lambda cc: wob[:, cc, :], D, stg, None, lambda cc: f"wob{cc}", piece=1024)
    lw = sb("lw", [128, D], F32)
    lb = sb("lb", [128, D], F32)
    P.dma("sync", lw[:], lnw.partition_broadcast(128), [], ["lw"], "lw")
    P.dma("sync", lb[:], lnb.partition_broadcast(128), [], ["lb"], "lb")
    eps2 = sb("eps2", [128, 1], F32)
    P.op("vector", lambda e: e.memset(eps2[:], LNX_EPS), [], ["eps2"])
    yt = [sb(f"yt{i}", [128, D], F32) for i in range(2)]
    gt = [sb(f"gt{i}", [128, D], F32) for i in range(2)]
    bt = [sb(f"bt{i}", [128, D], F32) for i in range(2)]
    xt = [sb(f"xt{i}", [128, D], F32) for i in range(2)]
    sq = sb("sq", [128, D], F32)
    ygb = sb("ygb", [128, D], BF16)
    ygT = [sb(f"ygT{i}", [128, 8, 128], BF16) for i in range(2)]
    s1 = sb("s1", [128, 16], F32)
    s2 = sb("s2", [128, 16], F32)
    msq = sb("msq", [128, 16], F32)
    ptr = ps("ptr", [128, 8, 128], BF16)
    po = [ps(f"po{i}", [128, 512], F32) for i in range(2)]
    outs = []
    ocnt = 0
    for n in range(T // 128):
        b = n % 2
        t0 = n * 128
        y, g, bo, x = yt[b], gt[b], bt[b], xt[b]
        P.dma("sync", y[:], y_in[t0:t0 + 128, :], [], [f"yt{b}"], f"yt{b}")
        P.dma("sync", g[:], gate_in[t0:t0 + 128, :], [], [f"gt{b}"], f"gt{b}")
        P.dma("sync", bo[:], bonus_in[t0:t0 + 128, :], [], [f"bt{b}"], f"bt{b}")
        P.dma("sync", x[:], x_in[t0:t0 + 128, :], [], [f"xt{b}"], f"xt{b}")
        y3 = y[:].rearrange("p (h d) -> p h d", d=64)
        sq3 = sq[:].rearrange("p (h d) -> p h d", d=64)
        bc16 = lambda t: t[:].unsqueeze(2).broadcast_to([128, 16, 64])
        P.op("vector", lambda e, y3=y3: e.tensor_reduce(out=s1[:], in_=y3, axis=AX.X, op=ALU.add), [f"yt{b}"], ["s1"])
        P.op("gpsimd", lambda e, y=y: e.tensor_tensor(out=sq[:], in0=y[:], in1=y[:], op=ALU.mult), [f"yt{b}"], ["sq"])
        P.op("vector", lambda e: e.tensor_reduce(out=s2[:], in_=sq3, axis=AX.X, op=ALU.add), ["sq"], ["s2"])
        P.op("vector", lambda e: e.tensor_scalar(out=s1[:], in0=s1[:], scalar1=1.0 / 64, scalar2=None, op0=ALU.mult), ["s1"], ["s1"])
        P.op("vector", lambda e: e.tensor_tensor(out=msq[:], in0=s1[:], in1=s1[:], op=ALU.mult), ["s1"], ["msq"])
        P.op("vector", lambda e: e.scalar_tensor_tensor(out=s2[:], in0=s2[:], scalar=1.0 / 64, in1=msq[:], op0=ALU.mult, op1=ALU.subtract), ["s2", "msq"], ["s2"])
        P.op("scalar", lambda e: e.activation(out=s2[:], in_=s2[:], func=AF.Sqrt, bias=eps2[:], scale=1.0), ["s2", "eps2"], ["s2"])
        P.op("vector", lambda e: e.reciprocal(out=s2[:], in_=s2[:]), ["s2"], ["s2"])
        P.op("vector", lambda e, y3=y3: e.tensor_tensor(out=y3, in0=y3, in1=bc16(s1), op=ALU.subtract), [f"yt{b}", "s1"], [f"yt{b}"])
        P.op("vector", lambda e, y3=y3: e.tensor_tensor(out=y3, in0=y3, in1=bc16(s2), op=ALU.mult), [f"yt{b}", "s2"], [f"yt{b}"])
        P.op("gpsimd", lambda e, y=y: e.tensor_tensor(out=y[:], in0=y[:], in1=lw[:], op=ALU.mult), [f"yt{b}", "lw"], [f"yt{b}"])
        P.op("gpsimd", lambda e, bo=bo: e.tensor_tensor(out=bo[:], in0=bo[:], in1=lb[:], op=ALU.add), [f"bt{b}", "lb"], [f"bt{b}"])
        P.op("vector", lambda e, y=y, bo=bo: e.tensor_tensor(out=y[:], in0=y[:], in1=bo[:], op=ALU.add), [f"yt{b}", f"bt{b}"], [f"yt{b}"])
        P.op("vector", lambda e, y=y, g=g: e.tensor_tensor(out=ygb[:], in0=y[:], in1=g[:], op=ALU.mult), [f"yt{b}", f"gt{b}"], ["ygb"])
        for cc in range(8):
            P.op("tensor", lambda e, cc=cc: e.transpose(out=ptr[:, cc, :], in_=ygb[:, cc * 128:(cc + 1) * 128], identity=ident[:]), ["ygb", "ident"], ["ptr"])
        P.op("scalar", lambda e, b=b: e.copy(out=ygT[b][:], in_=ptr[:]), ["ptr"], [f"ygT{b}"])
        for half in range(2):
            pb_ = ocnt % 2
            ocnt += 1
            for cc in range(8):
                P.op("tensor", lambda e, cc=cc, half=half, pb_=pb_, b=b: e.matmul(out=po[pb_][:], lhsT=ygT[b][:, cc, :], rhs=wob[:, cc, half * 512:(half + 1) * 512], start=(cc == 0), stop=(cc == 7)),
                     [f"ygT{b}", f"wob{cc}"], [f"po{pb_}"])
            P.op("vector", lambda e, half=half, pb_=pb_, x=x: e.tensor_tensor(out=x[:, half * 512:(half + 1) * 512], in0=po[pb_][:], in1=x[:, half * 512:(half + 1) * 512], op=ALU.add), [f"po{pb_}", f"xt{b}"], [f"xt{b}"])
        P.dma("gpsimd", x_out[t0:t0 + 128, :], x[:], [f"xt{b}"], [f"out_{t0}"], f"xt{b}o")
        outs.append(f"out_{t0}")
    P.op("sync", lambda e: None, outs, [])
    P.op("gpsimd", lambda e: None, outs, [])


def launch_rwkv_proj(x, xh, gvec, prm, vfirst):
    N = x.shape[0]
    T = N // NCORES
    NT = T // 128
    has_v = vfirst is not None
    nc = bass.Bass("TRN2", target_bir_lowering=False)
    xa = _dram(nc, "x", [T, D], F32, "ExternalInput")
    xha = _dram(nc, "xh", [128, D], F32, "ExternalInput")
    ga = _dram(nc, "g", [D], F32, "ExternalInput")
    pr = {}
    shapes = {"mu": [6, D], "w_rkv": [3, D, D], "w0": [D], "w1": [D, 64], "w2": [64, D], "a0": [D], "a1": [D, 64], "a2": [64, D],
              "g1": [D, 160], "g2": [160, D], "k_k": [D], "k_a": [D], "r_k": [D]}
    if has_v:
        shapes.update({"v0": [D], "v1": [D, 32], "v2": [32, D]})
    for nm, sh in shapes.items():
        pr[nm] = _dram(nc, "p_" + nm, sh, F32, "ExternalInput")
    QGo = _dram(nc, "QG", [NT, 16, 128, 128], BF16, "ExternalOutput")
    PHo = _dram(nc, "PH", [NT, 16, 128, 128], BF16, "ExternalOutput")
    vbo = _dram(nc, "vb", [T, D], BF16, "ExternalOutput")
    gto = _dram(nc, "gate", [T, D], F32, "ExternalOutput")
    boo = _dram(nc, "bonus", [T, D], F32, "ExternalOutput")
    vfo = _dram(nc, "vfirst", [T, D], F32, "ExternalInput" if has_v else "ExternalOutput")
    P = Prog(nc)
    with ExitStack() as st:
        emit_rwkv_proj(P, nc, st, xa, xha, ga, pr, QGo, PHo, vbo, gto, boo, vfo, T, has_v)
        P.emit()
    maps = []
    for i in range(NCORES):
        m = {"x": x[i * T:(i + 1) * T], "xh": xh[i], "g": gvec}
        for nm in shapes:
            m["p_" + nm] = np.ascontiguousarray(prm[nm]).reshape(shapes[nm])
        if has_v:
            m["vfirst"] = vfirst[i * T:(i + 1) * T]
        maps.append(m)
    return _run(nc, maps)


def launch_rwkv_scan(QG_c, PH_c, V_c, S):
    NTs = S // 128
    nc = bass.Bass("TRN2", target_bir_lowering=False)
    qa = _dram(nc, "QG", [NTs, 8, 128, 128], BF16, "ExternalInput")
    pa = _dram(nc, "PH", [NTs, 8, 128, 128], BF16, "ExternalInput")
    va = _dram(nc, "V", [S, 512], BF16, "ExternalInput")
    yo = _dram(nc, "y", [S, 512], F32, "ExternalOutput")
    P = Prog(nc)
    with ExitStack() as st:
        emit_rwkv_scan(P, nc, st, qa, pa, va, yo, NTs)
        P.emit()
    maps = [{"QG": QG_c[i], "PH": PH_c[i], "V": V_c[i]} for i in range(NCORES)]
    r = _run(nc, maps)
    return [ri["y"] for ri in r]


def launch_rwkv_out_ffn(x, y, gate, bonus, lnw, lnb, w_o, gffn, wg, wu, wd, final_g=None):
    N = x.shape[0]
    T = N // NCORES
    nc = bass.Bass("TRN2", target_bir_lowering=False)
    xa = _dram(nc, "x", [T, D], F32, "ExternalInput")
    ya_ = _dram(nc, "yin", [T, D], F32, "ExternalInput")
    gta = _dram(nc, "gate", [T, D], F32, "ExternalInput")
    boa = _dram(nc, "bonus", [T, D], F32, "ExternalInput")
    lwa = _dram(nc, "lnw", [D], F32, "ExternalInput")
    lba = _dram(nc, "lnb", [D], F32, "ExternalInput")
    woa = _dram(nc, "w_o", [D, D], F32, "ExternalInput")
    ga = _dram(nc, "g", [D], F32, "ExternalInput")
    wga = _dram(nc, "wg", [D, DFF], F32, "ExternalInput")
    wua = _dram(nc, "wu", [D, DFF], F32, "ExternalInput")
    wda = _dram(nc, "wd", [DFF, D], F32, "ExternalInput")
    out = _dram(nc, "y", [T, D], F32, "ExternalOutput")
    fa = _dram(nc, "gfin", [D], F32, "ExternalInput") if final_g is not None else None
    P1 = Prog(nc)
    with ExitStack() as st:
        emit_rwkv_out(P1, nc, st, xa, ya_, gta, boa, lwa, lba, woa, out, T)
        P1.emit()
    P2 = Prog(nc)
    with ExitStack() as st:
        emit_ffn(P2, nc, st, out, ga, wga, wua, wda, out, T, final_g=fa)
        P2.emit()
    maps = []
    for i in range(NCORES):
        sl = slice(i * T, (i + 1) * T)
        m = {"x": x[sl], "yin": y[sl], "gate": gate[i], "bonus": bonus[i], "lnw": lnw, "lnb": lnb, "w_o": w_o, "g": gffn, "wg": wg, "wu": wu, "wd": wd}
        if final_g is not None:
            m["gfin"] = final_g
        maps.append(m)
    r = _run(nc, maps)
    return np.concatenate([ri["y"] for ri in r], axis=0)


def rwkv_layer(x, B, S, g_mix, prm, vfirst, lnw, lnb, w_o, g_ffn, wg, wu, wd, final_g=None):
    N = B * S
    T = N // NCORES
    per = S // T
    xh = []
    for i in range(NCORES):
        t0 = i * T
        hrow = np.zeros((128, D), np.float32)
        if t0 % S != 0:
            hrow[0] = x[t0 - 1]
        xh.append(hrow)
    r = launch_rwkv_proj(x, xh, g_mix, prm, vfirst)
    if vfirst is None:
        vfirst = np.concatenate([ri["vfirst"] for ri in r], axis=0)
    QG_c, PH_c, V_c = [], [], []
    for i in range(NCORES):
        b, hh = i // 2, i % 2
        cs = range(b * per, (b + 1) * per)
        QG_c.append(np.ascontiguousarray(np.concatenate([r[j]["QG"][:, hh * 8:(hh + 1) * 8] for j in cs], axis=0)))
        PH_c.append(np.ascontiguousarray(np.concatenate([r[j]["PH"][:, hh * 8:(hh + 1) * 8] for j in cs], axis=0)))
        V_c.append(np.ascontiguousarray(np.concatenate([r[j]["vb"][:, hh * 512:(hh + 1) * 512] for j in cs], axis=0)))
    ys = launch_rwkv_scan(QG_c, PH_c, V_c, S)
    y = np.concatenate([np.concatenate([ys[2 * b], ys[2 * b + 1]], axis=1) for b in range(B)], axis=0)
    xn = launch_rwkv_out_ffn(x, y, [ri["gate"] for ri in r], [ri["bonus"] for ri in r], lnw, lnb, w_o, g_ffn, wg, wu, wd, final_g=final_g)
    return xn, vfirst


def kernel_unfused(x, positions, norm_mix, norm_ffn, norm_final, attn_w_in, attn_b_in, attn_sinks, attn_w_out,
           rwkv_mu, rwkv_w_rkv, rwkv_w0, rwkv_w1, rwkv_w2, rwkv_a0, rwkv_a1, rwkv_a2, rwkv_g1, rwkv_g2,
           rwkv_k_k, rwkv_k_a, rwkv_r_k, rwkv_lnx_w, rwkv_lnx_b, rwkv_w_o, rwkv_v0, rwkv_v1, rwkv_v2,
           ffn_w_gate, ffn_w_up, ffn_w_down):
    f = lambda a: np.ascontiguousarray(np.asarray(a, dtype=np.float32))
    x = f(x)
    B, S, _ = x.shape
    xc = x.reshape(B * S, D)
    pos = np.ascontiguousarray(np.asarray(positions, dtype=np.int32)).reshape(-1)
    vfirst = None
    depth = norm_mix.shape[0]
    for layer in range(depth):
        i = layer // 2
        fin = f(norm_final) if layer == depth - 1 else None
        if layer % 2 == 0:
            xc = attn_layer(xc, pos, B, S, f(norm_mix[layer]), f(attn_w_in[i]), f(attn_b_in[i]), f(attn_sinks[i]), f(attn_w_out[i]),
                            f(norm_ffn[layer]), f(ffn_w_gate[layer]), f(ffn_w_up[layer]), f(ffn_w_down[layer]), final_g=fin)
        else:
            prm = {"mu": f(rwkv_mu[i]), "w_rkv": f(rwkv_w_rkv[i]), "w0": f(rwkv_w0[i]), "w1": f(rwkv_w1[i]), "w2": f(rwkv_w2[i]),
                   "a0": f(rwkv_a0[i]), "a1": f(rwkv_a1[i]), "a2": f(rwkv_a2[i]), "g1": f(rwkv_g1[i]), "g2": f(rwkv_g2[i]),
                   "k_k": f(rwkv_k_k[i]), "k_a": f(rwkv_k_a[i]), "r_k": f(rwkv_r_k[i]).reshape(-1)}
            vf_in = None
            if i > 0:
                prm.update({"v0": f(rwkv_v0[i - 1]), "v1": f(rwkv_v1[i - 1]), "v2": f(rwkv_v2[i - 1])})
                vf_in = vfirst
            xc, vf = rwkv_layer(xc, B, S, f(norm_mix[layer]), prm, vf_in, f(rwkv_lnx_w[i]), f(rwkv_lnx_b[i]), f(rwkv_w_o[i]),
                                f(norm_ffn[layer]), f(ffn_w_gate[layer]), f(ffn_w_up[layer]), f(ffn_w_down[layer]), final_g=fin)
            if vfirst is None:
                vfirst = vf
    return xc.reshape(B, S, D).astype(np.float32)


def _phase(nc, fn):
    with nc.cleanup_on_exit():
        P = Prog(nc)
        P.persist_sems = True
        with ExitStack() as st:
            fn(P, st)
            P.emit()
        nc.all_engine_barrier()


_W_SHAPES = {
    "norm_mix": [4, D], "norm_ffn": [4, D], "norm_final": [D], "attn_w_in": [2, D, INW], "attn_b_in": [2, INW], "attn_sinks": [2, 12],
    "attn_w_out": [2, D, D], "rwkv_mu": [2, 6, D], "rwkv_w_rkv": [2, 3, D, D], "rwkv_w0": [2, D], "rwkv_w1": [2, D, 64], "rwkv_w2": [2, 64, D],
    "rwkv_a0": [2, D], "rwkv_a1": [2, D, 64], "rwkv_a2": [2, 64, D], "rwkv_g1": [2, D, 160], "rwkv_g2": [2, 160, D], "rwkv_k_k": [2, D],
    "rwkv_k_a": [2, D], "rwkv_r_k": [2, D], "rwkv_lnx_w": [2, D], "rwkv_lnx_b": [2, D], "rwkv_w_o": [2, D, D], "rwkv_v0": [1, D],
    "rwkv_v1": [1, D, 32], "rwkv_v2": [1, 32, D], "ffn_w_gate": [4, D, DFF], "ffn_w_up": [4, D, DFF], "ffn_w_down": [4, DFF, D]}


def build_fused(S, depth=4):
    nc = bass.Bass("TRN2", target_bir_lowering=False)
    NTs = S // 128
    xin = _dram(nc, "x", [S, D], F32, "ExternalInput")
    pos = _dram(nc, "positions", [S], I32, "ExternalInput")
    zer = _dram(nc, "zeros", [128, D], F32, "ExternalInput")
    W = {k: _dram(nc, k, sh, F32, "ExternalInput") for k, sh in _W_SHAPES.items()}
    yout = _dram(nc, "y", [S, D], F32, "ExternalOutput")
    I = "Internal"
    qkT = _dram(nc, "s_qkT", [NH_ROT, 64, S], BF16, I)
    vsc = _dram(nc, "s_v", [S, NV * 64], BF16, I)
    oT = _dram(nc, "s_oT", [16, 64, S], BF16, I)
    xs = [_dram(nc, "s_xa", [S, D], F32, I), _dram(nc, "s_xb", [S, D], F32, I)]
    QG = _dram(nc, "s_QG", [NTs, 16, 128, 128], BF16, I)
    PH = _dram(nc, "s_PH", [NTs, 16, 128, 128], BF16, I)
    vb = _dram(nc, "s_vb", [S, D], BF16, I)
    gate = _dram(nc, "s_gate", [S, D], F32, I)
    bonus = _dram(nc, "s_bonus", [S, D], F32, I)
    vfirst = _dram(nc, "s_vfirst", [S, D], F32, I)
    ysc = _dram(nc, "s_y", [S, D], F32, I)
    cur = xin
    counts = []
    for layer in range(depth):
        i = layer // 2
        last = layer == depth - 1
        nxt_ = xs[layer % 2]
        dst = yout if last else nxt_
        fin = W["norm_final"] if last else None
        if layer % 2 == 0:
            _phase(nc, lambda P, st: emit_attn_proj(P, nc, st, cur, pos, W["norm_mix"][layer], W["attn_w_in"][i], W["attn_b_in"][i], qkT, vsc, S))
            _phase(nc, lambda P, st: emit_attn_core(P, nc, st, qkT, vsc, W["attn_sinks"][i], oT, S))
            _phase(nc, lambda P, st: emit_head_outproj(P, nc, st, cur, oT, W["attn_w_out"][i], nxt_, S))
        else:
            pr = {"mu": W["rwkv_mu"][i], "w_rkv": W["rwkv_w_rkv"][i], "w0": W["rwkv_w0"][i], "w1": W["rwkv_w1"][i], "w2": W["rwkv_w2"][i],
                  "a0": W["rwkv_a0"][i], "a1": W["rwkv_a1"][i], "a2": W["rwkv_a2"][i], "g1": W["rwkv_g1"][i], "g2": W["rwkv_g2"][i],
                  "k_k": W["rwkv_k_k"][i], "k_a": W["rwkv_k_a"][i], "r_k": W["rwkv_r_k"][i]}
            has_v = i > 0
            if has_v:
                pr.update({"v0": W["rwkv_v0"][i - 1], "v1": W["rwkv_v1"][i - 1], "v2": W["rwkv_v2"][i - 1]})
            _phase(nc, lambda P, st: emit_rwkv_proj(P, nc, st, cur, zer, W["norm_mix"][layer], pr, QG, PH, vb, gate, bonus, vfirst, S, has_v))
            for hh in range(2):
                _phase(nc, lambda P, st: emit_rwkv_scan(P, nc, st, QG[:, hh * 8:(hh + 1) * 8], PH[:, hh * 8:(hh + 1) * 8], vb[:, hh * 512:(hh + 1) * 512],
                                                        ysc[:, hh * 512:(hh + 1) * 512], NTs))
            _phase(nc, lambda P, st: emit_rwkv_out(P, nc, st, cur, ysc, gate, bonus, W["rwkv_lnx_w"][i], W["rwkv_lnx_b"][i], W["rwkv_w_o"][i], nxt_, S))
        _phase(nc, lambda P, st: emit_ffn(P, nc, st, nxt_, W["norm_ffn"][layer], W["ffn_w_gate"][layer], W["ffn_w_up"][layer], W["ffn_w_down"][layer], dst, S, final_g=fin))
        cur = nxt_
    return nc


def kernel(x, positions, norm_mix, norm_ffn, norm_final, attn_w_in, attn_b_in, attn_sinks, attn_w_out,
           rwkv_mu, rwkv_w_rkv, rwkv_w0, rwkv_w1, rwkv_w2, rwkv_a0, rwkv_a1, rwkv_a2, rwkv_g1, rwkv_g2,
           rwkv_k_k, rwkv_k_a, rwkv_r_k, rwkv_lnx_w, rwkv_lnx_b, rwkv_w_o, rwkv_v0, rwkv_v1, rwkv_v2,
           ffn_w_gate, ffn_w_up, ffn_w_down):
    f = lambda a: np.ascontiguousarray(np.asarray(a, dtype=np.float32))
    x = f(x)
    B, S, _ = x.shape
    pos = np.ascontiguousarray(np.asarray(positions, dtype=np.int32))
    loc = dict(norm_mix=norm_mix, norm_ffn=norm_ffn, norm_final=norm_final, attn_w_in=np.asarray(attn_w_in)[:, :, _PERM],
               attn_b_in=np.asarray(attn_b_in)[:, _PERM], attn_sinks=attn_sinks, attn_w_out=attn_w_out, rwkv_mu=rwkv_mu, rwkv_w_rkv=rwkv_w_rkv,
               rwkv_w0=rwkv_w0, rwkv_w1=rwkv_w1, rwkv_w2=rwkv_w2, rwkv_a0=rwkv_a0, rwkv_a1=rwkv_a1, rwkv_a2=rwkv_a2, rwkv_g1=rwkv_g1,
               rwkv_g2=rwkv_g2, rwkv_k_k=rwkv_k_k, rwkv_k_a=rwkv_k_a, rwkv_r_k=rwkv_r_k, rwkv_lnx_w=rwkv_lnx_w, rwkv_lnx_b=rwkv_lnx_b,
               rwkv_w_o=rwkv_w_o, rwkv_v0=rwkv_v0, rwkv_v1=rwkv_v1, rwkv_v2=rwkv_v2, ffn_w_gate=ffn_w_gate, ffn_w_up=ffn_w_up, ffn_w_down=ffn_w_down)
    wts = {k: f(v).reshape(_W_SHAPES[k]) for k, v in loc.items()}
    zeros = np.zeros((128, D), np.float32)
    nc = build_fused(S, depth=np.asarray(norm_mix).shape[0])
    maps = []
    for c in range(NCORES):
        b = c % B
        m = {"x": x[b], "positions": pos[b], "zeros": zeros}
        m.update(wts)
        maps.append(m)
    r = _run(nc, maps)
    return np.stack([r[b]["y"] for b in range(B)], axis=0).astype(np.float32)
```

```python
import numpy as np
import concourse.bass as bass
import concourse.mybir as mybir
from contextlib import ExitStack
from concourse.bass_utils import run_bass_kernel_spmd

F32 = mybir.dt.float32
BF16 = mybir.dt.bfloat16
I32 = mybir.dt.int32
ALU = mybir.AluOpType
AF = mybir.ActivationFunctionType
AX = mybir.AxisListType

ENGS = ("tensor", "vector", "scalar", "gpsimd", "sync")


class _Op:
    __slots__ = ("eng", "fn", "deps", "idx", "dma_key", "signal", "semval", "waits")

    def __init__(self, eng, fn, dma_key):
        self.eng = eng
        self.fn = fn
        self.deps = []
        self.dma_key = dma_key
        self.signal = False
        self.semval = None
        self.waits = []


class Prog:
    def __init__(self, nc):
        self.nc = nc
        self.ops = {e: [] for e in ENGS}
        self.last_w = {}
        self.readers = {}
        self.all_ops = []
        self.bar = {e: None for e in ENGS}
        self.last_dma = {}

    def barrier(self):
        lasts = [self.ops[e][-1] for e in ENGS if self.ops[e]] + list(self.last_dma.values())
        for e in ENGS:
            self.bar[e] = list(lasts) + (self.bar[e] or [])
        self.last_w = {}
        self.readers = {}

    def op(self, eng, fn, reads=(), writes=(), dma_key=None):
        o = _Op(eng, fn, dma_key)
        deps = []
        for r in reads:
            w = self.last_w.get(r)
            if w is not None:
                deps.append(w)
        for w_ in writes:
            w = self.last_w.get(w_)
            if w is not None:
                deps.append(w)
            deps.extend(self.readers.get(w_, ()))
        if self.bar[eng] is not None:
            deps.extend(self.bar[eng])
            self.bar[eng] = None
        if dma_key is not None:
            self.last_dma[dma_key] = o
        o.deps = deps
        o.idx = len(self.ops[eng])
        self.ops[eng].append(o)
        self.all_ops.append(o)
        for r in reads:
            self.readers.setdefault(r, []).append(o)
        for w_ in writes:
            self.last_w[w_] = o
            self.readers[w_] = []
        return o

    def dma(self, eng, out, in_, reads, writes, key, **kw):
        def fn(e, out=out, in_=in_):
            return e.dma_start(out=out, in_=in_, **kw)
        return self.op(eng, fn, reads, writes, dma_key=key)

    def emit(self, extra_final_wait=True):
        nc = self.nc
        dma_count = {}
        for o in self.all_ops:
            if o.dma_key is not None:
                dma_count[o.dma_key] = dma_count.get(o.dma_key, 0) + 1
                o.semval = 16 * dma_count[o.dma_key]
        seen = {e: {} for e in ENGS}
        for e in ENGS:
            for o in self.ops[e]:
                need = {}
                for d in o.deps:
                    if d.dma_key is not None:
                        k = ("D", d.dma_key)
                        v = d.semval
                    else:
                        if d.eng == e and (d.idx >= o.idx or e == "tensor"):
                            continue
                        k = ("E", d.eng)
                        v = d.idx + 1
                    if v > need.get(k, (0, None))[0]:
                        need[k] = (v, d)
                for k, (v, d) in need.items():
                    if seen[e].get(k, 0) >= v:
                        continue
                    seen[e][k] = v
                    o.waits.append(d)
                    d.signal = True
        for e in ENGS:
            c = 0
            for o in self.ops[e]:
                if o.dma_key is None and o.signal:
                    c += 1
                    o.semval = c
        keys = sorted(dma_count.keys())
        with ExitStack() as st:
            if getattr(self, "persist_sems", False):
                _PH[0] += 1
                esem = {e: nc.alloc_semaphore(name=f"s{_PH[0]}_" + e) for e in ENGS}
                dsem = {k: nc.alloc_semaphore(name=f"d{_PH[0]}_" + str(k)) for k in keys}
            else:
                esem = {e: st.enter_context(nc.semaphore("s_" + e)) for e in ENGS}
                dsem = {k: st.enter_context(nc.semaphore("d_" + str(k))) for k in keys}
            block = st.enter_context(nc.Block())

            def mk(e):
                def body(eng):
                    for o in self.ops[e]:
                        for d in o.waits:
                            if d.dma_key is not None:
                                eng.wait_ge(dsem[d.dma_key], d.semval)
                            else:
                                eng.wait_ge(esem[d.eng], d.semval)
                        ins = o.fn(eng)
                        if ins is None:
                            continue
                        if o.dma_key is not None:
                            ins.then_inc(dsem[o.dma_key], 16)
                        elif o.signal:
                            ins.then_inc(esem[e], 1)
                return body

            for e in ENGS:
                if self.ops[e]:
                    getattr(block, e)(mk(e))
        return {e: len(self.ops[e]) for e in ENGS}


D = 1024
DFF = 2816
EPS = 1e-5
NH_ROT = 39
NV = 15
INW = 3456
PI = float(np.pi)
ROPE_THETA = 500000.0


_PH = [0]


def _alloc(nc, st):
    _PH[0] += 1
    pre = f"p{_PH[0]}_"
    sb = lambda name, shape, dt: st.enter_context(nc.sbuf_tensor(pre + name, shape, dt))
    ps = lambda name, shape, dt: st.enter_context(nc.psum_tensor(pre + name, shape, dt))
    return sb, ps


def emit_consts(P, nc, sb):
    c = {}
    c["identf"] = sb("identf", [128, 128], F32)
    c["ident"] = sb("ident", [128, 128], BF16)
    c["epsb"] = sb("epsb", [128, 1], F32)
    identf, ident, epsb = c["identf"], c["ident"], c["epsb"]
    P.op("gpsimd", lambda e: e.memset(identf[:], 0.0), [], ["identf"])
    P.op("gpsimd", lambda e: e.affine_select(out=identf[:], in_=identf[:], pattern=[[-1, 128]], compare_op=ALU.not_equal,
                                             fill=1.0, base=0, channel_multiplier=1), ["identf"], ["identf"])
    P.op("vector", lambda e: e.tensor_copy(out=ident[:], in_=identf[:]), ["identf"], ["ident"])
    P.op("vector", lambda e: e.memset(epsb[:], EPS), [], ["epsb"])
    return c


def emit_norm_T(P, c, x, xn, ss, rstd, hb, ptr, hT_dst, hT_name, b, eps_name="epsb"):
    epsb, ident = c["epsb"], c["ident"]
    P.op("scalar", lambda e: e.activation(out=hb[:], in_=x[:], func=AF.Square, accum_out=ss[:]), [xn], [f"hb{b}", f"ss{b}"])
    P.op("scalar", lambda e: e.activation(out=rstd[:], in_=ss[:], func=AF.Sqrt, bias=epsb[:], scale=1.0 / D), [f"ss{b}", "epsb"], [f"rstd{b}"])
    P.op("vector", lambda e: e.reciprocal(out=rstd[:], in_=rstd[:]), [f"rstd{b}"], [f"rstd{b}"])
    P.op("vector", lambda e: e.tensor_scalar(out=hb[:], in0=x[:], scalar1=rstd[:], scalar2=None, op0=ALU.mult), [xn, f"rstd{b}"], [f"hb{b}"])
    for cc in range(8):
        P.op("tensor", lambda e, cc=cc: e.transpose(out=ptr[:, cc, :], in_=hb[:, cc * 128:(cc + 1) * 128], identity=ident[:]),
             [f"hb{b}", "ident"], ["ptr"])
    P.op("scalar", lambda e: e.copy(out=hT_dst, in_=ptr[:]), ["ptr"], [hT_name])


def load_weight_bf16(P, nc, w_rows, wb_dst_fn, ncols, stg, gcol_fn, names_fn, k0=0, piece=1408):
    k = k0
    for c, wr in enumerate(w_rows):
        for h0 in range(0, ncols, piece):
            h1 = min(ncols, h0 + piece)
            s = stg[k % 2]
            sn = f"stg{k % 2}"
            P.dma("sync", s[:, 0:h1 - h0], wr[:, h0:h1], [], [sn], sn)
            eng = "vector" if k % 2 == 0 else "gpsimd"
            g = gcol_fn(c) if gcol_fn is not None else None
            if g is not None:
                P.op(eng, lambda e, s=s, c=c, h0=h0, h1=h1, g=g: e.tensor_scalar(out=wb_dst_fn(c)[:, h0:h1], in0=s[:, 0:h1 - h0], scalar1=g, scalar2=None, op0=ALU.mult),
                     [sn, "gcol"], [names_fn(c)])
            else:
                P.op(eng, lambda e, s=s, c=c, h0=h0, h1=h1: e.tensor_copy(out=wb_dst_fn(c)[:, h0:h1], in_=s[:, 0:h1 - h0]), [sn], [names_fn(c)])
            k += 1
    return k


def emit_ffn(P, nc, st, x_in, gvec, wg, wu, wd, x_out, T, TG=256, final_g=None):
    NJ = TG // 128
    NF = DFF // 128
    sb, ps = _alloc(nc, st)
    c = emit_consts(P, nc, sb)
    wgb = sb("wgb", [128, 8, DFF], BF16)
    wub = sb("wub", [128, 8, DFF], BF16)
    wdb = sb("wdb", [128, NF, D], BF16)
    stg = [sb(f"stg{i}", [128, 1408], F32) for i in range(2)]
    gcol = sb("gcol", [128, 8], F32)
    P.dma("sync", gcol[:], gvec.rearrange("(c p) -> p c", p=128), [], ["gcol"], "gcol", allow_slow_non_contiguous=True)
    k = load_weight_bf16(P, nc, [wg[cc * 128:(cc + 1) * 128, :] for cc in range(8)], lambda cc: wgb[:, cc, :], DFF, stg,
                         lambda cc: gcol[:, cc:cc + 1], lambda cc: f"wgb{cc}")
    k = load_weight_bf16(P, nc, [wu[cc * 128:(cc + 1) * 128, :] for cc in range(8)], lambda cc: wub[:, cc, :], DFF, stg,
                         lambda cc: gcol[:, cc:cc + 1], lambda cc: f"wub{cc}", k0=k)
    k = load_weight_bf16(P, nc, [wd[f * 128:(f + 1) * 128, :] for f in range(NF)], lambda f: wdb[:, f, :], D, stg,
                         None, lambda f: f"wdb{f}", k0=k)
    if final_g is not None:
        gfin = sb("gfin", [128, D], F32)
        P.dma("sync", gfin[:], final_g.partition_broadcast(128), [], ["gfin"], "gfin")

    xt = [sb(f"xt{j}", [128, D], F32) for j in range(2 * NJ)]
    hb = [sb(f"hb{j}", [128, D], BF16) for j in range(2)]
    ss = [sb(f"ss{j}", [128, 1], F32) for j in range(2)]
    rstd = [sb(f"rstd{j}", [128, 1], F32) for j in range(2)]
    hT = [sb(f"hT{i}", [128, 8, TG], BF16) for i in range(2)]
    act = sb("act", [128, NF, TG], BF16)
    sg = [sb(f"sg{i}", [128, TG], F32) for i in range(2)]
    ptr = ps("ptr", [128, 8, 128], BF16)
    pg = [ps(f"pg{i}", [128, 512], F32) for i in range(2)]
    pu = [ps(f"pu{i}", [128, 512], F32) for i in range(2)]
    po = [ps(f"po{i}", [128, 512], F32) for i in range(2)]

    ng = T // TG
    cnt = fcnt = ocnt = 0
    outs = []
    for g in range(ng):
        hTg = hT[g % 2]
        hTn = f"hT{g % 2}"
        for j in range(NJ):
            xi = (g % 2) * NJ + j
            x, xn = xt[xi], f"xt{xi}"
            t0 = g * TG + j * 128
            P.dma("sync", x[:], x_in[t0:t0 + 128, :], [], [xn], xn)
            b = cnt % 2
            cnt += 1
            emit_norm_T(P, c, x, xn, ss[b], rstd[b], hb[b], ptr, hTg[:, :, j * 128:(j + 1) * 128], hTn + f"_{j}", b)
        hreads = [hTn + f"_{j}" for j in range(NJ)]
        for f in range(NF):
            b = fcnt % 2
            fcnt += 1
            for cc in range(8):
                P.op("tensor", lambda e, b=b, cc=cc, f=f, hTg=hTg: e.matmul(out=pg[b][:, 0:TG], lhsT=wgb[:, cc, f * 128:(f + 1) * 128], rhs=hTg[:, cc, :], start=(cc == 0), stop=(cc == 7)),
                     hreads + [f"wgb{cc}"], [f"pg{b}"])
            for cc in range(8):
                P.op("tensor", lambda e, b=b, cc=cc, f=f, hTg=hTg: e.matmul(out=pu[b][:, 0:TG], lhsT=wub[:, cc, f * 128:(f + 1) * 128], rhs=hTg[:, cc, :], start=(cc == 0), stop=(cc == 7)),
                     hreads + [f"wub{cc}"], [f"pu{b}"])
            P.op("scalar", lambda e, b=b: e.activation(out=sg[b][:], in_=pg[b][:, 0:TG], func=AF.Silu), [f"pg{b}"], [f"sg{b}"])
            P.op("vector", lambda e, b=b, f=f: e.tensor_tensor(out=act[:, f, :], in0=sg[b][:], in1=pu[b][:, 0:TG], op=ALU.mult), [f"sg{b}", f"pu{b}"], [f"act{f}"])
        for j in range(NJ):
            xi = (g % 2) * NJ + j
            x, xn = xt[xi], f"xt{xi}"
            t0 = g * TG + j * 128
            for n in range(2):
                b = ocnt % 2
                ocnt += 1
                for f in range(NF):
                    P.op("tensor", lambda e, b=b, f=f, j=j, n=n: e.matmul(out=po[b][:], lhsT=act[:, f, j * 128:(j + 1) * 128], rhs=wdb[:, f, n * 512:(n + 1) * 512], start=(f == 0), stop=(f == NF - 1)),
                         [f"act{f}", f"wdb{f}"], [f"po{b}"])
                P.op("vector", lambda e, b=b, x=x, n=n: e.tensor_tensor(out=x[:, n * 512:(n + 1) * 512], in0=po[b][:], in1=x[:, n * 512:(n + 1) * 512], op=ALU.add), [f"po{b}", xn], [xn])
            if final_g is not None:
                b = cnt % 2
                cnt += 1
                P.op("scalar", lambda e, x=x, b=b: e.activation(out=hb[b][:], in_=x[:], func=AF.Square, accum_out=ss[b][:]), [xn], [f"hb{b}", f"ss{b}"])
                P.op("scalar", lambda e, b=b: e.activation(out=rstd[b][:], in_=ss[b][:], func=AF.Sqrt, bias=c["epsb"][:], scale=1.0 / D), [f"ss{b}", "epsb"], [f"rstd{b}"])
                P.op("vector", lambda e, b=b: e.reciprocal(out=rstd[b][:], in_=rstd[b][:]), [f"rstd{b}"], [f"rstd{b}"])
                P.op("vector", lambda e, x=x, b=b: e.scalar_tensor_tensor(out=x[:], in0=x[:], scalar=rstd[b][:], in1=gfin[:], op0=ALU.mult, op1=ALU.mult), [xn, f"rstd{b}", "gfin"], [xn])
            P.dma("gpsimd", x_out[t0:t0 + 128, :], x[:], [xn], [f"out_{t0}"], xn + "o")
            outs.append(f"out_{t0}")
    P.op("sync", lambda e: None, outs, [])
    P.op("gpsimd", lambda e: None, outs, [])


def emit_attn_proj(P, nc, st, x_in, pos, gvec, w_in, b_in, qkT_out, v_out, T):
    sb, ps = _alloc(nc, st)
    c = emit_consts(P, nc, sb)
    NT = T // 128
    winb = sb("winb", [128, 8, INW], BF16)
    stg = [sb(f"stg{i}", [128, 1728], F32) for i in range(2)]
    gcol = sb("gcol", [128, 8], F32)
    P.dma("sync", gcol[:], gvec.rearrange("(c p) -> p c", p=128), [], ["gcol"], "gcol", allow_slow_non_contiguous=True)
    load_weight_bf16(P, nc, [w_in[cc * 128:(cc + 1) * 128, :] for cc in range(8)], lambda cc: winb[:, cc, :], INW, stg,
                     lambda cc: gcol[:, cc:cc + 1], lambda cc: f"winb{cc}", piece=1728)
    biasb = sb("biasb", [1, INW], BF16)
    ones1 = sb("ones1", [1, 128], BF16)
    b2 = b_in.rearrange("(o n) -> o n", o=1)
    for hi in range(2):
        P.dma("sync", stg[hi][0:1, :], b2[:, hi * 1728:(hi + 1) * 1728], [], [f"stg{hi}"], f"stg{hi}")
        P.op("vector", lambda e, hi=hi: e.tensor_copy(out=biasb[:, hi * 1728:(hi + 1) * 1728], in_=stg[hi][0:1, :]), [f"stg{hi}"], ["biasb"])
    P.op("vector", lambda e: e.memset(ones1[:], 1.0), [], ["ones1"])
    posi = sb("posi", [128, NT], I32)
    posf = sb("posf", [128, NT], F32)
    ftab = sb("ftab", [128, 32], F32)
    otab = sb("otab", [128, 32], F32)
    ang = sb("ang", [128, NT, 32], F32)
    cs = sb("cs", [128, NT, 32], F32)
    qkv = [sb(f"qkv{i}", [128, INW], F32) for i in range(2)]
    ang2 = qkv[0][:, 0:NT * 32].rearrange("p (n f) -> p n f", f=32)
    angi = qkv[1][:, 0:NT * 32].bitcast(I32).rearrange("p (n f) -> p n f", f=32)
    P.dma("sync", posi[:], pos.rearrange("(n p) -> p n", p=128), [], ["posi"], "posi", allow_slow_non_contiguous=True)
    P.op("vector", lambda e: e.tensor_copy(out=posf[:], in_=posi[:]), ["posi"], ["posf"])
    ft3 = ftab[:].rearrange("p (a i) -> p a i", i=8)
    for i in range(8):
        fi = float(np.power(np.float32(ROPE_THETA), np.float32(-2.0 * i / 16.0)))
        P.op("gpsimd", lambda e, i=i, fi=fi: e.memset(ft3[:, :, i], fi), [], ["ftab"])
    for a, off in enumerate((PI / 2, PI / 2, PI, 0.0)):
        P.op("gpsimd", lambda e, a=a, off=off: e.memset(otab[:, a * 8:(a + 1) * 8], off), [], ["otab"])
    P.op("vector", lambda e: e.tensor_tensor(out=ang[:], in0=ftab[:].unsqueeze(1).broadcast_to([128, NT, 32]),
                                             in1=posf[:].unsqueeze(2).broadcast_to([128, NT, 32]), op=ALU.mult), ["ftab", "posf"], ["ang"])
    P.op("vector", lambda e: e.tensor_tensor(out=ang[:], in0=ang[:], in1=otab[:].unsqueeze(1).broadcast_to([128, NT, 32]), op=ALU.add), ["ang", "otab"], ["ang"])
    P.op("vector", lambda e: e.tensor_scalar(out=angi[:], in0=ang[:], scalar1=1.0 / (2 * PI), scalar2=None, op0=ALU.mult), ["ang"], ["angi"])
    P.op("vector", lambda e: e.tensor_copy(out=ang2[:], in_=angi[:]), ["angi"], ["ang2"])
    P.op("vector", lambda e: e.scalar_tensor_tensor(out=ang[:], in0=ang2[:], scalar=-2 * PI, in1=ang[:], op0=ALU.mult, op1=ALU.add), ["ang2", "ang"], ["ang"])
    P.op("vector", lambda e: e.tensor_scalar(out=ang2[:], in0=ang[:], scalar1=PI, scalar2=-2 * PI, op0=ALU.is_gt, op1=ALU.mult), ["ang"], ["ang2"])
    P.op("vector", lambda e: e.tensor_tensor(out=ang[:], in0=ang[:], in1=ang2[:], op=ALU.add), ["ang", "ang2"], ["ang"])
    P.op("vector", lambda e: e.tensor_scalar(out=ang2[:], in0=ang[:], scalar1=-PI, scalar2=2 * PI, op0=ALU.is_lt, op1=ALU.mult), ["ang"], ["ang2"])
    P.op("vector", lambda e: e.tensor_tensor(out=ang[:], in0=ang[:], in1=ang2[:], op=ALU.add), ["ang", "ang2"], ["ang"])
    P.op("scalar", lambda e: e.activation(out=cs[:], in_=ang[:], func=AF.Sin), ["ang"], ["cs"])

    xt = [sb(f"xt{j}", [128, D], F32) for j in range(2)]
    hb = [sb(f"hb{j}", [128, D], BF16) for j in range(2)]
    ss = [sb(f"ss{j}", [128, 1], F32) for j in range(2)]
    rstd = [sb(f"rstd{j}", [128, 1], F32) for j in range(2)]
    hT = [sb(f"hT{i}", [128, 8, 128], BF16) for i in range(2)]
    P.barrier()
    tA = sb("tA", [128, NH_ROT, 16], F32)
    tB = sb("tB", [128, NH_ROT, 16], F32)
    qkb = [sb(f"qkb{i}", [128, NH_ROT * 64], BF16) for i in range(2)]
    vb = [sb(f"vb{i}", [128, NV * 64], BF16) for i in range(2)]
    qkTs = [sb(f"qkTs{i}", [64, NH_ROT, 256], BF16) for i in range(2)]
    ptr = ps("ptr", [128, 8, 128], BF16)
    pp = [ps(f"pp{i}", [128, 512], F32) for i in range(2)]
    ptq = [ps(f"ptq{i}", [64, 8, 128], BF16) for i in range(2)]
    ident = c["ident"]
    pcnt = 0
    qcnt = 0
    outs = []
    for t in range(NT):
        b = t % 2
        x, xn = xt[b], f"xt{b}"
        t0 = t * 128
        P.dma("sync", x[:], x_in[t0:t0 + 128, :], [], [xn], xn)
        emit_norm_T(P, c, x, xn, ss[b], rstd[b], hb[b], ptr, hT[b][:], f"hT{b}", b)
        for n0 in range(0, INW, 512):
            n1 = min(INW, n0 + 512)
            pb = pcnt % 2
            pcnt += 1
            for cc in range(8):
                P.op("tensor", lambda e, pb=pb, cc=cc, n0=n0, n1=n1, b=b: e.matmul(out=pp[pb][:, 0:n1 - n0], lhsT=hT[b][:, cc, :], rhs=winb[:, cc, n0:n1], start=(cc == 0), stop=False),
                     [f"hT{b}", f"winb{cc}"], [f"pp{pb}"])
            P.op("tensor", lambda e, pb=pb, n0=n0, n1=n1: e.matmul(out=pp[pb][:, 0:n1 - n0], lhsT=ones1[:], rhs=biasb[:, n0:n1], start=False, stop=True),
                 ["ones1", "biasb"], [f"pp{pb}"])
            eng = "scalar" if (pcnt % 2 == 0) else "vector"
            if eng == "scalar":
                P.op("scalar", lambda e, pb=pb, n0=n0, n1=n1, b=b: e.copy(out=qkv[b][:, n0:n1], in_=pp[pb][:, 0:n1 - n0]), [f"pp{pb}"], [f"qkv{b}"])
            else:
                P.op("vector", lambda e, pb=pb, n0=n0, n1=n1, b=b: e.tensor_copy(out=qkv[b][:, n0:n1], in_=pp[pb][:, 0:n1 - n0]), [f"pp{pb}"], [f"qkv{b}"])
        Q3 = qkv[b][:, 0:NH_ROT * 64].rearrange("p (h d) -> p h d", d=64)
        O3 = qkb[b][:].rearrange("p (h d) -> p h d", d=64)
        cst = cs[:, t, :]
        P.op("gpsimd", lambda e, Q3=Q3, cst=cst: e.tensor_tensor(out=tA[:], in0=Q3[:, :, 0:16], in1=cst[:, 0:16].unsqueeze(1).broadcast_to([128, NH_ROT, 16]), op=ALU.mult), [f"qkv{b}", "cs"], ["tA"])
        P.op("vector", lambda e, Q3=Q3, cst=cst: e.tensor_tensor(out=tB[:, :, 0:8], in0=Q3[:, :, 8:16], in1=cst[:, 16:24].unsqueeze(1).broadcast_to([128, NH_ROT, 8]), op=ALU.mult), [f"qkv{b}", "cs"], ["tB0"])
        P.op("vector", lambda e, Q3=Q3, cst=cst: e.tensor_tensor(out=tB[:, :, 8:16], in0=Q3[:, :, 0:8], in1=cst[:, 24:32].unsqueeze(1).broadcast_to([128, NH_ROT, 8]), op=ALU.mult), [f"qkv{b}", "cs"], ["tB1"])
        P.op("vector", lambda e, O3=O3: e.tensor_tensor(out=O3[:, :, 0:16], in0=tA[:], in1=tB[:], op=ALU.add), ["tA", "tB0", "tB1"], [f"qkb{b}r"])
        P.op("scalar", lambda e, O3=O3, Q3=Q3: e.copy(out=O3[:, :, 16:64], in_=Q3[:, :, 16:64]), [f"qkv{b}"], [f"qkb{b}c"])
        P.op("gpsimd", lambda e, b=b: e.tensor_copy(out=vb[b][:], in_=qkv[b][:, NH_ROT * 64:INW]), [f"qkv{b}"], [f"vb{b}"])
        P.dma("gpsimd", v_out[t0:t0 + 128, :], vb[b][:], [f"vb{b}"], [f"vout{t}"], f"vb{b}o")
        outs.append(f"vout{t}")
        sbi = (t // 2) % 2
        so = (t % 2) * 128
        for h0 in range(0, NH_ROT, 8):
            h1 = min(NH_ROT, h0 + 8)
            qb_ = qcnt % 2
            qcnt += 1
            for h in range(h0, h1):
                P.op("tensor", lambda e, qb_=qb_, h=h, h0=h0, b=b: e.transpose(out=ptq[qb_][:, h - h0, :], in_=qkb[b][:, h * 64:(h + 1) * 64], identity=ident[:]),
                     [f"qkb{b}r", f"qkb{b}c", "ident"], [f"ptq{qb_}"])
            eng = "scalar" if (qcnt % 2 == 0) else "vector"
            if eng == "scalar":
                P.op("scalar", lambda e, qb_=qb_, h0=h0, h1=h1, sbi=sbi, so=so: e.copy(out=qkTs[sbi][:, h0:h1, so:so + 128], in_=ptq[qb_][:, 0:h1 - h0, :]), [f"ptq{qb_}"], [f"qkTs{sbi}_{t % 2}_{h0}"])
            else:
                P.op("vector", lambda e, qb_=qb_, h0=h0, h1=h1, sbi=sbi, so=so: e.tensor_copy(out=qkTs[sbi][:, h0:h1, so:so + 128], in_=ptq[qb_][:, 0:h1 - h0, :]), [f"ptq{qb_}"], [f"qkTs{sbi}_{t % 2}_{h0}"])
        if t % 2 == 1:
            tt0 = (t - 1) * 128
            rd = [f"qkTs{sbi}_{u}_{h0}" for u in range(2) for h0 in range(0, NH_ROT, 8)]
            for qi, (ha, hb_) in enumerate(((0, 13), (13, 26), (26, 39))):
                P.dma("sync", qkT_out[ha:hb_, :, tt0:tt0 + 256].rearrange("h d t -> d h t"), qkTs[sbi][:, ha:hb_, :], rd, [f"qout{t}_{qi}"], f"qkTs{sbi}o{qi}")
                outs.append(f"qout{t}_{qi}")
    P.op("sync", lambda e: None, outs, [])
    P.op("gpsimd", lambda e: None, outs, [])


def emit_attn_core(P, nc, st, qkT, v, sinks, oT_out, S):
    sb, ps = _alloc(nc, st)
    CH = 2048
    NCH = S // CH
    ones64 = sb("ones64", [128, 64], BF16)
    mfull = sb("mfull", [128, 4, 128], F32)
    m_own = sb("m_own", [128, 4, 128], BF16)
    m_pA = sb("m_pA", [128, 4, 128], BF16)
    m_pB = sb("m_pB", [128, 4, 128], BF16)
    P.op("vector", lambda e: e.memset(ones64[:], 1.0), [], ["ones64"])
    for (m, cm, pat, cmp_, nm) in ((m_own, -1, 1, ALU.is_ge, "m_own"), (m_pA, 1, -1, ALU.is_gt, "m_pA"), (m_pB, 1, -1, ALU.is_ge, "m_pB")):
        P.op("gpsimd", lambda e: e.memset(mfull[:], 1.0), [], ["mfull"])
        P.op("gpsimd", lambda e, cm=cm, pat=pat, cmp_=cmp_: e.affine_select(out=mfull[:], in_=mfull[:], pattern=[[0, 4], [pat, 128]], compare_op=cmp_,
                                                                           fill=0.0, base=0, channel_multiplier=cm), ["mfull"], ["mfull"])
        P.op("gpsimd", lambda e, m=m: e.tensor_copy(out=m[:], in_=mfull[:]), ["mfull"], [nm])
    sk = sb("sk", [64, 12], F32)
    esink = sb("esink", [64, 12, 128], F32)
    P.dma("sync", sk[:], sinks.partition_broadcast(64), [], ["sk"], "sk")
    P.op("scalar", lambda e: e.activation(out=sk[:], in_=sk[:], func=AF.Exp), ["sk"], ["sk"])
    P.op("vector", lambda e: e.memset(esink[:], 0.0), [], ["esink"])
    for h in range(12):
        P.op("vector", lambda e, h=h: e.tensor_scalar(out=esink[:, h, :], in0=esink[:, h, :], scalar1=sk[:, h:h + 1], scalar2=None, op0=ALU.add), ["esink", "sk"], ["esink"])

    qt = sb("qt", [64, 4, CH], BF16)
    kt = sb("kt", [64, 4, 2 * CH], BF16)
    vt = sb("vt", [128, 32, 256], BF16)
    vA = sb("vA", [128, 17, 64], BF16)
    accn = sb("accn", [64, 4, CH], F32)
    accd = sb("accd", [64, 4, CH], F32)
    pT = [sb(f"pT{i}", [128, 4, 128], BF16) for i in range(4)]
    tmp = [sb(f"tmp{i}", [64, 4, 128], F32) for i in range(2)]
    oA = [sb(f"oA{i}", [64, 4, 128], BF16) for i in range(2)]
    oB = [sb(f"oB{i}", [64, 4, 512], BF16) for i in range(2)]
    rdB = sb("rdB", [64, 4, 512], F32)
    psc = [ps(f"psc{i}", [128, 4, 128], F32) for i in range(4)]
    pnum = [ps(f"pnum{i}", [64, 4, 128], F32) for i in range(2)]
    pden = [ps(f"pden{i}", [64, 4, 128], F32) for i in range(2)]
    outs = []
    ucnt = 0

    def unit(kind, g, c, first, score_fn, pv_fn, out_fn):
        nonlocal ucnt
        u = ucnt % 2
        ucnt += 1
        kbs = (0,) if first else (0, 1)
        for kb in kbs:
            si = 2 * u + kb
            score_fn(kb, psc[si], f"psc{si}")
            P.op("scalar", lambda e, si=si: e.activation(out=pT[si][:], in_=psc[si][:], func=AF.Exp, scale=0.125), [f"psc{si}"], [f"pT{si}"])
            m, mn = (m_own, "m_own") if kb == 0 else ((m_pA, "m_pA") if kind == "A" else (m_pB, "m_pB"))
            P.op("gpsimd", lambda e, si=si, m=m: e.tensor_tensor(out=pT[si][:], in0=pT[si][:], in1=m[:], op=ALU.mult), [f"pT{si}", mn], [f"pT{si}"])
        pv_fn(kbs, [pT[2 * u + kb] for kb in kbs], [f"pT{2 * u + kb}" for kb in kbs], pnum[u], f"pnum{u}")
        for i, kb in enumerate(kbs):
            P.op("tensor", lambda e, u=u, kb=kb, i=i: e.matmul(out=pden[u][:], lhsT=ones64[:], rhs=pT[2 * u + kb][:], start=(i == 0), stop=(i == len(kbs) - 1)),
                 [f"pT{2 * u + kb}", "ones64"], [f"pden{u}"])
        out_fn(u)

    for c in range(NCH):
        c0 = c * CH
        for g in range(3):
            P.dma("sync", qt[:], qkT[4 * g:4 * g + 4, :, c0:c0 + CH].rearrange("h d t -> d h t"), [], ["qt"], "qt")
            if c == 0:
                P.dma("sync", kt[:, 0, CH:2 * CH], qkT[12 + g, :, c0:c0 + CH], [], ["kt"], "kt")
                for n0 in (0, 8):
                    P.dma("sync", vA[:, 1 + n0:9 + n0, :], v[c0 + n0 * 128:c0 + (n0 + 8) * 128, g * 64:(g + 1) * 64].rearrange("(n p) d -> p n d", p=128), [], ["vA"], f"vA{n0}")
            else:
                P.dma("sync", kt[:, 0, CH - 128:2 * CH], qkT[12 + g, :, c0 - 128:c0 + CH], [], ["kt"], "kt")
                for n0, n1 in ((0, 9), (9, 17)):
                    P.dma("sync", vA[:, n0:n1, :], v[c0 - 128 + n0 * 128:c0 - 128 + n1 * 128, g * 64:(g + 1) * 64].rearrange("(n p) d -> p n d", p=128), [], ["vA"], f"vA{n0}")
            for i in range(16):
                first = (c == 0 and i == 0)

                def score_fn(kb, pt, pn, i=i):
                    k0 = CH + (i - kb) * 128
                    P.op("tensor", lambda e: e.matmul(out=pt[:], lhsT=kt[:, 0, k0:k0 + 128], rhs=qt[:, :, i * 128:(i + 1) * 128], start=True, stop=True),
                         ["qt", "kt"], [pn])

                def pv_fn(kbs, pts, ptn, pn_, pnn, i=i):
                    for ii, kb in enumerate(kbs):
                        P.op("tensor", lambda e, ii=ii, kb=kb: e.matmul(out=pn_[:], lhsT=vA[:, i + 1 - kb, :], rhs=pts[ii][:], start=(ii == 0), stop=(ii == len(kbs) - 1)),
                             [ptn[ii], "vA"], [pnn])

                def out_fn(u, i=i, g=g, c0=c0):
                    P.op("vector", lambda e: e.tensor_tensor(out=tmp[u][:], in0=pden[u][:], in1=esink[:, 4 * g:4 * g + 4, :], op=ALU.add), [f"pden{u}", "esink"], [f"tmp{u}"])
                    P.op("vector", lambda e: e.reciprocal(out=tmp[u][:], in_=tmp[u][:]), [f"tmp{u}"], [f"tmp{u}"])
                    P.op("vector", lambda e: e.tensor_tensor(out=oA[u][:], in0=pnum[u][:], in1=tmp[u][:], op=ALU.mult), [f"pnum{u}", f"tmp{u}"], [f"oA{u}"])
                    nm = f"oo_{c0}_{g}_{i}"
                    P.dma("gpsimd", oT_out[4 * g:4 * g + 4, :, c0 + i * 128:c0 + (i + 1) * 128].rearrange("h d t -> d h t"), oA[u][:], [f"oA{u}"], [nm], f"oA{u}o")
                    outs.append(nm)

                unit("A", g, c, first, score_fn, pv_fn, out_fn)
        for p, d in enumerate((1, 4, 16)):
            halo = 128 * d
            NIq = CH // (128 * d)
            P.dma("sync", qt[:], qkT[15 + 4 * p:19 + 4 * p, :, c0:c0 + CH].rearrange("h d t -> d h t"), [], ["qt"], "qt")
            lo = 0 if c == 0 else halo
            P.dma("sync", kt[:, :, CH - lo:2 * CH], qkT[27 + 4 * p:31 + 4 * p, :, c0 - lo:c0 + CH].rearrange("h d t -> d h t"), [], ["kt"], "kt")
            vt4 = vt[:, 0:(NIq + 1) * d, :].rearrange("j (I r) f -> j I r f", r=d)
            fcol = 192 + p * 256
            for I in range(0 if c > 0 else 1, NIq + 1):
                tok0 = c0 - halo + I * 128 * d
                for r0 in range(0, d, 8):
                    r1 = min(d, r0 + 8)
                    P.dma("sync", vt4[:, I, r0:r1, :], v[tok0:tok0 + 128 * d, fcol:fcol + 256].rearrange("(j r) f -> j r f", r=d)[:, r0:r1, :], [], [f"vt"], f"vt{(I + r0 // 8) % 8}")
            qv = [qt[:, h, :].rearrange("p (I j r) -> p I j r", j=128, r=d) for h in range(4)]
            kv = [kt[:, h, CH - halo:2 * CH].rearrange("p (I j r) -> p I j r", j=128, r=d) for h in range(4)]
            accn4 = [accn[:, h, :].rearrange("p (I j r) -> p I j r", j=128, r=d) for h in range(4)]
            accd4 = [accd[:, h, :].rearrange("p (I j r) -> p I j r", j=128, r=d) for h in range(4)]
            for Iq in range(NIq):
                for r in range(d):
                    first = (c == 0 and Iq == 0)

                    def score_fn(kb, pt, pn, Iq=Iq, r=r, kv=kv, qv=qv):
                        for h in range(4):
                            P.op("tensor", lambda e, h=h: e.matmul(out=pt[:, h, :], lhsT=kv[h][:, Iq + 1 - kb, :, r], rhs=qv[h][:, Iq, :, r], start=True, stop=True),
                                 ["qt", "kt"], [pn])

                    def pv_fn(kbs, pts, ptn, pn_, pnn, Iq=Iq, r=r, vt4=vt4):
                        for h in range(4):
                            for ii, kb in enumerate(kbs):
                                P.op("tensor", lambda e, h=h, ii=ii, kb=kb: e.matmul(out=pn_[:, h, :], lhsT=vt4[:, Iq + 1 - kb, r, h * 64:(h + 1) * 64], rhs=pts[ii][:, h, :],
                                                                                     start=(ii == 0), stop=(ii == len(kbs) - 1)),
                                     [ptn[ii], "vt"], [pnn])

                    def out_fn(u, Iq=Iq, r=r, p=p, accn4=accn4, accd4=accd4):
                        for h in range(4):
                            if p == 0:
                                P.op("scalar", lambda e, h=h: e.copy(out=accn4[h][:, Iq, :, r], in_=pnum[u][:, h, :]), [f"pnum{u}"], ["accn"])
                                P.op("vector", lambda e, h=h: e.tensor_copy(out=accd4[h][:, Iq, :, r], in_=pden[u][:, h, :]), [f"pden{u}"], ["accd"])
                            else:
                                P.op("vector", lambda e, h=h: e.tensor_tensor(out=accn4[h][:, Iq, :, r], in0=pnum[u][:, h, :], in1=accn4[h][:, Iq, :, r], op=ALU.add), [f"pnum{u}", "accn"], ["accn"])
                                P.op("vector", lambda e, h=h: e.tensor_tensor(out=accd4[h][:, Iq, :, r], in0=pden[u][:, h, :], in1=accd4[h][:, Iq, :, r], op=ALU.add), [f"pden{u}", "accd"], ["accd"])

                    unit("B", p, c, first, score_fn, pv_fn, out_fn)
        for q in range(CH // 512):
            u = q % 2
            P.op("vector", lambda e, q=q: e.reciprocal(out=rdB[:], in_=accd[:, :, q * 512:(q + 1) * 512]), ["accd"], ["rdB"])
            P.op("vector", lambda e, q=q, u=u: e.tensor_tensor(out=oB[u][:], in0=accn[:, :, q * 512:(q + 1) * 512], in1=rdB[:], op=ALU.mult), ["accn", "rdB"], [f"oB{u}"])
            nm = f"oob_{c0}_{q}"
            P.dma("gpsimd", oT_out[12:16, :, c0 + q * 512:c0 + (q + 1) * 512].rearrange("h d t -> d h t"), oB[u][:], [f"oB{u}"], [nm], f"oB{u}o")
            outs.append(nm)
    P.op("sync", lambda e: None, outs, [])
    P.op("gpsimd", lambda e: None, outs, [])


def emit_head_outproj(P, nc, st, x_in, oT, w_out, x_out, T, nheads=16):
    sb, ps = _alloc(nc, st)
    wob = sb("wob", [64, nheads, D], BF16)
    stg = [sb(f"stg{i}", [64, D], F32) for i in range(2)]
    w3 = w_out.rearrange("(h d) n -> d h n", d=64)
    for h in range(nheads):
        s, sn = stg[h % 2], f"stg{h % 2}"
        P.dma("sync", s[:], w3[:, h, :], [], [sn], sn)
        P.op("vector" if h % 2 == 0 else "gpsimd", lambda e, s=s, h=h: e.tensor_copy(out=wob[:, h, :], in_=s[:]), [sn], [f"wob{h}"])
    TG = 512
    oTt = [sb(f"oTt{i}", [64, nheads, TG], BF16) for i in range(2)]
    xt = [sb(f"xt{i}", [128, D], F32) for i in range(4)]
    po = [ps(f"po{i}", [128, 512], F32) for i in range(4)]
    outs = []
    cnt = 0
    ocnt = 0
    for g in range(T // TG):
        ob = g % 2
        for hh in range(0, nheads, 4):
            P.dma("sync", oTt[ob][:, hh:hh + 4, :], oT[hh:hh + 4, :, g * TG:(g + 1) * TG].rearrange("h d t -> d h t"), [], [f"oTt{ob}_{hh}"], f"oTt{ob}_{hh}")
        ords = [f"oTt{ob}_{hh}" for hh in range(0, nheads, 4)]
        for j in range(TG // 128):
            xi = cnt % 4
            cnt += 1
            x, xn = xt[xi], f"xt{xi}"
            t0 = g * TG + j * 128
            P.dma("sync", x[:], x_in[t0:t0 + 128, :], [], [xn], xn)
            for n in range(2):
                b = ocnt % 4
                ocnt += 1
                for h in range(nheads):
                    P.op("tensor", lambda e, b=b, h=h, j=j, n=n, ob=ob: e.matmul(out=po[b][:], lhsT=oTt[ob][:, h, j * 128:(j + 1) * 128], rhs=wob[:, h, n * 512:(n + 1) * 512],
                                                                              start=(h == 0), stop=(h == nheads - 1)), ords + [f"wob{h}"], [f"po{b}"])
                P.op("vector", lambda e, b=b, x=x, n=n: e.tensor_tensor(out=x[:, n * 512:(n + 1) * 512], in0=po[b][:], in1=x[:, n * 512:(n + 1) * 512], op=ALU.add), [f"po{b}", xn], [xn])
            P.dma("gpsimd", x_out[t0:t0 + 128, :], x[:], [xn], [f"out_{t0}"], xn + "o")
            outs.append(f"out_{t0}")
    P.op("sync", lambda e: None, outs, [])
    P.op("gpsimd", lambda e: None, outs, [])


NCORES = 8
_PERM = np.concatenate([np.arange(0, 768), np.arange(768, 960), np.arange(1152, 1920), np.arange(1920, 2688),
                        np.arange(960, 1152), np.arange(2688, 3456)])


def _run(nc, in_maps):
    res = run_bass_kernel_spmd(nc, in_maps, core_ids=list(range(NCORES)))
    return res.results


def _dram(nc, name, shape, dt, kind):
    return nc.dram_tensor(name, list(shape), dt, kind=kind).ap()


def launch_attn_proj(x, pos, gvec, w_in, b_in):
    N = x.shape[0]
    T = N // NCORES
    nc = bass.Bass("TRN2", target_bir_lowering=False)
    xa = _dram(nc, "x", [T, D], F32, "ExternalInput")
    pa = _dram(nc, "pos", [T], I32, "ExternalInput")
    ga = _dram(nc, "g", [D], F32, "ExternalInput")
    wa = _dram(nc, "w_in", [D, INW], F32, "ExternalInput")
    ba = _dram(nc, "b_in", [INW], F32, "ExternalInput")
    qo = _dram(nc, "qkT", [NH_ROT, 64, T], BF16, "ExternalOutput")
    vo = _dram(nc, "v", [T, NV * 64], BF16, "ExternalOutput")
    P = Prog(nc)
    with ExitStack() as st:
        emit_attn_proj(P, nc, st, xa, pa, ga, wa, ba, qo, vo, T)
        P.emit()
    wp = np.ascontiguousarray(w_in[:, _PERM])
    bp = np.ascontiguousarray(b_in[_PERM])
    maps = [{"x": x[i * T:(i + 1) * T], "pos": pos[i * T:(i + 1) * T], "g": gvec, "w_in": wp, "b_in": bp} for i in range(NCORES)]
    r = _run(nc, maps)
    return [ri["qkT"] for ri in r], [ri["v"] for ri in r]


def launch_attn_core(qkT_seq, v_seq, sinks, S):
    nc = bass.Bass("TRN2", target_bir_lowering=False)
    qa = _dram(nc, "qkT", [NH_ROT, 64, S], BF16, "ExternalInput")
    va = _dram(nc, "v", [S, NV * 64], BF16, "ExternalInput")
    sa = _dram(nc, "sinks", [12], F32, "ExternalInput")
    oo = _dram(nc, "oT", [16, 64, S], BF16, "ExternalOutput")
    P = Prog(nc)
    with ExitStack() as st:
        emit_attn_core(P, nc, st, qa, va, sa, oo, S)
        P.emit()
    nb = len(qkT_seq)
    maps = [{"qkT": qkT_seq[i % nb], "v": v_seq[i % nb], "sinks": sinks} for i in range(NCORES)]
    r = _run(nc, maps)
    return [r[i]["oT"] for i in range(nb)]


def launch_mix_ffn(x, oT_cores, w_out, gffn, wg, wu, wd, final_g=None):
    N = x.shape[0]
    T = N // NCORES
    nc = bass.Bass("TRN2", target_bir_lowering=False)
    xa = _dram(nc, "x", [T, D], F32, "ExternalInput")
    oa = _dram(nc, "oT", [16, 64, T], BF16, "ExternalInput")
    woa = _dram(nc, "w_out", [D, D], F32, "ExternalInput")
    ga = _dram(nc, "g", [D], F32, "ExternalInput")
    wga = _dram(nc, "wg", [D, DFF], F32, "ExternalInput")
    wua = _dram(nc, "wu", [D, DFF], F32, "ExternalInput")
    wda = _dram(nc, "wd", [DFF, D], F32, "ExternalInput")
    ya = _dram(nc, "y", [T, D], F32, "ExternalOutput")
    fa = _dram(nc, "gfin", [D], F32, "ExternalInput") if final_g is not None else None
    P1 = Prog(nc)
    with ExitStack() as st:
        emit_head_outproj(P1, nc, st, xa, oa, woa, ya, T)
        P1.emit()
    P2 = Prog(nc)
    with ExitStack() as st:
        emit_ffn(P2, nc, st, ya, ga, wga, wua, wda, ya, T, final_g=fa)
        P2.emit()
    maps = []
    for i in range(NCORES):
        m = {"x": x[i * T:(i + 1) * T], "oT": oT_cores[i], "w_out": w_out, "g": gffn, "wg": wg, "wu": wu, "wd": wd}
        if final_g is not None:
            m["gfin"] = final_g
        maps.append(m)
    r = _run(nc, maps)
    return np.concatenate([ri["y"] for ri in r], axis=0)


def attn_layer(x, pos, B, S, g_mix, w_in, b_in, sinks, w_out, g_ffn, wg, wu, wd, final_g=None):
    N = B * S
    T = N // NCORES
    qkT_c, v_c = launch_attn_proj(x, pos, g_mix, w_in, b_in)
    per = S // T
    qkT_seq = [np.concatenate(qkT_c[b * per:(b + 1) * per], axis=2) for b in range(B)]
    v_seq = [np.concatenate(v_c[b * per:(b + 1) * per], axis=0) for b in range(B)]
    oT_seq = launch_attn_core(qkT_seq, v_seq, sinks, S)
    oT_cores = [np.ascontiguousarray(oT_seq[i // per][:, :, (i % per) * T:((i % per) + 1) * T]) for i in range(NCORES)]
    return launch_mix_ffn(x, oT_cores, w_out, g_ffn, wg, wu, wd, final_g=final_g)


WSC = float(np.exp(-0.5))


def emit_rwkv_proj(P, nc, st, x_in, xh_in, gvec, pr, QG_out, PH_out, vb_out, gate_out, bonus_out, vfirst_io, T, has_vlora):
    sb, ps = _alloc(nc, st)
    c = emit_consts(P, nc, sb)
    ident, identf = c["ident"], c["identf"]
    NT = T // 128
    Et = sb("Et", [128, D], F32)
    t1 = sb("t1", [128, D], F32)
    stg = [Et, t1]
    gcol = sb("gcol", [128, 8], F32)
    P.dma("sync", gcol[:], gvec.rearrange("(c p) -> p c", p=128), [], ["gcol"], "gcol", allow_slow_non_contiguous=True)
    wrkv = [sb(f"w{nm}", [128, 8, D], BF16) for nm in "rkv"]
    k = 0
    for i in range(3):
        k = load_weight_bf16(P, nc, [pr["w_rkv"][i, cc * 128:(cc + 1) * 128, :] for cc in range(8)], lambda cc, i=i: wrkv[i][:, cc, :], D, stg,
                             lambda cc: gcol[:, cc:cc + 1], lambda cc, i=i: f"wrkv{i}_{cc}", k0=k, piece=1024)
    LW = {"w1": 64, "a1": 64, "g1": 160}
    if has_vlora:
        LW["v1"] = 32
    l1 = {}
    for nm, wd_ in LW.items():
        l1[nm] = sb("l1" + nm, [128, 8, wd_], BF16)
        k = load_weight_bf16(P, nc, [pr[nm][cc * 128:(cc + 1) * 128, :] for cc in range(8)], lambda cc, nm=nm: l1[nm][:, cc, :], wd_, stg,
                             lambda cc: gcol[:, cc:cc + 1], lambda cc, nm=nm: f"l1{nm}_{cc}", k0=k, piece=1024)
    l2 = {}
    for nm, rows in (("w2", 64), ("a2", 64), ("g2a", 128), ("g2b", 32)) + ((("v2", 32),) if has_vlora else ()):
        l2[nm] = sb("l2" + nm, [rows, D], BF16)
        src = pr["g2"][0:128, :] if nm == "g2a" else (pr["g2"][128:160, :] if nm == "g2b" else pr[nm])
        s, sn = stg[k % 2], f"stg{k % 2}"
        P.dma("sync", s[0:rows, :], src, [], [sn], sn)
        P.op("vector", lambda e, s=s, nm=nm, rows=rows: e.tensor_copy(out=l2[nm][:], in_=s[0:rows, :]), [sn], ["l2" + nm])
        k += 1
    bc = {}
    for nm in ("w0", "a0", "k_k", "k_a", "r_k") + (("v0",) if has_vlora else ()):
        bc[nm] = sb("bc" + nm, [128, D], F32)
        P.dma("sync", bc[nm][:], pr[nm].partition_broadcast(128), [], ["bc" + nm], "bc" + nm)
    mucol = sb("mucol", [128, 6, 8], F32)
    for i in range(6):
        P.dma("sync", mucol[:, i, :], pr["mu"][i].rearrange("(c p) -> p c", p=128), [], ["mucol"], f"mucol{i}", allow_slow_non_contiguous=True)
    ltri = sb("ltri", [128, 128], F32)
    sel0 = sb("sel0", [128, 128], F32)
    sel1 = sb("sel1", [128, 128], F32)
    mlo = sb("mlo", [128, 128], F32)
    mup = sb("mup", [128, 128], F32)
    mle = sb("mle", [128, 64], F32)
    addI = sb("addI", [128, 128], BF16)
    addI2 = sb("addI2", [128, 128], BF16)

    def gp(fn, r, w):
        P.op("gpsimd", fn, r, w)
    for tname, tl in (("ltri", ltri), ("mlo", mlo), ("mup", mup), ("sel0", sel0), ("sel1", sel1)):
        gp(lambda e, tl=tl: e.memset(tl[:], 1.0), [], [tname])
    for tname, tl in (("ltri", ltri), ("mlo", mlo), ("mup", mup)):
        gp(lambda e, tl=tl: e.memset(tl[0:64, 64:128], 0.0), [tname], [tname])
        gp(lambda e, tl=tl: e.memset(tl[64:128, 0:64], 0.0), [tname], [tname])
    gp(lambda e: e.affine_select(out=ltri[:], in_=ltri[:], pattern=[[1, 128]], compare_op=ALU.is_ge, fill=0.0, base=0, channel_multiplier=-1), ["ltri"], ["ltri"])
    gp(lambda e: e.affine_select(out=mup[:], in_=mup[:], pattern=[[1, 128]], compare_op=ALU.is_gt, fill=0.0, base=0, channel_multiplier=-1), ["mup"], ["mup"])
    gp(lambda e: e.affine_select(out=mlo[:], in_=mlo[:], pattern=[[-1, 128]], compare_op=ALU.is_gt, fill=0.0, base=0, channel_multiplier=1), ["mlo"], ["mlo"])
    gp(lambda e: e.affine_select(out=sel0[:], in_=sel0[:], pattern=[[0, 128]], compare_op=ALU.is_equal, fill=0.0, base=-63, channel_multiplier=1), ["sel0"], ["sel0"])
    gp(lambda e: e.affine_select(out=sel1[:], in_=sel1[:], pattern=[[0, 128]], compare_op=ALU.is_equal, fill=0.0, base=-127, channel_multiplier=1), ["sel1"], ["sel1"])
    gp(lambda e: e.tensor_copy(out=mle[0:64, :], in_=ltri[0:64, 0:64]), ["ltri"], ["mle"])
    gp(lambda e: e.tensor_copy(out=mle[64:128, :], in_=ltri[64:128, 64:128]), ["ltri"], ["mle"])
    P.op("vector", lambda e: e.memset(addI[:], 0.0), [], ["addI"])
    P.op("vector", lambda e: e.memset(addI2[:], 0.0), [], ["addI2"])
    for tcs in (slice(0, 64), slice(64, 128)):
        P.op("vector", lambda e, tcs=tcs: e.tensor_copy(out=addI[tcs, 0:64], in_=identf[tcs, tcs]), ["addI", "identf"], ["addI"])
        P.op("vector", lambda e, tcs=tcs: e.tensor_copy(out=addI2[tcs, 64:128], in_=identf[tcs, tcs]), ["addI2", "identf"], ["addI2"])

    xt = [sb("xt0", [128, D], F32)] * 2
    hb = [sb("hb0", [128, D], BF16)] * 2
    ss = [sb(f"ss{i}", [128, 1], F32) for i in range(2)]
    rstd = [sb(f"rstd{i}", [128, 1], F32) for i in range(2)]
    hT = [sb(f"hT{i}", [128, 8, 129], BF16) for i in range(2)]
    xxT = sb("xxT", [128, 8, 128], BF16)
    mixT = [sb(f"mixT{i}", [128, 8, 128], BF16) for i in range(6)]
    rt = sb("rt", [128, D], F32)
    kt_ = sb("kt_", [128, D], F32)
    vt_ = sb("vt_", [128, D], F32)
    sgm = sb("sgm", [128, D], F32)
    asg = sb("asg", [128, D], F32)
    kkn = sb("kkn", [128, D], F32)
    t2 = sb("t2", [128, D], F32)
    ge = [sb(f"ge{i}", [128, D], F32) for i in range(2)]
    nrm = sb("nrm", [128, 16], F32)
    bsc = sb("bsc", [128, 16], F32)
    tmA = sb("tmA", [128, D], BF16)
    tmB = sb("tmB", [128, D], BF16)
    tmK = sb("tmK", [128, D], BF16)
    tmR = sb("tmR", [128, D], BF16)
    vbf = hb[0]
    fm = {nm: sb("fm" + nm, [64, 16, 128], BF16) for nm in "ABKR"}
    lo1 = {nm: sb("lo1" + nm, [wd_ if wd_ <= 128 else 128, 128], BF16) for nm, wd_ in LW.items()}
    lo1g2 = sb("lo1g2", [32, 128], BF16)
    NCH = 4
    bdNT_c = [[sb(f"bdNT{ch}j{j}", [128, 128], BF16) for j in range(6)] for ch in range(NCH)]
    bdN_c = [[sb(f"bdN{ch}j{j}", [128, 128], BF16) for j in range(5)] for ch in range(NCH)]
    bdAk_c = [sb(f"bdAk{ch}", [128, 128], BF16) for ch in range(NCH)]
    Z_c = [[sb(f"Z{ch}_{i}", [128, 128], BF16) for i in range(2)] for ch in range(NCH)]
    ADD_c = [sb(f"ADD{ch}", [128, 128], BF16) for ch in range(NCH)]
    QGs = [sb(f"QGs{i}", [128, 16, 128], BF16) for i in range(1)] * 2
    PHs = [sb(f"PHs{i}", [128, 16, 128], BF16) for i in range(1)] * 2
    ptr = ps("ptr", [128, 8, 128], BF16)
    pA = ps("pA", [128, 512], F32)
    pB = ps("pB", [128, 512], F32)
    pc = [pA, pB]
    pool = [ps(f"pk{i}", [128, 512], F32) for i in range(5)]
    for i_ in range(5):
        P.op("vector", lambda e, i_=i_: e.memset(pool[i_][:], 0.0), [], [f"pk{i_}"])
    pl = pool[0][:].rearrange("p (s c) -> p s c", c=128)
    P.barrier()
    P.dma("sync", xt[1][:], xh_in, [], ["xt0"], "xt0")
    emit_norm_T(P, c, xt[1], "xt0", ss[0], rstd[0], hb[0], ptr, hT[0][:, :, 1:129], "hT0", 0)
    P.op("gpsimd", lambda e: e.tensor_copy(out=hT[1][:, :, 128:129], in_=hT[0][:, :, 1:2]), ["hT0"], ["hT1"])
    import os
    STOP = int(os.environ.get('R1_STOP', '9'))
    outs = []
    pwc = 0
    for n in range(NT if STOP > 1 else 0):
        b = n % 2
        x, xn = xt[b], "xt0"
        t0 = n * 128
        P.dma("sync", x[:], x_in[t0:t0 + 128, :], [], [xn], xn)
        emit_norm_T(P, c, x, xn, ss[0], rstd[0], hb[0], ptr, hT[b][:, :, 1:129], f"hT{b}", 0)
        P.op("gpsimd", lambda e, b=b: e.tensor_copy(out=hT[b][:, :, 0:1], in_=hT[1 - b][:, :, 128:129]), [f"hT{1 - b}"], [f"hT{b}h"])
        hr = [f"hT{b}", f"hT{b}h"]
        P.op("vector", lambda e, b=b: e.tensor_tensor(out=xxT[:], in0=hT[b][:, :, 0:128], in1=hT[b][:, :, 1:129], op=ALU.subtract), hr, ["xxT"])
        for i in range(6):
            for cc in range(8):
                P.op("vector", lambda e, i=i, cc=cc, b=b: e.scalar_tensor_tensor(out=mixT[i][:, cc, :], in0=xxT[:, cc, :], scalar=mucol[:, i, cc:cc + 1], in1=hT[b][:, cc, 1:129],
                                                                               op0=ALU.mult, op1=ALU.add), ["xxT", "mucol"] + hr, [f"mixT{i}"])
        slot = {"w1": 0, "a1": 1, "g1": 2, "v1": 3}
        msrc = {"w1": 1, "a1": 4, "g1": 5, "v1": 3}
        for nm, wd_ in LW.items():
            s_ = slot[nm]
            w0_ = min(wd_, 128)
            for cc in range(8):
                P.op("tensor", lambda e, nm=nm, cc=cc, s_=s_, w0_=w0_: e.matmul(out=pl[0:w0_, s_, :], lhsT=l1[nm][:, cc, 0:w0_], rhs=mixT[msrc[nm]][:, cc, :], start=(cc == 0), stop=(cc == 7)),
                     [f"mixT{msrc[nm]}", f"l1{nm}_{cc}"], ["pk0"])
        for cc in range(8):
            P.op("tensor", lambda e, cc=cc: e.matmul(out=pB[0:32, 0:128], lhsT=l1["g1"][:, cc, 128:160], rhs=mixT[5][:, cc, :], start=(cc == 0), stop=(cc == 7)),
                 ["mixT5", f"l1g1_{cc}"], ["pB"])
        P.op("scalar", lambda e: e.activation(out=lo1["w1"][:], in_=pl[0:64, 0, :], func=AF.Tanh), ["pk0"], ["lo1w1"])
        P.op("scalar", lambda e: e.copy(out=lo1["a1"][:], in_=pl[0:64, 1, :]), ["pk0"], ["lo1a1"])
        P.op("scalar", lambda e: e.activation(out=lo1["g1"][:], in_=pl[:, 2, :], func=AF.Sigmoid), ["pk0"], ["lo1g1"])
        P.op("scalar", lambda e: e.activation(out=lo1g2[:], in_=pB[0:32, 0:128], func=AF.Sigmoid), ["pB"], ["lo1g2"])
        if has_vlora:
            P.op("scalar", lambda e: e.copy(out=lo1["v1"][:], in_=pl[0:32, 3, :]), ["pk0"], ["lo1v1"])
        if STOP <= 2:
            continue

        def proj(dst_fn, lhs_list, reads):
            for half, pp_, pn in ((0, pA, "pA"), (1, pB, "pB")):
                nk = len(lhs_list)
                for ki, (lh, rh, rn) in enumerate(lhs_list):
                    P.op("tensor", lambda e, lh=lh, rh=rh, half=half, pp_=pp_, ki=ki, nk=nk: e.matmul(out=pp_[:], lhsT=lh, rhs=rh[:, half * 512:(half + 1) * 512], start=(ki == 0), stop=(ki == nk - 1)),
                         reads + [rn], [pn])
                dst_fn(half, pp_, pn)

        H = lambda half: slice(half * 512, (half + 1) * 512)
        for (mi, wi, dst, dn) in ((0, 0, rt, "rt"), (2, 1, kt_, "kt_"), (3, 2, vt_, "vt_")):
            proj(lambda half, pp_, pn, dst=dst, dn=dn: P.op("scalar", lambda e: e.copy(out=dst[:, H(half)], in_=pp_[:]), [pn], [dn + str(half)]),
                 [(mixT[mi][:, cc, :], wrkv[wi][:, cc, :], f"wrkv{wi}_{cc}") for cc in range(8)], [f"mixT{mi}"])
        RT, KT, VT = ["rt0", "rt1"], ["kt_0", "kt_1"], ["vt_0", "vt_1"]
        def ev_sig(dst, dn, bcn):
            def f(half, pp_, pn):
                P.op("vector", lambda e: e.tensor_tensor(out=dst[:, H(half)], in0=pp_[:], in1=bc[bcn][:, H(half)], op=ALU.add), [pn, "bc" + bcn], [dn + str(half)])
                P.op("scalar", lambda e: e.activation(out=dst[:, H(half)], in_=dst[:, H(half)], func=AF.Sigmoid), [dn + str(half)], [dn + str(half)])
            return f
        proj(ev_sig(sgm, "sgm", "w0"), [(lo1["w1"][:], l2["w2"], "l2w2")], ["lo1w1"])
        proj(ev_sig(asg, "asg", "a0"), [(lo1["a1"][:], l2["a2"], "l2a2")], ["lo1a1"])
        SG, AS = ["sgm0", "sgm1"], ["asg0", "asg1"]
        proj(lambda half, pp_, pn: P.op("scalar", lambda e: e.copy(out=t1[:, H(half)], in_=pp_[:]), [pn], ["t1"]),
             [(lo1["g1"][:], l2["g2a"], "l2g2a"), (lo1g2[:], l2["g2b"], "l2g2b")], ["lo1g1", "lo1g2"])
        P.dma("gpsimd", gate_out[t0:t0 + 128, :], t1[:], ["t1"], [f"go{n}"], "t1o")
        outs.append(f"go{n}")
        if has_vlora:
            proj(ev_sig(Et, "Et", "v0"), [(lo1["v1"][:], l2["v2"], "l2v2")], ["lo1v1"])
            P.dma("sync", t2[:], vfirst_io[t0:t0 + 128, :], [], ["t2"], "t2i")
            P.op("gpsimd", lambda e: e.tensor_tensor(out=t2[:], in0=t2[:], in1=vt_[:], op=ALU.subtract), ["t2"] + VT, ["t2"])
            P.op("gpsimd", lambda e: e.tensor_tensor(out=t2[:], in0=t2[:], in1=Et[:], op=ALU.mult), ["t2", "Et0", "Et1"], ["t2"])
            P.op("gpsimd", lambda e: e.tensor_tensor(out=vt_[:], in0=vt_[:], in1=t2[:], op=ALU.add), ["t2"] + VT, VT)
        else:
            P.dma("gpsimd", vfirst_io[t0:t0 + 128, :], vt_[:], VT, [f"vfo{n}"], "vfo")
            outs.append(f"vfo{n}")
        P.op("gpsimd", lambda e: e.tensor_copy(out=vbf[:], in_=vt_[:]), VT, ["hb0"])
        P.dma("gpsimd", vb_out[t0:t0 + 128, :], vbf[:], ["hb0"], [f"vbo{n}"], "vbfo")
        outs.append(f"vbo{n}")
        kkn3 = kkn[:].rearrange("p (h d) -> p h d", d=64)
        t13 = t1[:].rearrange("p (h d) -> p h d", d=64)
        t23 = t2[:].rearrange("p (h d) -> p h d", d=64)
        P.op("vector", lambda e: e.tensor_tensor(out=kkn[:], in0=kt_[:], in1=bc["k_k"][:], op=ALU.mult), KT + ["bck_k"], ["kkn"])
        P.op("gpsimd", lambda e: e.tensor_tensor(out=t2[:], in0=kkn[:], in1=kkn[:], op=ALU.mult), ["kkn"], ["t2"])
        P.op("vector", lambda e: e.tensor_reduce(out=nrm[:], in_=t23, axis=AX.X, op=ALU.add), ["t2"], ["nrm"])
        P.op("scalar", lambda e: e.activation(out=nrm[:], in_=nrm[:], func=AF.Sqrt), ["nrm"], ["nrm"])
        P.op("vector", lambda e: e.tensor_scalar(out=nrm[:], in0=nrm[:], scalar1=1e-12, scalar2=None, op0=ALU.max), ["nrm"], ["nrm"])
        P.op("vector", lambda e: e.reciprocal(out=nrm[:], in_=nrm[:]), ["nrm"], ["nrm"])
        P.op("vector", lambda e: e.tensor_tensor(out=kkn3, in0=kkn3, in1=nrm[:].unsqueeze(2).broadcast_to([128, 16, 64]), op=ALU.mult), ["kkn", "nrm"], ["kkn"])
        P.op("vector", lambda e: e.scalar_tensor_tensor(out=t2[:], in0=asg[:], scalar=-1.0, in1=bc["k_a"][:], op0=ALU.add, op1=ALU.mult), AS + ["bck_a", "t2"], ["t2"])
        P.op("vector", lambda e: e.scalar_tensor_tensor(out=kt_[:], in0=t2[:], scalar=1.0, in1=kt_[:], op0=ALU.add, op1=ALU.mult), ["t2"] + KT, KT)
        P.op("gpsimd", lambda e: e.tensor_tensor(out=t2[:], in0=rt[:], in1=kt_[:], op=ALU.mult), RT + KT + ["t2"], ["t2"])
        P.op("gpsimd", lambda e: e.tensor_tensor(out=t2[:], in0=t2[:], in1=bc["r_k"][:], op=ALU.mult), ["t2", "bcr_k"], ["t2"])
        P.op("vector", lambda e: e.tensor_reduce(out=bsc[:], in_=t23, axis=AX.X, op=ALU.add), ["t2"], ["bsc"])
        P.op("vector", lambda e: e.tensor_tensor(out=t23, in0=vt_[:].rearrange("p (h d) -> p h d", d=64), in1=bsc[:].unsqueeze(2).broadcast_to([128, 16, 64]), op=ALU.mult), VT + ["bsc", "t2"], ["t2"])
        P.dma("gpsimd", bonus_out[t0:t0 + 128, :], t2[:], ["t2"], [f"bo{n}"], "t2o")
        outs.append(f"bo{n}")
        for half in range(2):
            P.op("tensor", lambda e, half=half: e.matmul(out=pc[half][:], lhsT=ltri[:], rhs=sgm[:, H(half)], start=True, stop=True), [SG[half], "ltri"], [("pA", "pB")[half]])
        for half in range(2):
            P.op("scalar", lambda e, half=half: e.activation(out=Et[:, H(half)], in_=pc[half][:], func=AF.Exp, scale=-WSC), [("pA", "pB")[half]], [f"Et{half}"])
        P.op("vector", lambda e: e.tensor_tensor(out=tmR[:], in0=rt[:], in1=Et[:], op=ALU.mult), RT + ["Et0", "Et1"], ["tmR"])
        for half in range(2):
            P.op("scalar", lambda e, half=half: e.activation(out=Et[:, H(half)], in_=pc[half][:], func=AF.Exp, scale=WSC), [("pA", "pB")[half], "tmR"], [f"Et{half}"])
        P.op("vector", lambda e: e.tensor_tensor(out=tmK[:], in0=kt_[:], in1=Et[:], op=ALU.mult), KT + ["Et0", "Et1"], ["tmK"])
        P.op("gpsimd", lambda e: e.tensor_tensor(out=t1[:], in0=kkn[:], in1=asg[:], op=ALU.mult), ["kkn", "t1"] + AS, ["t1"])
        P.op("vector", lambda e: e.tensor_tensor(out=tmB[:], in0=t1[:], in1=Et[:], op=ALU.mult), ["t1", "Et0", "Et1"], ["tmB"])
        for half in range(2):
            P.op("vector", lambda e, half=half: e.tensor_tensor(out=t1[:, H(half)], in0=pc[half][:], in1=sgm[:, H(half)], op=ALU.subtract), [("pA", "pB")[half], SG[half], "t1", "tmB"], ["t1"])
        P.op("scalar", lambda e: e.activation(out=Et[:], in_=t1[:], func=AF.Exp, scale=-WSC), ["t1", "tmK", "tmB", "Et0", "Et1"], ["Et0", "Et1"])
        P.op("vector", lambda e: e.scalar_tensor_tensor(out=tmA[:], in0=kkn[:], scalar=-1.0, in1=Et[:], op0=ALU.mult, op1=ALU.mult), ["kkn", "Et0", "Et1"], ["tmA"])
        for half in range(2):
            P.op("scalar", lambda e, half=half: e.copy(out=t1[:, H(half)], in_=pc[half][:]), [("pA", "pB")[half], "t1"], ["t1"])
        for s_, sel in ((0, sel0), (1, sel1)):
            for half in range(2):
                P.op("tensor", lambda e, half=half, sel=sel: e.matmul(out=pc[half][:], lhsT=sel[:], rhs=t1[:, H(half)], start=True, stop=True), ["t1", "sel0", "sel1"], [("pA", "pB")[half]])
                P.op("scalar", lambda e, half=half, s_=s_: e.activation(out=ge[s_][:, H(half)], in_=pc[half][:], func=AF.Exp, scale=-WSC), [("pA", "pB")[half]], [f"ge{s_}"])
        if STOP <= 3:
            continue
        for nm, src, sn in (("A", tmA, "tmA"), ("B", tmB, "tmB"), ("K", tmK, "tmK"), ("R", tmR, "tmR")):
            for h0 in (0, 8):
                for hh in range(8):
                    P.op("tensor", lambda e, hh=hh, h0=h0, src=src: e.transpose(out=ptr[0:64, hh, :], in_=src[:, (h0 + hh) * 64:(h0 + hh + 1) * 64], identity=ident[:]), [sn, "ident"], ["ptr"])
                if nm in "AK":
                    P.op("scalar", lambda e, nm=nm, h0=h0: e.copy(out=fm[nm][:, h0:h0 + 8, :], in_=ptr[0:64, :, :]), ["ptr"], ["fm" + nm])
                else:
                    P.op("vector", lambda e, nm=nm, h0=h0: e.tensor_copy(out=fm[nm][:, h0:h0 + 8, :], in_=ptr[0:64, :, :]), ["ptr"], ["fm" + nm])
        ob = 0

        def nxt():
            nonlocal pwc
            k_ = pwc % 5
            pwc += 1
            return pool[k_], f"pk{k_}"

        def head_steps(h, ch):
            bdNT, bdN, bdAk, Z, ADD = bdNT_c[ch], bdN_c[ch], bdAk_c[ch], Z_c[ch], ADD_c[ch]
            R_ = lambda nm: f"{nm}_{ch}"
            hc = slice(h * 64, (h + 1) * 64)
            fA, fB, fK, fR = (fm[q][:, h, :] for q in "ABKR")
            pgb, pgn = nxt()
            pg_ = pgb[:].rearrange("p (s c) -> p s c", c=128)
            for cc in range(2):
                tc = slice(cc * 64, (cc + 1) * 64)
                for (slot_, lh, rh, rn1, rn2) in ((0, fA, fB, "fmA", "fmB"), (1, fA, fK, "fmA", "fmK"), (2, fB, fA, "fmB", "fmA")):
                    P.op("tensor", lambda e, slot_=slot_, lh=lh, rh=rh, tc=tc: e.matmul(out=pg_[tc, slot_, tc], lhsT=lh[:, tc], rhs=rh[:, tc], start=True, stop=True), [rn1, rn2], [pgn])
                P.op("tensor", lambda e, tc=tc: e.matmul(out=pg_[tc, 3, 0:64], lhsT=fB[:, tc], rhs=fR[:, tc], start=True, stop=True), ["fmB", "fmR"], [pgn])
                P.op("tensor", lambda e, tc=tc: e.matmul(out=pg_[tc, 3, 64:128], lhsT=fK[:, tc], rhs=fR[:, tc], start=True, stop=True), ["fmK", "fmR"], [pgn])
            P.op("vector", lambda e: e.tensor_tensor(out=bdNT[0][:], in0=pg_[:, 0, :], in1=mlo[:], op=ALU.mult), [pgn, "mlo"], [R_("bdNT0")])
            P.op("vector", lambda e: e.tensor_tensor(out=bdN[0][:], in0=pg_[:, 2, :], in1=mup[:], op=ALU.mult), [pgn, "mup"], [R_("bdN0")])
            P.op("vector", lambda e: e.tensor_tensor(out=bdAk[:], in0=pg_[:, 1, :], in1=mlo[:], op=ALU.mult), [pgn, "mlo"], [R_("bdAk")])
            P.op("vector", lambda e: e.tensor_tensor(out=Z[0][:, 0:64], in0=pg_[:, 3, 0:64], in1=mle[:], op=ALU.mult), [pgn, "mle"], [R_("Z0a")])
            P.op("vector", lambda e: e.tensor_tensor(out=ADD[:, 0:64], in0=pg_[:, 3, 64:128], in1=mle[:], op=ALU.mult), [pgn, "mle"], [R_("ADDa")])
            P.op("gpsimd", lambda e: e.tensor_copy(out=Z[0][:, 64:128], in_=tmB[:, hc]), ["tmB"], [R_("Z0b")])
            P.op("gpsimd", lambda e: e.tensor_copy(out=ADD[:, 64:128], in_=tmK[:, hc]), ["tmK"], [R_("ADDb")])
            yield
            for j in range(5):
                pa_, pan = nxt()
                P.op("tensor", lambda e, j=j, pa_=pa_: e.matmul(out=pa_[:, 0:128], lhsT=bdN[j][:], rhs=bdNT[j][:], start=True, stop=True), [R_(f"bdN{j}"), R_(f"bdNT{j}")], [pan])
                P.op("scalar", lambda e, j=j, pa_=pa_: e.copy(out=bdNT[j + 1][:], in_=pa_[:, 0:128]), [pan], [R_(f"bdNT{j + 1}")])
                if j < 4:
                    pb2, pbn = nxt()
                    P.op("tensor", lambda e, j=j, pb2=pb2: e.matmul(out=pb2[:, 0:128], lhsT=bdNT[j][:], rhs=bdN[j][:], start=True, stop=True), [R_(f"bdN{j}"), R_(f"bdNT{j}")], [pbn])
                    P.op("vector", lambda e, j=j, pb2=pb2: e.tensor_copy(out=bdN[j + 1][:], in_=pb2[:, 0:128]), [pbn], [R_(f"bdN{j + 1}")])
                yield
            zi = 0
            zr = [R_("Z0a"), R_("Z0b")]
            for j in range(5, -1, -1):
                pa_, pan = nxt()
                P.op("tensor", lambda e, zi=zi, pa_=pa_: e.matmul(out=pa_[:, 0:128], lhsT=ident[:], rhs=Z[zi][:], start=True, stop=False), zr + ["ident"], [pan])
                P.op("tensor", lambda e, zi=zi, pa_=pa_, j=j: e.matmul(out=pa_[:, 0:128], lhsT=bdNT[j][:], rhs=Z[zi][:], start=False, stop=True), zr + [R_(f"bdNT{j}")], [pan])
                zi = 1 - zi
                znew = [R_(f"Z{zi}a"), R_(f"Z{zi}b")]
                if j % 2:
                    P.op("scalar", lambda e, zi=zi, pa_=pa_: e.copy(out=Z[zi][:], in_=pa_[:, 0:128]), [pan], znew)
                else:
                    P.op("vector", lambda e, zi=zi, pa_=pa_: e.tensor_copy(out=Z[zi][:], in_=pa_[:, 0:128]), [pan], znew)
                zr = znew
                yield
            Zf = Z[zi]
            for cc in range(2):
                tc = slice(cc * 64, (cc + 1) * 64)
                pbk, pnm = nxt()
                P.op("tensor", lambda e, tc=tc, pbk=pbk: e.matmul(out=pbk[tc, 0:128], lhsT=tmA[tc, hc], rhs=Zf[tc, :], start=True, stop=False), zr + ["tmA"], [pnm])
                P.op("tensor", lambda e, tc=tc, pbk=pbk: e.matmul(out=pbk[tc, 0:128], lhsT=tmR[tc, hc], rhs=addI[tc, :], start=False, stop=False), ["tmR", "addI"], [pnm])
                P.op("tensor", lambda e, tc=tc, pbk=pbk: e.matmul(out=pbk[tc, 0:128], lhsT=ident[tc, tc], rhs=addI2[tc, :], start=False, stop=True), ["ident", "addI2"], [pnm])
                P.op("scalar", lambda e, cc=cc, tc=tc, pbk=pbk: e.copy(out=QGs[ob][tc, h, 0:64], in_=pbk[tc, 0:64]), [pnm], [f"QGs{ob}_{h}_{cc}q"])
                P.op("scalar", lambda e, cc=cc, tc=tc, pbk=pbk: e.copy(out=t2[tc, hc], in_=pbk[tc, 64:128]), [pnm], ["t2", f"t2g_{h}_{cc}"])
                P.op("gpsimd", lambda e, cc=cc, tc=tc: e.tensor_tensor(out=QGs[ob][tc, h, 64:128], in0=t2[tc, hc], in1=ge[cc][tc, hc], op=ALU.mult),
                     [f"t2g_{h}_{cc}", "t2", f"ge{cc}"], [f"QGs{ob}_{h}_{cc}g"])
            yield
            pa_, pan = nxt()
            P.op("tensor", lambda e, pa_=pa_: e.matmul(out=pa_[:, 0:128], lhsT=bdAk[:], rhs=Zf[:], start=True, stop=False), zr + [R_("bdAk")], [pan])
            P.op("tensor", lambda e, pa_=pa_: e.matmul(out=pa_[:, 0:128], lhsT=ident[:], rhs=ADD[:], start=False, stop=True), [R_("ADDa"), R_("ADDb"), "ident"], [pan])
            P.op("vector", lambda e, pa_=pa_: e.tensor_copy(out=PHs[ob][:, h, 0:64], in_=pa_[:, 0:64]), [pan], [f"PHs{ob}_{h}p"])
            for cc in range(2):
                tc = slice(cc * 64, (cc + 1) * 64)
                P.op("vector", lambda e, pa_=pa_, tc=tc, cc=cc: e.tensor_tensor(out=PHs[ob][tc, h, 64:128], in0=pa_[tc, 64:128], in1=ge[cc][tc, hc], op=ALU.mult),
                     [pan, f"ge{cc}"], [f"PHs{ob}_{h}h{cc}"])
            yield

        for g0 in range(0, 16, NCH):
            active = [head_steps(g0 + ch, ch) for ch in range(NCH)]
            while active:
                for g_ in list(active):
                    try:
                        next(g_)
                    except StopIteration:
                        active.remove(g_)
        if STOP <= 7:
            continue
        qrd = [f"QGs{ob}_{h}_{cc}{q}" for h in range(16) for cc in range(2) for q in "qg"]
        prd = [f"PHs{ob}_{h}{q}" for h in range(16) for q in ("p", "h0", "h1")]
        for hh in range(0, 16, 4):
            P.dma("sync", QG_out[n, hh:hh + 4].rearrange("h i c -> i h c"), QGs[ob][:, hh:hh + 4, :], qrd, [f"qgo{n}_{hh}"], f"QGs{ob}o{hh}")
            P.dma("sync", PH_out[n, hh:hh + 4].rearrange("h i c -> i h c"), PHs[ob][:, hh:hh + 4, :], prd, [f"pho{n}_{hh}"], f"PHs{ob}o{hh}")
            outs += [f"qgo{n}_{hh}", f"pho{n}_{hh}"]
    P.op("sync", lambda e: None, outs, [])
    P.op("gpsimd", lambda e: None, outs, [])


def emit_rwkv_scan(P, nc, st, QG, PH, V, y_out, NTs):
    sb, ps = _alloc(nc, st)
    NB = 3
    qg = [sb(f"qg{i}", [128, 8, 128], BF16) for i in range(NB)]
    ph = [sb(f"ph{i}", [128, 8, 128], BF16) for i in range(NB)]
    vv = [sb(f"vv{i}", [128, 512], BF16) for i in range(NB)]
    S = sb("S", [128, 8, 64], BF16)
    yt = [sb(f"yt{i}", [128, 512], F32) for i in range(2)]
    pS = [ps(f"pS{i}", [128, 8, 64], F32) for i in range(2)]
    pY = [ps(f"pY{i}", [128, 8, 64], F32) for i in range(4)]
    P.op("vector", lambda e: e.memset(S[:], 0.0), [], ["S0", "S1"])
    outs = []
    for n in range(NTs):
        b = n % NB
        P.dma("sync", qg[b][:], QG[n].rearrange("h i c -> i h c"), [], [f"qg{b}"], f"qg{b}")
        P.dma("sync", ph[b][:], PH[n].rearrange("h i c -> i h c"), [], [f"ph{b}"], f"ph{b}")
        P.dma("sync", vv[b][:], V[n * 128:(n + 1) * 128, :], [], [f"vv{b}"], f"vv{b}")
        yb = n % 2
        for cc in range(2):
            tc = slice(cc * 64, (cc + 1) * 64)
            to = slice((1 - cc) * 64, (2 - cc) * 64)
            pSb = pS[cc]
            rds = [f"qg{b}", f"ph{b}", f"vv{b}", f"S{cc}"]
            for h in range(8):
                P.op("tensor", lambda e, h=h, b=b, tc=tc, yb=yb, cc=cc: e.matmul(out=pY[2 * yb + cc][tc, h, :], lhsT=qg[b][tc, h, 0:64], rhs=S[tc, h, :], start=True, stop=False), rds, [f"pY{2 * yb + cc}"])
                P.op("tensor", lambda e, h=h, b=b, tc=tc, yb=yb, cc=cc: e.matmul(out=pY[2 * yb + cc][tc, h, :], lhsT=ph[b][tc, h, 0:64], rhs=vv[b][tc, h * 64:(h + 1) * 64], start=False, stop=True), rds, [f"pY{2 * yb + cc}"])
                P.op("tensor", lambda e, h=h, b=b, tc=tc, to=to, pSb=pSb: e.matmul(out=pSb[to, h, :], lhsT=qg[b][tc, h, 64:128], rhs=S[tc, h, :], start=True, stop=False), rds, [f"pS{cc}"])
                P.op("tensor", lambda e, h=h, b=b, tc=tc, to=to, pSb=pSb: e.matmul(out=pSb[to, h, :], lhsT=ph[b][tc, h, 64:128], rhs=vv[b][tc, h * 64:(h + 1) * 64], start=False, stop=True), rds, [f"pS{cc}"])
            P.op("scalar", lambda e, to=to, pSb=pSb: e.copy(out=S[to, :, :], in_=pSb[to, :, :]), [f"pS{cc}"], [f"S{1 - cc}"])
        for cc in range(2):
            tc = slice(cc * 64, (cc + 1) * 64)
            P.op("vector", lambda e, yb=yb, cc=cc, tc=tc: e.tensor_copy(out=yt[yb][tc, :], in_=pY[2 * yb + cc][tc, :, :].rearrange("p h d -> p (h d)")), [f"pY{2 * yb + cc}"], [f"yt{yb}"])
        P.dma("gpsimd", y_out[n * 128:(n + 1) * 128, :], yt[yb][:], [f"yt{yb}"], [f"yo{n}"], f"yt{yb}o")
        outs.append(f"yo{n}")
    P.op("sync", lambda e: None, outs, [])
    P.op("gpsimd", lambda e: None, outs, [])


LNX_EPS = 64e-5


def emit_rwkv_out(P, nc, st, x_in, y_in, gate_in, bonus_in, lnw, lnb, w_o, x_out, T):
    sb, ps = _alloc(nc, st)
    c = emit_consts(P, nc, sb)
    ident = c["ident"]
    wob = sb("wob", [128, 8, D], BF16)
    stg = [sb(f"stg{i}", [128, D], F32) for i in range(2)]
    load_weight_bf16(P, nc, [w_o[cc * 128:(cc + 1) * 128, :] for cc in range(8)], lambda cc: wob[:, cc, :], D, stg, None, lambda cc: f"wob{cc}", piece=1024)
    lw = sb("lw", [128, D], F32)
    lb = sb("lb", [128, D], F32)
    P.dma("sync", lw[:], lnw.partition_broadcast(128), [], ["lw"], "lw")
    P.dma("sync", lb[:], lnb.partition_broadcast(128), [], ["lb"], "lb")
    eps2 = sb("eps2", [128, 1], F32)
    P.op("vector", lambda e: e.memset(eps2[:], LNX_EPS), [], ["eps2"])
    yt = [sb(f"yt{i}", [128, D], F32) for i in range(2)]
    gt = [sb(f"gt{i}", [128, D], F32) for i in range(2)]
    bt = [sb(f"bt{i}", [128, D], F32) for i in range(2)]
    xt = [sb(f"xt{i}", [128, D], F32) for i in range(2)]
    sq = sb("sq", [128, D], F32)
    ygb = sb("ygb", [128, D], BF16)
    ygT = [sb(f"ygT{i}", [128, 8, 128], BF16) for i in range(2)]
    s1 = sb("s1", [128, 16], F32)
    s2 = sb("s2", [128, 16], F32)
    msq = sb("msq", [128, 16], F32)
    ptr = ps("ptr", [128, 8, 128], BF16)
    po = [ps(f"po{i}", [128, 512], F32) for i in range(2)]
    outs = []
    ocnt = 0
    for n in range(T // 128):
        b = n % 2
        t0 = n * 128
        y, g, bo, x = yt[b], gt[b], bt[b], xt[b]
        P.dma("sync", y[:], y_in[t0:t0 + 128, :], [], [f"yt{b}"], f"yt{b}")
        P.dma("sync", g[:], gate_in[t0:t0 + 128, :], [], [f"gt{b}"], f"gt{b}")
        P.dma("sync", bo[:], bonus_in[t0:t0 + 128, :], [], [f"bt{b}"], f"bt{b}")
        P.dma("sync", x[:], x_in[t0:t0 + 128, :], [], [f"xt{b}"], f"xt{b}")
        y3 = y[:].rearrange("p (h d) -> p h d", d=64)
        sq3 = sq[:].rearrange("p (h d) -> p h d", d=64)
        bc16 = lambda t: t[:].unsqueeze(2).broadcast_to([128, 16, 64])
        P.op("vector", lambda e, y3=y3: e.tensor_reduce(out=s1[:], in_=y3, axis=AX.X, op=ALU.add), [f"yt{b}"], ["s1"])
        P.op("gpsimd", lambda e, y=y: e.tensor_tensor(out=sq[:], in0=y[:], in1=y[:], op=ALU.mult), [f"yt{b}"], ["sq"])
        P.op("vector", lambda e: e.tensor_reduce(out=s2[:], in_=sq3, axis=AX.X, op=ALU.add), ["sq"], ["s2"])
        P.op("vector", lambda e: e.tensor_scalar(out=s1[:], in0=s1[:], scalar1=1.0 / 64, scalar2=None, op0=ALU.mult), ["s1"], ["s1"])
        P.op("vector", lambda e: e.tensor_tensor(out=msq[:], in0=s1[:], in1=s1[:], op=ALU.mult), ["s1"], ["msq"])
        P.op("vector", lambda e: e.scalar_tensor_tensor(out=s2[:], in0=s2[:], scalar=1.0 / 64, in1=msq[:], op0=ALU.mult, op1=ALU.subtract), ["s2", "msq"], ["s2"])
        P.op("scalar", lambda e: e.activation(out=s2[:], in_=s2[:], func=AF.Sqrt, bias=eps2[:], scale=1.0), ["s2", "eps2"], ["s2"])
        P.op("vector", lambda e: e.reciprocal(out=s2[:], in_=s2[:]), ["s2"], ["s2"])
        P.op("vector", lambda e, y3=y3: e.tensor_tensor(out=y3, in0=y3, in1=bc16(s1), op=ALU.subtract), [f"yt{b}", "s1"], [f"yt{b}"])
        P.op("vector", lambda e, y3=y3: e.tensor_tensor(out=y3, in0=y3, in1=bc16(s2), op=ALU.mult), [f"yt{b}", "s2"], [f"yt{b}"])
        P.op("gpsimd", lambda e, y=y: e.tensor_tensor(out=y[:], in0=y[:], in1=lw[:], op=ALU.mult), [f"yt{b}", "lw"], [f"yt{b}"])
        P.op("gpsimd", lambda e, bo=bo: e.tensor_tensor(out=bo[:], in0=bo[:], in1=lb[:], op=ALU.add), [f"bt{b}", "lb"], [f"bt{b}"])
        P.op("vector", lambda e, y=y, bo=bo: e.tensor_tensor(out=y[:], in0=y[:], in1=bo[:], op=ALU.add), [f"yt{b}", f"bt{b}"], [f"yt{b}"])
        P.op("vector", lambda e, y=y, g=g: e.tensor_tensor(out=ygb[:], in0=y[:], in1=g[:], op=ALU.mult), [f"yt{b}", f"gt{b}"], ["ygb"])
        for cc in range(8):
            P.op("tensor", lambda e, cc=cc: e.transpose(out=ptr[:, cc, :], in_=ygb[:, cc * 128:(cc + 1) * 128], identity=ident[:]), ["ygb", "ident"], ["ptr"])
        P.op("scalar", lambda e, b=b: e.copy(out=ygT[b][:], in_=ptr[:]), ["ptr"], [f"ygT{b}"])
        for half in range(2):
            pb_ = ocnt % 2
            ocnt += 1
            for cc in range(8):
                P.op("tensor", lambda e, cc=cc, half=half, pb_=pb_, b=b: e.matmul(out=po[pb_][:], lhsT=ygT[b][:, cc, :], rhs=wob[:, cc, half * 512:(half + 1) * 512], start=(cc == 0), stop=(cc == 7)),
                     [f"ygT{b}", f"wob{cc}"], [f"po{pb_}"])
            P.op("vector", lambda e, half=half, pb_=pb_, x=x: e.tensor_tensor(out=x[:, half * 512:(half + 1) * 512], in0=po[pb_][:], in1=x[:, half * 512:(half + 1) * 512], op=ALU.add), [f"po{pb_}", f"xt{b}"], [f"xt{b}"])
        P.dma("gpsimd", x_out[t0:t0 + 128, :], x[:], [f"xt{b}"], [f"out_{t0}"], f"xt{b}o")
        outs.append(f"out_{t0}")
    P.op("sync", lambda e: None, outs, [])
    P.op("gpsimd", lambda e: None, outs, [])


def launch_rwkv_proj(x, xh, gvec, prm, vfirst):
    N = x.shape[0]
    T = N // NCORES
    NT = T // 128
    has_v = vfirst is not None
    nc = bass.Bass("TRN2", target_bir_lowering=False)
    xa = _dram(nc, "x", [T, D], F32, "ExternalInput")
    xha = _dram(nc, "xh", [128, D], F32, "ExternalInput")
    ga = _dram(nc, "g", [D], F32, "ExternalInput")
    pr = {}
    shapes = {"mu": [6, D], "w_rkv": [3, D, D], "w0": [D], "w1": [D, 64], "w2": [64, D], "a0": [D], "a1": [D, 64], "a2": [64, D],
              "g1": [D, 160], "g2": [160, D], "k_k": [D], "k_a": [D], "r_k": [D]}
    if has_v:
        shapes.update({"v0": [D], "v1": [D, 32], "v2": [32, D]})
    for nm, sh in shapes.items():
        pr[nm] = _dram(nc, "p_" + nm, sh, F32, "ExternalInput")
    QGo = _dram(nc, "QG", [NT, 16, 128, 128], BF16, "ExternalOutput")
    PHo = _dram(nc, "PH", [NT, 16, 128, 128], BF16, "ExternalOutput")
    vbo = _dram(nc, "vb", [T, D], BF16, "ExternalOutput")
    gto = _dram(nc, "gate", [T, D], F32, "ExternalOutput")
    boo = _dram(nc, "bonus", [T, D], F32, "ExternalOutput")
    vfo = _dram(nc, "vfirst", [T, D], F32, "ExternalInput" if has_v else "ExternalOutput")
    P = Prog(nc)
    with ExitStack() as st:
        emit_rwkv_proj(P, nc, st, xa, xha, ga, pr, QGo, PHo, vbo, gto, boo, vfo, T, has_v)
        P.emit()
    maps = []
    for i in range(NCORES):
        m = {"x": x[i * T:(i + 1) * T], "xh": xh[i], "g": gvec}
        for nm in shapes:
            m["p_" + nm] = np.ascontiguousarray(prm[nm]).reshape(shapes[nm])
        if has_v:
            m["vfirst"] = vfirst[i * T:(i + 1) * T]
        maps.append(m)
    return _run(nc, maps)


def launch_rwkv_scan(QG_c, PH_c, V_c, S):
    NTs = S // 128
    nc = bass.Bass("TRN2", target_bir_lowering=False)
    qa = _dram(nc, "QG", [NTs, 8, 128, 128], BF16, "ExternalInput")
    pa = _dram(nc, "PH", [NTs, 8, 128, 128], BF16, "ExternalInput")
    va = _dram(nc, "V", [S, 512], BF16, "ExternalInput")
    yo = _dram(nc, "y", [S, 512], F32, "ExternalOutput")
    P = Prog(nc)
    with ExitStack() as st:
        emit_rwkv_scan(P, nc, st, qa, pa, va, yo, NTs)
        P.emit()
    maps = [{"QG": QG_c[i], "PH": PH_c[i], "V": V_c[i]} for i in range(NCORES)]
    r = _run(nc, maps)
    return [ri["y"] for ri in r]


def launch_rwkv_out_ffn(x, y, gate, bonus, lnw, lnb, w_o, gffn, wg, wu, wd, final_g=None):
    N = x.shape[0]
    T = N // NCORES
    nc = bass.Bass("TRN2", target_bir_lowering=False)
    xa = _dram(nc, "x", [T, D], F32, "ExternalInput")
    ya_ = _dram(nc, "yin", [T, D], F32, "ExternalInput")
    gta = _dram(nc, "gate", [T, D], F32, "ExternalInput")
    boa = _dram(nc, "bonus", [T, D], F32, "ExternalInput")
    lwa = _dram(nc, "lnw", [D], F32, "ExternalInput")
    lba = _dram(nc, "lnb", [D], F32, "ExternalInput")
    woa = _dram(nc, "w_o", [D, D], F32, "ExternalInput")
    ga = _dram(nc, "g", [D], F32, "ExternalInput")
    wga = _dram(nc, "wg", [D, DFF], F32, "ExternalInput")
    wua = _dram(nc, "wu", [D, DFF], F32, "ExternalInput")
    wda = _dram(nc, "wd", [DFF, D], F32, "ExternalInput")
    out = _dram(nc, "y", [T, D], F32, "ExternalOutput")
    fa = _dram(nc, "gfin", [D], F32, "ExternalInput") if final_g is not None else None
    P1 = Prog(nc)
    with ExitStack() as st:
        emit_rwkv_out(P1, nc, st, xa, ya_, gta, boa, lwa, lba, woa, out, T)
        P1.emit()
    P2 = Prog(nc)
    with ExitStack() as st:
        emit_ffn(P2, nc, st, out, ga, wga, wua, wda, out, T, final_g=fa)
        P2.emit()
    maps = []
    for i in range(NCORES):
        sl = slice(i * T, (i + 1) * T)
        m = {"x": x[sl], "yin": y[sl], "gate": gate[i], "bonus": bonus[i], "lnw": lnw, "lnb": lnb, "w_o": w_o, "g": gffn, "wg": wg, "wu": wu, "wd": wd}
        if final_g is not None:
            m["gfin"] = final_g
        maps.append(m)
    r = _run(nc, maps)
    return np.concatenate([ri["y"] for ri in r], axis=0)


def rwkv_layer(x, B, S, g_mix, prm, vfirst, lnw, lnb, w_o, g_ffn, wg, wu, wd, final_g=None):
    N = B * S
    T = N // NCORES
    per = S // T
    xh = []
    for i in range(NCORES):
        t0 = i * T
        hrow = np.zeros((128, D), np.float32)
        if t0 % S != 0:
            hrow[0] = x[t0 - 1]
        xh.append(hrow)
    r = launch_rwkv_proj(x, xh, g_mix, prm, vfirst)
    if vfirst is None:
        vfirst = np.concatenate([ri["vfirst"] for ri in r], axis=0)
    QG_c, PH_c, V_c = [], [], []
    for i in range(NCORES):
        b, hh = i // 2, i % 2
        cs = range(b * per, (b + 1) * per)
        QG_c.append(np.ascontiguousarray(np.concatenate([r[j]["QG"][:, hh * 8:(hh + 1) * 8] for j in cs], axis=0)))
        PH_c.append(np.ascontiguousarray(np.concatenate([r[j]["PH"][:, hh * 8:(hh + 1) * 8] for j in cs], axis=0)))
        V_c.append(np.ascontiguousarray(np.concatenate([r[j]["vb"][:, hh * 512:(hh + 1) * 512] for j in cs], axis=0)))
    ys = launch_rwkv_scan(QG_c, PH_c, V_c, S)
    y = np.concatenate([np.concatenate([ys[2 * b], ys[2 * b + 1]], axis=1) for b in range(B)], axis=0)
    xn = launch_rwkv_out_ffn(x, y, [ri["gate"] for ri in r], [ri["bonus"] for ri in r], lnw, lnb, w_o, g_ffn, wg, wu, wd, final_g=final_g)
    return xn, vfirst


def kernel_unfused(x, positions, norm_mix, norm_ffn, norm_final, attn_w_in, attn_b_in, attn_sinks, attn_w_out,
           rwkv_mu, rwkv_w_rkv, rwkv_w0, rwkv_w1, rwkv_w2, rwkv_a0, rwkv_a1, rwkv_a2, rwkv_g1, rwkv_g2,
           rwkv_k_k, rwkv_k_a, rwkv_r_k, rwkv_lnx_w, rwkv_lnx_b, rwkv_w_o, rwkv_v0, rwkv_v1, rwkv_v2,
           ffn_w_gate, ffn_w_up, ffn_w_down):
    f = lambda a: np.ascontiguousarray(np.asarray(a, dtype=np.float32))
    x = f(x)
    B, S, _ = x.shape
    xc = x.reshape(B * S, D)
    pos = np.ascontiguousarray(np.asarray(positions, dtype=np.int32)).reshape(-1)
    vfirst = None
    depth = norm_mix.shape[0]
    for layer in range(depth):
        i = layer // 2
        fin = f(norm_final) if layer == depth - 1 else None
        if layer % 2 == 0:
            xc = attn_layer(xc, pos, B, S, f(norm_mix[layer]), f(attn_w_in[i]), f(attn_b_in[i]), f(attn_sinks[i]), f(attn_w_out[i]),
                            f(norm_ffn[layer]), f(ffn_w_gate[layer]), f(ffn_w_up[layer]), f(ffn_w_down[layer]), final_g=fin)
        else:
            prm = {"mu": f(rwkv_mu[i]), "w_rkv": f(rwkv_w_rkv[i]), "w0": f(rwkv_w0[i]), "w1": f(rwkv_w1[i]), "w2": f(rwkv_w2[i]),
                   "a0": f(rwkv_a0[i]), "a1": f(rwkv_a1[i]), "a2": f(rwkv_a2[i]), "g1": f(rwkv_g1[i]), "g2": f(rwkv_g2[i]),
                   "k_k": f(rwkv_k_k[i]), "k_a": f(rwkv_k_a[i]), "r_k": f(rwkv_r_k[i]).reshape(-1)}
            vf_in = None
            if i > 0:
                prm.update({"v0": f(rwkv_v0[i - 1]), "v1": f(rwkv_v1[i - 1]), "v2": f(rwkv_v2[i - 1])})
                vf_in = vfirst
            xc, vf = rwkv_layer(xc, B, S, f(norm_mix[layer]), prm, vf_in, f(rwkv_lnx_w[i]), f(rwkv_lnx_b[i]), f(rwkv_w_o[i]),
                                f(norm_ffn[layer]), f(ffn_w_gate[layer]), f(ffn_w_up[layer]), f(ffn_w_down[layer]), final_g=fin)
            if vfirst is None:
                vfirst = vf
    return xc.reshape(B, S, D).astype(np.float32)


def _phase(nc, fn):
    with nc.cleanup_on_exit():
        P = Prog(nc)
        P.persist_sems = True
        with ExitStack() as st:
            fn(P, st)
            P.emit()
        nc.all_engine_barrier()


_W_SHAPES = {
    "norm_mix": [4, D], "norm_ffn": [4, D], "norm_final": [D], "attn_w_in": [2, D, INW], "attn_b_in": [2, INW], "attn_sinks": [2, 12],
    "attn_w_out": [2, D, D], "rwkv_mu": [2, 6, D], "rwkv_w_rkv": [2, 3, D, D], "rwkv_w0": [2, D], "rwkv_w1": [2, D, 64], "rwkv_w2": [2, 64, D],
    "rwkv_a0": [2, D], "rwkv_a1": [2, D, 64], "rwkv_a2": [2, 64, D], "rwkv_g1": [2, D, 160], "rwkv_g2": [2, 160, D], "rwkv_k_k": [2, D],
    "rwkv_k_a": [2, D], "rwkv_r_k": [2, D], "rwkv_lnx_w": [2, D], "rwkv_lnx_b": [2, D], "rwkv_w_o": [2, D, D], "rwkv_v0": [1, D],
    "rwkv_v1": [1, D, 32], "rwkv_v2": [1, 32, D], "ffn_w_gate": [4, D, DFF], "ffn_w_up": [4, D, DFF], "ffn_w_down": [4, DFF, D]}


def build_fused(S, depth=4):
    nc = bass.Bass("TRN2", target_bir_lowering=False)
    NTs = S // 128
    xin = _dram(nc, "x", [S, D], F32, "ExternalInput")
    pos = _dram(nc, "positions", [S], I32, "ExternalInput")
    zer = _dram(nc, "zeros", [128, D], F32, "ExternalInput")
    W = {k: _dram(nc, k, sh, F32, "ExternalInput") for k, sh in _W_SHAPES.items()}
    yout = _dram(nc, "y", [S, D], F32, "ExternalOutput")
    I = "Internal"
    qkT = _dram(nc, "s_qkT", [NH_ROT, 64, S], BF16, I)
    vsc = _dram(nc, "s_v", [S, NV * 64], BF16, I)
    oT = _dram(nc, "s_oT", [16, 64, S], BF16, I)
    xs = [_dram(nc, "s_xa", [S, D], F32, I), _dram(nc, "s_xb", [S, D], F32, I)]
    QG = _dram(nc, "s_QG", [NTs, 16, 128, 128], BF16, I)
    PH = _dram(nc, "s_PH", [NTs, 16, 128, 128], BF16, I)
    vb = _dram(nc, "s_vb", [S, D], BF16, I)
    gate = _dram(nc, "s_gate", [S, D], F32, I)
    bonus = _dram(nc, "s_bonus", [S, D], F32, I)
    vfirst = _dram(nc, "s_vfirst", [S, D], F32, I)
    ysc = _dram(nc, "s_y", [S, D], F32, I)
    cur = xin
    counts = []
    for layer in range(depth):
        i = layer // 2
        last = layer == depth - 1
        nxt_ = xs[layer % 2]
        dst = yout if last else nxt_
        fin = W["norm_final"] if last else None
        if layer % 2 == 0:
            _phase(nc, lambda P, st: emit_attn_proj(P, nc, st, cur, pos, W["norm_mix"][layer], W["attn_w_in"][i], W["attn_b_in"][i], qkT, vsc, S))
            _phase(nc, lambda P, st: emit_attn_core(P, nc, st, qkT, vsc, W["attn_sinks"][i], oT, S))
            _phase(nc, lambda P, st: emit_head_outproj(P, nc, st, cur, oT, W["attn_w_out"][i], nxt_, S))
        else:
            pr = {"mu": W["rwkv_mu"][i], "w_rkv": W["rwkv_w_rkv"][i], "w0": W["rwkv_w0"][i], "w1": W["rwkv_w1"][i], "w2": W["rwkv_w2"][i],
                  "a0": W["rwkv_a0"][i], "a1": W["rwkv_a1"][i], "a2": W["rwkv_a2"][i], "g1": W["rwkv_g1"][i], "g2": W["rwkv_g2"][i],
                  "k_k": W["rwkv_k_k"][i], "k_a": W["rwkv_k_a"][i], "r_k": W["rwkv_r_k"][i]}
            has_v = i > 0
            if has_v:
                pr.update({"v0": W["rwkv_v0"][i - 1], "v1": W["rwkv_v1"][i - 1], "v2": W["rwkv_v2"][i - 1]})
            _phase(nc, lambda P, st: emit_rwkv_proj(P, nc, st, cur, zer, W["norm_mix"][layer], pr, QG, PH, vb, gate, bonus, vfirst, S, has_v))
            for hh in range(2):
                _phase(nc, lambda P, st: emit_rwkv_scan(P, nc, st, QG[:, hh * 8:(hh + 1) * 8], PH[:, hh * 8:(hh + 1) * 8], vb[:, hh * 512:(hh + 1) * 512],
                                                        ysc[:, hh * 512:(hh + 1) * 512], NTs))
            _phase(nc, lambda P, st: emit_rwkv_out(P, nc, st, cur, ysc, gate, bonus, W["rwkv_lnx_w"][i], W["rwkv_lnx_b"][i], W["rwkv_w_o"][i], nxt_, S))
        _phase(nc, lambda P, st: emit_ffn(P, nc, st, nxt_, W["norm_ffn"][layer], W["ffn_w_gate"][layer], W["ffn_w_up"][layer], W["ffn_w_down"][layer], dst, S, final_g=fin))
        cur = nxt_
    return nc


def kernel(x, positions, norm_mix, norm_ffn, norm_final, attn_w_in, attn_b_in, attn_sinks, attn_w_out,
           rwkv_mu, rwkv_w_rkv, rwkv_w0, rwkv_w1, rwkv_w2, rwkv_a0, rwkv_a1, rwkv_a2, rwkv_g1, rwkv_g2,
           rwkv_k_k, rwkv_k_a, rwkv_r_k, rwkv_lnx_w, rwkv_lnx_b, rwkv_w_o, rwkv_v0, rwkv_v1, rwkv_v2,
           ffn_w_gate, ffn_w_up, ffn_w_down):
    f = lambda a: np.ascontiguousarray(np.asarray(a, dtype=np.float32))
    x = f(x)
    B, S, _ = x.shape
    pos = np.ascontiguousarray(np.asarray(positions, dtype=np.int32))
    loc = dict(norm_mix=norm_mix, norm_ffn=norm_ffn, norm_final=norm_final, attn_w_in=np.asarray(attn_w_in)[:, :, _PERM],
               attn_b_in=np.asarray(attn_b_in)[:, _PERM], attn_sinks=attn_sinks, attn_w_out=attn_w_out, rwkv_mu=rwkv_mu, rwkv_w_rkv=rwkv_w_rkv,
               rwkv_w0=rwkv_w0, rwkv_w1=rwkv_w1, rwkv_w2=rwkv_w2, rwkv_a0=rwkv_a0, rwkv_a1=rwkv_a1, rwkv_a2=rwkv_a2, rwkv_g1=rwkv_g1,
               rwkv_g2=rwkv_g2, rwkv_k_k=rwkv_k_k, rwkv_k_a=rwkv_k_a, rwkv_r_k=rwkv_r_k, rwkv_lnx_w=rwkv_lnx_w, rwkv_lnx_b=rwkv_lnx_b,
               rwkv_w_o=rwkv_w_o, rwkv_v0=rwkv_v0, rwkv_v1=rwkv_v1, rwkv_v2=rwkv_v2, ffn_w_gate=ffn_w_gate, ffn_w_up=ffn_w_up, ffn_w_down=ffn_w_down)
    wts = {k: f(v).reshape(_W_SHAPES[k]) for k, v in loc.items()}
    zeros = np.zeros((128, D), np.float32)
    nc = build_fused(S, depth=np.asarray(norm_mix).shape[0])
    maps = []
    for c in range(NCORES):
        b = c % B
        m = {"x": x[b], "positions": pos[b], "zeros": zeros}
        m.update(wts)
        maps.append(m)
    r = _run(nc, maps)
    return np.stack([r[b]["y"] for b in range(B)], axis=0).astype(np.float32)
```

```python
import numpy as np
import concourse.bass as bass
import concourse.mybir as mybir
from contextlib import ExitStack
from concourse.bass_utils import run_bass_kernel_spmd

F32 = mybir.dt.float32
BF16 = mybir.dt.bfloat16
I32 = mybir.dt.int32
ALU = mybir.AluOpType
AF = mybir.ActivationFunctionType
AX = mybir.AxisListType

ENGS = ("tensor", "vector", "scalar", "gpsimd", "sync")


class _Op:
    __slots__ = ("eng", "fn", "deps", "idx", "dma_key", "signal", "semval", "waits")

    def __init__(self, eng, fn, dma_key):
        self.eng = eng
        self.fn = fn
        self.deps = []
        self.dma_key = dma_key
        self.signal = False
        self.semval = None
        self.waits = []


class Prog:
    def __init__(self, nc):
        self.nc = nc
        self.ops = {e: [] for e in ENGS}
        self.last_w = {}
        self.readers = {}
        self.all_ops = []
        self.bar = {e: None for e in ENGS}
        self.last_dma = {}

    def barrier(self):
        lasts = [self.ops[e][-1] for e in ENGS if self.ops[e]] + list(self.last_dma.values())
        for e in ENGS:
            self.bar[e] = list(lasts) + (self.bar[e] or [])
        self.last_w = {}
        self.readers = {}

    def op(self, eng, fn, reads=(), writes=(), dma_key=None):
        o = _Op(eng, fn, dma_key)
        deps = []
        for r in reads:
            w = self.last_w.get(r)
            if w is not None:
                deps.append(w)
        for w_ in writes:
            w = self.last_w.get(w_)
            if w is not None:
                deps.append(w)
            deps.extend(self.readers.get(w_, ()))
        if self.bar[eng] is not None:
            deps.extend(self.bar[eng])
            self.bar[eng] = None
        if dma_key is not None:
            self.last_dma[dma_key] = o
        o.deps = deps
        o.idx = len(self.ops[eng])
        self.ops[eng].append(o)
        self.all_ops.append(o)
        for r in reads:
            self.readers.setdefault(r, []).append(o)
        for w_ in writes:
            self.last_w[w_] = o
            self.readers[w_] = []
        return o

    def dma(self, eng, out, in_, reads, writes, key, **kw):
        def fn(e, out=out, in_=in_):
            return e.dma_start(out=out, in_=in_, **kw)
        return self.op(eng, fn, reads, writes, dma_key=key)

    def emit(self, extra_final_wait=True):
        nc = self.nc
        dma_count = {}
        for o in self.all_ops:
            if o.dma_key is not None:
                dma_count[o.dma_key] = dma_count.get(o.dma_key, 0) + 1
                o.semval = 16 * dma_count[o.dma_key]
        seen = {e: {} for e in ENGS}
        for e in ENGS:
            for o in self.ops[e]:
                need = {}
                for d in o.deps:
                    if d.dma_key is not None:
                        k = ("D", d.dma_key)
                        v = d.semval
                    else:
                        if d.eng == e and (d.idx >= o.idx or e == "tensor"):
                            continue
                        k = ("E", d.eng)
                        v = d.idx + 1
                    if v > need.get(k, (0, None))[0]:
                        need[k] = (v, d)
                for k, (v, d) in need.items():
                    if seen[e].get(k, 0) >= v:
                        continue
                    seen[e][k] = v
                    o.waits.append(d)
                    d.signal = True
        for e in ENGS:
            c = 0
            for o in self.ops[e]:
                if o.dma_key is None and o.signal:
                    c += 1
                    o.semval = c
        keys = sorted(dma_count.keys())
        with ExitStack() as st:
            if getattr(self, "persist_sems", False):
                _PH[0] += 1
                esem = {e: nc.alloc_semaphore(name=f"s{_PH[0]}_" + e) for e in ENGS}
                dsem = {k: nc.alloc_semaphore(name=f"d{_PH[0]}_" + str(k)) for k in keys}
            else:
                esem = {e: st.enter_context(nc.semaphore("s_" + e)) for e in ENGS}
                dsem = {k: st.enter_context(nc.semaphore("d_" + str(k))) for k in keys}
            block = st.enter_context(nc.Block())

            def mk(e):
                def body(eng):
                    for o in self.ops[e]:
                        for d in o.waits:
                            if d.dma_key is not None:
                                eng.wait_ge(dsem[d.dma_key], d.semval)
                            else:
                                eng.wait_ge(esem[d.eng], d.semval)
                        ins = o.fn(eng)
                        if ins is None:
                            continue
                        if o.dma_key is not None:
                            ins.then_inc(dsem[o.dma_key], 16)
                        elif o.signal:
                            ins.then_inc(esem[e], 1)
                return body

            for e in ENGS:
                if self.ops[e]:
                    getattr(block, e)(mk(e))
        return {e: len(self.ops[e]) for e in ENGS}


D = 1024
DFF = 2816
EPS = 1e-5
NH_ROT = 39
NV = 15
INW = 3456
PI = float(np.pi)
ROPE_THETA = 500000.0


_PH = [0]


def _alloc(nc, st):
    _PH[0] += 1
    pre = f"p{_PH[0]}_"
    sb = lambda name, shape, dt: st.enter_context(nc.sbuf_tensor(pre + name, shape, dt))
    ps = lambda name, shape, dt: st.enter_context(nc.psum_tensor(pre + name, shape, dt))
    return sb, ps


def emit_consts(P, nc, sb):
    c = {}
    c["identf"] = sb("identf", [128, 128], F32)
    c["ident"] = sb("ident", [128, 128], BF16)
    c["epsb"] = sb("epsb", [128, 1], F32)
    identf, ident, epsb = c["identf"], c["ident"], c["epsb"]
    P.op("gpsimd", lambda e: e.memset(identf[:], 0.0), [], ["identf"])
    P.op("gpsimd", lambda e: e.affine_select(out=identf[:], in_=identf[:], pattern=[[-1, 128]], compare_op=ALU.not_equal,
                                             fill=1.0, base=0, channel_multiplier=1), ["identf"], ["identf"])
    P.op("vector", lambda e: e.tensor_copy(out=ident[:], in_=identf[:]), ["identf"], ["ident"])
    P.op("vector", lambda e: e.memset(epsb[:], EPS), [], ["epsb"])
    return c


def emit_norm_T(P, c, x, xn, ss, rstd, hb, ptr, hT_dst, hT_name, b, eps_name="epsb"):
    epsb, ident = c["epsb"], c["ident"]
    P.op("scalar", lambda e: e.activation(out=hb[:], in_=x[:], func=AF.Square, accum_out=ss[:]), [xn], [f"hb{b}", f"ss{b}"])
    P.op("scalar", lambda e: e.activation(out=rstd[:], in_=ss[:], func=AF.Sqrt, bias=epsb[:], scale=1.0 / D), [f"ss{b}", "epsb"], [f"rstd{b}"])
    P.op("vector", lambda e: e.reciprocal(out=rstd[:], in_=rstd[:]), [f"rstd{b}"], [f"rstd{b}"])
    P.op("vector", lambda e: e.tensor_scalar(out=hb[:], in0=x[:], scalar1=rstd[:], scalar2=None, op0=ALU.mult), [xn, f"rstd{b}"], [f"hb{b}"])
    for cc in range(8):
        P.op("tensor", lambda e, cc=cc: e.transpose(out=ptr[:, cc, :], in_=hb[:, cc * 128:(cc + 1) * 128], identity=ident[:]),
             [f"hb{b}", "ident"], ["ptr"])
    P.op("scalar", lambda e: e.copy(out=hT_dst, in_=ptr[:]), ["ptr"], [hT_name])


def load_weight_bf16(P, nc, w_rows, wb_dst_fn, ncols, stg, gcol_fn, names_fn, k0=0, piece=1408):
    k = k0
    for c, wr in enumerate(w_rows):
        for h0 in range(0, ncols, piece):
            h1 = min(ncols, h0 + piece)
            s = stg[k % 2]
            sn = f"stg{k % 2}"
            P.dma("sync", s[:, 0:h1 - h0], wr[:, h0:h1], [], [sn], sn)
            eng = "vector" if k % 2 == 0 else "gpsimd"
            g = gcol_fn(c) if gcol_fn is not None else None
            if g is not None:
                P.op(eng, lambda e, s=s, c=c, h0=h0, h1=h1, g=g: e.tensor_scalar(out=wb_dst_fn(c)[:, h0:h1], in0=s[:, 0:h1 - h0], scalar1=g, scalar2=None, op0=ALU.mult),
                     [sn, "gcol"], [names_fn(c)])
            else:
                P.op(eng, lambda e, s=s, c=c, h0=h0, h1=h1: e.tensor_copy(out=wb_dst_fn(c)[:, h0:h1], in_=s[:, 0:h1 - h0]), [sn], [names_fn(c)])
            k += 1
    return k


def emit_ffn(P, nc, st, x_in, gvec, wg, wu, wd, x_out, T, TG=256, final_g=None):
    NJ = TG // 128
    NF = DFF // 128
    sb, ps = _alloc(nc, st)
    c = emit_consts(P, nc, sb)
    wgb = sb("wgb", [128, 8, DFF], BF16)
    wub = sb("wub", [128, 8, DFF], BF16)
    wdb = sb("wdb", [128, NF, D], BF16)
    stg = [sb(f"stg{i}", [128, 1408], F32) for i in range(2)]
    gcol = sb("gcol", [128, 8], F32)
    P.dma("sync", gcol[:], gvec.rearrange("(c p) -> p c", p=128), [], ["gcol"], "gcol", allow_slow_non_contiguous=True)
    k = load_weight_bf16(P, nc, [wg[cc * 128:(cc + 1) * 128, :] for cc in range(8)], lambda cc: wgb[:, cc, :], DFF, stg,
                         lambda cc: gcol[:, cc:cc + 1], lambda cc: f"wgb{cc}")
    k = load_weight_bf16(P, nc, [wu[cc * 128:(cc + 1) * 128, :] for cc in range(8)], lambda cc: wub[:, cc, :], DFF, stg,
                         lambda cc: gcol[:, cc:cc + 1], lambda cc: f"wub{cc}", k0=k)
    k = load_weight_bf16(P, nc, [wd[f * 128:(f + 1) * 128, :] for f in range(NF)], lambda f: wdb[:, f, :], D, stg,
                         None, lambda f: f"wdb{f}", k0=k)
    if final_g is not None:
        gfin = sb("gfin", [128, D], F32)
        P.dma("sync", gfin[:], final_g.partition_broadcast(128), [], ["gfin"], "gfin")

    xt = [sb(f"xt{j}", [128, D], F32) for j in range(2 * NJ)]
    hb = [sb(f"hb{j}", [128, D], BF16) for j in range(2)]
    ss = [sb(f"ss{j}", [128, 1], F32) for j in range(2)]
    rstd = [sb(f"rstd{j}", [128, 1], F32) for j in range(2)]
    hT = [sb(f"hT{i}", [128, 8, TG], BF16) for i in range(2)]
    act = sb("act", [128, NF, TG], BF16)
    sg = [sb(f"sg{i}", [128, TG], F32) for i in range(2)]
    ptr = ps("ptr", [128, 8, 128], BF16)
    pg = [ps(f"pg{i}", [128, 512], F32) for i in range(2)]
    pu = [ps(f"pu{i}", [128, 512], F32) for i in range(2)]
    po = [ps(f"po{i}", [128, 512], F32) for i in range(2)]

    ng = T // TG
    cnt = fcnt = ocnt = 0
    outs = []
    for g in range(ng):
        hTg = hT[g % 2]
        hTn = f"hT{g % 2}"
        for j in range(NJ):
            xi = (g % 2) * NJ + j
            x, xn = xt[xi], f"xt{xi}"
            t0 = g * TG + j * 128
            P.dma("sync", x[:], x_in[t0:t0 + 128, :], [], [xn], xn)
            b = cnt % 2
            cnt += 1
            emit_norm_T(P, c, x, xn, ss[b], rstd[b], hb[b], ptr, hTg[:, :, j * 128:(j + 1) * 128], hTn + f"_{j}", b)
        hreads = [hTn + f"_{j}" for j in range(NJ)]
        for f in range(NF):
            b = fcnt % 2
            fcnt += 1
            for cc in range(8):
                P.op("tensor", lambda e, b=b, cc=cc, f=f, hTg=hTg: e.matmul(out=pg[b][:, 0:TG], lhsT=wgb[:, cc, f * 128:(f + 1) * 128], rhs=hTg[:, cc, :], start=(cc == 0), stop=(cc == 7)),
                     hreads + [f"wgb{cc}"], [f"pg{b}"])
            for cc in range(8):
                P.op("tensor", lambda e, b=b, cc=cc, f=f, hTg=hTg: e.matmul(out=pu[b][:, 0:TG], lhsT=wub[:, cc, f * 128:(f + 1) * 128], rhs=hTg[:, cc, :], start=(cc == 0), stop=(cc == 7)),
                     hreads + [f"wub{cc}"], [f"pu{b}"])
            P.op("scalar", lambda e, b=b: e.activation(out=sg[b][:], in_=pg[b][:, 0:TG], func=AF.Silu), [f"pg{b}"], [f"sg{b}"])
            P.op("vector", lambda e, b=b, f=f: e.tensor_tensor(out=act[:, f, :], in0=sg[b][:], in1=pu[b][:, 0:TG], op=ALU.mult), [f"sg{b}", f"pu{b}"], [f"act{f}"])
        for j in range(NJ):
            xi = (g % 2) * NJ + j
            x, xn = xt[xi], f"xt{xi}"
            t0 = g * TG + j * 128
            for n in range(2):
                b = ocnt % 2
                ocnt += 1
                for f in range(NF):
                    P.op("tensor", lambda e, b=b, f=f, j=j, n=n: e.matmul(out=po[b][:], lhsT=act[:, f, j * 128:(j + 1) * 128], rhs=wdb[:, f, n * 512:(n + 1) * 512], start=(f == 0), stop=(f == NF - 1)),
                         [f"act{f}", f"wdb{f}"], [f"po{b}"])
                P.op("vector", lambda e, b=b, x=x, n=n: e.tensor_tensor(out=x[:, n * 512:(n + 1) * 512], in0=po[b][:], in1=x[:, n * 512:(n + 1) * 512], op=ALU.add), [f"po{b}", xn], [xn])
            if final_g is not None:
                b = cnt % 2
                cnt += 1
                P.op("scalar", lambda e, x=x, b=b: e.activation(out=hb[b][:], in_=x[:], func=AF.Square, accum_out=ss[b][:]), [xn], [f"hb{b}", f"ss{b}"])
                P.op("scalar", lambda e, b=b: e.activation(out=rstd[b][:], in_=ss[b][:], func=AF.Sqrt, bias=c["epsb"][:], scale=1.0 / D), [f"ss{b}", "epsb"], [f"rstd{b}"])
                P.op("vector", lambda e, b=b: e.reciprocal(out=rstd[b][:], in_=rstd[b][:]), [f"rstd{b}"], [f"rstd{b}"])
                P.op("vector", lambda e, x=x, b=b: e.scalar_tensor_tensor(out=x[:], in0=x[:], scalar=rstd[b][:], in1=gfin[:], op0=ALU.mult, op1=ALU.mult), [xn, f"rstd{b}", "gfin"], [xn])
            P.dma("gpsimd", x_out[t0:t0 + 128, :], x[:], [xn], [f"out_{t0}"], xn + "o")
            outs.append(f"out_{t0}")
    P.op("sync", lambda e: None, outs, [])
    P.op("gpsimd", lambda e: None, outs, [])


def emit_attn_proj(P, nc, st, x_in, pos, gvec, w_in, b_in, qkT_out, v_out, T):
    sb, ps = _alloc(nc, st)
    c = emit_consts(P, nc, sb)
    NT = T // 128
    winb = sb("winb", [128, 8, INW], BF16)
    stg = [sb(f"stg{i}", [128, 1728], F32) for i in range(2)]
    gcol = sb("gcol", [128, 8], F32)
    P.dma("sync", gcol[:], gvec.rearrange("(c p) -> p c", p=128), [], ["gcol"], "gcol", allow_slow_non_contiguous=True)
    load_weight_bf16(P, nc, [w_in[cc * 128:(cc + 1) * 128, :] for cc in range(8)], lambda cc: winb[:, cc, :], INW, stg,
                     lambda cc: gcol[:, cc:cc + 1], lambda cc: f"winb{cc}", piece=1728)
    biasb = sb("biasb", [1, INW], BF16)
    ones1 = sb("ones1", [1, 128], BF16)
    b2 = b_in.rearrange("(o n) -> o n", o=1)
    for hi in range(2):
        P.dma("sync", stg[hi][0:1, :], b2[:, hi * 1728:(hi + 1) * 1728], [], [f"stg{hi}"], f"stg{hi}")
        P.op("vector", lambda e, hi=hi: e.tensor_copy(out=biasb[:, hi * 1728:(hi + 1) * 1728], in_=stg[hi][0:1, :]), [f"stg{hi}"], ["biasb"])
    P.op("vector", lambda e: e.memset(ones1[:], 1.0), [], ["ones1"])
    posi = sb("posi", [128, NT], I32)
    posf = sb("posf", [128, NT], F32)
    ftab = sb("ftab", [128, 32], F32)
    otab = sb("otab", [128, 32], F32)
    ang = sb("ang", [128, NT, 32], F32)
    cs = sb("cs", [128, NT, 32], F32)
    qkv = [sb(f"qkv{i}", [128, INW], F32) for i in range(2)]
    ang2 = qkv[0][:, 0:NT * 32].rearrange("p (n f) -> p n f", f=32)
    angi = qkv[1][:, 0:NT * 32].bitcast(I32).rearrange("p (n f) -> p n f", f=32)
    P.dma("sync", posi[:], pos.rearrange("(n p) -> p n", p=128), [], ["posi"], "posi", allow_slow_non_contiguous=True)
    P.op("vector", lambda e: e.tensor_copy(out=posf[:], in_=posi[:]), ["posi"], ["posf"])
    ft3 = ftab[:].rearrange("p (a i) -> p a i", i=8)
    for i in range(8):
        fi = float(np.power(np.float32(ROPE_THETA), np.float32(-2.0 * i / 16.0)))
        P.op("gpsimd", lambda e, i=i, fi=fi: e.memset(ft3[:, :, i], fi), [], ["ftab"])
    for a, off in enumerate((PI / 2, PI / 2, PI, 0.0)):
        P.op("gpsimd", lambda e, a=a, off=off: e.memset(otab[:, a * 8:(a + 1) * 8], off), [], ["otab"])
    P.op("vector", lambda e: e.tensor_tensor(out=ang[:], in0=ftab[:].unsqueeze(1).broadcast_to([128, NT, 32]),
                                             in1=posf[:].unsqueeze(2).broadcast_to([128, NT, 32]), op=ALU.mult), ["ftab", "posf"], ["ang"])
    P.op("vector", lambda e: e.tensor_tensor(out=ang[:], in0=ang[:], in1=otab[:].unsqueeze(1).broadcast_to([128, NT, 32]), op=ALU.add), ["ang", "otab"], ["ang"])
    P.op("vector", lambda e: e.tensor_scalar(out=angi[:], in0=ang[:], scalar1=1.0 / (2 * PI), scalar2=None, op0=ALU.mult), ["ang"], ["angi"])
    P.op("vector", lambda e: e.tensor_copy(out=ang2[:], in_=angi[:]), ["angi"], ["ang2"])
    P.op("vector", lambda e: e.scalar_tensor_tensor(out=ang[:], in0=ang2[:], scalar=-2 * PI, in1=ang[:], op0=ALU.mult, op1=ALU.add), ["ang2", "ang"], ["ang"])
    P.op("vector", lambda e: e.tensor_scalar(out=ang2[:], in0=ang[:], scalar1=PI, scalar2=-2 * PI, op0=ALU.is_gt, op1=ALU.mult), ["ang"], ["ang2"])
    P.op("vector", lambda e: e.tensor_tensor(out=ang[:], in0=ang[:], in1=ang2[:], op=ALU.add), ["ang", "ang2"], ["ang"])
    P.op("vector", lambda e: e.tensor_scalar(out=ang2[:], in0=ang[:], scalar1=-PI, scalar2=2 * PI, op0=ALU.is_lt, op1=ALU.mult), ["ang"], ["ang2"])
    P.op("vector", lambda e: e.tensor_tensor(out=ang[:], in0=ang[:], in1=ang2[:], op=ALU.add), ["ang", "ang2"], ["ang"])
    P.op("scalar", lambda e: e.activation(out=cs[:], in_=ang[:], func=AF.Sin), ["ang"], ["cs"])

    xt = [sb(f"xt{j}", [128, D], F32) for j in range(2)]
    hb = [sb(f"hb{j}", [128, D], BF16) for j in range(2)]
    ss = [sb(f"ss{j}", [128, 1], F32) for j in range(2)]
    rstd = [sb(f"rstd{j}", [128, 1], F32) for j in range(2)]
    hT = [sb(f"hT{i}", [128, 8, 128], BF16) for i in range(2)]
    P.barrier()
    tA = sb("tA", [128, NH_ROT, 16], F32)
    tB = sb("tB", [128, NH_ROT, 16], F32)
    qkb = [sb(f"qkb{i}", [128, NH_ROT * 64], BF16) for i in range(2)]
    vb = [sb(f"vb{i}", [128, NV * 64], BF16) for i in range(2)]
    qkTs = [sb(f"qkTs{i}", [64, NH_ROT, 256], BF16) for i in range(2)]
    ptr = ps("ptr", [128, 8, 128], BF16)
    pp = [ps(f"pp{i}", [128, 512], F32) for i in range(2)]
    ptq = [ps(f"ptq{i}", [64, 8, 128], BF16) for i in range(2)]
    ident = c["ident"]
    pcnt = 0
    qcnt = 0
    outs = []
    for t in range(NT):
        b = t % 2
        x, xn = xt[b], f"xt{b}"
        t0 = t * 128
        P.dma("sync", x[:], x_in[t0:t0 + 128, :], [], [xn], xn)
        emit_norm_T(P, c, x, xn, ss[b], rstd[b], hb[b], ptr, hT[b][:], f"hT{b}", b)
        for n0 in range(0, INW, 512):
            n1 = min(INW, n0 + 512)
            pb = pcnt % 2
            pcnt += 1
            for cc in range(8):
                P.op("tensor", lambda e, pb=pb, cc=cc, n0=n0, n1=n1, b=b: e.matmul(out=pp[pb][:, 0:n1 - n0], lhsT=hT[b][:, cc, :], rhs=winb[:, cc, n0:n1], start=(cc == 0), stop=False),
                     [f"hT{b}", f"winb{cc}"], [f"pp{pb}"])
            P.op("tensor", lambda e, pb=pb, n0=n0, n1=n1: e.matmul(out=pp[pb][:, 0:n1 - n0], lhsT=ones1[:], rhs=biasb[:, n0:n1], start=False, stop=True),
                 ["ones1", "biasb"], [f"pp{pb}"])
            eng = "scalar" if (pcnt % 2 == 0) else "vector"
            if eng == "scalar":
                P.op("scalar", lambda e, pb=pb, n0=n0, n1=n1, b=b: e.copy(out=qkv[b][:, n0:n1], in_=pp[pb][:, 0:n1 - n0]), [f"pp{pb}"], [f"qkv{b}"])
            else:
                P.op("vector", lambda e, pb=pb, n0=n0, n1=n1, b=b: e.tensor_copy(out=qkv[b][:, n0:n1], in_=pp[pb][:, 0:n1 - n0]), [f"pp{pb}"], [f"qkv{b}"])
        Q3 = qkv[b][:, 0:NH_ROT * 64].rearrange("p (h d) -> p h d", d=64)
        O3 = qkb[b][:].rearrange("p (h d) -> p h d", d=64)
        cst = cs[:, t, :]
        P.op("gpsimd", lambda e, Q3=Q3, cst=cst: e.tensor_tensor(out=tA[:], in0=Q3[:, :, 0:16], in1=cst[:, 0:16].unsqueeze(1).broadcast_to([128, NH_ROT, 16]), op=ALU.mult), [f"qkv{b}", "cs"], ["tA"])
        P.op("vector", lambda e, Q3=Q3, cst=cst: e.tensor_tensor(out=tB[:, :, 0:8], in0=Q3[:, :, 8:16], in1=cst[:, 16:24].unsqueeze(1).broadcast_to([128, NH_ROT, 8]), op=ALU.mult), [f"qkv{b}", "cs"], ["tB0"])
        P.op("vector", lambda e, Q3=Q3, cst=cst: e.tensor_tensor(out=tB[:, :, 8:16], in0=Q3[:, :, 0:8], in1=cst[:, 24:32].unsqueeze(1).broadcast_to([128, NH_ROT, 8]), op=ALU.mult), [f"qkv{b}", "cs"], ["tB1"])
        P.op("vector", lambda e, O3=O3: e.tensor_tensor(out=O3[:, :, 0:16], in0=tA[:], in1=tB[:], op=ALU.add), ["tA", "tB0", "tB1"], [f"qkb{b}r"])
        P.op("scalar", lambda e, O3=O3, Q3=Q3: e.copy(out=O3[:, :, 16:64], in_=Q3[:, :, 16:64]), [f"qkv{b}"], [f"qkb{b}c"])
        P.op("gpsimd", lambda e, b=b: e.tensor_copy(out=vb[b][:], in_=qkv[b][:, NH_ROT * 64:INW]), [f"qkv{b}"], [f"vb{b}"])
        P.dma("gpsimd", v_out[t0:t0 + 128, :], vb[b][:], [f"vb{b}"], [f"vout{t}"], f"vb{b}o")
        outs.append(f"vout{t}")
        sbi = (t // 2) % 2
        so = (t % 2) * 128
        for h0 in range(0, NH_ROT, 8):
            h1 = min(NH_ROT, h0 + 8)
            qb_ = qcnt % 2
            qcnt += 1
            for h in range(h0, h1):
                P.op("tensor", lambda e, qb_=qb_, h=h, h0=h0, b=b: e.transpose(out=ptq[qb_][:, h - h0, :], in_=qkb[b][:, h * 64:(h + 1) * 64], identity=ident[:]),
                     [f"qkb{b}r", f"qkb{b}c", "ident"], [f"ptq{qb_}"])
            eng = "scalar" if (qcnt % 2 == 0) else "vector"
            if eng == "scalar":
                P.op("scalar", lambda e, qb_=qb_, h0=h0, h1=h1, sbi=sbi, so=so: e.copy(out=qkTs[sbi][:, h0:h1, so:so + 128], in_=ptq[qb_][:, 0:h1 - h0, :]), [f"ptq{qb_}"], [f"qkTs{sbi}_{t % 2}_{h0}"])
            else:
                P.op("vector", lambda e, qb_=qb_, h0=h0, h1=h1, sbi=sbi, so=so: e.tensor_copy(out=qkTs[sbi][:, h0:h1, so:so + 128], in_=ptq[qb_][:, 0:h1 - h0, :]), [f"ptq{qb_}"], [f"qkTs{sbi}_{t % 2}_{h0}"])
        if t % 2 == 1:
            tt0 = (t - 1) * 128
            rd = [f"qkTs{sbi}_{u}_{h0}" for u in range(2) for h0 in range(0, NH_ROT, 8)]
            for qi, (ha, hb_) in enumerate(((0, 13), (13, 26), (26, 39))):
                P.dma("sync", qkT_out[ha:hb_, :, tt0:tt0 + 256].rearrange("h d t -> d h t"), qkTs[sbi][:, ha:hb_, :], rd, [f"qout{t}_{qi}"], f"qkTs{sbi}o{qi}")
                outs.append(f"qout{t}_{qi}")
    P.op("sync", lambda e: None, outs, [])
    P.op("gpsimd", lambda e: None, outs, [])


def emit_attn_core(P, nc, st, qkT, v, sinks, oT_out, S):
    sb, ps = _alloc(nc, st)
    CH = 2048
    NCH = S // CH
    ones64 = sb("ones64", [128, 64], BF16)
    mfull = sb("mfull", [128, 4, 128], F32)
    m_own = sb("m_own", [128, 4, 128], BF16)
    m_pA = sb("m_pA", [128, 4, 128], BF16)
    m_pB = sb("m_pB", [128, 4, 128], BF16)
    P.op("vector", lambda e: e.memset(ones64[:], 1.0), [], ["ones64"])
    for (m, cm, pat, cmp_, nm) in ((m_own, -1, 1, ALU.is_ge, "m_own"), (m_pA, 1, -1, ALU.is_gt, "m_pA"), (m_pB, 1, -1, ALU.is_ge, "m_pB")):
        P.op("gpsimd", lambda e: e.memset(mfull[:], 1.0), [], ["mfull"])
        P.op("gpsimd", lambda e, cm=cm, pat=pat, cmp_=cmp_: e.affine_select(out=mfull[:], in_=mfull[:], pattern=[[0, 4], [pat, 128]], compare_op=cmp_,
                                                                           fill=0.0, base=0, channel_multiplier=cm), ["mfull"], ["mfull"])
        P.op("gpsimd", lambda e, m=m: e.tensor_copy(out=m[:], in_=mfull[:]), ["mfull"], [nm])
    sk = sb("sk", [64, 12], F32)
    esink = sb("esink", [64, 12, 128], F32)
    P.dma("sync", sk[:], sinks.partition_broadcast(64), [], ["sk"], "sk")
    P.op("scalar", lambda e: e.activation(out=sk[:], in_=sk[:], func=AF.Exp), ["sk"], ["sk"])
    P.op("vector", lambda e: e.memset(esink[:], 0.0), [], ["esink"])
    for h in range(12):
        P.op("vector", lambda e, h=h: e.tensor_scalar(out=esink[:, h, :], in0=esink[:, h, :], scalar1=sk[:, h:h + 1], scalar2=None, op0=ALU.add), ["esink", "sk"], ["esink"])

    qt = sb("qt", [64, 4, CH], BF16)
    kt = sb("kt", [64, 4, 2 * CH], BF16)
    vt = sb("vt", [128, 32, 256], BF16)
    vA = sb("vA", [128, 17, 64], BF16)
    accn = sb("accn", [64, 4, CH], F32)
    accd = sb("accd", [64, 4, CH], F32)
    pT = [sb(f"pT{i}", [128, 4, 128], BF16) for i in range(4)]
    tmp = [sb(f"tmp{i}", [64, 4, 128], F32) for i in range(2)]
    oA = [sb(f"oA{i}", [64, 4, 128], BF16) for i in range(2)]
    oB = [sb(f"oB{i}", [64, 4, 512], BF16) for i in range(2)]
    rdB = sb("rdB", [64, 4, 512], F32)
    psc = [ps(f"psc{i}", [128, 4, 128], F32) for i in range(4)]
    pnum = [ps(f"pnum{i}", [64, 4, 128], F32) for i in range(2)]
    pden = [ps(f"pden{i}", [64, 4, 128], F32) for i in range(2)]
    outs = []
    ucnt = 0

    def unit(kind, g, c, first, score_fn, pv_fn, out_fn):
        nonlocal ucnt
        u = ucnt % 2
        ucnt += 1
        kbs = (0,) if first else (0, 1)
        for kb in kbs:
            si = 2 * u + kb
            score_fn(kb, psc[si], f"psc{si}")
            P.op("scalar", lambda e, si=si: e.activation(out=pT[si][:], in_=psc[si][:], func=AF.Exp, scale=0.125), [f"psc{si}"], [f"pT{si}"])
            m, mn = (m_own, "m_own") if kb == 0 else ((m_pA, "m_pA") if kind == "A" else (m_pB, "m_pB"))
            P.op("vector", lambda e, si=si, m=m: e.tensor_tensor(out=pT[si][:], in0=pT[si][:], in1=m[:], op=ALU.mult), [f"pT{si}", mn], [f"pT{si}"])
        pv_fn(kbs, [pT[2 * u + kb] for kb in kbs], [f"pT{2 * u + kb}" for kb in kbs], pnum[u], f"pnum{u}")
        for i, kb in enumerate(kbs):
            P.op("tensor", lambda e, u=u, kb=kb, i=i: e.matmul(out=pden[u][:], lhsT=ones64[:], rhs=pT[2 * u + kb][:], start=(i == 0), stop=(i == len(kbs) - 1)),
                 [f"pT{2 * u + kb}", "ones64"], [f"pden{u}"])
        out_fn(u)

    for c in range(NCH):
        c0 = c * CH
        for g in range(3):
            P.dma("sync", qt[:], qkT[4 * g:4 * g + 4, :, c0:c0 + CH].rearrange("h d t -> d h t"), [], ["qt"], "qt")
            if c == 0:
                P.dma("sync", kt[:, 0, CH:2 * CH], qkT[12 + g, :, c0:c0 + CH], [], ["kt"], "kt")
                for n0 in (0, 8):
                    P.dma("sync", vA[:, 1 + n0:9 + n0, :], v[c0 + n0 * 128:c0 + (n0 + 8) * 128, g * 64:(g + 1) * 64].rearrange("(n p) d -> p n d", p=128), [], ["vA"], f"vA{n0}")
            else:
                P.dma("sync", kt[:, 0, CH - 128:2 * CH], qkT[12 + g, :, c0 - 128:c0 + CH], [], ["kt"], "kt")
                for n0, n1 in ((0, 9), (9, 17)):
                    P.dma("sync", vA[:, n0:n1, :], v[c0 - 128 + n0 * 128:c0 - 128 + n1 * 128, g * 64:(g + 1) * 64].rearrange("(n p) d -> p n d", p=128), [], ["vA"], f"vA{n0}")
            for i in range(16):
                first = (c == 0 and i == 0)

                def score_fn(kb, pt, pn, i=i):
                    k0 = CH + (i - kb) * 128
                    P.op("tensor", lambda e: e.matmul(out=pt[:], lhsT=kt[:, 0, k0:k0 + 128], rhs=qt[:, :, i * 128:(i + 1) * 128], start=True, stop=True),
                         ["qt", "kt"], [pn])

                def pv_fn(kbs, pts, ptn, pn_, pnn, i=i):
                    for ii, kb in enumerate(kbs):
                        P.op("tensor", lambda e, ii=ii, kb=kb: e.matmul(out=pn_[:], lhsT=vA[:, i + 1 - kb, :], rhs=pts[ii][:], start=(ii == 0), stop=(ii == len(kbs) - 1)),
                             [ptn[ii], "vA"], [pnn])

                def out_fn(u, i=i, g=g, c0=c0):
                    P.op("vector", lambda e: e.tensor_tensor(out=tmp[u][:], in0=pden[u][:], in1=esink[:, 4 * g:4 * g + 4, :], op=ALU.add), [f"pden{u}", "esink"], [f"tmp{u}"])
                    P.op("vector", lambda e: e.reciprocal(out=tmp[u][:], in_=tmp[u][:]), [f"tmp{u}"], [f"tmp{u}"])
                    P.op("vector", lambda e: e.tensor_tensor(out=oA[u][:], in0=pnum[u][:], in1=tmp[u][:], op=ALU.mult), [f"pnum{u}", f"tmp{u}"], [f"oA{u}"])
                    nm = f"oo_{c0}_{g}_{i}"
                    P.dma("gpsimd", oT_out[4 * g:4 * g + 4, :, c0 + i * 128:c0 + (i + 1) * 128].rearrange("h d t -> d h t"), oA[u][:], [f"oA{u}"], [nm], f"oA{u}o")
                    outs.append(nm)

                unit("A", g, c, first, score_fn, pv_fn, out_fn)
        for p, d in enumerate((1, 4, 16)):
            halo = 128 * d
            NIq = CH // (128 * d)
            P.dma("sync", qt[:], qkT[15 + 4 * p:19 + 4 * p, :, c0:c0 + CH].rearrange("h d t -> d h t"), [], ["qt"], "qt")
            lo = 0 if c == 0 else halo
            P.dma("sync", kt[:, :, CH - lo:2 * CH], qkT[27 + 4 * p:31 + 4 * p, :, c0 - lo:c0 + CH].rearrange("h d t -> d h t"), [], ["kt"], "kt")
            vt4 = vt[:, 0:(NIq + 1) * d, :].rearrange("j (I r) f -> j I r f", r=d)
            fcol = 192 + p * 256
            for I in range(0 if c > 0 else 1, NIq + 1):
                tok0 = c0 - halo + I * 128 * d
                for r0 in range(0, d, 8):
                    r1 = min(d, r0 + 8)
                    P.dma("sync", vt4[:, I, r0:r1, :], v[tok0:tok0 + 128 * d, fcol:fcol + 256].rearrange("(j r) f -> j r f", r=d)[:, r0:r1, :], [], [f"vt"], f"vt{(I + r0 // 8) % 8}")
            qv = [qt[:, h, :].rearrange("p (I j r) -> p I j r", j=128, r=d) for h in range(4)]
            kv = [kt[:, h, CH - halo:2 * CH].rearrange("p (I j r) -> p I j r", j=128, r=d) for h in range(4)]
            accn4 = [accn[:, h, :].rearrange("p (I j r) -> p I j r", j=128, r=d) for h in range(4)]
            accd4 = [accd[:, h, :].rearrange("p (I j r) -> p I j r", j=128, r=d) for h in range(4)]
            for Iq in range(NIq):
                for r in range(d):
                    first = (c == 0 and Iq == 0)

                    def score_fn(kb, pt, pn, Iq=Iq, r=r, kv=kv, qv=qv):
                        for h in range(4):
                            P.op("tensor", lambda e, h=h: e.matmul(out=pt[:, h, :], lhsT=kv[h][:, Iq + 1 - kb, :, r], rhs=qv[h][:, Iq, :, r], start=True, stop=True),
                                 ["qt", "kt"], [pn])

                    def pv_fn(kbs, pts, ptn, pn_, pnn, Iq=Iq, r=r, vt4=vt4):
                        for h in range(4):
                            for ii, kb in enumerate(kbs):
                                P.op("tensor", lambda e, h=h, ii=ii, kb=kb: e.matmul(out=pn_[:, h, :], lhsT=vt4[:, Iq + 1 - kb, r, h * 64:(h + 1) * 64], rhs=pts[ii][:, h, :],
                                                                                     start=(ii == 0), stop=(ii == len(kbs) - 1)),
                                     [ptn[ii], "vt"], [pnn])

                    def out_fn(u, Iq=Iq, r=r, p=p, d=d):
                        an = accn[:].rearrange("p h (I j r) -> p h I j r", j=128, r=d)[:, :, Iq, :, r]
                        ad = accd[:].rearrange("p h (I j r) -> p h I j r", j=128, r=d)[:, :, Iq, :, r]
                        if p == 0:
                            P.op("scalar", lambda e: e.copy(out=an, in_=pnum[u][:]), [f"pnum{u}"], ["accn"])
                            P.op("vector", lambda e: e.tensor_copy(out=ad, in_=pden[u][:]), [f"pden{u}"], ["accd"])
                        else:
                            P.op("vector", lambda e: e.tensor_tensor(out=an, in0=pnum[u][:], in1=an, op=ALU.add), [f"pnum{u}", "accn"], ["accn"])
                            P.op("vector", lambda e: e.tensor_tensor(out=ad, in0=pden[u][:], in1=ad, op=ALU.add), [f"pden{u}", "accd"], ["accd"])

                    unit("B", p, c, first, score_fn, pv_fn, out_fn)
        for q in range(CH // 512):
            u = q % 2
            P.op("vector", lambda e, q=q: e.reciprocal(out=rdB[:], in_=accd[:, :, q * 512:(q + 1) * 512]), ["accd"], ["rdB"])
            P.op("vector", lambda e, q=q, u=u: e.tensor_tensor(out=oB[u][:], in0=accn[:, :, q * 512:(q + 1) * 512], in1=rdB[:], op=ALU.mult), ["accn", "rdB"], [f"oB{u}"])
            nm = f"oob_{c0}_{q}"
            P.dma("gpsimd", oT_out[12:16, :, c0 + q * 512:c0 + (q + 1) * 512].rearrange("h d t -> d h t"), oB[u][:], [f"oB{u}"], [nm], f"oB{u}o")
            outs.append(nm)
    P.op("sync", lambda e: None, outs, [])
    P.op("gpsimd", lambda e: None, outs, [])


def emit_head_outproj(P, nc, st, x_in, oT, w_out, x_out, T, nheads=16):
    sb, ps = _alloc(nc, st)
    wob = sb("wob", [64, nheads, D], BF16)
    stg = [sb(f"stg{i}", [64, D], F32) for i in range(2)]
    w3 = w_out.rearrange("(h d) n -> d h n", d=64)
    for h in range(nheads):
        s, sn = stg[h % 2], f"stg{h % 2}"
        P.dma("sync", s[:], w3[:, h, :], [], [sn], sn)
        P.op("vector" if h % 2 == 0 else "gpsimd", lambda e, s=s, h=h: e.tensor_copy(out=wob[:, h, :], in_=s[:]), [sn], [f"wob{h}"])
    TG = 512
    oTt = [sb(f"oTt{i}", [64, nheads, TG], BF16) for i in range(2)]
    xt = [sb(f"xt{i}", [128, D], F32) for i in range(4)]
    po = [ps(f"po{i}", [128, 512], F32) for i in range(4)]
    outs = []
    cnt = 0
    ocnt = 0
    for g in range(T // TG):
        ob = g % 2
        for hh in range(0, nheads, 4):
            P.dma("sync", oTt[ob][:, hh:hh + 4, :], oT[hh:hh + 4, :, g * TG:(g + 1) * TG].rearrange("h d t -> d h t"), [], [f"oTt{ob}_{hh}"], f"oTt{ob}_{hh}")
        ords = [f"oTt{ob}_{hh}" for hh in range(0, nheads, 4)]
        for j in range(TG // 128):
            xi = cnt % 4
            cnt += 1
            x, xn = xt[xi], f"xt{xi}"
            t0 = g * TG + j * 128
            P.dma("sync", x[:], x_in[t0:t0 + 128, :], [], [xn], xn)
            for n in range(2):
                b = ocnt % 4
                ocnt += 1
                for h in range(nheads):
                    P.op("tensor", lambda e, b=b, h=h, j=j, n=n, ob=ob: e.matmul(out=po[b][:], lhsT=oTt[ob][:, h, j * 128:(j + 1) * 128], rhs=wob[:, h, n * 512:(n + 1) * 512],
                                                                              start=(h == 0), stop=(h == nheads - 1)), ords + [f"wob{h}"], [f"po{b}"])
                P.op("vector", lambda e, b=b, x=x, n=n: e.tensor_tensor(out=x[:, n * 512:(n + 1) * 512], in0=po[b][:], in1=x[:, n * 512:(n + 1) * 512], op=ALU.add), [f"po{b}", xn], [xn])
            P.dma("gpsimd", x_out[t0:t0 + 128, :], x[:], [xn], [f"out_{t0}"], xn + "o")
            outs.append(f"out_{t0}")
    P.op("sync", lambda e: None, outs, [])
    P.op("gpsimd", lambda e: None, outs, [])


NCORES = 8
_PERM = np.concatenate([np.arange(0, 768), np.arange(768, 960), np.arange(1152, 1920), np.arange(1920, 2688),
                        np.arange(960, 1152), np.arange(2688, 3456)])


def _run(nc, in_maps):
    res = run_bass_kernel_spmd(nc, in_maps, core_ids=list(range(NCORES)))
    return res.results


def _dram(nc, name, shape, dt, kind):
    return nc.dram_tensor(name, list(shape), dt, kind=kind).ap()


def launch_attn_proj(x, pos, gvec, w_in, b_in):
    N = x.shape[0]
    T = N // NCORES
    nc = bass.Bass("TRN2", target_bir_lowering=False)
    xa = _dram(nc, "x", [T, D], F32, "ExternalInput")
    pa = _dram(nc, "pos", [T], I32, "ExternalInput")
    ga = _dram(nc, "g", [D], F32, "ExternalInput")
    wa = _dram(nc, "w_in", [D, INW], F32, "ExternalInput")
    ba = _dram(nc, "b_in", [INW], F32, "ExternalInput")
    qo = _dram(nc, "qkT", [NH_ROT, 64, T], BF16, "ExternalOutput")
    vo = _dram(nc, "v", [T, NV * 64], BF16, "ExternalOutput")
    P = Prog(nc)
    with ExitStack() as st:
        emit_attn_proj(P, nc, st, xa, pa, ga, wa, ba, qo, vo, T)
        P.emit()
    wp = np.ascontiguousarray(w_in[:, _PERM])
    bp = np.ascontiguousarray(b_in[_PERM])
    maps = [{"x": x[i * T:(i + 1) * T], "pos": pos[i * T:(i + 1) * T], "g": gvec, "w_in": wp, "b_in": bp} for i in range(NCORES)]
    r = _run(nc, maps)
    return [ri["qkT"] for ri in r], [ri["v"] for ri in r]


def launch_attn_core(qkT_seq, v_seq, sinks, S):
    nc = bass.Bass("TRN2", target_bir_lowering=False)
    qa = _dram(nc, "qkT", [NH_ROT, 64, S], BF16, "ExternalInput")
    va = _dram(nc, "v", [S, NV * 64], BF16, "ExternalInput")
    sa = _dram(nc, "sinks", [12], F32, "ExternalInput")
    oo = _dram(nc, "oT", [16, 64, S], BF16, "ExternalOutput")
    P = Prog(nc)
    with ExitStack() as st:
        emit_attn_core(P, nc, st, qa, va, sa, oo, S)
        P.emit()
    nb = len(qkT_seq)
    maps = [{"qkT": qkT_seq[i % nb], "v": v_seq[i % nb], "sinks": sinks} for i in range(NCORES)]
    r = _run(nc, maps)
    return [r[i]["oT"] for i in range(nb)]


def launch_mix_ffn(x, oT_cores, w_out, gffn, wg, wu, wd, final_g=None):
    N = x.shape[0]
    T = N // NCORES
    nc = bass.Bass("TRN2", target_bir_lowering=False)
    xa = _dram(nc, "x", [T, D], F32, "ExternalInput")
    oa = _dram(nc, "oT", [16, 64, T], BF16, "ExternalInput")
    woa = _dram(nc, "w_out", [D, D], F32, "ExternalInput")
    ga = _dram(nc, "g", [D], F32, "ExternalInput")
    wga = _dram(nc, "wg", [D, DFF], F32, "ExternalInput")
    wua = _dram(nc, "wu", [D, DFF], F32, "ExternalInput")
    wda = _dram(nc, "wd", [DFF, D], F32, "ExternalInput")
    ya = _dram(nc, "y", [T, D], F32, "ExternalOutput")
    fa = _dram(nc, "gfin", [D], F32, "ExternalInput") if final_g is not None else None
    P1 = Prog(nc)
    with ExitStack() as st:
        emit_head_outproj(P1, nc, st, xa, oa, woa, ya, T)
        P1.emit()
    P2 = Prog(nc)
    with ExitStack() as st:
        emit_ffn(P2, nc, st, ya, ga, wga, wua, wda, ya, T, final_g=fa)
        P2.emit()
    maps = []
    for i in range(NCORES):
        m = {"x": x[i * T:(i + 1) * T], "oT": oT_cores[i], "w_out": w_out, "g": gffn, "wg": wg, "wu": wu, "wd": wd}
        if final_g is not None:
            m["gfin"] = final_g
        maps.append(m)
    r = _run(nc, maps)
    return np.concatenate([ri["y"] for ri in r], axis=0)


def attn_layer(x, pos, B, S, g_mix, w_in, b_in, sinks, w_out, g_ffn, wg, wu, wd, final_g=None):
    N = B * S
    T = N // NCORES
    qkT_c, v_c = launch_attn_proj(x, pos, g_mix, w_in, b_in)
    per = S // T
    qkT_seq = [np.concatenate(qkT_c[b * per:(b + 1) * per], axis=2) for b in range(B)]
    v_seq = [np.concatenate(v_c[b * per:(b + 1) * per], axis=0) for b in range(B)]
    oT_seq = launch_attn_core(qkT_seq, v_seq, sinks, S)
    oT_cores = [np.ascontiguousarray(oT_seq[i // per][:, :, (i % per) * T:((i % per) + 1) * T]) for i in range(NCORES)]
    return launch_mix_ffn(x, oT_cores, w_out, g_ffn, wg, wu, wd, final_g=final_g)


WSC = float(np.exp(-0.5))


def emit_rwkv_proj(P, nc, st, x_in, xh_in, gvec, pr, QG_out, PH_out, vb_out, gate_out, bonus_out, vfirst_io, T, has_vlora):
    sb, ps = _alloc(nc, st)
    c = emit_consts(P, nc, sb)
    ident, identf = c["ident"], c["identf"]
    NT = T // 128
    Et = sb("Et", [128, D], F32)
    t1 = sb("t1", [128, D], F32)
    stg = [Et, t1]
    gcol = sb("gcol", [128, 8], F32)
    P.dma("sync", gcol[:], gvec.rearrange("(c p) -> p c", p=128), [], ["gcol"], "gcol", allow_slow_non_contiguous=True)
    wrkv = [sb(f"w{nm}", [128, 8, D], BF16) for nm in "rkv"]
    k = 0
    for i in range(3):
        k = load_weight_bf16(P, nc, [pr["w_rkv"][i, cc * 128:(cc + 1) * 128, :] for cc in range(8)], lambda cc, i=i: wrkv[i][:, cc, :], D, stg,
                             lambda cc: gcol[:, cc:cc + 1], lambda cc, i=i: f"wrkv{i}_{cc}", k0=k, piece=1024)
    LW = {"w1": 64, "a1": 64, "g1": 160}
    if has_vlora:
        LW["v1"] = 32
    l1 = {}
    for nm, wd_ in LW.items():
        l1[nm] = sb("l1" + nm, [128, 8, wd_], BF16)
        k = load_weight_bf16(P, nc, [pr[nm][cc * 128:(cc + 1) * 128, :] for cc in range(8)], lambda cc, nm=nm: l1[nm][:, cc, :], wd_, stg,
                             lambda cc: gcol[:, cc:cc + 1], lambda cc, nm=nm: f"l1{nm}_{cc}", k0=k, piece=1024)
    l2 = {}
    for nm, rows in (("w2", 64), ("a2", 64), ("g2a", 128), ("g2b", 32)) + ((("v2", 32),) if has_vlora else ()):
        l2[nm] = sb("l2" + nm, [rows, D], BF16)
        src = pr["g2"][0:128, :] if nm == "g2a" else (pr["g2"][128:160, :] if nm == "g2b" else pr[nm])
        s, sn = stg[k % 2], f"stg{k % 2}"
        P.dma("sync", s[0:rows, :], src, [], [sn], sn)
        P.op("vector", lambda e, s=s, nm=nm, rows=rows: e.tensor_copy(out=l2[nm][:], in_=s[0:rows, :]), [sn], ["l2" + nm])
        k += 1
    bc = {}
    for nm in ("w0", "a0", "k_k", "k_a", "r_k") + (("v0",) if has_vlora else ()):
        bc[nm] = sb("bc" + nm, [128, D], F32)
        P.dma("sync", bc[nm][:], pr[nm].partition_broadcast(128), [], ["bc" + nm], "bc" + nm)
    mucol = sb("mucol", [128, 6, 8], F32)
    for i in range(6):
        P.dma("sync", mucol[:, i, :], pr["mu"][i].rearrange("(c p) -> p c", p=128), [], ["mucol"], f"mucol{i}", allow_slow_non_contiguous=True)
    ltri = sb("ltri", [128, 128], F32)
    sel0 = sb("sel0", [128, 128], F32)
    sel1 = sb("sel1", [128, 128], F32)
    mlo = sb("mlo", [128, 128], F32)
    mup = sb("mup", [128, 128], F32)
    mle = sb("mle", [128, 64], F32)
    addI = sb("addI", [128, 128], BF16)
    addI2 = sb("addI2", [128, 128], BF16)

    def gp(fn, r, w):
        P.op("gpsimd", fn, r, w)
    for tname, tl in (("ltri", ltri), ("mlo", mlo), ("mup", mup), ("sel0", sel0), ("sel1", sel1)):
        gp(lambda e, tl=tl: e.memset(tl[:], 1.0), [], [tname])
    for tname, tl in (("ltri", ltri), ("mlo", mlo), ("mup", mup)):
        gp(lambda e, tl=tl: e.memset(tl[0:64, 64:128], 0.0), [tname], [tname])
        gp(lambda e, tl=tl: e.memset(tl[64:128, 0:64], 0.0), [tname], [tname])
    gp(lambda e: e.affine_select(out=ltri[:], in_=ltri[:], pattern=[[1, 128]], compare_op=ALU.is_ge, fill=0.0, base=0, channel_multiplier=-1), ["ltri"], ["ltri"])
    gp(lambda e: e.affine_select(out=mup[:], in_=mup[:], pattern=[[1, 128]], compare_op=ALU.is_gt, fill=0.0, base=0, channel_multiplier=-1), ["mup"], ["mup"])
    gp(lambda e: e.affine_select(out=mlo[:], in_=mlo[:], pattern=[[-1, 128]], compare_op=ALU.is_gt, fill=0.0, base=0, channel_multiplier=1), ["mlo"], ["mlo"])
    gp(lambda e: e.affine_select(out=sel0[:], in_=sel0[:], pattern=[[0, 128]], compare_op=ALU.is_equal, fill=0.0, base=-63, channel_multiplier=1), ["sel0"], ["sel0"])
    gp(lambda e: e.affine_select(out=sel1[:], in_=sel1[:], pattern=[[0, 128]], compare_op=ALU.is_equal, fill=0.0, base=-127, channel_multiplier=1), ["sel1"], ["sel1"])
    gp(lambda e: e.tensor_copy(out=mle[0:64, :], in_=ltri[0:64, 0:64]), ["ltri"], ["mle"])
    gp(lambda e: e.tensor_copy(out=mle[64:128, :], in_=ltri[64:128, 64:128]), ["ltri"], ["mle"])
    P.op("vector", lambda e: e.memset(addI[:], 0.0), [], ["addI"])
    P.op("vector", lambda e: e.memset(addI2[:], 0.0), [], ["addI2"])
    for tcs in (slice(0, 64), slice(64, 128)):
        P.op("vector", lambda e, tcs=tcs: e.tensor_copy(out=addI[tcs, 0:64], in_=identf[tcs, tcs]), ["addI", "identf"], ["addI"])
        P.op("vector", lambda e, tcs=tcs: e.tensor_copy(out=addI2[tcs, 64:128], in_=identf[tcs, tcs]), ["addI2", "identf"], ["addI2"])

    xt = [sb("xt0", [128, D], F32)] * 2
    hb = [sb("hb0", [128, D], BF16)] * 2
    ss = [sb(f"ss{i}", [128, 1], F32) for i in range(2)]
    rstd = [sb(f"rstd{i}", [128, 1], F32) for i in range(2)]
    hT = [sb(f"hT{i}", [128, 8, 129], BF16) for i in range(2)]
    xxT = sb("xxT", [128, 8, 128], BF16)
    mixT = [sb(f"mixT{i}", [128, 8, 128], BF16) for i in range(6)]
    rt = sb("rt", [128, D], F32)
    kt_ = sb("kt_", [128, D], F32)
    vt_ = sb("vt_", [128, D], F32)
    sgm = sb("sgm", [128, D], F32)
    asg = sb("asg", [128, D], F32)
    kkn = sb("kkn", [128, D], F32)
    t2 = sb("t2", [128, D], F32)
    ge = [sb(f"ge{i}", [128, D], F32) for i in range(2)]
    nrm = sb("nrm", [128, 16], F32)
    bsc = sb("bsc", [128, 16], F32)
    tmA = sb("tmA", [128, D], BF16)
    tmB = sb("tmB", [128, D], BF16)
    tmK = sb("tmK", [128, D], BF16)
    tmR = sb("tmR", [128, D], BF16)
    vbf = hb[0]
    fm = {nm: sb("fm" + nm, [64, 16, 128], BF16) for nm in "ABKR"}
    lo1 = {nm: sb("lo1" + nm, [wd_ if wd_ <= 128 else 128, 128], BF16) for nm, wd_ in LW.items()}
    lo1g2 = sb("lo1g2", [32, 128], BF16)
    NCH = 4
    bdNT_c = [[sb(f"bdNT{ch}j{j}", [128, 128], BF16) for j in range(6)] for ch in range(NCH)]
    bdN_c = [[sb(f"bdN{ch}j{j}", [128, 128], BF16) for j in range(5)] for ch in range(NCH)]
    bdAk_c = [sb(f"bdAk{ch}", [128, 128], BF16) for ch in range(NCH)]
    Z_c = [[sb(f"Z{ch}_{i}", [128, 128], BF16) for i in range(2)] for ch in range(NCH)]
    ADD_c = [sb(f"ADD{ch}", [128, 128], BF16) for ch in range(NCH)]
    QGs = [sb(f"QGs{i}", [128, 16, 128], BF16) for i in range(1)] * 2
    PHs = [sb(f"PHs{i}", [128, 16, 128], BF16) for i in range(1)] * 2
    ptr = ps("ptr", [128, 8, 128], BF16)
    pA = ps("pA", [128, 512], F32)
    pB = ps("pB", [128, 512], F32)
    pc = [pA, pB]
    pool = [ps(f"pk{i}", [128, 512], F32) for i in range(5)]
    for i_ in range(5):
        P.op("vector", lambda e, i_=i_: e.memset(pool[i_][:], 0.0), [], [f"pk{i_}"])
    pl = pool[0][:].rearrange("p (s c) -> p s c", c=128)
    P.barrier()
    P.dma("sync", xt[1][:], xh_in, [], ["xt0"], "xt0")
    emit_norm_T(P, c, xt[1], "xt0", ss[0], rstd[0], hb[0], ptr, hT[0][:, :, 1:129], "hT0", 0)
    P.op("gpsimd", lambda e: e.tensor_copy(out=hT[1][:, :, 128:129], in_=hT[0][:, :, 1:2]), ["hT0"], ["hT1"])
    import os
    STOP = int(os.environ.get('R1_STOP', '9'))
    outs = []
    pwc = 0
    for n in range(NT if STOP > 1 else 0):
        b = n % 2
        x, xn = xt[b], "xt0"
        t0 = n * 128
        P.dma("sync", x[:], x_in[t0:t0 + 128, :], [], [xn], xn)
        emit_norm_T(P, c, x, xn, ss[0], rstd[0], hb[0], ptr, hT[b][:, :, 1:129], f"hT{b}", 0)
        P.op("gpsimd", lambda e, b=b: e.tensor_copy(out=hT[b][:, :, 0:1], in_=hT[1 - b][:, :, 128:129]), [f"hT{1 - b}"], [f"hT{b}h"])
        hr = [f"hT{b}", f"hT{b}h"]
        P.op("vector", lambda e, b=b: e.tensor_tensor(out=xxT[:], in0=hT[b][:, :, 0:128], in1=hT[b][:, :, 1:129], op=ALU.subtract), hr, ["xxT"])
        for i in range(6):
            for cc in range(8):
                P.op("vector", lambda e, i=i, cc=cc, b=b: e.scalar_tensor_tensor(out=mixT[i][:, cc, :], in0=xxT[:, cc, :], scalar=mucol[:, i, cc:cc + 1], in1=hT[b][:, cc, 1:129],
                                                                               op0=ALU.mult, op1=ALU.add), ["xxT", "mucol"] + hr, [f"mixT{i}"])
        slot = {"w1": 0, "a1": 1, "g1": 2, "v1": 3}
        msrc = {"w1": 1, "a1": 4, "g1": 5, "v1": 3}
        for nm, wd_ in LW.items():
            s_ = slot[nm]
            w0_ = min(wd_, 128)
            for cc in range(8):
                P.op("tensor", lambda e, nm=nm, cc=cc, s_=s_, w0_=w0_: e.matmul(out=pl[0:w0_, s_, :], lhsT=l1[nm][:, cc, 0:w0_], rhs=mixT[msrc[nm]][:, cc, :], start=(cc == 0), stop=(cc == 7)),
                     [f"mixT{msrc[nm]}", f"l1{nm}_{cc}"], ["pk0"])
        for cc in range(8):
            P.op("tensor", lambda e, cc=cc: e.matmul(out=pB[0:32, 0:128], lhsT=l1["g1"][:, cc, 128:160], rhs=mixT[5][:, cc, :], start=(cc == 0), stop=(cc == 7)),
                 ["mixT5", f"l1g1_{cc}"], ["pB"])
        P.op("scalar", lambda e: e.activation(out=lo1["w1"][:], in_=pl[0:64, 0, :], func=AF.Tanh), ["pk0"], ["lo1w1"])
        P.op("scalar", lambda e: e.copy(out=lo1["a1"][:], in_=pl[0:64, 1, :]), ["pk0"], ["lo1a1"])
        P.op("scalar", lambda e: e.activation(out=lo1["g1"][:], in_=pl[:, 2, :], func=AF.Sigmoid), ["pk0"], ["lo1g1"])
        P.op("scalar", lambda e: e.activation(out=lo1g2[:], in_=pB[0:32, 0:128], func=AF.Sigmoid), ["pB"], ["lo1g2"])
        if has_vlora:
            P.op("scalar", lambda e: e.copy(out=lo1["v1"][:], in_=pl[0:32, 3, :]), ["pk0"], ["lo1v1"])
        if STOP <= 2:
            continue

        def proj(dst_fn, lhs_list, reads):
            for half, pp_, pn in ((0, pA, "pA"), (1, pB, "pB")):
                nk = len(lhs_list)
                for ki, (lh, rh, rn) in enumerate(lhs_list):
                    P.op("tensor", lambda e, lh=lh, rh=rh, half=half, pp_=pp_, ki=ki, nk=nk: e.matmul(out=pp_[:], lhsT=lh, rhs=rh[:, half * 512:(half + 1) * 512], start=(ki == 0), stop=(ki == nk - 1)),
                         reads + [rn], [pn])
                dst_fn(half, pp_, pn)

        H = lambda half: slice(half * 512, (half + 1) * 512)
        for (mi, wi, dst, dn) in ((0, 0, rt, "rt"), (2, 1, kt_, "kt_"), (3, 2, vt_, "vt_")):
            proj(lambda half, pp_, pn, dst=dst, dn=dn: P.op("scalar", lambda e: e.copy(out=dst[:, H(half)], in_=pp_[:]), [pn], [dn + str(half)]),
                 [(mixT[mi][:, cc, :], wrkv[wi][:, cc, :], f"wrkv{wi}_{cc}") for cc in range(8)], [f"mixT{mi}"])
        RT, KT, VT = ["rt0", "rt1"], ["kt_0", "kt_1"], ["vt_0", "vt_1"]
        def ev_sig(dst, dn, bcn):
            def f(half, pp_, pn):
                P.op("vector", lambda e: e.tensor_tensor(out=dst[:, H(half)], in0=pp_[:], in1=bc[bcn][:, H(half)], op=ALU.add), [pn, "bc" + bcn], [dn + str(half)])
                P.op("scalar", lambda e: e.activation(out=dst[:, H(half)], in_=dst[:, H(half)], func=AF.Sigmoid), [dn + str(half)], [dn + str(half)])
            return f
        proj(ev_sig(sgm, "sgm", "w0"), [(lo1["w1"][:], l2["w2"], "l2w2")], ["lo1w1"])
        proj(ev_sig(asg, "asg", "a0"), [(lo1["a1"][:], l2["a2"], "l2a2")], ["lo1a1"])
        SG, AS = ["sgm0", "sgm1"], ["asg0", "asg1"]
        proj(lambda half, pp_, pn: P.op("scalar", lambda e: e.copy(out=t1[:, H(half)], in_=pp_[:]), [pn], ["t1"]),
             [(lo1["g1"][:], l2["g2a"], "l2g2a"), (lo1g2[:], l2["g2b"], "l2g2b")], ["lo1g1", "lo1g2"])
        P.dma("gpsimd", gate_out[t0:t0 + 128, :], t1[:], ["t1"], [f"go{n}"], "t1o")
        outs.append(f"go{n}")
        if has_vlora:
            proj(ev_sig(Et, "Et", "v0"), [(lo1["v1"][:], l2["v2"], "l2v2")], ["lo1v1"])
            P.dma("sync", t2[:], vfirst_io[t0:t0 + 128, :], [], ["t2"], "t2i")
            P.op("gpsimd", lambda e: e.tensor_tensor(out=t2[:], in0=t2[:], in1=vt_[:], op=ALU.subtract), ["t2"] + VT, ["t2"])
            P.op("gpsimd", lambda e: e.tensor_tensor(out=t2[:], in0=t2[:], in1=Et[:], op=ALU.mult), ["t2", "Et0", "Et1"], ["t2"])
            P.op("gpsimd", lambda e: e.tensor_tensor(out=vt_[:], in0=vt_[:], in1=t2[:], op=ALU.add), ["t2"] + VT, VT)
        else:
            P.dma("gpsimd", vfirst_io[t0:t0 + 128, :], vt_[:], VT, [f"vfo{n}"], "vfo")
            outs.append(f"vfo{n}")
        P.op("gpsimd", lambda e: e.tensor_copy(out=vbf[:], in_=vt_[:]), VT, ["hb0"])
        P.dma("gpsimd", vb_out[t0:t0 + 128, :], vbf[:], ["hb0"], [f"vbo{n}"], "vbfo")
        outs.append(f"vbo{n}")
        kkn3 = kkn[:].rearrange("p (h d) -> p h d", d=64)
        t13 = t1[:].rearrange("p (h d) -> p h d", d=64)
        t23 = t2[:].rearrange("p (h d) -> p h d", d=64)
        P.op("vector", lambda e: e.tensor_tensor(out=kkn[:], in0=kt_[:], in1=bc["k_k"][:], op=ALU.mult), KT + ["bck_k"], ["kkn"])
        P.op("gpsimd", lambda e: e.tensor_tensor(out=t2[:], in0=kkn[:], in1=kkn[:], op=ALU.mult), ["kkn"], ["t2"])
        P.op("vector", lambda e: e.tensor_reduce(out=nrm[:], in_=t23, axis=AX.X, op=ALU.add), ["t2"], ["nrm"])
        P.op("scalar", lambda e: e.activation(out=nrm[:], in_=nrm[:], func=AF.Sqrt), ["nrm"], ["nrm"])
        P.op("vector", lambda e: e.tensor_scalar(out=nrm[:], in0=nrm[:], scalar1=1e-12, scalar2=None, op0=ALU.max), ["nrm"], ["nrm"])
        P.op("vector", lambda e: e.reciprocal(out=nrm[:], in_=nrm[:]), ["nrm"], ["nrm"])
        P.op("vector", lambda e: e.tensor_tensor(out=kkn3, in0=kkn3, in1=nrm[:].unsqueeze(2).broadcast_to([128, 16, 64]), op=ALU.mult), ["kkn", "nrm"], ["kkn"])
        P.op("vector", lambda e: e.scalar_tensor_tensor(out=t2[:], in0=asg[:], scalar=-1.0, in1=bc["k_a"][:], op0=ALU.add, op1=ALU.mult), AS + ["bck_a", "t2"], ["t2"])
        P.op("vector", lambda e: e.scalar_tensor_tensor(out=kt_[:], in0=t2[:], scalar=1.0, in1=kt_[:], op0=ALU.add, op1=ALU.mult), ["t2"] + KT, KT)
        P.op("gpsimd", lambda e: e.tensor_tensor(out=t2[:], in0=rt[:], in1=kt_[:], op=ALU.mult), RT + KT + ["t2"], ["t2"])
        P.op("gpsimd", lambda e: e.tensor_tensor(out=t2[:], in0=t2[:], in1=bc["r_k"][:], op=ALU.mult), ["t2", "bcr_k"], ["t2"])
        P.op("vector", lambda e: e.tensor_reduce(out=bsc[:], in_=t23, axis=AX.X, op=ALU.add), ["t2"], ["bsc"])
        P.op("vector", lambda e: e.tensor_tensor(out=t23, in0=vt_[:].rearrange("p (h d) -> p h d", d=64), in1=bsc[:].unsqueeze(2).broadcast_to([128, 16, 64]), op=ALU.mult), VT + ["bsc", "t2"], ["t2"])
        P.dma("gpsimd", bonus_out[t0:t0 + 128, :], t2[:], ["t2"], [f"bo{n}"], "t2o")
        outs.append(f"bo{n}")
        for half in range(2):
            P.op("tensor", lambda e, half=half: e.matmul(out=pc[half][:], lhsT=ltri[:], rhs=sgm[:, H(half)], start=True, stop=True), [SG[half], "ltri"], [("pA", "pB")[half]])
        for half in range(2):
            P.op("scalar", lambda e, half=half: e.activation(out=Et[:, H(half)], in_=pc[half][:], func=AF.Exp, scale=-WSC), [("pA", "pB")[half]], [f"Et{half}"])
        P.op("vector", lambda e: e.tensor_tensor(out=tmR[:], in0=rt[:], in1=Et[:], op=ALU.mult), RT + ["Et0", "Et1"], ["tmR"])
        for half in range(2):
            P.op("scalar", lambda e, half=half: e.activation(out=Et[:, H(half)], in_=pc[half][:], func=AF.Exp, scale=WSC), [("pA", "pB")[half], "tmR"], [f"Et{half}"])
        P.op("vector", lambda e: e.tensor_tensor(out=tmK[:], in0=kt_[:], in1=Et[:], op=ALU.mult), KT + ["Et0", "Et1"], ["tmK"])
        P.op("gpsimd", lambda e: e.tensor_tensor(out=t1[:], in0=kkn[:], in1=asg[:], op=ALU.mult), ["kkn", "t1"] + AS, ["t1"])
        P.op("vector", lambda e: e.tensor_tensor(out=tmB[:], in0=t1[:], in1=Et[:], op=ALU.mult), ["t1", "Et0", "Et1"], ["tmB"])
        for half in range(2):
            P.op("vector", lambda e, half=half: e.tensor_tensor(out=t1[:, H(half)], in0=pc[half][:], in1=sgm[:, H(half)], op=ALU.subtract), [("pA", "pB")[half], SG[half], "t1", "tmB"], ["t1"])
        P.op("scalar", lambda e: e.activation(out=Et[:], in_=t1[:], func=AF.Exp, scale=-WSC), ["t1", "tmK", "tmB", "Et0", "Et1"], ["Et0", "Et1"])
        P.op("vector", lambda e: e.scalar_tensor_tensor(out=tmA[:], in0=kkn[:], scalar=-1.0, in1=Et[:], op0=ALU.mult, op1=ALU.mult), ["kkn", "Et0", "Et1"], ["tmA"])
        for half in range(2):
            P.op("scalar", lambda e, half=half: e.copy(out=t1[:, H(half)], in_=pc[half][:]), [("pA", "pB")[half], "t1"], ["t1"])
        for s_, sel in ((0, sel0), (1, sel1)):
            for half in range(2):
                P.op("tensor", lambda e, half=half, sel=sel: e.matmul(out=pc[half][:], lhsT=sel[:], rhs=t1[:, H(half)], start=True, stop=True), ["t1", "sel0", "sel1"], [("pA", "pB")[half]])
                P.op("scalar", lambda e, half=half, s_=s_: e.activation(out=ge[s_][:, H(half)], in_=pc[half][:], func=AF.Exp, scale=-WSC), [("pA", "pB")[half]], [f"ge{s_}"])
        if STOP <= 3:
            continue
        for nm, src, sn in (("A", tmA, "tmA"), ("B", tmB, "tmB"), ("K", tmK, "tmK"), ("R", tmR, "tmR")):
            for h0 in (0, 8):
                for hh in range(8):
                    P.op("tensor", lambda e, hh=hh, h0=h0, src=src: e.transpose(out=ptr[0:64, hh, :], in_=src[:, (h0 + hh) * 64:(h0 + hh + 1) * 64], identity=ident[:]), [sn, "ident"], ["ptr"])
                if nm in "AK":
                    P.op("scalar", lambda e, nm=nm, h0=h0: e.copy(out=fm[nm][:, h0:h0 + 8, :], in_=ptr[0:64, :, :]), ["ptr"], ["fm" + nm])
                else:
                    P.op("vector", lambda e, nm=nm, h0=h0: e.tensor_copy(out=fm[nm][:, h0:h0 + 8, :], in_=ptr[0:64, :, :]), ["ptr"], ["fm" + nm])
        ob = 0

        def nxt():
            nonlocal pwc
            k_ = pwc % 5
            pwc += 1
            return pool[k_], f"pk{k_}"

        def head_steps(h, ch):
            bdNT, bdN, bdAk, Z, ADD = bdNT_c[ch], bdN_c[ch], bdAk_c[ch], Z_c[ch], ADD_c[ch]
            R_ = lambda nm: f"{nm}_{ch}"
            hc = slice(h * 64, (h + 1) * 64)
            fA, fB, fK, fR = (fm[q][:, h, :] for q in "ABKR")
            pgb, pgn = nxt()
            pg_ = pgb[:].rearrange("p (s c) -> p s c", c=128)
            for cc in range(2):
                tc = slice(cc * 64, (cc + 1) * 64)
                for (slot_, lh, rh, rn1, rn2) in ((0, fA, fB, "fmA", "fmB"), (1, fA, fK, "fmA", "fmK"), (2, fB, fA, "fmB", "fmA")):
                    P.op("tensor", lambda e, slot_=slot_, lh=lh, rh=rh, tc=tc: e.matmul(out=pg_[tc, slot_, tc], lhsT=lh[:, tc], rhs=rh[:, tc], start=True, stop=True), [rn1, rn2], [pgn])
                P.op("tensor", lambda e, tc=tc: e.matmul(out=pg_[tc, 3, 0:64], lhsT=fB[:, tc], rhs=fR[:, tc], start=True, stop=True), ["fmB", "fmR"], [pgn])
                P.op("tensor", lambda e, tc=tc: e.matmul(out=pg_[tc, 3, 64:128], lhsT=fK[:, tc], rhs=fR[:, tc], start=True, stop=True), ["fmK", "fmR"], [pgn])
            P.op("vector", lambda e: e.tensor_tensor(out=bdNT[0][:], in0=pg_[:, 0, :], in1=mlo[:], op=ALU.mult), [pgn, "mlo"], [R_("bdNT0")])
            P.op("vector", lambda e: e.tensor_tensor(out=bdN[0][:], in0=pg_[:, 2, :], in1=mup[:], op=ALU.mult), [pgn, "mup"], [R_("bdN0")])
            P.op("vector", lambda e: e.tensor_tensor(out=bdAk[:], in0=pg_[:, 1, :], in1=mlo[:], op=ALU.mult), [pgn, "mlo"], [R_("bdAk")])
            P.op("vector", lambda e: e.tensor_tensor(out=Z[0][:, 0:64], in0=pg_[:, 3, 0:64], in1=mle[:], op=ALU.mult), [pgn, "mle"], [R_("Z0a")])
            P.op("vector", lambda e: e.tensor_tensor(out=ADD[:, 0:64], in0=pg_[:, 3, 64:128], in1=mle[:], op=ALU.mult), [pgn, "mle"], [R_("ADDa")])
            P.op("gpsimd", lambda e: e.tensor_copy(out=Z[0][:, 64:128], in_=tmB[:, hc]), ["tmB"], [R_("Z0b")])
            P.op("gpsimd", lambda e: e.tensor_copy(out=ADD[:, 64:128], in_=tmK[:, hc]), ["tmK"], [R_("ADDb")])
            yield
            for j in range(5):
                pa_, pan = nxt()
                P.op("tensor", lambda e, j=j, pa_=pa_: e.matmul(out=pa_[:, 0:128], lhsT=bdN[j][:], rhs=bdNT[j][:], start=True, stop=True), [R_(f"bdN{j}"), R_(f"bdNT{j}")], [pan])
                P.op("scalar", lambda e, j=j, pa_=pa_: e.copy(out=bdNT[j + 1][:], in_=pa_[:, 0:128]), [pan], [R_(f"bdNT{j + 1}")])
                if j < 4:
                    pb2, pbn = nxt()
                    P.op("tensor", lambda e, j=j, pb2=pb2: e.matmul(out=pb2[:, 0:128], lhsT=bdNT[j][:], rhs=bdN[j][:], start=True, stop=True), [R_(f"bdN{j}"), R_(f"bdNT{j}")], [pbn])
                    P.op("vector", lambda e, j=j, pb2=pb2: e.tensor_copy(out=bdN[j + 1][:], in_=pb2[:, 0:128]), [pbn], [R_(f"bdN{j + 1}")])
                yield
            zi = 0
            zr = [R_("Z0a"), R_("Z0b")]
            for j in range(5, -1, -1):
                pa_, pan = nxt()
                P.op("tensor", lambda e, zi=zi, pa_=pa_: e.matmul(out=pa_[:, 0:128], lhsT=ident[:], rhs=Z[zi][:], start=True, stop=False), zr + ["ident"], [pan])
                P.op("tensor", lambda e, zi=zi, pa_=pa_, j=j: e.matmul(out=pa_[:, 0:128], lhsT=bdNT[j][:], rhs=Z[zi][:], start=False, stop=True), zr + [R_(f"bdNT{j}")], [pan])
                zi = 1 - zi
                znew = [R_(f"Z{zi}a"), R_(f"Z{zi}b")]
                if j % 2:
                    P.op("scalar", lambda e, zi=zi, pa_=pa_: e.copy(out=Z[zi][:], in_=pa_[:, 0:128]), [pan], znew)
                else:
                    P.op("vector", lambda e, zi=zi, pa_=pa_: e.tensor_copy(out=Z[zi][:], in_=pa_[:, 0:128]), [pan], znew)
                zr = znew
                yield
            Zf = Z[zi]
            for cc in range(2):
                tc = slice(cc * 64, (cc + 1) * 64)
                pbk, pnm = nxt()
                P.op("tensor", lambda e, tc=tc, pbk=pbk: e.matmul(out=pbk[tc, 0:128], lhsT=tmA[tc, hc], rhs=Zf[tc, :], start=True, stop=False), zr + ["tmA"], [pnm])
                P.op("tensor", lambda e, tc=tc, pbk=pbk: e.matmul(out=pbk[tc, 0:128], lhsT=tmR[tc, hc], rhs=addI[tc, :], start=False, stop=False), ["tmR", "addI"], [pnm])
                P.op("tensor", lambda e, tc=tc, pbk=pbk: e.matmul(out=pbk[tc, 0:128], lhsT=ident[tc, tc], rhs=addI2[tc, :], start=False, stop=True), ["ident", "addI2"], [pnm])
                P.op("scalar", lambda e, cc=cc, tc=tc, pbk=pbk: e.copy(out=QGs[ob][tc, h, 0:64], in_=pbk[tc, 0:64]), [pnm], [f"QGs{ob}_{h}_{cc}q"])
                P.op("scalar", lambda e, cc=cc, tc=tc, pbk=pbk: e.copy(out=t2[tc, hc], in_=pbk[tc, 64:128]), [pnm], ["t2", f"t2g_{h}_{cc}"])
                P.op("gpsimd", lambda e, cc=cc, tc=tc: e.tensor_tensor(out=QGs[ob][tc, h, 64:128], in0=t2[tc, hc], in1=ge[cc][tc, hc], op=ALU.mult),
                     [f"t2g_{h}_{cc}", "t2", f"ge{cc}"], [f"QGs{ob}_{h}_{cc}g"])
            yield
            pa_, pan = nxt()
            P.op("tensor", lambda e, pa_=pa_: e.matmul(out=pa_[:, 0:128], lhsT=bdAk[:], rhs=Zf[:], start=True, stop=False), zr + [R_("bdAk")], [pan])
            P.op("tensor", lambda e, pa_=pa_: e.matmul(out=pa_[:, 0:128], lhsT=ident[:], rhs=ADD[:], start=False, stop=True), [R_("ADDa"), R_("ADDb"), "ident"], [pan])
            P.op("vector", lambda e, pa_=pa_: e.tensor_copy(out=PHs[ob][:, h, 0:64], in_=pa_[:, 0:64]), [pan], [f"PHs{ob}_{h}p"])
            for cc in range(2):
                tc = slice(cc * 64, (cc + 1) * 64)
                P.op("vector", lambda e, pa_=pa_, tc=tc, cc=cc: e.tensor_tensor(out=PHs[ob][tc, h, 64:128], in0=pa_[tc, 64:128], in1=ge[cc][tc, hc], op=ALU.mult),
                     [pan, f"ge{cc}"], [f"PHs{ob}_{h}h{cc}"])
            yield

        for g0 in range(0, 16, NCH):
            active = [head_steps(g0 + ch, ch) for ch in range(NCH)]
            while active:
                for g_ in list(active):
                    try:
                        next(g_)
                    except StopIteration:
                        active.remove(g_)
        if STOP <= 7:
            continue
        qrd = [f"QGs{ob}_{h}_{cc}{q}" for h in range(16) for cc in range(2) for q in "qg"]
        prd = [f"PHs{ob}_{h}{q}" for h in range(16) for q in ("p", "h0", "h1")]
        for hh in range(0, 16, 4):
            P.dma("sync", QG_out[n, hh:hh + 4].rearrange("h i c -> i h c"), QGs[ob][:, hh:hh + 4, :], qrd, [f"qgo{n}_{hh}"], f"QGs{ob}o{hh}")
            P.dma("sync", PH_out[n, hh:hh + 4].rearrange("h i c -> i h c"), PHs[ob][:, hh:hh + 4, :], prd, [f"pho{n}_{hh}"], f"PHs{ob}o{hh}")
            outs += [f"qgo{n}_{hh}", f"pho{n}_{hh}"]
    P.op("sync", lambda e: None, outs, [])
    P.op("gpsimd", lambda e: None, outs, [])


def emit_rwkv_scan(P, nc, st, QG, PH, V, y_out, NTs):
    sb, ps = _alloc(nc, st)
    NB = 3
    qg = [sb(f"qg{i}", [128, 8, 128], BF16) for i in range(NB)]
    ph = [sb(f"ph{i}", [128, 8, 128], BF16) for i in range(NB)]
    vv = [sb(f"vv{i}", [128, 512], BF16) for i in range(NB)]
    S = sb("S", [128, 8, 64], BF16)
    yt = [sb(f"yt{i}", [128, 512], F32) for i in range(2)]
    pS = [ps(f"pS{i}", [128, 8, 64], F32) for i in range(2)]
    pY = [ps(f"pY{i}", [128, 8, 64], F32) for i in range(4)]
    P.op("vector", lambda e: e.memset(S[:], 0.0), [], ["S0", "S1"])
    outs = []
    for n in range(NTs):
        b = n % NB
        P.dma("sync", qg[b][:], QG[n].rearrange("h i c -> i h c"), [], [f"qg{b}"], f"qg{b}")
        P.dma("sync", ph[b][:], PH[n].rearrange("h i c -> i h c"), [], [f"ph{b}"], f"ph{b}")
        P.dma("sync", vv[b][:], V[n * 128:(n + 1) * 128, :], [], [f"vv{b}"], f"vv{b}")
        yb = n % 2
        for cc in range(2):
            tc = slice(cc * 64, (cc + 1) * 64)
            to = slice((1 - cc) * 64, (2 - cc) * 64)
            pSb = pS[cc]
            rds = [f"qg{b}", f"ph{b}", f"vv{b}", f"S{cc}"]
            for h in range(8):
                P.op("tensor", lambda e, h=h, b=b, tc=tc, yb=yb, cc=cc: e.matmul(out=pY[2 * yb + cc][tc, h, :], lhsT=qg[b][tc, h, 0:64], rhs=S[tc, h, :], start=True, stop=False), rds, [f"pY{2 * yb + cc}"])
                P.op("tensor", lambda e, h=h, b=b, tc=tc, yb=yb, cc=cc: e.matmul(out=pY[2 * yb + cc][tc, h, :], lhsT=ph[b][tc, h, 0:64], rhs=vv[b][tc, h * 64:(h + 1) * 64], start=False, stop=True), rds, [f"pY{2 * yb + cc}"])
                P.op("tensor", lambda e, h=h, b=b, tc=tc, to=to, pSb=pSb: e.matmul(out=pSb[to, h, :], lhsT=qg[b][tc, h, 64:128], rhs=S[tc, h, :], start=True, stop=False), rds, [f"pS{cc}"])
                P.op("tensor", lambda e, h=h, b=b, tc=tc, to=to, pSb=pSb: e.matmul(out=pSb[to, h, :], lhsT=ph[b][tc, h, 64:128], rhs=vv[b][tc, h * 64:(h + 1) * 64], start=False, stop=True), rds, [f"pS{cc}"])
            P.op("scalar", lambda e, to=to, pSb=pSb: e.copy(out=S[to, :, :], in_=pSb[to, :, :]), [f"pS{cc}"], [f"S{1 - cc}"])
        for cc in range(2):
            tc = slice(cc * 64, (cc + 1) * 64)
            P.op("vector", lambda e, yb=yb, cc=cc, tc=tc: e.tensor_copy(out=yt[yb][tc, :], in_=pY[2 * yb + cc][tc, :, :].rearrange("p h d -> p (h d)")), [f"pY{2 * yb + cc}"], [f"yt{yb}"])
        P.dma("gpsimd", y_out[n * 128:(n + 1) * 128, :], yt[yb][:], [f"yt{yb}"], [f"yo{n}"], f"yt{yb}o")
        outs.append(f"yo{n}")
    P.op("sync", lambda e: None, outs, [])
    P.op("gpsimd", lambda e: None, outs, [])


LNX_EPS = 64e-5


def emit_rwkv_out(P, nc, st, x_in, y_in, gate_in, bonus_in, lnw, lnb, w_o, x_out, T):
    sb, ps = _alloc(nc, st)
    c = emit_consts(P, nc, sb)
    ident = c["ident"]
    wob = sb("wob", [128, 8, D], BF16)
    stg = [sb(f"stg{i}", [128, D], F32) for i in range(2)]
    load_weight_bf16(P, nc, [w_o[cc * 128:(cc + 1) * 128, :] for cc in range(8)], lambda cc: wob[:, cc, :], D, stg, None, lambda cc: f"wob{cc}", piece=1024)
    lw = sb("lw", [128, D], F32)
    lb = sb("lb", [128, D], F32)
    P.dma("sync", lw[:], lnw.partition_broadcast(128), [], ["lw"], "lw")
    P.dma("sync", lb[:], lnb.partition_broadcast(128), [], ["lb"], "lb")
    eps2 = sb("eps2", [128, 1], F32)
    P.op("vector", lambda e: e.memset(eps2[:], LNX_EPS), [], ["eps2"])
    yt = [sb(f"yt{i}", [128, D], F32) for i in range(2)]
    gt = [sb(f"gt{i}", [128, D], F32) for i in range(2)]
    bt = [sb(f"bt{i}", [128, D], F32) for i in range(2)]
    xt = [sb(f"xt{i}", [128, D], F32) for i in range(2)]
    sq = sb("sq", [128, D], F32)
    ygb = sb("ygb", [128, D], BF16)
    ygT = [sb(f"ygT{i}", [128, 8, 128], BF16) for i in range(2)]
    s1 = sb("s1", [128, 16], F32)
    s2 = sb("s2", [128, 16], F32)
    msq = sb("msq", [128, 16], F32)
    ptr = ps("ptr", [128, 8, 128], BF16)
    po = [ps(f"po{i}", [128, 512], F32) for i in range(2)]
    outs = []
    ocnt = 0
    for n in range(T // 128):
        b = n % 2
        t0 = n * 128
        y, g, bo, x = yt[b], gt[b], bt[b], xt[b]
        P.dma("sync", y[:], y_in[t0:t0 + 128, :], [], [f"yt{b}"], f"yt{b}")
        P.dma("sync", g[:], gate_in[t0:t0 + 128, :], [], [f"gt{b}"], f"gt{b}")
        P.dma("sync", bo[:], bonus_in[t0:t0 + 128, :], [], [f"bt{b}"], f"bt{b}")
        P.dma("sync", x[:], x_in[t0:t0 + 128, :], [], [f"xt{b}"], f"xt{b}")
        y3 = y[:].rearrange("p (h d) -> p h d", d=64)
        sq3 = sq[:].rearrange("p (h d) -> p h d", d=64)
        bc16 = lambda t: t[:].unsqueeze(2).broadcast_to([128, 16, 64])
        P.op("vector", lambda e, y3=y3: e.tensor_reduce(out=s1[:], in_=y3, axis=AX.X, op=ALU.add), [f"yt{b}"], ["s1"])
        P.op("gpsimd", lambda e, y=y: e.tensor_tensor(out=sq[:], in0=y[:], in1=y[:], op=ALU.mult), [f"yt{b}"], ["sq"])
        P.op("vector", lambda e: e.tensor_reduce(out=s2[:], in_=sq3, axis=AX.X, op=ALU.add), ["sq"], ["s2"])
        P.op("vector", lambda e: e.tensor_scalar(out=s1[:], in0=s1[:], scalar1=1.0 / 64, scalar2=None, op0=ALU.mult), ["s1"], ["s1"])
        P.op("vector", lambda e: e.tensor_tensor(out=msq[:], in0=s1[:], in1=s1[:], op=ALU.mult), ["s1"], ["msq"])
        P.op("vector", lambda e: e.scalar_tensor_tensor(out=s2[:], in0=s2[:], scalar=1.0 / 64, in1=msq[:], op0=ALU.mult, op1=ALU.subtract), ["s2", "msq"], ["s2"])
        P.op("scalar", lambda e: e.activation(out=s2[:], in_=s2[:], func=AF.Sqrt, bias=eps2[:], scale=1.0), ["s2", "eps2"], ["s2"])
        P.op("vector", lambda e: e.reciprocal(out=s2[:], in_=s2[:]), ["s2"], ["s2"])
        P.op("vector", lambda e, y3=y3: e.tensor_tensor(out=y3, in0=y3, in1=bc16(s1), op=ALU.subtract), [f"yt{b}", "s1"], [f"yt{b}"])
        P.op("vector", lambda e, y3=y3: e.tensor_tensor(out=y3, in0=y3, in1=bc16(s2), op=ALU.mult), [f"yt{b}", "s2"], [f"yt{b}"])
        P.op("gpsimd", lambda e, y=y: e.tensor_tensor(out=y[:], in0=y[:], in1=lw[:], op=ALU.mult), [f"yt{b}", "lw"], [f"yt{b}"])
        P.op("gpsimd", lambda e, bo=bo: e.tensor_tensor(out=bo[:], in0=bo[:], in1=lb[:], op=ALU.add), [f"bt{b}", "lb"], [f"bt{b}"])
        P.op("vector", lambda e, y=y, bo=bo: e.tensor_tensor(out=y[:], in0=y[:], in1=bo[:], op=ALU.add), [f"yt{b}", f"bt{b}"], [f"yt{b}"])
        P.op("vector", lambda e, y=y, g=g: e.tensor_tensor(out=ygb[:], in0=y[:], in1=g[:], op=ALU.mult), [f"yt{b}", f"gt{b}"], ["ygb"])
        for cc in range(8):
            P.op("tensor", lambda e, cc=cc: e.transpose(out=ptr[:, cc, :], in_=ygb[:, cc * 128:(cc + 1) * 128], identity=ident[:]), ["ygb", "ident"], ["ptr"])
        P.op("scalar", lambda e, b=b: e.copy(out=ygT[b][:], in_=ptr[:]), ["ptr"], [f"ygT{b}"])
        for half in range(2):
            pb_ = ocnt % 2
            ocnt += 1
            for cc in range(8):
                P.op("tensor", lambda e, cc=cc, half=half, pb_=pb_, b=b: e.matmul(out=po[pb_][:], lhsT=ygT[b][:, cc, :], rhs=wob[:, cc, half * 512:(half + 1) * 512], start=(cc == 0), stop=(cc == 7)),
                     [f"ygT{b}", f"wob{cc}"], [f"po{pb_}"])
            P.op("vector", lambda e, half=half, pb_=pb_, x=x: e.tensor_tensor(out=x[:, half * 512:(half + 1) * 512], in0=po[pb_][:], in1=x[:, half * 512:(half + 1) * 512], op=ALU.add), [f"po{pb_}", f"xt{b}"], [f"xt{b}"])
        P.dma("gpsimd", x_out[t0:t0 + 128, :], x[:], [f"xt{b}"], [f"out_{t0}"], f"xt{b}o")
        outs.append(f"out_{t0}")
    P.op("sync", lambda e: None, outs, [])
    P.op("gpsimd", lambda e: None, outs, [])


def launch_rwkv_proj(x, xh, gvec, prm, vfirst):
    N = x.shape[0]
    T = N // NCORES
    NT = T // 128
    has_v = vfirst is not None
    nc = bass.Bass("TRN2", target_bir_lowering=False)
    xa = _dram(nc, "x", [T, D], F32, "ExternalInput")
    xha = _dram(nc, "xh", [128, D], F32, "ExternalInput")
    ga = _dram(nc, "g", [D], F32, "ExternalInput")
    pr = {}
    shapes = {"mu": [6, D], "w_rkv": [3, D, D], "w0": [D], "w1": [D, 64], "w2": [64, D], "a0": [D], "a1": [D, 64], "a2": [64, D],
              "g1": [D, 160], "g2": [160, D], "k_k": [D], "k_a": [D], "r_k": [D]}
    if has_v:
        shapes.update({"v0": [D], "v1": [D, 32], "v2": [32, D]})
    for nm, sh in shapes.items():
        pr[nm] = _dram(nc, "p_" + nm, sh, F32, "ExternalInput")
    QGo = _dram(nc, "QG", [NT, 16, 128, 128], BF16, "ExternalOutput")
    PHo = _dram(nc, "PH", [NT, 16, 128, 128], BF16, "ExternalOutput")
    vbo = _dram(nc, "vb", [T, D], BF16, "ExternalOutput")
    gto = _dram(nc, "gate", [T, D], F32, "ExternalOutput")
    boo = _dram(nc, "bonus", [T, D], F32, "ExternalOutput")
    vfo = _dram(nc, "vfirst", [T, D], F32, "ExternalInput" if has_v else "ExternalOutput")
    P = Prog(nc)
    with ExitStack() as st:
        emit_rwkv_proj(P, nc, st, xa, xha, ga, pr, QGo, PHo, vbo, gto, boo, vfo, T, has_v)
        P.emit()
    maps = []
    for i in range(NCORES):
        m = {"x": x[i * T:(i + 1) * T], "xh": xh[i], "g": gvec}
        for nm in shapes:
            m["p_" + nm] = np.ascontiguousarray(prm[nm]).reshape(shapes[nm])
        if has_v:
            m["vfirst"] = vfirst[i * T:(i + 1) * T]
        maps.append(m)
    return _run(nc, maps)


def launch_rwkv_scan(QG_c, PH_c, V_c, S):
    NTs = S // 128
    nc = bass.Bass("TRN2", target_bir_lowering=False)
    qa = _dram(nc, "QG", [NTs, 8, 128, 128], BF16, "ExternalInput")
    pa = _dram(nc, "PH", [NTs, 8, 128, 128], BF16, "ExternalInput")
    va = _dram(nc, "V", [S, 512], BF16, "ExternalInput")
    yo = _dram(nc, "y", [S, 512], F32, "ExternalOutput")
    P = Prog(nc)
    with ExitStack() as st:
        emit_rwkv_scan(P, nc, st, qa, pa, va, yo, NTs)
        P.emit()
    maps = [{"QG": QG_c[i], "PH": PH_c[i], "V": V_c[i]} for i in range(NCORES)]
    r = _run(nc, maps)
    return [ri["y"] for ri in r]


def launch_rwkv_out_ffn(x, y, gate, bonus, lnw, lnb, w_o, gffn, wg, wu, wd, final_g=None):
    N = x.shape[0]
    T = N // NCORES
    nc = bass.Bass("TRN2", target_bir_lowering=False)
    xa = _dram(nc, "x", [T, D], F32, "ExternalInput")
    ya_ = _dram(nc, "yin", [T, D], F32, "ExternalInput")
    gta = _dram(nc, "gate", [T, D], F32, "ExternalInput")
    boa = _dram(nc, "bonus", [T, D], F32, "ExternalInput")
    lwa = _dram(nc, "lnw", [D], F32, "ExternalInput")
    lba = _dram(nc, "lnb", [D], F32, "ExternalInput")
    woa = _dram(nc, "w_o", [D, D], F32, "ExternalInput")
    ga = _dram(nc, "g", [D], F32, "ExternalInput")
    wga = _dram(nc, "wg", [D, DFF], F32, "ExternalInput")
    wua = _dram(nc, "wu", [D, DFF], F32, "ExternalInput")
    wda = _dram(nc, "wd", [DFF, D], F32, "ExternalInput")
    out = _dram(nc, "y", [T, D], F32, "ExternalOutput")
    fa = _dram(nc, "gfin", [D], F32, "ExternalInput") if final_g is not None else None
    P1 = Prog(nc)
    with ExitStack() as st:
        emit_rwkv_out(P1, nc, st, xa, ya_, gta, boa, lwa, lba, woa, out, T)
        P1.emit()
    P2 = Prog(nc)
    with ExitStack() as st:
        emit_ffn(P2, nc, st, out, ga, wga, wua, wda, out, T, final_g=fa)
        P2.emit()
    maps = []
    for i in range(NCORES):
        sl = slice(i * T, (i + 1) * T)
        m = {"x": x[sl], "yin": y[sl], "gate": gate[i], "bonus": bonus[i], "lnw": lnw, "lnb": lnb, "w_o": w_o, "g": gffn, "wg": wg, "wu": wu, "wd": wd}
        if final_g is not None:
            m["gfin"] = final_g
        maps.append(m)
    r = _run(nc, maps)
    return np.concatenate([ri["y"] for ri in r], axis=0)


def rwkv_layer(x, B, S, g_mix, prm, vfirst, lnw, lnb, w_o, g_ffn, wg, wu, wd, final_g=None):
    N = B * S
    T = N // NCORES
    per = S // T
    xh = []
    for i in range(NCORES):
        t0 = i * T
        hrow = np.zeros((128, D), np.float32)
        if t0 % S != 0:
            hrow[0] = x[t0 - 1]
        xh.append(hrow)
    r = launch_rwkv_proj(x, xh, g_mix, prm, vfirst)
    if vfirst is None:
        vfirst = np.concatenate([ri["vfirst"] for ri in r], axis=0)
    QG_c, PH_c, V_c = [], [], []
    for i in range(NCORES):
        b, hh = i // 2, i % 2
        cs = range(b * per, (b + 1) * per)
        QG_c.append(np.ascontiguousarray(np.concatenate([r[j]["QG"][:, hh * 8:(hh + 1) * 8] for j in cs], axis=0)))
        PH_c.append(np.ascontiguousarray(np.concatenate([r[j]["PH"][:, hh * 8:(hh + 1) * 8] for j in cs], axis=0)))
        V_c.append(np.ascontiguousarray(np.concatenate([r[j]["vb"][:, hh * 512:(hh + 1) * 512] for j in cs], axis=0)))
    ys = launch_rwkv_scan(QG_c, PH_c, V_c, S)
    y = np.concatenate([np.concatenate([ys[2 * b], ys[2 * b + 1]], axis=1) for b in range(B)], axis=0)
    xn = launch_rwkv_out_ffn(x, y, [ri["gate"] for ri in r], [ri["bonus"] for ri in r], lnw, lnb, w_o, g_ffn, wg, wu, wd, final_g=final_g)
    return xn, vfirst


def kernel_unfused(x, positions, norm_mix, norm_ffn, norm_final, attn_w_in, attn_b_in, attn_sinks, attn_w_out,
           rwkv_mu, rwkv_w_rkv, rwkv_w0, rwkv_w1, rwkv_w2, rwkv_a0, rwkv_a1, rwkv_a2, rwkv_g1, rwkv_g2,
           rwkv_k_k, rwkv_k_a, rwkv_r_k, rwkv_lnx_w, rwkv_lnx_b, rwkv_w_o, rwkv_v0, rwkv_v1, rwkv_v2,
           ffn_w_gate, ffn_w_up, ffn_w_down):
    f = lambda a: np.ascontiguousarray(np.asarray(a, dtype=np.float32))
    x = f(x)
    B, S, _ = x.shape
    xc = x.reshape(B * S, D)
    pos = np.ascontiguousarray(np.asarray(positions, dtype=np.int32)).reshape(-1)
    vfirst = None
    depth = norm_mix.shape[0]
    for layer in range(depth):
        i = layer // 2
        fin = f(norm_final) if layer == depth - 1 else None
        if layer % 2 == 0:
            xc = attn_layer(xc, pos, B, S, f(norm_mix[layer]), f(attn_w_in[i]), f(attn_b_in[i]), f(attn_sinks[i]), f(attn_w_out[i]),
                            f(norm_ffn[layer]), f(ffn_w_gate[layer]), f(ffn_w_up[layer]), f(ffn_w_down[layer]), final_g=fin)
        else:
            prm = {"mu": f(rwkv_mu[i]), "w_rkv": f(rwkv_w_rkv[i]), "w0": f(rwkv_w0[i]), "w1": f(rwkv_w1[i]), "w2": f(rwkv_w2[i]),
                   "a0": f(rwkv_a0[i]), "a1": f(rwkv_a1[i]), "a2": f(rwkv_a2[i]), "g1": f(rwkv_g1[i]), "g2": f(rwkv_g2[i]),
                   "k_k": f(rwkv_k_k[i]), "k_a": f(rwkv_k_a[i]), "r_k": f(rwkv_r_k[i]).reshape(-1)}
            vf_in = None
            if i > 0:
                prm.update({"v0": f(rwkv_v0[i - 1]), "v1": f(rwkv_v1[i - 1]), "v2": f(rwkv_v2[i - 1])})
                vf_in = vfirst
            xc, vf = rwkv_layer(xc, B, S, f(norm_mix[layer]), prm, vf_in, f(rwkv_lnx_w[i]), f(rwkv_lnx_b[i]), f(rwkv_w_o[i]),
                                f(norm_ffn[layer]), f(ffn_w_gate[layer]), f(ffn_w_up[layer]), f(ffn_w_down[layer]), final_g=fin)
            if vfirst is None:
                vfirst = vf
    return xc.reshape(B, S, D).astype(np.float32)


def _phase(nc, fn):
    with nc.cleanup_on_exit():
        P = Prog(nc)
        P.persist_sems = True
        with ExitStack() as st:
            fn(P, st)
            P.emit()
        nc.all_engine_barrier()


_W_SHAPES = {
    "norm_mix": [4, D], "norm_ffn": [4, D], "norm_final": [D], "attn_w_in": [2, D, INW], "attn_b_in": [2, INW], "attn_sinks": [2, 12],
    "attn_w_out": [2, D, D], "rwkv_mu": [2, 6, D], "rwkv_w_rkv": [2, 3, D, D], "rwkv_w0": [2, D], "rwkv_w1": [2, D, 64], "rwkv_w2": [2, 64, D],
    "rwkv_a0": [2, D], "rwkv_a1": [2, D, 64], "rwkv_a2": [2, 64, D], "rwkv_g1": [2, D, 160], "rwkv_g2": [2, 160, D], "rwkv_k_k": [2, D],
    "rwkv_k_a": [2, D], "rwkv_r_k": [2, D], "rwkv_lnx_w": [2, D], "rwkv_lnx_b": [2, D], "rwkv_w_o": [2, D, D], "rwkv_v0": [1, D],
    "rwkv_v1": [1, D, 32], "rwkv_v2": [1, 32, D], "ffn_w_gate": [4, D, DFF], "ffn_w_up": [4, D, DFF], "ffn_w_down": [4, DFF, D]}


def build_fused(S, depth=4):
    nc = bass.Bass("TRN2", target_bir_lowering=False)
    NTs = S // 128
    xin = _dram(nc, "x", [S, D], F32, "ExternalInput")
    pos = _dram(nc, "positions", [S], I32, "ExternalInput")
    zer = _dram(nc, "zeros", [128, D], F32, "ExternalInput")
    W = {k: _dram(nc, k, sh, F32, "ExternalInput") for k, sh in _W_SHAPES.items()}
    yout = _dram(nc, "y", [S, D], F32, "ExternalOutput")
    I = "Internal"
    qkT = _dram(nc, "s_qkT", [NH_ROT, 64, S], BF16, I)
    vsc = _dram(nc, "s_v", [S, NV * 64], BF16, I)
    oT = _dram(nc, "s_oT", [16, 64, S], BF16, I)
    xs = [_dram(nc, "s_xa", [S, D], F32, I), _dram(nc, "s_xb", [S, D], F32, I)]
    QG = _dram(nc, "s_QG", [NTs, 16, 128, 128], BF16, I)
    PH = _dram(nc, "s_PH", [NTs, 16, 128, 128], BF16, I)
    vb = _dram(nc, "s_vb", [S, D], BF16, I)
    gate = _dram(nc, "s_gate", [S, D], F32, I)
    bonus = _dram(nc, "s_bonus", [S, D], F32, I)
    vfirst = _dram(nc, "s_vfirst", [S, D], F32, I)
    ysc = _dram(nc, "s_y", [S, D], F32, I)
    cur = xin
    counts = []
    for layer in range(depth):
        i = layer // 2
        last = layer == depth - 1
        nxt_ = xs[layer % 2]
        dst = yout if last else nxt_
        fin = W["norm_final"] if last else None
        if layer % 2 == 0:
            _phase(nc, lambda P, st: emit_attn_proj(P, nc, st, cur, pos, W["norm_mix"][layer], W["attn_w_in"][i], W["attn_b_in"][i], qkT, vsc, S))
            _phase(nc, lambda P, st: emit_attn_core(P, nc, st, qkT, vsc, W["attn_sinks"][i], oT, S))
            _phase(nc, lambda P, st: emit_head_outproj(P, nc, st, cur, oT, W["attn_w_out"][i], nxt_, S))
        else:
            pr = {"mu": W["rwkv_mu"][i], "w_rkv": W["rwkv_w_rkv"][i], "w0": W["rwkv_w0"][i], "w1": W["rwkv_w1"][i], "w2": W["rwkv_w2"][i],
                  "a0": W["rwkv_a0"][i], "a1": W["rwkv_a1"][i], "a2": W["rwkv_a2"][i], "g1": W["rwkv_g1"][i], "g2": W["rwkv_g2"][i],
                  "k_k": W["rwkv_k_k"][i], "k_a": W["rwkv_k_a"][i], "r_k": W["rwkv_r_k"][i]}
            has_v = i > 0
            if has_v:
                pr.update({"v0": W["rwkv_v0"][i - 1], "v1": W["rwkv_v1"][i - 1], "v2": W["rwkv_v2"][i - 1]})
            _phase(nc, lambda P, st: emit_rwkv_proj(P, nc, st, cur, zer, W["norm_mix"][layer], pr, QG, PH, vb, gate, bonus, vfirst, S, has_v))
            for hh in range(2):
                _phase(nc, lambda P, st: emit_rwkv_scan(P, nc, st, QG[:, hh * 8:(hh + 1) * 8], PH[:, hh * 8:(hh + 1) * 8], vb[:, hh * 512:(hh + 1) * 512],
                                                        ysc[:, hh * 512:(hh + 1) * 512], NTs))
            _phase(nc, lambda P, st: emit_rwkv_out(P, nc, st, cur, ysc, gate, bonus, W["rwkv_lnx_w"][i], W["rwkv_lnx_b"][i], W["rwkv_w_o"][i], nxt_, S))
        _phase(nc, lambda P, st: emit_ffn(P, nc, st, nxt_, W["norm_ffn"][layer], W["ffn_w_gate"][layer], W["ffn_w_up"][layer], W["ffn_w_down"][layer], dst, S, final_g=fin))
        cur = nxt_
    return nc


def kernel(x, positions, norm_mix, norm_ffn, norm_final, attn_w_in, attn_b_in, attn_sinks, attn_w_out,
           rwkv_mu, rwkv_w_rkv, rwkv_w0, rwkv_w1, rwkv_w2, rwkv_a0, rwkv_a1, rwkv_a2, rwkv_g1, rwkv_g2,
           rwkv_k_k, rwkv_k_a, rwkv_r_k, rwkv_lnx_w, rwkv_lnx_b, rwkv_w_o, rwkv_v0, rwkv_v1, rwkv_v2,
           ffn_w_gate, ffn_w_up, ffn_w_down):
    f = lambda a: np.ascontiguousarray(np.asarray(a, dtype=np.float32))
    x = f(x)
    B, S, _ = x.shape
    pos = np.ascontiguousarray(np.asarray(positions, dtype=np.int32))
    loc = dict(norm_mix=norm_mix, norm_ffn=norm_ffn, norm_final=norm_final, attn_w_in=np.asarray(attn_w_in)[:, :, _PERM],
               attn_b_in=np.asarray(attn_b_in)[:, _PERM], attn_sinks=attn_sinks, attn_w_out=attn_w_out, rwkv_mu=rwkv_mu, rwkv_w_rkv=rwkv_w_rkv,
               rwkv_w0=rwkv_w0, rwkv_w1=rwkv_w1, rwkv_w2=rwkv_w2, rwkv_a0=rwkv_a0, rwkv_a1=rwkv_a1, rwkv_a2=rwkv_a2, rwkv_g1=rwkv_g1,
               rwkv_g2=rwkv_g2, rwkv_k_k=rwkv_k_k, rwkv_k_a=rwkv_k_a, rwkv_r_k=rwkv_r_k, rwkv_lnx_w=rwkv_lnx_w, rwkv_lnx_b=rwkv_lnx_b,
               rwkv_w_o=rwkv_w_o, rwkv_v0=rwkv_v0, rwkv_v1=rwkv_v1, rwkv_v2=rwkv_v2, ffn_w_gate=ffn_w_gate, ffn_w_up=ffn_w_up, ffn_w_down=ffn_w_down)
    wts = {k: f(v).reshape(_W_SHAPES[k]) for k, v in loc.items()}
    zeros = np.zeros((128, D), np.float32)
    nc = build_fused(S, depth=np.asarray(norm_mix).shape[0])
    maps = []
    for c in range(NCORES):
        b = c % B
        m = {"x": x[b], "positions": pos[b], "zeros": zeros}
        m.update(wts)
        maps.append(m)
    r = _run(nc, maps)
    return np.stack([r[b]["y"] for b in range(B)], axis=0).astype(np.float32)
```

```python
import numpy as np
import concourse.bass as bass
import concourse.mybir as mybir
from contextlib import ExitStack
from concourse.bass_utils import run_bass_kernel_spmd

F32 = mybir.dt.float32
BF16 = mybir.dt.bfloat16
I32 = mybir.dt.int32
ALU = mybir.AluOpType
AF = mybir.ActivationFunctionType
AX = mybir.AxisListType

ENGS = ("tensor", "vector", "scalar", "gpsimd", "sync")


class _Op:
    __slots__ = ("eng", "fn", "deps", "idx", "dma_key", "signal", "semval", "waits")

    def __init__(self, eng, fn, dma_key):
        self.eng = eng
        self.fn = fn
        self.deps = []
        self.dma_key = dma_key
        self.signal = False
        self.semval = None
        self.waits = []


class Prog:
    def __init__(self, nc):
        self.nc = nc
        self.ops = {e: [] for e in ENGS}
        self.last_w = {}
        self.readers = {}
        self.all_ops = []
        self.bar = {e: None for e in ENGS}
        self.last_dma = {}

    def barrier(self):
        lasts = [self.ops[e][-1] for e in ENGS if self.ops[e]] + list(self.last_dma.values())
        for e in ENGS:
            self.bar[e] = list(lasts) + (self.bar[e] or [])
        self.last_w = {}
        self.readers = {}

    def op(self, eng, fn, reads=(), writes=(), dma_key=None):
        o = _Op(eng, fn, dma_key)
        deps = []
        for r in reads:
            w = self.last_w.get(r)
            if w is not None:
                deps.append(w)
        for w_ in writes:
            w = self.last_w.get(w_)
            if w is not None:
                deps.append(w)
            deps.extend(self.readers.get(w_, ()))
        if self.bar[eng] is not None:
            deps.extend(self.bar[eng])
            self.bar[eng] = None
        if dma_key is not None:
            self.last_dma[dma_key] = o
        o.deps = deps
        o.idx = len(self.ops[eng])
        self.ops[eng].append(o)
        self.all_ops.append(o)
        for r in reads:
            self.readers.setdefault(r, []).append(o)
        for w_ in writes:
            self.last_w[w_] = o
            self.readers[w_] = []
        return o

    def dma(self, eng, out, in_, reads, writes, key, **kw):
        def fn(e, out=out, in_=in_):
            return e.dma_start(out=out, in_=in_, **kw)
        return self.op(eng, fn, reads, writes, dma_key=key)

    def emit(self, extra_final_wait=True):
        nc = self.nc
        dma_count = {}
        for o in self.all_ops:
            if o.dma_key is not None:
                dma_count[o.dma_key] = dma_count.get(o.dma_key, 0) + 1
                o.semval = 16 * dma_count[o.dma_key]
        seen = {e: {} for e in ENGS}
        for e in ENGS:
            for o in self.ops[e]:
                need = {}
                for d in o.deps:
                    if d.dma_key is not None:
                        k = ("D", d.dma_key)
                        v = d.semval
                    else:
                        if d.eng == e and (d.idx >= o.idx or e == "tensor"):
                            continue
                        k = ("E", d.eng)
                        v = d.idx + 1
                    if v > need.get(k, (0, None))[0]:
                        need[k] = (v, d)
                for k, (v, d) in need.items():
                    if seen[e].get(k, 0) >= v:
                        continue
                    seen[e][k] = v
                    o.waits.append(d)
                    d.signal = True
        for e in ENGS:
            c = 0
            for o in self.ops[e]:
                if o.dma_key is None and o.signal:
                    c += 1
                    o.semval = c
        keys = sorted(dma_count.keys())
        with ExitStack() as st:
            if getattr(self, "persist_sems", False):
                _PH[0] += 1
                esem = {e: nc.alloc_semaphore(name=f"s{_PH[0]}_" + e) for e in ENGS}
                dsem = {k: nc.alloc_semaphore(name=f"d{_PH[0]}_" + str(k)) for k in keys}
            else:
                esem = {e: st.enter_context(nc.semaphore("s_" + e)) for e in ENGS}
                dsem = {k: st.enter_context(nc.semaphore("d_" + str(k))) for k in keys}
            block = st.enter_context(nc.Block())

            def mk(e):
                def body(eng):
                    for o in self.ops[e]:
                        for d in o.waits:
                            if d.dma_key is not None:
                                eng.wait_ge(dsem[d.dma_key], d.semval)
                            else:
                                eng.wait_ge(esem[d.eng], d.semval)
                        ins = o.fn(eng)
                        if ins is None:
                            continue
                        if o.dma_key is not None:
                            ins.then_inc(dsem[o.dma_key], 16)
                        elif o.signal:
                            ins.then_inc(esem[e], 1)
                return body

            for e in ENGS:
                if self.ops[e]:
                    getattr(block, e)(mk(e))
        return {e: len(self.ops[e]) for e in ENGS}


D = 1024
DFF = 2816
EPS = 1e-5
NH_ROT = 39
NV = 15
INW = 3456
PI = float(np.pi)
ROPE_THETA = 500000.0


_PH = [0]


def _alloc(nc, st):
    _PH[0] += 1
    pre = f"p{_PH[0]}_"
    sb = lambda name, shape, dt: st.enter_context(nc.sbuf_tensor(pre + name, shape, dt))
    ps = lambda name, shape, dt: st.enter_context(nc.psum_tensor(pre + name, shape, dt))
    return sb, ps


def emit_consts(P, nc, sb):
    c = {}
    c["identf"] = sb("identf", [128, 128], F32)
    c["ident"] = sb("ident", [128, 128], BF16)
    c["epsb"] = sb("epsb", [128, 1], F32)
    identf, ident, epsb = c["identf"], c["ident"], c["epsb"]
    P.op("gpsimd", lambda e: e.memset(identf[:], 0.0), [], ["identf"])
    P.op("gpsimd", lambda e: e.affine_select(out=identf[:], in_=identf[:], pattern=[[-1, 128]], compare_op=ALU.not_equal,
                                             fill=1.0, base=0, channel_multiplier=1), ["identf"], ["identf"])
    P.op("vector", lambda e: e.tensor_copy(out=ident[:], in_=identf[:]), ["identf"], ["ident"])
    P.op("vector", lambda e: e.memset(epsb[:], EPS), [], ["epsb"])
    return c


def emit_norm_T(P, c, x, xn, ss, rstd, hb, ptr, hT_dst, hT_name, b, eps_name="epsb"):
    epsb, ident = c["epsb"], c["ident"]
    P.op("scalar", lambda e: e.activation(out=hb[:], in_=x[:], func=AF.Square, accum_out=ss[:]), [xn], [f"hb{b}", f"ss{b}"])
    P.op("scalar", lambda e: e.activation(out=rstd[:], in_=ss[:], func=AF.Sqrt, bias=epsb[:], scale=1.0 / D), [f"ss{b}", "epsb"], [f"rstd{b}"])
    P.op("vector", lambda e: e.reciprocal(out=rstd[:], in_=rstd[:]), [f"rstd{b}"], [f"rstd{b}"])
    P.op("vector", lambda e: e.tensor_scalar(out=hb[:], in0=x[:], scalar1=rstd[:], scalar2=None, op0=ALU.mult), [xn, f"rstd{b}"], [f"hb{b}"])
    for cc in range(8):
        P.op("tensor", lambda e, cc=cc: e.transpose(out=ptr[:, cc, :], in_=hb[:, cc * 128:(cc + 1) * 128], identity=ident[:]),
             [f"hb{b}", "ident"], ["ptr"])
    P.op("scalar", lambda e: e.copy(out=hT_dst, in_=ptr[:]), ["ptr"], [hT_name])


def load_weight_bf16(P, nc, w_rows, wb_dst_fn, ncols, stg, gcol_fn, names_fn, k0=0, piece=1408):
    k = k0
    for c, wr in enumerate(w_rows):
        for h0 in range(0, ncols, piece):
            h1 = min(ncols, h0 + piece)
            s = stg[k % 2]
            sn = f"stg{k % 2}"
            P.dma("sync", s[:, 0:h1 - h0], wr[:, h0:h1], [], [sn], sn)
            eng = "vector" if k % 2 == 0 else "gpsimd"
            g = gcol_fn(c) if gcol_fn is not None else None
            if g is not None:
                P.op(eng, lambda e, s=s, c=c, h0=h0, h1=h1, g=g: e.tensor_scalar(out=wb_dst_fn(c)[:, h0:h1], in0=s[:, 0:h1 - h0], scalar1=g, scalar2=None, op0=ALU.mult),
                     [sn, "gcol"], [names_fn(c)])
            else:
                P.op(eng, lambda e, s=s, c=c, h0=h0, h1=h1: e.tensor_copy(out=wb_dst_fn(c)[:, h0:h1], in_=s[:, 0:h1 - h0]), [sn], [names_fn(c)])
            k += 1
    return k


def emit_ffn(P, nc, st, x_in, gvec, wg, wu, wd, x_out, T, TG=256, final_g=None):
    NJ = TG // 128
    NF = DFF // 128
    sb, ps = _alloc(nc, st)
    c = emit_consts(P, nc, sb)
    wgb = sb("wgb", [128, 8, DFF], BF16)
    wub = sb("wub", [128, 8, DFF], BF16)
    wdb = sb("wdb", [128, NF, D], BF16)
    stg = [sb(f"stg{i}", [128, 1408], F32) for i in range(2)]
    gcol = sb("gcol", [128, 8], F32)
    P.dma("sync", gcol[:], gvec.rearrange("(c p) -> p c", p=128), [], ["gcol"], "gcol", allow_slow_non_contiguous=True)
    k = load_weight_bf16(P, nc, [wg[cc * 128:(cc + 1) * 128, :] for cc in range(8)], lambda cc: wgb[:, cc, :], DFF, stg,
                         lambda cc: gcol[:, cc:cc + 1], lambda cc: f"wgb{cc}")
    k = load_weight_bf16(P, nc, [wu[cc * 128:(cc + 1) * 128, :] for cc in range(8)], lambda cc: wub[:, cc, :], DFF, stg,
                         lambda cc: gcol[:, cc:cc + 1], lambda cc: f"wub{cc}", k0=k)
    k = load_weight_bf16(P, nc, [wd[f * 128:(f + 1) * 128, :] for f in range(NF)], lambda f: wdb[:, f, :], D, stg,
                         None, lambda f: f"wdb{f}", k0=k)
    if final_g is not None:
        gfin = sb("gfin", [128, D], F32)
        P.dma("sync", gfin[:], final_g.partition_broadcast(128), [], ["gfin"], "gfin")

    xt = [sb(f"xt{j}", [128, D], F32) for j in range(2 * NJ)]
    hb = [sb(f"hb{j}", [128, D], BF16) for j in range(2)]
    ss = [sb(f"ss{j}", [128, 1], F32) for j in range(2)]
    rstd = [sb(f"rstd{j}", [128, 1], F32) for j in range(2)]
    hT = [sb(f"hT{i}", [128, 8, TG], BF16) for i in range(2)]
    act = sb("act", [128, NF, TG], BF16)
    sg = [sb(f"sg{i}", [128, TG], F32) for i in range(2)]
    ptr = ps("ptr", [128, 8, 128], BF16)
    pg = [ps(f"pg{i}", [128, 512], F32) for i in range(2)]
    pu = [ps(f"pu{i}", [128, 512], F32) for i in range(2)]
    po = [ps(f"po{i}", [128, 512], F32) for i in range(2)]

    ng = T // TG
    cnt = fcnt = ocnt = 0
    outs = []
    for g in range(ng):
        hTg = hT[g % 2]
        hTn = f"hT{g % 2}"
        for j in range(NJ):
            xi = (g % 2) * NJ + j
            x, xn = xt[xi], f"xt{xi}"
            t0 = g * TG + j * 128
            P.dma("sync", x[:], x_in[t0:t0 + 128, :], [], [xn], xn)
            b = cnt % 2
            cnt += 1
            emit_norm_T(P, c, x, xn, ss[b], rstd[b], hb[b], ptr, hTg[:, :, j * 128:(j + 1) * 128], hTn + f"_{j}", b)
        hreads = [hTn + f"_{j}" for j in range(NJ)]
        for f in range(NF):
            b = fcnt % 2
            fcnt += 1
            for cc in range(8):
                P.op("tensor", lambda e, b=b, cc=cc, f=f, hTg=hTg: e.matmul(out=pg[b][:, 0:TG], lhsT=wgb[:, cc, f * 128:(f + 1) * 128], rhs=hTg[:, cc, :], start=(cc == 0), stop=(cc == 7)),
                     hreads + [f"wgb{cc}"], [f"pg{b}"])
            for cc in range(8):
                P.op("tensor", lambda e, b=b, cc=cc, f=f, hTg=hTg: e.matmul(out=pu[b][:, 0:TG], lhsT=wub[:, cc, f * 128:(f + 1) * 128], rhs=hTg[:, cc, :], start=(cc == 0), stop=(cc == 7)),
                     hreads + [f"wub{cc}"], [f"pu{b}"])
            P.op("scalar", lambda e, b=b: e.activation(out=sg[b][:], in_=pg[b][:, 0:TG], func=AF.Silu), [f"pg{b}"], [f"sg{b}"])
            P.op("vector", lambda e, b=b, f=f: e.tensor_tensor(out=act[:, f, :], in0=sg[b][:], in1=pu[b][:, 0:TG], op=ALU.mult), [f"sg{b}", f"pu{b}"], [f"act{f}"])
        for j in range(NJ):
            xi = (g % 2) * NJ + j
            x, xn = xt[xi], f"xt{xi}"
            t0 = g * TG + j * 128
            for n in range(2):
                b = ocnt % 2
                ocnt += 1
                for f in range(NF):
                    P.op("tensor", lambda e, b=b, f=f, j=j, n=n: e.matmul(out=po[b][:], lhsT=act[:, f, j * 128:(j + 1) * 128], rhs=wdb[:, f, n * 512:(n + 1) * 512], start=(f == 0), stop=(f == NF - 1)),
                         [f"act{f}", f"wdb{f}"], [f"po{b}"])
                P.op("vector", lambda e, b=b, x=x, n=n: e.tensor_tensor(out=x[:, n * 512:(n + 1) * 512], in0=po[b][:], in1=x[:, n * 512:(n + 1) * 512], op=ALU.add), [f"po{b}", xn], [xn])
            if final_g is not None:
                b = cnt % 2
                cnt += 1
                P.op("scalar", lambda e, x=x, b=b: e.activation(out=hb[b][:], in_=x[:], func=AF.Square, accum_out=ss[b][:]), [xn], [f"hb{b}", f"ss{b}"])
                P.op("scalar", lambda e, b=b: e.activation(out=rstd[b][:], in_=ss[b][:], func=AF.Sqrt, bias=c["epsb"][:], scale=1.0 / D), [f"ss{b}", "epsb"], [f"rstd{b}"])
                P.op("vector", lambda e, b=b: e.reciprocal(out=rstd[b][:], in_=rstd[b][:]), [f"rstd{b}"], [f"rstd{b}"])
                P.op("vector", lambda e, x=x, b=b: e.scalar_tensor_tensor(out=x[:], in0=x[:], scalar=rstd[b][:], in1=gfin[:], op0=ALU.mult, op1=ALU.mult), [xn, f"rstd{b}", "gfin"], [xn])
            P.dma("gpsimd", x_out[t0:t0 + 128, :], x[:], [xn], [f"out_{t0}"], xn + "o")
            outs.append(f"out_{t0}")
    P.op("sync", lambda e: None, outs, [])
    P.op("gpsimd", lambda e: None, outs, [])


def emit_attn_proj(P, nc, st, x_in, pos, gvec, w_in, b_in, qkT_out, v_out, T):
    sb, ps = _alloc(nc, st)
    c = emit_consts(P, nc, sb)
    NT = T // 128
    winb = sb("winb", [128, 8, INW], BF16)
    stg = [sb(f"stg{i}", [128, 1728], F32) for i in range(2)]
    gcol = sb("gcol", [128, 8], F32)
    P.dma("sync", gcol[:], gvec.rearrange("(c p) -> p c", p=128), [], ["gcol"], "gcol", allow_slow_non_contiguous=True)
    load_weight_bf16(P, nc, [w_in[cc * 128:(cc + 1) * 128, :] for cc in range(8)], lambda cc: winb[:, cc, :], INW, stg,
                     lambda cc: gcol[:, cc:cc + 1], lambda cc: f"winb{cc}", piece=1728)
    biasb = sb("biasb", [1, INW], BF16)
    ones1 = sb("ones1", [1, 128], BF16)
    b2 = b_in.rearrange("(o n) -> o n", o=1)
    for hi in range(2):
        P.dma("sync", stg[hi][0:1, :], b2[:, hi * 1728:(hi + 1) * 1728], [], [f"stg{hi}"], f"stg{hi}")
        P.op("vector", lambda e, hi=hi: e.tensor_copy(out=biasb[:, hi * 1728:(hi + 1) * 1728], in_=stg[hi][0:1, :]), [f"stg{hi}"], ["biasb"])
    P.op("vector", lambda e: e.memset(ones1[:], 1.0), [], ["ones1"])
    posi = sb("posi", [128, NT], I32)
    posf = sb("posf", [128, NT], F32)
    ftab = sb("ftab", [128, 32], F32)
    otab = sb("otab", [128, 32], F32)
    ang = sb("ang", [128, NT, 32], F32)
    cs = sb("cs", [128, NT, 32], F32)
    qkv = [sb(f"qkv{i}", [128, INW], F32) for i in range(2)]
    ang2 = qkv[0][:, 0:NT * 32].rearrange("p (n f) -> p n f", f=32)
    angi = qkv[1][:, 0:NT * 32].bitcast(I32).rearrange("p (n f) -> p n f", f=32)
    P.dma("sync", posi[:], pos.rearrange("(n p) -> p n", p=128), [], ["posi"], "posi", allow_slow_non_contiguous=True)
    P.op("vector", lambda e: e.tensor_copy(out=posf[:], in_=posi[:]), ["posi"], ["posf"])
    ft3 = ftab[:].rearrange("p (a i) -> p a i", i=8)
    for i in range(8):
        fi = float(np.power(np.float32(ROPE_THETA), np.float32(-2.0 * i / 16.0)))
        P.op("gpsimd", lambda e, i=i, fi=fi: e.memset(ft3[:, :, i], fi), [], ["ftab"])
    for a, off in enumerate((PI / 2, PI / 2, PI, 0.0)):
        P.op("gpsimd", lambda e, a=a, off=off: e.memset(otab[:, a * 8:(a + 1) * 8], off), [], ["otab"])
    P.op("vector", lambda e: e.tensor_tensor(out=ang[:], in0=ftab[:].unsqueeze(1).broadcast_to([128, NT, 32]),
                                             in1=posf[:].unsqueeze(2).broadcast_to([128, NT, 32]), op=ALU.mult), ["ftab", "posf"], ["ang"])
    P.op("vector", lambda e: e.tensor_tensor(out=ang[:], in0=ang[:], in1=otab[:].unsqueeze(1).broadcast_to([128, NT, 32]), op=ALU.add), ["ang", "otab"], ["ang"])
    P.op("vector", lambda e: e.tensor_scalar(out=angi[:], in0=ang[:], scalar1=1.0 / (2 * PI), scalar2=None, op0=ALU.mult), ["ang"], ["angi"])
    P.op("vector", lambda e: e.tensor_copy(out=ang2[:], in_=angi[:]), ["angi"], ["ang2"])
    P.op("vector", lambda e: e.scalar_tensor_tensor(out=ang[:], in0=ang2[:], scalar=-2 * PI, in1=ang[:], op0=ALU.mult, op1=ALU.add), ["ang2", "ang"], ["ang"])
    P.op("vector", lambda e: e.tensor_scalar(out=ang2[:], in0=ang[:], scalar1=PI, scalar2=-2 * PI, op0=ALU.is_gt, op1=ALU.mult), ["ang"], ["ang2"])
    P.op("vector", lambda e: e.tensor_tensor(out=ang[:], in0=ang[:], in1=ang2[:], op=ALU.add), ["ang", "ang2"], ["ang"])
    P.op("vector", lambda e: e.tensor_scalar(out=ang2[:], in0=ang[:], scalar1=-PI, scalar2=2 * PI, op0=ALU.is_lt, op1=ALU.mult), ["ang"], ["ang2"])
    P.op("vector", lambda e: e.tensor_tensor(out=ang[:], in0=ang[:], in1=ang2[:], op=ALU.add), ["ang", "ang2"], ["ang"])
    P.op("scalar", lambda e: e.activation(out=cs[:], in_=ang[:], func=AF.Sin), ["ang"], ["cs"])

    xt = [sb(f"xt{j}", [128, D], F32) for j in range(2)]
    hb = [sb(f"hb{j}", [128, D], BF16) for j in range(2)]
    ss = [sb(f"ss{j}", [128, 1], F32) for j in range(2)]
    rstd = [sb(f"rstd{j}", [128, 1], F32) for j in range(2)]
    hT = [sb(f"hT{i}", [128, 8, 128], BF16) for i in range(2)]
    P.barrier()
    tA = sb("tA", [128, NH_ROT, 16], F32)
    tB = sb("tB", [128, NH_ROT, 16], F32)
    qkb = [sb(f"qkb{i}", [128, NH_ROT * 64], BF16) for i in range(2)]
    vb = [sb(f"vb{i}", [128, NV * 64], BF16) for i in range(2)]
    qkTs = [sb(f"qkTs{i}", [64, NH_ROT, 256], BF16) for i in range(2)]
    ptr = ps("ptr", [128, 8, 128], BF16)
    pp = [ps(f"pp{i}", [128, 512], F32) for i in range(2)]
    ptq = [ps(f"ptq{i}", [64, 8, 128], BF16) for i in range(2)]
    ident = c["ident"]
    pcnt = 0
    qcnt = 0
    outs = []
    for t in range(NT):
        b = t % 2
        x, xn = xt[b], f"xt{b}"
        t0 = t * 128
        P.dma("sync", x[:], x_in[t0:t0 + 128, :], [], [xn], xn)
        emit_norm_T(P, c, x, xn, ss[b], rstd[b], hb[b], ptr, hT[b][:], f"hT{b}", b)
        for n0 in range(0, INW, 512):
            n1 = min(INW, n0 + 512)
            pb = pcnt % 2
            pcnt += 1
            for cc in range(8):
                P.op("tensor", lambda e, pb=pb, cc=cc, n0=n0, n1=n1, b=b: e.matmul(out=pp[pb][:, 0:n1 - n0], lhsT=hT[b][:, cc, :], rhs=winb[:, cc, n0:n1], start=(cc == 0), stop=False),
                     [f"hT{b}", f"winb{cc}"], [f"pp{pb}"])
            P.op("tensor", lambda e, pb=pb, n0=n0, n1=n1: e.matmul(out=pp[pb][:, 0:n1 - n0], lhsT=ones1[:], rhs=biasb[:, n0:n1], start=False, stop=True),
                 ["ones1", "biasb"], [f"pp{pb}"])
            eng = "scalar" if (pcnt % 2 == 0) else "vector"
            if eng == "scalar":
                P.op("scalar", lambda e, pb=pb, n0=n0, n1=n1, b=b: e.copy(out=qkv[b][:, n0:n1], in_=pp[pb][:, 0:n1 - n0]), [f"pp{pb}"], [f"qkv{b}"])
            else:
                P.op("vector", lambda e, pb=pb, n0=n0, n1=n1, b=b: e.tensor_copy(out=qkv[b][:, n0:n1], in_=pp[pb][:, 0:n1 - n0]), [f"pp{pb}"], [f"qkv{b}"])
        Q3 = qkv[b][:, 0:NH_ROT * 64].rearrange("p (h d) -> p h d", d=64)
        O3 = qkb[b][:].rearrange("p (h d) -> p h d", d=64)
        cst = cs[:, t, :]
        P.op("gpsimd", lambda e, Q3=Q3, cst=cst: e.tensor_tensor(out=tA[:], in0=Q3[:, :, 0:16], in1=cst[:, 0:16].unsqueeze(1).broadcast_to([128, NH_ROT, 16]), op=ALU.mult), [f"qkv{b}", "cs"], ["tA"])
        P.op("vector", lambda e, Q3=Q3, cst=cst: e.tensor_tensor(out=tB[:, :, 0:8], in0=Q3[:, :, 8:16], in1=cst[:, 16:24].unsqueeze(1).broadcast_to([128, NH_ROT, 8]), op=ALU.mult), [f"qkv{b}", "cs"], ["tB0"])
        P.op("vector", lambda e, Q3=Q3, cst=cst: e.tensor_tensor(out=tB[:, :, 8:16], in0=Q3[:, :, 0:8], in1=cst[:, 24:32].unsqueeze(1).broadcast_to([128, NH_ROT, 8]), op=ALU.mult), [f"qkv{b}", "cs"], ["tB1"])
        P.op("vector", lambda e, O3=O3: e.tensor_tensor(out=O3[:, :, 0:16], in0=tA[:], in1=tB[:], op=ALU.add), ["tA", "tB0", "tB1"], [f"qkb{b}r"])
        P.op("scalar", lambda e, O3=O3, Q3=Q3: e.copy(out=O3[:, :, 16:64], in_=Q3[:, :, 16:64]), [f"qkv{b}"], [f"qkb{b}c"])
        P.op("gpsimd", lambda e, b=b: e.tensor_copy(out=vb[b][:], in_=qkv[b][:, NH_ROT * 64:INW]), [f"qkv{b}"], [f"vb{b}"])
        P.dma("gpsimd", v_out[t0:t0 + 128, :], vb[b][:], [f"vb{b}"], [f"vout{t}"], f"vb{b}o")
        outs.append(f"vout{t}")
        sbi = (t // 2) % 2
        so = (t % 2) * 128
        for h0 in range(0, NH_ROT, 8):
            h1 = min(NH_ROT, h0 + 8)
            qb_ = qcnt % 2
            qcnt += 1
            for h in range(h0, h1):
                P.op("tensor", lambda e, qb_=qb_, h=h, h0=h0, b=b: e.transpose(out=ptq[qb_][:, h - h0, :], in_=qkb[b][:, h * 64:(h + 1) * 64], identity=ident[:]),
                     [f"qkb{b}r", f"qkb{b}c", "ident"], [f"ptq{qb_}"])
            eng = "scalar" if (qcnt % 2 == 0) else "vector"
            if eng == "scalar":
                P.op("scalar", lambda e, qb_=qb_, h0=h0, h1=h1, sbi=sbi, so=so: e.copy(out=qkTs[sbi][:, h0:h1, so:so + 128], in_=ptq[qb_][:, 0:h1 - h0, :]), [f"ptq{qb_}"], [f"qkTs{sbi}_{t % 2}_{h0}"])
            else:
                P.op("vector", lambda e, qb_=qb_, h0=h0, h1=h1, sbi=sbi, so=so: e.tensor_copy(out=qkTs[sbi][:, h0:h1, so:so + 128], in_=ptq[qb_][:, 0:h1 - h0, :]), [f"ptq{qb_}"], [f"qkTs{sbi}_{t % 2}_{h0}"])
        if t % 2 == 1:
            tt0 = (t - 1) * 128
            rd = [f"qkTs{sbi}_{u}_{h0}" for u in range(2) for h0 in range(0, NH_ROT, 8)]
            for qi, (ha, hb_) in enumerate(((0, 13), (13, 26), (26, 39))):
                P.dma("sync", qkT_out[ha:hb_, :, tt0:tt0 + 256].rearrange("h d t -> d h t"), qkTs[sbi][:, ha:hb_, :], rd, [f"qout{t}_{qi}"], f"qkTs{sbi}o{qi}")
                outs.append(f"qout{t}_{qi}")
    P.op("sync", lambda e: None, outs, [])
    P.op("gpsimd", lambda e: None, outs, [])


def emit_attn_core(P, nc, st, qkT, v, sinks, oT_out, S):
    sb, ps = _alloc(nc, st)
    CH = 2048
    NCH = S // CH
    ones64 = sb("ones64", [128, 64], BF16)
    mfull = sb("mfull", [128, 4, 128], F32)
    m_own = sb("m_own", [128, 4, 128], BF16)
    m_pA = sb("m_pA", [128, 4, 128], BF16)
    m_pB = sb("m_pB", [128, 4, 128], BF16)
    P.op("vector", lambda e: e.memset(ones64[:], 1.0), [], ["ones64"])
    for (m, cm, pat, cmp_, nm) in ((m_own, -1, 1, ALU.is_ge, "m_own"), (m_pA, 1, -1, ALU.is_gt, "m_pA"), (m_pB, 1, -1, ALU.is_ge, "m_pB")):
        P.op("gpsimd", lambda e: e.memset(mfull[:], 1.0), [], ["mfull"])
        P.op("gpsimd", lambda e, cm=cm, pat=pat, cmp_=cmp_: e.affine_select(out=mfull[:], in_=mfull[:], pattern=[[0, 4], [pat, 128]], compare_op=cmp_,
                                                                           fill=0.0, base=0, channel_multiplier=cm), ["mfull"], ["mfull"])
        P.op("gpsimd", lambda e, m=m: e.tensor_copy(out=m[:], in_=mfull[:]), ["mfull"], [nm])
    sk = sb("sk", [64, 12], F32)
    esink = sb("esink", [64, 12, 128], F32)
    P.dma("sync", sk[:], sinks.partition_broadcast(64), [], ["sk"], "sk")
    P.op("scalar", lambda e: e.activation(out=sk[:], in_=sk[:], func=AF.Exp), ["sk"], ["sk"])
    P.op("vector", lambda e: e.memset(esink[:], 0.0), [], ["esink"])
    for h in range(12):
        P.op("vector", lambda e, h=h: e.tensor_scalar(out=esink[:, h, :], in0=esink[:, h, :], scalar1=sk[:, h:h + 1], scalar2=None, op0=ALU.add), ["esink", "sk"], ["esink"])

    qt = sb("qt", [64, 4, CH], BF16)
    kt = sb("kt", [64, 4, 2 * CH], BF16)
    vt = sb("vt", [128, 32, 256], BF16)
    vA = sb("vA", [128, 17, 64], BF16)
    accn = sb("accn", [64, 4, CH], F32)
    accd = sb("accd", [64, 4, CH], F32)
    pT = [sb(f"pT{i}", [128, 4, 128], BF16) for i in range(4)]
    tmp = [sb(f"tmp{i}", [64, 4, 128], F32) for i in range(2)]
    oA = [sb(f"oA{i}", [64, 4, 128], BF16) for i in range(2)]
    oB = [sb(f"oB{i}", [64, 4, 512], BF16) for i in range(2)]
    rdB = sb("rdB", [64, 4, 512], F32)
    psc = [ps(f"psc{i}", [128, 4, 128], F32) for i in range(4)]
    pnum = [ps(f"pnum{i}", [64, 4, 128], F32) for i in range(2)]
    pden = [ps(f"pden{i}", [64, 4, 128], F32) for i in range(2)]
    outs = []
    ucnt = 0

    def unit(kind, g, c, first, score_fn, pv_fn, out_fn):
        nonlocal ucnt
        u = ucnt % 2
        ucnt += 1
        kbs = (0,) if first else (0, 1)
        for kb in kbs:
            si = 2 * u + kb
            score_fn(kb, psc[si], f"psc{si}")
            P.op("scalar", lambda e, si=si: e.activation(out=pT[si][:], in_=psc[si][:], func=AF.Exp, scale=0.125), [f"psc{si}"], [f"pT{si}"])
            m, mn = (m_own, "m_own") if kb == 0 else ((m_pA, "m_pA") if kind == "A" else (m_pB, "m_pB"))
            P.op("vector", lambda e, si=si, m=m: e.tensor_tensor(out=pT[si][:], in0=pT[si][:], in1=m[:], op=ALU.mult), [f"pT{si}", mn], [f"pT{si}"])
        pv_fn(kbs, [pT[2 * u + kb] for kb in kbs], [f"pT{2 * u + kb}" for kb in kbs], pnum[u], f"pnum{u}")
        for i, kb in enumerate(kbs):
            P.op("tensor", lambda e, u=u, kb=kb, i=i: e.matmul(out=pden[u][:], lhsT=ones64[:], rhs=pT[2 * u + kb][:], start=(i == 0), stop=(i == len(kbs) - 1)),
                 [f"pT{2 * u + kb}", "ones64"], [f"pden{u}"])
        out_fn(u)

    for c in range(NCH):
        c0 = c * CH
        for g in range(3):
            P.dma("sync", qt[:], qkT[4 * g:4 * g + 4, :, c0:c0 + CH].rearrange("h d t -> d h t"), [], ["qt"], "qt")
            if c == 0:
                P.dma("sync", kt[:, 0, CH:2 * CH], qkT[12 + g, :, c0:c0 + CH], [], ["kt"], "kt")
                for n0 in (0, 8):
                    P.dma("sync", vA[:, 1 + n0:9 + n0, :], v[c0 + n0 * 128:c0 + (n0 + 8) * 128, g * 64:(g + 1) * 64].rearrange("(n p) d -> p n d", p=128), [], ["vA"], f"vA{n0}")
            else:
                P.dma("sync", kt[:, 0, CH - 128:2 * CH], qkT[12 + g, :, c0 - 128:c0 + CH], [], ["kt"], "kt")
                for n0, n1 in ((0, 9), (9, 17)):
                    P.dma("sync", vA[:, n0:n1, :], v[c0 - 128 + n0 * 128:c0 - 128 + n1 * 128, g * 64:(g + 1) * 64].rearrange("(n p) d -> p n d", p=128), [], ["vA"], f"vA{n0}")
            for i in range(16):
                first = (c == 0 and i == 0)

                def score_fn(kb, pt, pn, i=i):
                    k0 = CH + (i - kb) * 128
                    P.op("tensor", lambda e: e.matmul(out=pt[:], lhsT=kt[:, 0, k0:k0 + 128], rhs=qt[:, :, i * 128:(i + 1) * 128], start=True, stop=True),
                         ["qt", "kt"], [pn])

                def pv_fn(kbs, pts, ptn, pn_, pnn, i=i):
                    for ii, kb in enumerate(kbs):
                        P.op("tensor", lambda e, ii=ii, kb=kb: e.matmul(out=pn_[:], lhsT=vA[:, i + 1 - kb, :], rhs=pts[ii][:], start=(ii == 0), stop=(ii == len(kbs) - 1)),
                             [ptn[ii], "vA"], [pnn])

                def out_fn(u, i=i, g=g, c0=c0):
                    P.op("vector", lambda e: e.tensor_tensor(out=tmp[u][:], in0=pden[u][:], in1=esink[:, 4 * g:4 * g + 4, :], op=ALU.add), [f"pden{u}", "esink"], [f"tmp{u}"])
                    P.op("vector", lambda e: e.reciprocal(out=tmp[u][:], in_=tmp[u][:]), [f"tmp{u}"], [f"tmp{u}"])
                    P.op("vector", lambda e: e.tensor_tensor(out=oA[u][:], in0=pnum[u][:], in1=tmp[u][:], op=ALU.mult), [f"pnum{u}", f"tmp{u}"], [f"oA{u}"])
                    nm = f"oo_{c0}_{g}_{i}"
                    P.dma("gpsimd", oT_out[4 * g:4 * g + 4, :, c0 + i * 128:c0 + (i + 1) * 128].rearrange("h d t -> d h t"), oA[u][:], [f"oA{u}"], [nm], f"oA{u}o")
                    outs.append(nm)

                unit("A", g, c, first, score_fn, pv_fn, out_fn)
        for p, d in enumerate((1, 4, 16)):
            halo = 128 * d
            NIq = CH // (128 * d)
            P.dma("sync", qt[:], qkT[15 + 4 * p:19 + 4 * p, :, c0:c0 + CH].rearrange("h d t -> d h t"), [], ["qt"], "qt")
            lo = 0 if c == 0 else halo
            P.dma("sync", kt[:, :, CH - lo:2 * CH], qkT[27 + 4 * p:31 + 4 * p, :, c0 - lo:c0 + CH].rearrange("h d t -> d h t"), [], ["kt"], "kt")
            vt4 = vt[:, 0:(NIq + 1) * d, :].rearrange("j (I r) f -> j I r f", r=d)
            fcol = 192 + p * 256
            for I in range(0 if c > 0 else 1, NIq + 1):
                tok0 = c0 - halo + I * 128 * d
                for r0 in range(0, d, 8):
                    r1 = min(d, r0 + 8)
                    P.dma("sync", vt4[:, I, r0:r1, :], v[tok0:tok0 + 128 * d, fcol:fcol + 256].rearrange("(j r) f -> j r f", r=d)[:, r0:r1, :], [], [f"vt"], f"vt{(I + r0 // 8) % 8}")
            qv = [qt[:, h, :].rearrange("p (I j r) -> p I j r", j=128, r=d) for h in range(4)]
            kv = [kt[:, h, CH - halo:2 * CH].rearrange("p (I j r) -> p I j r", j=128, r=d) for h in range(4)]
            accn4 = [accn[:, h, :].rearrange("p (I j r) -> p I j r", j=128, r=d) for h in range(4)]
            accd4 = [accd[:, h, :].rearrange("p (I j r) -> p I j r", j=128, r=d) for h in range(4)]
            for Iq in range(NIq):
                for r in range(d):
                    first = (c == 0 and Iq == 0)

                    def score_fn(kb, pt, pn, Iq=Iq, r=r, kv=kv, qv=qv):
                        for h in range(4):
                            P.op("tensor", lambda e, h=h: e.matmul(out=pt[:, h, :], lhsT=kv[h][:, Iq + 1 - kb, :, r], rhs=qv[h][:, Iq, :, r], start=True, stop=True),
                                 ["qt", "kt"], [pn])

                    def pv_fn(kbs, pts, ptn, pn_, pnn, Iq=Iq, r=r, vt4=vt4):
                        for h in range(4):
                            for ii, kb in enumerate(kbs):
                                P.op("tensor", lambda e, h=h, ii=ii, kb=kb: e.matmul(out=pn_[:, h, :], lhsT=vt4[:, Iq + 1 - kb, r, h * 64:(h + 1) * 64], rhs=pts[ii][:, h, :],
                                                                                     start=(ii == 0), stop=(ii == len(kbs) - 1)),
                                     [ptn[ii], "vt"], [pnn])

                    def out_fn(u, Iq=Iq, r=r, p=p, d=d):
                        an = accn[:].rearrange("p h (I j r) -> p h I j r", j=128, r=d)[:, :, Iq, :, r]
                        ad = accd[:].rearrange("p h (I j r) -> p h I j r", j=128, r=d)[:, :, Iq, :, r]
                        if p == 0:
                            P.op("scalar", lambda e: e.copy(out=an, in_=pnum[u][:]), [f"pnum{u}"], ["accn"])
                            P.op("vector", lambda e: e.tensor_copy(out=ad, in_=pden[u][:]), [f"pden{u}"], ["accd"])
                        else:
                            P.op("vector", lambda e: e.tensor_tensor(out=an, in0=pnum[u][:], in1=an, op=ALU.add), [f"pnum{u}", "accn"], ["accn"])
                            P.op("vector", lambda e: e.tensor_tensor(out=ad, in0=pden[u][:], in1=ad, op=ALU.add), [f"pden{u}", "accd"], ["accd"])

                    unit("B", p, c, first, score_fn, pv_fn, out_fn)
        for q in range(CH // 512):
            u = q % 2
            P.op("vector", lambda e, q=q: e.reciprocal(out=rdB[:], in_=accd[:, :, q * 512:(q + 1) * 512]), ["accd"], ["rdB"])
            P.op("vector", lambda e, q=q, u=u: e.tensor_tensor(out=oB[u][:], in0=accn[:, :, q * 512:(q + 1) * 512], in1=rdB[:], op=ALU.mult), ["accn", "rdB"], [f"oB{u}"])
            nm = f"oob_{c0}_{q}"
            P.dma("gpsimd", oT_out[12:16, :, c0 + q * 512:c0 + (q + 1) * 512].rearrange("h d t -> d h t"), oB[u][:], [f"oB{u}"], [nm], f"oB{u}o")
            outs.append(nm)
    P.op("sync", lambda e: None, outs, [])
    P.op("gpsimd", lambda e: None, outs, [])


def emit_head_outproj(P, nc, st, x_in, oT, w_out, x_out, T, nheads=16):
    sb, ps = _alloc(nc, st)
    wob = sb("wob", [64, nheads, D], BF16)
    stg = [sb(f"stg{i}", [64, D], F32) for i in range(2)]
    w3 = w_out.rearrange("(h d) n -> d h n", d=64)
    for h in range(nheads):
        s, sn = stg[h % 2], f"stg{h % 2}"
        P.dma("sync", s[:], w3[:, h, :], [], [sn], sn)
        P.op("vector" if h % 2 == 0 else "gpsimd", lambda e, s=s, h=h: e.tensor_copy(out=wob[:, h, :], in_=s[:]), [sn], [f"wob{h}"])
    TG = 512
    oTt = [sb(f"oTt{i}", [64, nheads, TG], BF16) for i in range(2)]
    xt = [sb(f"xt{i}", [128, D], F32) for i in range(4)]
    po = [ps(f"po{i}", [128, 512], F32) for i in range(4)]
    outs = []
    cnt = 0
    ocnt = 0
    for g in range(T // TG):
        ob = g % 2
        for hh in range(0, nheads, 4):
            P.dma("sync", oTt[ob][:, hh:hh + 4, :], oT[hh:hh + 4, :, g * TG:(g + 1) * TG].rearrange("h d t -> d h t"), [], [f"oTt{ob}_{hh}"], f"oTt{ob}_{hh}")
        ords = [f"oTt{ob}_{hh}" for hh in range(0, nheads, 4)]
        for j in range(TG // 128):
            xi = cnt % 4
            cnt += 1
            x, xn = xt[xi], f"xt{xi}"
            t0 = g * TG + j * 128
            P.dma("sync", x[:], x_in[t0:t0 + 128, :], [], [xn], xn)
            for n in range(2):
                b = ocnt % 4
                ocnt += 1
                for h in range(nheads):
                    P.op("tensor", lambda e, b=b, h=h, j=j, n=n, ob=ob: e.matmul(out=po[b][:], lhsT=oTt[ob][:, h, j * 128:(j + 1) * 128], rhs=wob[:, h, n * 512:(n + 1) * 512],
                                                                              start=(h == 0), stop=(h == nheads - 1)), ords + [f"wob{h}"], [f"po{b}"])
                P.op("vector", lambda e, b=b, x=x, n=n: e.tensor_tensor(out=x[:, n * 512:(n + 1) * 512], in0=po[b][:], in1=x[:, n * 512:(n + 1) * 512], op=ALU.add), [f"po{b}", xn], [xn])
            P.dma("gpsimd", x_out[t0:t0 + 128, :], x[:], [xn], [f"out_{t0}"], xn + "o")
            outs.append(f"out_{t0}")
    P.op("sync", lambda e: None, outs, [])
    P.op("gpsimd", lambda e: None, outs, [])


NCORES = 8
_PERM = np.concatenate([np.arange(0, 768), np.arange(768, 960), np.arange(1152, 1920), np.arange(1920, 2688),
                        np.arange(960, 1152), np.arange(2688, 3456)])


def _run(nc, in_maps):
    res = run_bass_kernel_spmd(nc, in_maps, core_ids=list(range(NCORES)))
    return res.results


def _dram(nc, name, shape, dt, kind):
    return nc.dram_tensor(name, list(shape), dt, kind=kind).ap()


def launch_attn_proj(x, pos, gvec, w_in, b_in):
    N = x.shape[0]
    T = N // NCORES
    nc = bass.Bass("TRN2", target_bir_lowering=False)
    xa = _dram(nc, "x", [T, D], F32, "ExternalInput")
    pa = _dram(nc, "pos", [T], I32, "ExternalInput")
    ga = _dram(nc, "g", [D], F32, "ExternalInput")
    wa = _dram(nc, "w_in", [D, INW], F32, "ExternalInput")
    ba = _dram(nc, "b_in", [INW], F32, "ExternalInput")
    qo = _dram(nc, "qkT", [NH_ROT, 64, T], BF16, "ExternalOutput")
    vo = _dram(nc, "v", [T, NV * 64], BF16, "ExternalOutput")
    P = Prog(nc)
    with ExitStack() as st:
        emit_attn_proj(P, nc, st, xa, pa, ga, wa, ba, qo, vo, T)
        P.emit()
    wp = np.ascontiguousarray(w_in[:, _PERM])
    bp = np.ascontiguousarray(b_in[_PERM])
    maps = [{"x": x[i * T:(i + 1) * T], "pos": pos[i * T:(i + 1) * T], "g": gvec, "w_in": wp, "b_in": bp} for i in range(NCORES)]
    r = _run(nc, maps)
    return [ri["qkT"] for ri in r], [ri["v"] for ri in r]


def launch_attn_core(qkT_seq, v_seq, sinks, S):
    nc = bass.Bass("TRN2", target_bir_lowering=False)
    qa = _dram(nc, "qkT", [NH_ROT, 64, S], BF16, "ExternalInput")
    va = _dram(nc, "v", [S, NV * 64], BF16, "ExternalInput")
    sa = _dram(nc, "sinks", [12], F32, "ExternalInput")
    oo = _dram(nc, "oT", [16, 64, S], BF16, "ExternalOutput")
    P = Prog(nc)
    with ExitStack() as st:
        emit_attn_core(P, nc, st, qa, va, sa, oo, S)
        P.emit()
    nb = len(qkT_seq)
    maps = [{"qkT": qkT_seq[i % nb], "v": v_seq[i % nb], "sinks": sinks} for i in range(NCORES)]
    r = _run(nc, maps)
    return [r[i]["oT"] for i in range(nb)]


def launch_mix_ffn(x, oT_cores, w_out, gffn, wg, wu, wd, final_g=None):
    N = x.shape[0]
    T = N // NCORES
    nc = bass.Bass("TRN2", target_bir_lowering=False)
    xa = _dram(nc, "x", [T, D], F32, "ExternalInput")
    oa = _dram(nc, "oT", [16, 64, T], BF16, "ExternalInput")
    woa = _dram(nc, "w_out", [D, D], F32, "ExternalInput")
    ga = _dram(nc, "g", [D], F32, "ExternalInput")
    wga = _dram(nc, "wg", [D, DFF], F32, "ExternalInput")
    wua = _dram(nc, "wu", [D, DFF], F32, "ExternalInput")
    wda = _dram(nc, "wd", [DFF, D], F32, "ExternalInput")
    ya = _dram(nc, "y", [T, D], F32, "ExternalOutput")
    fa = _dram(nc, "gfin", [D], F32, "ExternalInput") if final_g is not None else None
    P1 = Prog(nc)
    with ExitStack() as st:
        emit_head_outproj(P1, nc, st, xa, oa, woa, ya, T)
        P1.emit()
    P2 = Prog(nc)
    with ExitStack() as st:
        emit_ffn(P2, nc, st, ya, ga, wga, wua, wda, ya, T, final_g=fa)
        P2.emit()
    maps = []
    for i in range(NCORES):
        m = {"x": x[i * T:(i + 1) * T], "oT": oT_cores[i], "w_out": w_out, "g": gffn, "wg": wg, "wu": wu, "wd": wd}
        if final_g is not None:
            m["gfin"] = final_g
        maps.append(m)
    r = _run(nc, maps)
    return np.concatenate([ri["y"] for ri in r], axis=0)


def attn_layer(x, pos, B, S, g_mix, w_in, b_in, sinks, w_out, g_ffn, wg, wu, wd, final_g=None):
    N = B * S
    T = N // NCORES
    qkT_c, v_c = launch_attn_proj(x, pos, g_mix, w_in, b_in)
    per = S // T
    qkT_seq = [np.concatenate(qkT_c[b * per:(b + 1) * per], axis=2) for b in range(B)]
    v_seq = [np.concatenate(v_c[b * per:(b + 1) * per], axis=0) for b in range(B)]
    oT_seq = launch_attn_core(qkT_seq, v_seq, sinks, S)
    oT_cores = [np.ascontiguousarray(oT_seq[i // per][:, :, (i % per) * T:((i % per) + 1) * T]) for i in range(NCORES)]
    return launch_mix_ffn(x, oT_cores, w_out, g_ffn, wg, wu, wd, final_g=final_g)


WSC = float(np.exp(-0.5))


def emit_rwkv_proj(P, nc, st, x_in, xh_in, gvec, pr, QG_out, PH_out, vb_out, gate_out, bonus_out, vfirst_io, T, has_vlora):
    sb, ps = _alloc(nc, st)
    c = emit_consts(P, nc, sb)
    ident, identf = c["ident"], c["identf"]
    NT = T // 128
    Et = sb("Et", [128, D], F32)
    t1 = sb("t1", [128, D], F32)
    stg = [Et, t1]
    gcol = sb("gcol", [128, 8], F32)
    P.dma("sync", gcol[:], gvec.rearrange("(c p) -> p c", p=128), [], ["gcol"], "gcol", allow_slow_non_contiguous=True)
    wrkv = [sb(f"w{nm}", [128, 8, D], BF16) for nm in "rkv"]
    k = 0
    for i in range(3):
        k = load_weight_bf16(P, nc, [pr["w_rkv"][i, cc * 128:(cc + 1) * 128, :] for cc in range(8)], lambda cc, i=i: wrkv[i][:, cc, :], D, stg,
                             lambda cc: gcol[:, cc:cc + 1], lambda cc, i=i: f"wrkv{i}_{cc}", k0=k, piece=1024)
    LW = {"w1": 64, "a1": 64, "g1": 160}
    if has_vlora:
        LW["v1"] = 32
    l1 = {}
    for nm, wd_ in LW.items():
        l1[nm] = sb("l1" + nm, [128, 8, wd_], BF16)
        k = load_weight_bf16(P, nc, [pr[nm][cc * 128:(cc + 1) * 128, :] for cc in range(8)], lambda cc, nm=nm: l1[nm][:, cc, :], wd_, stg,
                             lambda cc: gcol[:, cc:cc + 1], lambda cc, nm=nm: f"l1{nm}_{cc}", k0=k, piece=1024)
    l2 = {}
    for nm, rows in (("w2", 64), ("a2", 64), ("g2a", 128), ("g2b", 32)) + ((("v2", 32),) if has_vlora else ()):
        l2[nm] = sb("l2" + nm, [rows, D], BF16)
        src = pr["g2"][0:128, :] if nm == "g2a" else (pr["g2"][128:160, :] if nm == "g2b" else pr[nm])
        s, sn = stg[k % 2], f"stg{k % 2}"
        P.dma("sync", s[0:rows, :], src, [], [sn], sn)
        P.op("vector", lambda e, s=s, nm=nm, rows=rows: e.tensor_copy(out=l2[nm][:], in_=s[0:rows, :]), [sn], ["l2" + nm])
        k += 1
    bc = {}
    for nm in ("w0", "a0", "k_k", "k_a", "r_k") + (("v0",) if has_vlora else ()):
        bc[nm] = sb("bc" + nm, [128, D], F32)
        P.dma("sync", bc[nm][:], pr[nm].partition_broadcast(128), [], ["bc" + nm], "bc" + nm)
    mucol = sb("mucol", [128, 6, 8], F32)
    for i in range(6):
        P.dma("sync", mucol[:, i, :], pr["mu"][i].rearrange("(c p) -> p c", p=128), [], ["mucol"], f"mucol{i}", allow_slow_non_contiguous=True)
    ltri = sb("ltri", [128, 128], F32)
    sel0 = sb("sel0", [128, 128], F32)
    sel1 = sb("sel1", [128, 128], F32)
    mlo = sb("mlo", [128, 128], F32)
    mup = sb("mup", [128, 128], F32)
    mle = sb("mle", [128, 64], F32)
    addI = sb("addI", [128, 128], BF16)
    addI2 = sb("addI2", [128, 128], BF16)

    def gp(fn, r, w):
        P.op("gpsimd", fn, r, w)
    for tname, tl in (("ltri", ltri), ("mlo", mlo), ("mup", mup), ("sel0", sel0), ("sel1", sel1)):
        gp(lambda e, tl=tl: e.memset(tl[:], 1.0), [], [tname])
    for tname, tl in (("ltri", ltri), ("mlo", mlo), ("mup", mup)):
        gp(lambda e, tl=tl: e.memset(tl[0:64, 64:128], 0.0), [tname], [tname])
        gp(lambda e, tl=tl: e.memset(tl[64:128, 0:64], 0.0), [tname], [tname])
    gp(lambda e: e.affine_select(out=ltri[:], in_=ltri[:], pattern=[[1, 128]], compare_op=ALU.is_ge, fill=0.0, base=0, channel_multiplier=-1), ["ltri"], ["ltri"])
    gp(lambda e: e.affine_select(out=mup[:], in_=mup[:], pattern=[[1, 128]], compare_op=ALU.is_gt, fill=0.0, base=0, channel_multiplier=-1), ["mup"], ["mup"])
    gp(lambda e: e.affine_select(out=mlo[:], in_=mlo[:], pattern=[[-1, 128]], compare_op=ALU.is_gt, fill=0.0, base=0, channel_multiplier=1), ["mlo"], ["mlo"])
    gp(lambda e: e.affine_select(out=sel0[:], in_=sel0[:], pattern=[[0, 128]], compare_op=ALU.is_equal, fill=0.0, base=-63, channel_multiplier=1), ["sel0"], ["sel0"])
    gp(lambda e: e.affine_select(out=sel1[:], in_=sel1[:], pattern=[[0, 128]], compare_op=ALU.is_equal, fill=0.0, base=-127, channel_multiplier=1), ["sel1"], ["sel1"])
    gp(lambda e: e.tensor_copy(out=mle[0:64, :], in_=ltri[0:64, 0:64]), ["ltri"], ["mle"])
    gp(lambda e: e.tensor_copy(out=mle[64:128, :], in_=ltri[64:128, 64:128]), ["ltri"], ["mle"])
    P.op("vector", lambda e: e.memset(addI[:], 0.0), [], ["addI"])
    P.op("vector", lambda e: e.memset(addI2[:], 0.0), [], ["addI2"])
    for tcs in (slice(0, 64), slice(64, 128)):
        P.op("vector", lambda e, tcs=tcs: e.tensor_copy(out=addI[tcs, 0:64], in_=identf[tcs, tcs]), ["addI", "identf"], ["addI"])
        P.op("vector", lambda e, tcs=tcs: e.tensor_copy(out=addI2[tcs, 64:128], in_=identf[tcs, tcs]), ["addI2", "identf"], ["addI2"])

    xt = [sb("xt0", [128, D], F32)] * 2
    hb = [sb("hb0", [128, D], BF16)] * 2
    ss = [sb(f"ss{i}", [128, 1], F32) for i in range(2)]
    rstd = [sb(f"rstd{i}", [128, 1], F32) for i in range(2)]
    hT = [sb(f"hT{i}", [128, 8, 129], BF16) for i in range(2)]
    xxT = sb("xxT", [128, 8, 128], BF16)
    mixT = [sb(f"mixT{i}", [128, 8, 128], BF16) for i in range(6)]
    rt = sb("rt", [128, D], F32)
    kt_ = sb("kt_", [128, D], F32)
    vt_ = sb("vt_", [128, D], F32)
    sgm = sb("sgm", [128, D], F32)
    asg = sb("asg", [128, D], F32)
    kkn = sb("kkn", [128, D], F32)
    t2 = sb("t2", [128, D], F32)
    ge = [sb(f"ge{i}", [128, D], F32) for i in range(2)]
    nrm = sb("nrm", [128, 16], F32)
    bsc = sb("bsc", [128, 16], F32)
    tmA = sb("tmA", [128, D], BF16)
    tmB = sb("tmB", [128, D], BF16)
    tmK = sb("tmK", [128, D], BF16)
    tmR = sb("tmR", [128, D], BF16)
    vbf = hb[0]
    fm = {nm: sb("fm" + nm, [64, 16, 128], BF16) for nm in "ABKR"}
    lo1 = {nm: sb("lo1" + nm, [wd_ if wd_ <= 128 else 128, 128], BF16) for nm, wd_ in LW.items()}
    lo1g2 = sb("lo1g2", [32, 128], BF16)
    NCH = 4
    bdNT_c = [[sb(f"bdNT{ch}j{j}", [128, 128], BF16) for j in range(6)] for ch in range(NCH)]
    bdN_c = [[sb(f"bdN{ch}j{j}", [128, 128], BF16) for j in range(5)] for ch in range(NCH)]
    bdAk_c = [sb(f"bdAk{ch}", [128, 128], BF16) for ch in range(NCH)]
    Z_c = [[sb(f"Z{ch}_{i}", [128, 128], BF16) for i in range(2)] for ch in range(NCH)]
    ADD_c = [sb(f"ADD{ch}", [128, 128], BF16) for ch in range(NCH)]
    QGs = [sb(f"QGs{i}", [128, 16, 128], BF16) for i in range(1)] * 2
    PHs = [sb(f"PHs{i}", [128, 16, 128], BF16) for i in range(1)] * 2
    ptr = ps("ptr", [128, 8, 128], BF16)
    pA = ps("pA", [128, 512], F32)
    pB = ps("pB", [128, 512], F32)
    pc = [pA, pB]
    pool = [ps(f"pk{i}", [128, 512], F32) for i in range(5)]
    for i_ in range(5):
        P.op("vector", lambda e, i_=i_: e.memset(pool[i_][:], 0.0), [], [f"pk{i_}"])
    pl = pool[0][:].rearrange("p (s c) -> p s c", c=128)
    P.barrier()
    P.dma("sync", xt[1][:], xh_in, [], ["xt0"], "xt0")
    emit_norm_T(P, c, xt[1], "xt0", ss[0], rstd[0], hb[0], ptr, hT[0][:, :, 1:129], "hT0", 0)
    P.op("gpsimd", lambda e: e.tensor_copy(out=hT[1][:, :, 128:129], in_=hT[0][:, :, 1:2]), ["hT0"], ["hT1"])
    import os
    STOP = int(os.environ.get('R1_STOP', '9'))
    outs = []
    pwc = 0
    class _Rec:
        def __init__(self):
            self.calls, self.split, self.pos = [], None, 0
        def op(self, *a, **k):
            self.calls.append(("op", a, k))
        def dma(self, *a, **k):
            self.calls.append(("dma", a, k))
        def mark(self):
            self.split = len(self.calls)
        def replay(self, upto):
            while self.pos < upto:
                kind, a, k = self.calls[self.pos]
                getattr(P, kind)(*a, **k)
                self.pos += 1

    def token_level(n, P):
        b = n % 2
        x, xn = xt[b], "xt0"
        t0 = n * 128
        P.dma("sync", x[:], x_in[t0:t0 + 128, :], [], [xn], xn)
        emit_norm_T(P, c, x, xn, ss[0], rstd[0], hb[0], ptr, hT[b][:, :, 1:129], f"hT{b}", 0)
        P.op("gpsimd", lambda e, b=b: e.tensor_copy(out=hT[b][:, :, 0:1], in_=hT[1 - b][:, :, 128:129]), [f"hT{1 - b}"], [f"hT{b}h"])
        hr = [f"hT{b}", f"hT{b}h"]
        P.op("vector", lambda e, b=b: e.tensor_tensor(out=xxT[:], in0=hT[b][:, :, 0:128], in1=hT[b][:, :, 1:129], op=ALU.subtract), hr, ["xxT"])
        for i in range(6):
            for cc in range(8):
                P.op("vector", lambda e, i=i, cc=cc, b=b: e.scalar_tensor_tensor(out=mixT[i][:, cc, :], in0=xxT[:, cc, :], scalar=mucol[:, i, cc:cc + 1], in1=hT[b][:, cc, 1:129],
                                                                               op0=ALU.mult, op1=ALU.add), ["xxT", "mucol"] + hr, [f"mixT{i}"])
        slot = {"w1": 0, "a1": 1, "g1": 2, "v1": 3}
        msrc = {"w1": 1, "a1": 4, "g1": 5, "v1": 3}
        for nm, wd_ in LW.items():
            s_ = slot[nm]
            w0_ = min(wd_, 128)
            for cc in range(8):
                P.op("tensor", lambda e, nm=nm, cc=cc, s_=s_, w0_=w0_: e.matmul(out=pl[0:w0_, s_, :], lhsT=l1[nm][:, cc, 0:w0_], rhs=mixT[msrc[nm]][:, cc, :], start=(cc == 0), stop=(cc == 7)),
                     [f"mixT{msrc[nm]}", f"l1{nm}_{cc}"], ["pk0"])
        for cc in range(8):
            P.op("tensor", lambda e, cc=cc: e.matmul(out=pB[0:32, 0:128], lhsT=l1["g1"][:, cc, 128:160], rhs=mixT[5][:, cc, :], start=(cc == 0), stop=(cc == 7)),
                 ["mixT5", f"l1g1_{cc}"], ["pB"])
        P.op("scalar", lambda e: e.activation(out=lo1["w1"][:], in_=pl[0:64, 0, :], func=AF.Tanh), ["pk0"], ["lo1w1"])
        P.op("scalar", lambda e: e.copy(out=lo1["a1"][:], in_=pl[0:64, 1, :]), ["pk0"], ["lo1a1"])
        P.op("scalar", lambda e: e.activation(out=lo1["g1"][:], in_=pl[:, 2, :], func=AF.Sigmoid), ["pk0"], ["lo1g1"])
        P.op("scalar", lambda e: e.activation(out=lo1g2[:], in_=pB[0:32, 0:128], func=AF.Sigmoid), ["pB"], ["lo1g2"])
        if has_vlora:
            P.op("scalar", lambda e: e.copy(out=lo1["v1"][:], in_=pl[0:32, 3, :]), ["pk0"], ["lo1v1"])

        def proj(dst_fn, lhs_list, reads):
            for half, pp_, pn in ((0, pA, "pA"), (1, pB, "pB")):
                nk = len(lhs_list)
                for ki, (lh, rh, rn) in enumerate(lhs_list):
                    P.op("tensor", lambda e, lh=lh, rh=rh, half=half, pp_=pp_, ki=ki, nk=nk: e.matmul(out=pp_[:], lhsT=lh, rhs=rh[:, half * 512:(half + 1) * 512], start=(ki == 0), stop=(ki == nk - 1)),
                         reads + [rn], [pn])
                dst_fn(half, pp_, pn)

        H = lambda half: slice(half * 512, (half + 1) * 512)
        for (mi, wi, dst, dn) in ((0, 0, rt, "rt"), (2, 1, kt_, "kt_"), (3, 2, vt_, "vt_")):
            proj(lambda half, pp_, pn, dst=dst, dn=dn: P.op("scalar", lambda e: e.copy(out=dst[:, H(half)], in_=pp_[:]), [pn], [dn + str(half)]),
                 [(mixT[mi][:, cc, :], wrkv[wi][:, cc, :], f"wrkv{wi}_{cc}") for cc in range(8)], [f"mixT{mi}"])
        RT, KT, VT = ["rt0", "rt1"], ["kt_0", "kt_1"], ["vt_0", "vt_1"]
        def ev_sig(dst, dn, bcn):
            def f(half, pp_, pn):
                P.op("vector", lambda e: e.tensor_tensor(out=dst[:, H(half)], in0=pp_[:], in1=bc[bcn][:, H(half)], op=ALU.add), [pn, "bc" + bcn], [dn + str(half)])
                P.op("scalar", lambda e: e.activation(out=dst[:, H(half)], in_=dst[:, H(half)], func=AF.Sigmoid), [dn + str(half)], [dn + str(half)])
            return f
        proj(ev_sig(sgm, "sgm", "w0"), [(lo1["w1"][:], l2["w2"], "l2w2")], ["lo1w1"])
        proj(ev_sig(asg, "asg", "a0"), [(lo1["a1"][:], l2["a2"], "l2a2")], ["lo1a1"])
        SG, AS = ["sgm0", "sgm1"], ["asg0", "asg1"]
        proj(lambda half, pp_, pn: P.op("scalar", lambda e: e.copy(out=t1[:, H(half)], in_=pp_[:]), [pn], ["t1"]),
             [(lo1["g1"][:], l2["g2a"], "l2g2a"), (lo1g2[:], l2["g2b"], "l2g2b")], ["lo1g1", "lo1g2"])
        P.dma("gpsimd", gate_out[t0:t0 + 128, :], t1[:], ["t1"], [f"go{n}"], "t1o")
        outs.append(f"go{n}")
        if has_vlora:
            proj(ev_sig(Et, "Et", "v0"), [(lo1["v1"][:], l2["v2"], "l2v2")], ["lo1v1"])
            P.dma("sync", t2[:], vfirst_io[t0:t0 + 128, :], [], ["t2"], "t2i")
            P.op("gpsimd", lambda e: e.tensor_tensor(out=t2[:], in0=t2[:], in1=vt_[:], op=ALU.subtract), ["t2"] + VT, ["t2"])
            P.op("gpsimd", lambda e: e.tensor_tensor(out=t2[:], in0=t2[:], in1=Et[:], op=ALU.mult), ["t2", "Et0", "Et1"], ["t2"])
            P.op("gpsimd", lambda e: e.tensor_tensor(out=vt_[:], in0=vt_[:], in1=t2[:], op=ALU.add), ["t2"] + VT, VT)
        else:
            P.dma("gpsimd", vfirst_io[t0:t0 + 128, :], vt_[:], VT, [f"vfo{n}"], "vfo")
            outs.append(f"vfo{n}")
        P.op("gpsimd", lambda e: e.tensor_copy(out=vbf[:], in_=vt_[:]), VT, ["hb0"])
        P.dma("gpsimd", vb_out[t0:t0 + 128, :], vbf[:], ["hb0"], [f"vbo{n}"], "vbfo")
        outs.append(f"vbo{n}")
        kkn3 = kkn[:].rearrange("p (h d) -> p h d", d=64)
        t13 = t1[:].rearrange("p (h d) -> p h d", d=64)
        t23 = t2[:].rearrange("p (h d) -> p h d", d=64)
        P.op("vector", lambda e: e.tensor_tensor(out=kkn[:], in0=kt_[:], in1=bc["k_k"][:], op=ALU.mult), KT + ["bck_k"], ["kkn"])
        P.op("gpsimd", lambda e: e.tensor_tensor(out=t2[:], in0=kkn[:], in1=kkn[:], op=ALU.mult), ["kkn"], ["t2"])
        P.op("vector", lambda e: e.tensor_reduce(out=nrm[:], in_=t23, axis=AX.X, op=ALU.add), ["t2"], ["nrm"])
        P.op("scalar", lambda e: e.activation(out=nrm[:], in_=nrm[:], func=AF.Sqrt), ["nrm"], ["nrm"])
        P.op("vector", lambda e: e.tensor_scalar(out=nrm[:], in0=nrm[:], scalar1=1e-12, scalar2=None, op0=ALU.max), ["nrm"], ["nrm"])
        P.op("vector", lambda e: e.reciprocal(out=nrm[:], in_=nrm[:]), ["nrm"], ["nrm"])
        P.op("vector", lambda e: e.tensor_tensor(out=kkn3, in0=kkn3, in1=nrm[:].unsqueeze(2).broadcast_to([128, 16, 64]), op=ALU.mult), ["kkn", "nrm"], ["kkn"])
        P.op("vector", lambda e: e.scalar_tensor_tensor(out=t2[:], in0=asg[:], scalar=-1.0, in1=bc["k_a"][:], op0=ALU.add, op1=ALU.mult), AS + ["bck_a", "t2"], ["t2"])
        P.op("vector", lambda e: e.scalar_tensor_tensor(out=kt_[:], in0=t2[:], scalar=1.0, in1=kt_[:], op0=ALU.add, op1=ALU.mult), ["t2"] + KT, KT)
        P.op("gpsimd", lambda e: e.tensor_tensor(out=t2[:], in0=rt[:], in1=kt_[:], op=ALU.mult), RT + KT + ["t2"], ["t2"])
        P.op("gpsimd", lambda e: e.tensor_tensor(out=t2[:], in0=t2[:], in1=bc["r_k"][:], op=ALU.mult), ["t2", "bcr_k"], ["t2"])
        P.op("vector", lambda e: e.tensor_reduce(out=bsc[:], in_=t23, axis=AX.X, op=ALU.add), ["t2"], ["bsc"])
        P.op("vector", lambda e: e.tensor_tensor(out=t23, in0=vt_[:].rearrange("p (h d) -> p h d", d=64), in1=bsc[:].unsqueeze(2).broadcast_to([128, 16, 64]), op=ALU.mult), VT + ["bsc", "t2"], ["t2"])
        P.dma("gpsimd", bonus_out[t0:t0 + 128, :], t2[:], ["t2"], [f"bo{n}"], "t2o")
        outs.append(f"bo{n}")
        for half in range(2):
            P.op("tensor", lambda e, half=half: e.matmul(out=pc[half][:], lhsT=ltri[:], rhs=sgm[:, H(half)], start=True, stop=True), [SG[half], "ltri"], [("pA", "pB")[half]])
        P.mark()
        for half in range(2):
            P.op("scalar", lambda e, half=half: e.activation(out=Et[:, H(half)], in_=pc[half][:], func=AF.Exp, scale=-WSC), [("pA", "pB")[half]], [f"Et{half}"])
        P.op("vector", lambda e: e.tensor_tensor(out=tmR[:], in0=rt[:], in1=Et[:], op=ALU.mult), RT + ["Et0", "Et1"], ["tmR"])
        for half in range(2):
            P.op("scalar", lambda e, half=half: e.activation(out=Et[:, H(half)], in_=pc[half][:], func=AF.Exp, scale=WSC), [("pA", "pB")[half], "tmR"], [f"Et{half}"])
        P.op("vector", lambda e: e.tensor_tensor(out=tmK[:], in0=kt_[:], in1=Et[:], op=ALU.mult), KT + ["Et0", "Et1"], ["tmK"])
        P.op("gpsimd", lambda e: e.tensor_tensor(out=t1[:], in0=kkn[:], in1=asg[:], op=ALU.mult), ["kkn", "t1"] + AS, ["t1"])
        P.op("vector", lambda e: e.tensor_tensor(out=tmB[:], in0=t1[:], in1=Et[:], op=ALU.mult), ["t1", "Et0", "Et1"], ["tmB"])
        for half in range(2):
            P.op("vector", lambda e, half=half: e.tensor_tensor(out=t1[:, H(half)], in0=pc[half][:], in1=sgm[:, H(half)], op=ALU.subtract), [("pA", "pB")[half], SG[half], "t1", "tmB"], ["t1"])
        P.op("scalar", lambda e: e.activation(out=Et[:], in_=t1[:], func=AF.Exp, scale=-WSC), ["t1", "tmK", "tmB", "Et0", "Et1"], ["Et0", "Et1"])
        P.op("vector", lambda e: e.scalar_tensor_tensor(out=tmA[:], in0=kkn[:], scalar=-1.0, in1=Et[:], op0=ALU.mult, op1=ALU.mult), ["kkn", "Et0", "Et1"], ["tmA"])
        for half in range(2):
            P.op("scalar", lambda e, half=half: e.copy(out=t1[:, H(half)], in_=pc[half][:]), [("pA", "pB")[half], "t1"], ["t1"])
        for s_, sel in ((0, sel0), (1, sel1)):
            for half in range(2):
                P.op("tensor", lambda e, half=half, sel=sel: e.matmul(out=pc[half][:], lhsT=sel[:], rhs=t1[:, H(half)], start=True, stop=True), ["t1", "sel0", "sel1"], [("pA", "pB")[half]])
                P.op("scalar", lambda e, half=half, s_=s_: e.activation(out=ge[s_][:, H(half)], in_=pc[half][:], func=AF.Exp, scale=-WSC), [("pA", "pB")[half]], [f"ge{s_}"])
        for nm, src, sn in (("A", tmA, "tmA"), ("B", tmB, "tmB"), ("K", tmK, "tmK"), ("R", tmR, "tmR")):
            for h0 in (0, 8):
                for hh in range(8):
                    P.op("tensor", lambda e, hh=hh, h0=h0, src=src: e.transpose(out=ptr[0:64, hh, :], in_=src[:, (h0 + hh) * 64:(h0 + hh + 1) * 64], identity=ident[:]), [sn, "ident"], ["ptr"])
                if nm in "AK":
                    P.op("scalar", lambda e, nm=nm, h0=h0: e.copy(out=fm[nm][:, h0:h0 + 8, :], in_=ptr[0:64, :, :]), ["ptr"], ["fm" + nm])
                else:
                    P.op("vector", lambda e, nm=nm, h0=h0: e.tensor_copy(out=fm[nm][:, h0:h0 + 8, :], in_=ptr[0:64, :, :]), ["ptr"], ["fm" + nm])

    def chains_and_out(n, filler):
        ob = 0

        def nxt():
            nonlocal pwc
            k_ = 1 + pwc % 4
            pwc += 1
            return pool[k_], f"pk{k_}"

        def head_steps(h, ch):
            bdNT, bdN, bdAk, Z, ADD = bdNT_c[ch], bdN_c[ch], bdAk_c[ch], Z_c[ch], ADD_c[ch]
            R_ = lambda nm: f"{nm}_{ch}"
            hc = slice(h * 64, (h + 1) * 64)
            fA, fB, fK, fR = (fm[q][:, h, :] for q in "ABKR")
            pgb, pgn = nxt()
            pg_ = pgb[:].rearrange("p (s c) -> p s c", c=128)
            for cc in range(2):
                tc = slice(cc * 64, (cc + 1) * 64)
                for (slot_, lh, rh, rn1, rn2) in ((0, fA, fB, "fmA", "fmB"), (1, fA, fK, "fmA", "fmK"), (2, fB, fA, "fmB", "fmA")):
                    P.op("tensor", lambda e, slot_=slot_, lh=lh, rh=rh, tc=tc: e.matmul(out=pg_[tc, slot_, tc], lhsT=lh[:, tc], rhs=rh[:, tc], start=True, stop=True), [rn1, rn2], [pgn])
                P.op("tensor", lambda e, tc=tc: e.matmul(out=pg_[tc, 3, 0:64], lhsT=fB[:, tc], rhs=fR[:, tc], start=True, stop=True), ["fmB", "fmR"], [pgn])
                P.op("tensor", lambda e, tc=tc: e.matmul(out=pg_[tc, 3, 64:128], lhsT=fK[:, tc], rhs=fR[:, tc], start=True, stop=True), ["fmK", "fmR"], [pgn])
            P.op("vector", lambda e: e.tensor_tensor(out=bdNT[0][:], in0=pg_[:, 0, :], in1=mlo[:], op=ALU.mult), [pgn, "mlo"], [R_("bdNT0")])
            P.op("vector", lambda e: e.tensor_tensor(out=bdN[0][:], in0=pg_[:, 2, :], in1=mup[:], op=ALU.mult), [pgn, "mup"], [R_("bdN0")])
            P.op("vector", lambda e: e.tensor_tensor(out=bdAk[:], in0=pg_[:, 1, :], in1=mlo[:], op=ALU.mult), [pgn, "mlo"], [R_("bdAk")])
            P.op("vector", lambda e: e.tensor_tensor(out=Z[0][:, 0:64], in0=pg_[:, 3, 0:64], in1=mle[:], op=ALU.mult), [pgn, "mle"], [R_("Z0a")])
            P.op("vector", lambda e: e.tensor_tensor(out=ADD[:, 0:64], in0=pg_[:, 3, 64:128], in1=mle[:], op=ALU.mult), [pgn, "mle"], [R_("ADDa")])
            P.op("gpsimd", lambda e: e.tensor_copy(out=Z[0][:, 64:128], in_=tmB[:, hc]), ["tmB"], [R_("Z0b")])
            P.op("gpsimd", lambda e: e.tensor_copy(out=ADD[:, 64:128], in_=tmK[:, hc]), ["tmK"], [R_("ADDb")])
            yield
            for j in range(5):
                pa_, pan = nxt()
                P.op("tensor", lambda e, j=j, pa_=pa_: e.matmul(out=pa_[:, 0:128], lhsT=bdN[j][:], rhs=bdNT[j][:], start=True, stop=True), [R_(f"bdN{j}"), R_(f"bdNT{j}")], [pan])
                P.op("scalar", lambda e, j=j, pa_=pa_: e.copy(out=bdNT[j + 1][:], in_=pa_[:, 0:128]), [pan], [R_(f"bdNT{j + 1}")])
                if j < 4:
                    pb2, pbn = nxt()
                    P.op("tensor", lambda e, j=j, pb2=pb2: e.matmul(out=pb2[:, 0:128], lhsT=bdNT[j][:], rhs=bdN[j][:], start=True, stop=True), [R_(f"bdN{j}"), R_(f"bdNT{j}")], [pbn])
                    P.op("vector", lambda e, j=j, pb2=pb2: e.tensor_copy(out=bdN[j + 1][:], in_=pb2[:, 0:128]), [pbn], [R_(f"bdN{j + 1}")])
                yield
            zi = 0
            zr = [R_("Z0a"), R_("Z0b")]
            for j in range(5, -1, -1):
                pa_, pan = nxt()
                P.op("tensor", lambda e, zi=zi, pa_=pa_: e.matmul(out=pa_[:, 0:128], lhsT=ident[:], rhs=Z[zi][:], start=True, stop=False), zr + ["ident"], [pan])
                P.op("tensor", lambda e, zi=zi, pa_=pa_, j=j: e.matmul(out=pa_[:, 0:128], lhsT=bdNT[j][:], rhs=Z[zi][:], start=False, stop=True), zr + [R_(f"bdNT{j}")], [pan])
                zi = 1 - zi
                znew = [R_(f"Z{zi}a"), R_(f"Z{zi}b")]
                if j % 2:
                    P.op("scalar", lambda e, zi=zi, pa_=pa_: e.copy(out=Z[zi][:], in_=pa_[:, 0:128]), [pan], znew)
                else:
                    P.op("vector", lambda e, zi=zi, pa_=pa_: e.tensor_copy(out=Z[zi][:], in_=pa_[:, 0:128]), [pan], znew)
                zr = znew
                yield
            Zf = Z[zi]
            for cc in range(2):
                tc = slice(cc * 64, (cc + 1) * 64)
                pbk, pnm = nxt()
                P.op("tensor", lambda e, tc=tc, pbk=pbk: e.matmul(out=pbk[tc, 0:128], lhsT=tmA[tc, hc], rhs=Zf[tc, :], start=True, stop=False), zr + ["tmA"], [pnm])
                P.op("tensor", lambda e, tc=tc, pbk=pbk: e.matmul(out=pbk[tc, 0:128], lhsT=tmR[tc, hc], rhs=addI[tc, :], start=False, stop=False), ["tmR", "addI"], [pnm])
                P.op("tensor", lambda e, tc=tc, pbk=pbk: e.matmul(out=pbk[tc, 0:128], lhsT=ident[tc, tc], rhs=addI2[tc, :], start=False, stop=True), ["ident", "addI2"], [pnm])
                P.op("vector", lambda e, cc=cc, tc=tc, pbk=pbk: e.tensor_copy(out=QGs[ob][tc, h, 0:64], in_=pbk[tc, 0:64]), [pnm], [f"QGs{ob}_{h}_{cc}q"])
                P.op("vector", lambda e, cc=cc, tc=tc, pbk=pbk: e.tensor_tensor(out=QGs[ob][tc, h, 64:128], in0=pbk[tc, 64:128], in1=ge[cc][tc, hc], op=ALU.mult),
                     [pnm, f"ge{cc}"], [f"QGs{ob}_{h}_{cc}g"])
            yield
            pa_, pan = nxt()
            P.op("tensor", lambda e, pa_=pa_: e.matmul(out=pa_[:, 0:128], lhsT=bdAk[:], rhs=Zf[:], start=True, stop=False), zr + [R_("bdAk")], [pan])
            P.op("tensor", lambda e, pa_=pa_: e.matmul(out=pa_[:, 0:128], lhsT=ident[:], rhs=ADD[:], start=False, stop=True), [R_("ADDa"), R_("ADDb"), "ident"], [pan])
            P.op("vector", lambda e, pa_=pa_: e.tensor_copy(out=PHs[ob][:, h, 0:64], in_=pa_[:, 0:64]), [pan], [f"PHs{ob}_{h}p"])
            for cc in range(2):
                tc = slice(cc * 64, (cc + 1) * 64)
                P.op("vector", lambda e, pa_=pa_, tc=tc, cc=cc: e.tensor_tensor(out=PHs[ob][tc, h, 64:128], in0=pa_[tc, 64:128], in1=ge[cc][tc, hc], op=ALU.mult),
                     [pan, f"ge{cc}"], [f"PHs{ob}_{h}h{cc}"])
            yield

        for g0 in range(0, 16, NCH):
            active = [head_steps(g0 + ch, ch) for ch in range(NCH)]
            while active:
                filler()
                for g_ in list(active):
                    try:
                        next(g_)
                    except StopIteration:
                        active.remove(g_)
        rest_A()
        qrd = [f"QGs{ob}_{h}_{cc}{q}" for h in range(16) for cc in range(2) for q in "qg"]
        prd = [f"PHs{ob}_{h}{q}" for h in range(16) for q in ("p", "h0", "h1")]
        for hh in range(0, 16, 4):
            P.dma("sync", QG_out[n, hh:hh + 4].rearrange("h i c -> i h c"), QGs[ob][:, hh:hh + 4, :], qrd, [f"qgo{n}_{hh}"], f"QGs{ob}o{hh}")
            P.dma("sync", PH_out[n, hh:hh + 4].rearrange("h i c -> i h c"), PHs[ob][:, hh:hh + 4, :], prd, [f"pho{n}_{hh}"], f"PHs{ob}o{hh}")
            outs.extend([f"qgo{n}_{hh}", f"pho{n}_{hh}"])

    cur_rec = None
    if NT > 0 and STOP > 1:
        cur_rec = _Rec()
        token_level(0, cur_rec)
        cur_rec.replay(len(cur_rec.calls))
    for n in range(NT if STOP > 1 else 0):
        nrec = None
        if n + 1 < NT:
            nrec = _Rec()
            token_level(n + 1, nrec)
        per = 0 if nrec is None else -(-nrec.split // 48)
        filler = (lambda: None) if nrec is None else (lambda nrec=nrec, per=per: nrec.replay(min(nrec.split, nrec.pos + per)))
        rest_A = (lambda: None) if nrec is None else (lambda nrec=nrec: nrec.replay(nrec.split))
        chains_and_out(n, filler)
        if nrec is not None:
            nrec.replay(len(nrec.calls))
    P.op("sync", lambda e: None, outs, [])
    P.op("gpsimd", lambda e: None, outs, [])


def emit_rwkv_scan(P, nc, st, QG, PH, V, y_out, NTs):
    sb, ps = _alloc(nc, st)
    NB = 3
    qg = [sb(f"qg{i}", [128, 8, 128], BF16) for i in range(NB)]
    ph = [sb(f"ph{i}", [128, 8, 128], BF16) for i in range(NB)]
    vv = [sb(f"vv{i}", [128, 512], BF16) for i in range(NB)]
    S = sb("S", [128, 8, 64], BF16)
    yt = [sb(f"yt{i}", [128, 512], F32) for i in range(2)]
    pS = [ps(f"pS{i}", [128, 8, 64], F32) for i in range(2)]
    pY = [ps(f"pY{i}", [128, 8, 64], F32) for i in range(4)]
    P.op("vector", lambda e: e.memset(S[:], 0.0), [], ["S0", "S1"])
    outs = []
    for n in range(NTs):
        b = n % NB
        P.dma("sync", qg[b][:], QG[n].rearrange("h i c -> i h c"), [], [f"qg{b}"], f"qg{b}")
        P.dma("sync", ph[b][:], PH[n].rearrange("h i c -> i h c"), [], [f"ph{b}"], f"ph{b}")
        P.dma("sync", vv[b][:], V[n * 128:(n + 1) * 128, :], [], [f"vv{b}"], f"vv{b}")
        yb = n % 2
        for cc in range(2):
            tc = slice(cc * 64, (cc + 1) * 64)
            to = slice((1 - cc) * 64, (2 - cc) * 64)
            pSb = pS[cc]
            rds = [f"qg{b}", f"ph{b}", f"vv{b}", f"S{cc}"]
            for h in range(8):
                P.op("tensor", lambda e, h=h, b=b, tc=tc, yb=yb, cc=cc: e.matmul(out=pY[2 * yb + cc][tc, h, :], lhsT=qg[b][tc, h, 0:64], rhs=S[tc, h, :], start=True, stop=False), rds, [f"pY{2 * yb + cc}"])
                P.op("tensor", lambda e, h=h, b=b, tc=tc, yb=yb, cc=cc: e.matmul(out=pY[2 * yb + cc][tc, h, :], lhsT=ph[b][tc, h, 0:64], rhs=vv[b][tc, h * 64:(h + 1) * 64], start=False, stop=True), rds, [f"pY{2 * yb + cc}"])
                P.op("tensor", lambda e, h=h, b=b, tc=tc, to=to, pSb=pSb: e.matmul(out=pSb[to, h, :], lhsT=qg[b][tc, h, 64:128], rhs=S[tc, h, :], start=True, stop=False), rds, [f"pS{cc}"])
                P.op("tensor", lambda e, h=h, b=b, tc=tc, to=to, pSb=pSb: e.matmul(out=pSb[to, h, :], lhsT=ph[b][tc, h, 64:128], rhs=vv[b][tc, h * 64:(h + 1) * 64], start=False, stop=True), rds, [f"pS{cc}"])
            P.op("scalar", lambda e, to=to, pSb=pSb: e.copy(out=S[to, :, :], in_=pSb[to, :, :]), [f"pS{cc}"], [f"S{1 - cc}"])
        for cc in range(2):
            tc = slice(cc * 64, (cc + 1) * 64)
            P.op("vector", lambda e, yb=yb, cc=cc, tc=tc: e.tensor_copy(out=yt[yb][tc, :], in_=pY[2 * yb + cc][tc, :, :].rearrange("p h d -> p (h d)")), [f"pY{2 * yb + cc}"], [f"yt{yb}"])
        P.dma("gpsimd", y_out[n * 128:(n + 1) * 128, :], yt[yb][:], [f"yt{yb}"], [f"yo{n}"], f"yt{yb}o")
        outs.append(f"yo{n}")
    P.op("sync", lambda e: None, outs, [])
    P.op("gpsimd", lambda e: None, outs, [])


LNX_EPS = 64e-5


def emit_rwkv_out(P, nc, st, x_in, y_in, gate_in, bonus_in, lnw, lnb, w_o, x_out, T):
    sb, ps = _alloc(nc, st)
    c = emit_consts(P, nc, sb)
    ident = c["ident"]
    wob = sb("wob", [128, 8, D], BF16)
    stg = [sb(f"stg{i}", [128, D], F32) for i in range(2)]
    load_weight_bf16(P, nc, [w_o[cc * 128:(cc + 1) * 128, :] for cc in range(8)], lambda cc: wob[:, cc, :], D, stg, None, lambda cc: f"wob{cc}", piece=1024)
    lw = sb("lw", [128, D], F32)
    lb = sb("lb", [128, D], F32)
    P.dma("sync", lw[:], lnw.partition_broadcast(128), [], ["lw"], "lw")
    P.dma("sync", lb[:], lnb.partition_broadcast(128), [], ["lb"], "lb")
    eps2 = sb("eps2", [128, 1], F32)
    P.op("vector", lambda e: e.memset(eps2[:], LNX_EPS), [], ["eps2"])
    yt = [sb(f"yt{i}", [128, D], F32) for i in range(2)]
    gt = [sb(f"gt{i}", [128, D], F32) for i in range(2)]
    bt = [sb(f"bt{i}", [128, D], F32) for i in range(2)]
    xt = [sb(f"xt{i}", [128, D], F32) for i in range(2)]
    sq = sb("sq", [128, D], F32)
    ygb = sb("ygb", [128, D], BF16)
    ygT = [sb(f"ygT{i}", [128, 8, 128], BF16) for i in range(2)]
    s1 = sb("s1", [128, 16], F32)
    s2 = sb("s2", [128, 16], F32)
    msq = sb("msq", [128, 16], F32)
    ptr = ps("ptr", [128, 8, 128], BF16)
    po = [ps(f"po{i}", [128, 512], F32) for i in range(2)]
    outs = []
    ocnt = 0
    for n in range(T // 128):
        b = n % 2
        t0 = n * 128
        y, g, bo, x = yt[b], gt[b], bt[b], xt[b]
        P.dma("sync", y[:], y_in[t0:t0 + 128, :], [], [f"yt{b}"], f"yt{b}")
        P.dma("sync", g[:], gate_in[t0:t0 + 128, :], [], [f"gt{b}"], f"gt{b}")
        P.dma("sync", bo[:], bonus_in[t0:t0 + 128, :], [], [f"bt{b}"], f"bt{b}")
        P.dma("sync", x[:], x_in[t0:t0 + 128, :], [], [f"xt{b}"], f"xt{b}")
        y3 = y[:].rearrange("p (h d) -> p h d", d=64)
        sq3 = sq[:].rearrange("p (h d) -> p h d", d=64)
        bc16 = lambda t: t[:].unsqueeze(2).broadcast_to([128, 16, 64])
        P.op("vector", lambda e, y3=y3: e.tensor_reduce(out=s1[:], in_=y3, axis=AX.X, op=ALU.add), [f"yt{b}"], ["s1"])
        P.op("gpsimd", lambda e, y=y: e.tensor_tensor(out=sq[:], in0=y[:], in1=y[:], op=ALU.mult), [f"yt{b}"], ["sq"])
        P.op("vector", lambda e: e.tensor_reduce(out=s2[:], in_=sq3, axis=AX.X, op=ALU.add), ["sq"], ["s2"])
        P.op("vector", lambda e: e.tensor_scalar(out=s1[:], in0=s1[:], scalar1=1.0 / 64, scalar2=None, op0=ALU.mult), ["s1"], ["s1"])
        P.op("vector", lambda e: e.tensor_tensor(out=msq[:], in0=s1[:], in1=s1[:], op=ALU.mult), ["s1"], ["msq"])
        P.op("vector", lambda e: e.scalar_tensor_tensor(out=s2[:], in0=s2[:], scalar=1.0 / 64, in1=msq[:], op0=ALU.mult, op1=ALU.subtract), ["s2", "msq"], ["s2"])
        P.op("scalar", lambda e: e.activation(out=s2[:], in_=s2[:], func=AF.Sqrt, bias=eps2[:], scale=1.0), ["s2", "eps2"], ["s2"])
        P.op("vector", lambda e: e.reciprocal(out=s2[:], in_=s2[:]), ["s2"], ["s2"])
        P.op("vector", lambda e, y3=y3: e.tensor_tensor(out=y3, in0=y3, in1=bc16(s1), op=ALU.subtract), [f"yt{b}", "s1"], [f"yt{b}"])
        P.op("vector", lambda e, y3=y3: e.tensor_tensor(out=y3, in0=y3, in1=bc16(s2), op=ALU.mult), [f"yt{b}", "s2"], [f"yt{b}"])
        P.op("gpsimd", lambda e, y=y: e.tensor_tensor(out=y[:], in0=y[:], in1=lw[:], op=ALU.mult), [f"yt{b}", "lw"], [f"yt{b}"])
        P.op("gpsimd", lambda e, bo=bo: e.tensor_tensor(out=bo[:], in0=bo[:], in1=lb[:], op=ALU.add), [f"bt{b}", "lb"], [f"bt{b}"])
        P.op("vector", lambda e, y=y, bo=bo: e.tensor_tensor(out=y[:], in0=y[:], in1=bo[:], op=ALU.add), [f"yt{b}", f"bt{b}"], [f"yt{b}"])
        P.op("vector", lambda e, y=y, g=g: e.tensor_tensor(out=ygb[:], in0=y[:], in1=g[:], op=ALU.mult), [f"yt{b}", f"gt{b}"], ["ygb"])
        for cc in range(8):
            P.op("tensor", lambda e, cc=cc: e.transpose(out=ptr[:, cc, :], in_=ygb[:, cc * 128:(cc + 1) * 128], identity=ident[:]), ["ygb", "ident"], ["ptr"])
        P.op("scalar", lambda e, b=b: e.copy(out=ygT[b][:], in_=ptr[:]), ["ptr"], [f"ygT{b}"])
        for half in range(2):
            pb_ = ocnt % 2
            ocnt += 1
            for cc in range(8):
                P.op("tensor", lambda e, cc=cc, half=half, pb_=pb_, b=b: e.matmul(out=po[pb_][:], lhsT=ygT[b][:, cc, :], rhs=wob[:, cc, half * 512:(half + 1) * 512], start=(cc == 0), stop=(cc == 7)),
                     [f"ygT{b}", f"wob{cc}"], [f"po{pb_}"])
            P.op("vector", lambda e, half=half, pb_=pb_, x=x: e.tensor_tensor(out=x[:, half * 512:(half + 1) * 512], in0=po[pb_][:], in1=x[:, half * 512:(half + 1) * 512], op=ALU.add), [f"po{pb_}", f"xt{b}"], [f"xt{b}"])
        P.dma("gpsimd", x_out[t0:t0 + 128, :], x[:], [f"xt{b}"], [f"out_{t0}"], f"xt{b}o")
        outs.append(f"out_{t0}")
    P.op("sync", lambda e: None, outs, [])
    P.op("gpsimd", lambda e: None, outs, [])


def launch_rwkv_proj(x, xh, gvec, prm, vfirst):
    N = x.shape[0]
    T = N // NCORES
    NT = T // 128
    has_v = vfirst is not None
    nc = bass.Bass("TRN2", target_bir_lowering=False)
    xa = _dram(nc, "x", [T, D], F32, "ExternalInput")
    xha = _dram(nc, "xh", [128, D], F32, "ExternalInput")
    ga = _dram(nc, "g", [D], F32, "ExternalInput")
    pr = {}
    shapes = {"mu": [6, D], "w_rkv": [3, D, D], "w0": [D], "w1": [D, 64], "w2": [64, D], "a0": [D], "a1": [D, 64], "a2": [64, D],
              "g1": [D, 160], "g2": [160, D], "k_k": [D], "k_a": [D], "r_k": [D]}
    if has_v:
        shapes.update({"v0": [D], "v1": [D, 32], "v2": [32, D]})
    for nm, sh in shapes.items():
        pr[nm] = _dram(nc, "p_" + nm, sh, F32, "ExternalInput")
    QGo = _dram(nc, "QG", [NT, 16, 128, 128], BF16, "ExternalOutput")
    PHo = _dram(nc, "PH", [NT, 16, 128, 128], BF16, "ExternalOutput")
    vbo = _dram(nc, "vb", [T, D], BF16, "ExternalOutput")
    gto = _dram(nc, "gate", [T, D], F32, "ExternalOutput")
    boo = _dram(nc, "bonus", [T, D], F32, "ExternalOutput")
    vfo = _dram(nc, "vfirst", [T, D], F32, "ExternalInput" if has_v else "ExternalOutput")
    P = Prog(nc)
    with ExitStack() as st:
        emit_rwkv_proj(P, nc, st, xa, xha, ga, pr, QGo, PHo, vbo, gto, boo, vfo, T, has_v)
        P.emit()
    maps = []
    for i in range(NCORES):
        m = {"x": x[i * T:(i + 1) * T], "xh": xh[i], "g": gvec}
        for nm in shapes:
            m["p_" + nm] = np.ascontiguousarray(prm[nm]).reshape(shapes[nm])
        if has_v:
            m["vfirst"] = vfirst[i * T:(i + 1) * T]
        maps.append(m)
    return _run(nc, maps)


def launch_rwkv_scan(QG_c, PH_c, V_c, S):
    NTs = S // 128
    nc = bass.Bass("TRN2", target_bir_lowering=False)
    qa = _dram(nc, "QG", [NTs, 8, 128, 128], BF16, "ExternalInput")
    pa = _dram(nc, "PH", [NTs, 8, 128, 128], BF16, "ExternalInput")
    va = _dram(nc, "V", [S, 512], BF16, "ExternalInput")
    yo = _dram(nc, "y", [S, 512], F32, "ExternalOutput")
    P = Prog(nc)
    with ExitStack() as st:
        emit_rwkv_scan(P, nc, st, qa, pa, va, yo, NTs)
        P.emit()
    maps = [{"QG": QG_c[i], "PH": PH_c[i], "V": V_c[i]} for i in range(NCORES)]
    r = _run(nc, maps)
    return [ri["y"] for ri in r]


def launch_rwkv_out_ffn(x, y, gate, bonus, lnw, lnb, w_o, gffn, wg, wu, wd, final_g=None):
    N = x.shape[0]
    T = N // NCORES
    nc = bass.Bass("TRN2", target_bir_lowering=False)
    xa = _dram(nc, "x", [T, D], F32, "ExternalInput")
    ya_ = _dram(nc, "yin", [T, D], F32, "ExternalInput")
    gta = _dram(nc, "gate", [T, D], F32, "ExternalInput")
    boa = _dram(nc, "bonus", [T, D], F32, "ExternalInput")
    lwa = _dram(nc, "lnw", [D], F32, "ExternalInput")
    lba = _dram(nc, "lnb", [D], F32, "ExternalInput")
    woa = _dram(nc, "w_o", [D, D], F32, "ExternalInput")
    ga = _dram(nc, "g", [D], F32, "ExternalInput")
    wga = _dram(nc, "wg", [D, DFF], F32, "ExternalInput")
    wua = _dram(nc, "wu", [D, DFF], F32, "ExternalInput")
    wda = _dram(nc, "wd", [DFF, D], F32, "ExternalInput")
    out = _dram(nc, "y", [T, D], F32, "ExternalOutput")
    fa = _dram(nc, "gfin", [D], F32, "ExternalInput") if final_g is not None else None
    P1 = Prog(nc)
    with ExitStack() as st:
        emit_rwkv_out(P1, nc, st, xa, ya_, gta, boa, lwa, lba, woa, out, T)
        P1.emit()
    P2 = Prog(nc)
    with ExitStack() as st:
        emit_ffn(P2, nc, st, out, ga, wga, wua, wda, out, T, final_g=fa)
        P2.emit()
    maps = []
    for i in range(NCORES):
        sl = slice(i * T, (i + 1) * T)
        m = {"x": x[sl], "yin": y[sl], "gate": gate[i], "bonus": bonus[i], "lnw": lnw, "lnb": lnb, "w_o": w_o, "g": gffn, "wg": wg, "wu": wu, "wd": wd}
        if final_g is not None:
            m["gfin"] = final_g
        maps.append(m)
    r = _run(nc, maps)
    return np.concatenate([ri["y"] for ri in r], axis=0)


def rwkv_layer(x, B, S, g_mix, prm, vfirst, lnw, lnb, w_o, g_ffn, wg, wu, wd, final_g=None):
    N = B * S
    T = N // NCORES
    per = S // T
    xh = []
    for i in range(NCORES):
        t0 = i * T
        hrow = np.zeros((128, D), np.float32)
        if t0 % S != 0:
            hrow[0] = x[t0 - 1]
        xh.append(hrow)
    r = launch_rwkv_proj(x, xh, g_mix, prm, vfirst)
    if vfirst is None:
        vfirst = np.concatenate([ri["vfirst"] for ri in r], axis=0)
    QG_c, PH_c, V_c = [], [], []
    for i in range(NCORES):
        b, hh = i // 2, i % 2
        cs = range(b * per, (b + 1) * per)
        QG_c.append(np.ascontiguousarray(np.concatenate([r[j]["QG"][:, hh * 8:(hh + 1) * 8] for j in cs], axis=0)))
        PH_c.append(np.ascontiguousarray(np.concatenate([r[j]["PH"][:, hh * 8:(hh + 1) * 8] for j in cs], axis=0)))
        V_c.append(np.ascontiguousarray(np.concatenate([r[j]["vb"][:, hh * 512:(hh + 1) * 512] for j in cs], axis=0)))
    ys = launch_rwkv_scan(QG_c, PH_c, V_c, S)
    y = np.concatenate([np.concatenate([ys[2 * b], ys[2 * b + 1]], axis=1) for b in range(B)], axis=0)
    xn = launch_rwkv_out_ffn(x, y, [ri["gate"] for ri in r], [ri["bonus"] for ri in r], lnw, lnb, w_o, g_ffn, wg, wu, wd, final_g=final_g)
    return xn, vfirst


def kernel_unfused(x, positions, norm_mix, norm_ffn, norm_final, attn_w_in, attn_b_in, attn_sinks, attn_w_out,
           rwkv_mu, rwkv_w_rkv, rwkv_w0, rwkv_w1, rwkv_w2, rwkv_a0, rwkv_a1, rwkv_a2, rwkv_g1, rwkv_g2,
           rwkv_k_k, rwkv_k_a, rwkv_r_k, rwkv_lnx_w, rwkv_lnx_b, rwkv_w_o, rwkv_v0, rwkv_v1, rwkv_v2,
           ffn_w_gate, ffn_w_up, ffn_w_down):
    f = lambda a: np.ascontiguousarray(np.asarray(a, dtype=np.float32))
    x = f(x)
    B, S, _ = x.shape
    xc = x.reshape(B * S, D)
    pos = np.ascontiguousarray(np.asarray(positions, dtype=np.int32)).reshape(-1)
    vfirst = None
    depth = norm_mix.shape[0]
    for layer in range(depth):
        i = layer // 2
        fin = f(norm_final) if layer == depth - 1 else None
        if layer % 2 == 0:
            xc = attn_layer(xc, pos, B, S, f(norm_mix[layer]), f(attn_w_in[i]), f(attn_b_in[i]), f(attn_sinks[i]), f(attn_w_out[i]),
                            f(norm_ffn[layer]), f(ffn_w_gate[layer]), f(ffn_w_up[layer]), f(ffn_w_down[layer]), final_g=fin)
        else:
            prm = {"mu": f(rwkv_mu[i]), "w_rkv": f(rwkv_w_rkv[i]), "w0": f(rwkv_w0[i]), "w1": f(rwkv_w1[i]), "w2": f(rwkv_w2[i]),
                   "a0": f(rwkv_a0[i]), "a1": f(rwkv_a1[i]), "a2": f(rwkv_a2[i]), "g1": f(rwkv_g1[i]), "g2": f(rwkv_g2[i]),
                   "k_k": f(rwkv_k_k[i]), "k_a": f(rwkv_k_a[i]), "r_k": f(rwkv_r_k[i]).reshape(-1)}
            vf_in = None
            if i > 0:
                prm.update({"v0": f(rwkv_v0[i - 1]), "v1": f(rwkv_v1[i - 1]), "v2": f(rwkv_v2[i - 1])})
                vf_in = vfirst
            xc, vf = rwkv_layer(xc, B, S, f(norm_mix[layer]), prm, vf_in, f(rwkv_lnx_w[i]), f(rwkv_lnx_b[i]), f(rwkv_w_o[i]),
                                f(norm_ffn[layer]), f(ffn_w_gate[layer]), f(ffn_w_up[layer]), f(ffn_w_down[layer]), final_g=fin)
            if vfirst is None:
                vfirst = vf
    return xc.reshape(B, S, D).astype(np.float32)


def _phase(nc, fn):
    with nc.cleanup_on_exit():
        P = Prog(nc)
        P.persist_sems = True
        with ExitStack() as st:
            fn(P, st)
            P.emit()
        nc.all_engine_barrier()


_W_SHAPES = {
    "norm_mix": [4, D], "norm_ffn": [4, D], "norm_final": [D], "attn_w_in": [2, D, INW], "attn_b_in": [2, INW], "attn_sinks": [2, 12],
    "attn_w_out": [2, D, D], "rwkv_mu": [2, 6, D], "rwkv_w_rkv": [2, 3, D, D], "rwkv_w0": [2, D], "rwkv_w1": [2, D, 64], "rwkv_w2": [2, 64, D],
    "rwkv_a0": [2, D], "rwkv_a1": [2, D, 64], "rwkv_a2": [2, 64, D], "rwkv_g1": [2, D, 160], "rwkv_g2": [2, 160, D], "rwkv_k_k": [2, D],
    "rwkv_k_a": [2, D], "rwkv_r_k": [2, D], "rwkv_lnx_w": [2, D], "rwkv_lnx_b": [2, D], "rwkv_w_o": [2, D, D], "rwkv_v0": [1, D],
    "rwkv_v1": [1, D, 32], "rwkv_v2": [1, 32, D], "ffn_w_gate": [4, D, DFF], "ffn_w_up": [4, D, DFF], "ffn_w_down": [4, DFF, D]}


def build_fused(S, depth=4):
    nc = bass.Bass("TRN2", target_bir_lowering=False)
    NTs = S // 128
    xin = _dram(nc, "x", [S, D], F32, "ExternalInput")
    pos = _dram(nc, "positions", [S], I32, "ExternalInput")
    zer = _dram(nc, "zeros", [128, D], F32, "ExternalInput")
    W = {k: _dram(nc, k, sh, F32, "ExternalInput") for k, sh in _W_SHAPES.items()}
    yout = _dram(nc, "y", [S, D], F32, "ExternalOutput")
    I = "Internal"
    qkT = _dram(nc, "s_qkT", [NH_ROT, 64, S], BF16, I)
    vsc = _dram(nc, "s_v", [S, NV * 64], BF16, I)
    oT = _dram(nc, "s_oT", [16, 64, S], BF16, I)
    xs = [_dram(nc, "s_xa", [S, D], F32, I), _dram(nc, "s_xb", [S, D], F32, I)]
    QG = _dram(nc, "s_QG", [NTs, 16, 128, 128], BF16, I)
    PH = _dram(nc, "s_PH", [NTs, 16, 128, 128], BF16, I)
    vb = _dram(nc, "s_vb", [S, D], BF16, I)
    gate = _dram(nc, "s_gate", [S, D], F32, I)
    bonus = _dram(nc, "s_bonus", [S, D], F32, I)
    vfirst = _dram(nc, "s_vfirst", [S, D], F32, I)
    ysc = _dram(nc, "s_y", [S, D], F32, I)
    cur = xin
    counts = []
    for layer in range(depth):
        i = layer // 2
        last = layer == depth - 1
        nxt_ = xs[layer % 2]
        dst = yout if last else nxt_
        fin = W["norm_final"] if last else None
        if layer % 2 == 0:
            _phase(nc, lambda P, st: emit_attn_proj(P, nc, st, cur, pos, W["norm_mix"][layer], W["attn_w_in"][i], W["attn_b_in"][i], qkT, vsc, S))
            _phase(nc, lambda P, st: emit_attn_core(P, nc, st, qkT, vsc, W["attn_sinks"][i], oT, S))
            _phase(nc, lambda P, st: emit_head_outproj(P, nc, st, cur, oT, W["attn_w_out"][i], nxt_, S))
        else:
            pr = {"mu": W["rwkv_mu"][i], "w_rkv": W["rwkv_w_rkv"][i], "w0": W["rwkv_w0"][i], "w1": W["rwkv_w1"][i], "w2": W["rwkv_w2"][i],
                  "a0": W["rwkv_a0"][i], "a1": W["rwkv_a1"][i], "a2": W["rwkv_a2"][i], "g1": W["rwkv_g1"][i], "g2": W["rwkv_g2"][i],
                  "k_k": W["rwkv_k_k"][i], "k_a": W["rwkv_k_a"][i], "r_k": W["rwkv_r_k"][i]}
            has_v = i > 0
            if has_v:
                pr.update({"v0": W["rwkv_v0"][i - 1], "v1": W["rwkv_v1"][i - 1], "v2": W["rwkv_v2"][i - 1]})
            _phase(nc, lambda P, st: emit_rwkv_proj(P, nc, st, cur, zer, W["norm_mix"][layer], pr, QG, PH, vb, gate, bonus, vfirst, S, has_v))
            for hh in range(2):
                _phase(nc, lambda P, st: emit_rwkv_scan(P, nc, st, QG[:, hh * 8:(hh + 1) * 8], PH[:, hh * 8:(hh + 1) * 8], vb[:, hh * 512:(hh + 1) * 512],
                                                        ysc[:, hh * 512:(hh + 1) * 512], NTs))
            _phase(nc, lambda P, st: emit_rwkv_out(P, nc, st, cur, ysc, gate, bonus, W["rwkv_lnx_w"][i], W["rwkv_lnx_b"][i], W["rwkv_w_o"][i], nxt_, S))
        _phase(nc, lambda P, st: emit_ffn(P, nc, st, nxt_, W["norm_ffn"][layer], W["ffn_w_gate"][layer], W["ffn_w_up"][layer], W["ffn_w_down"][layer], dst, S, final_g=fin))
        cur = nxt_
    return nc


def kernel(x, positions, norm_mix, norm_ffn, norm_final, attn_w_in, attn_b_in, attn_sinks, attn_w_out,
           rwkv_mu, rwkv_w_rkv, rwkv_w0, rwkv_w1, rwkv_w2, rwkv_a0, rwkv_a1, rwkv_a2, rwkv_g1, rwkv_g2,
           rwkv_k_k, rwkv_k_a, rwkv_r_k, rwkv_lnx_w, rwkv_lnx_b, rwkv_w_o, rwkv_v0, rwkv_v1, rwkv_v2,
           ffn_w_gate, ffn_w_up, ffn_w_down):
    f = lambda a: np.ascontiguousarray(np.asarray(a, dtype=np.float32))
    x = f(x)
    B, S, _ = x.shape
    pos = np.ascontiguousarray(np.asarray(positions, dtype=np.int32))
    loc = dict(norm_mix=norm_mix, norm_ffn=norm_ffn, norm_final=norm_final, attn_w_in=np.asarray(attn_w_in)[:, :, _PERM],
               attn_b_in=np.asarray(attn_b_in)[:, _PERM], attn_sinks=attn_sinks, attn_w_out=attn_w_out, rwkv_mu=rwkv_mu, rwkv_w_rkv=rwkv_w_rkv,
               rwkv_w0=rwkv_w0, rwkv_w1=rwkv_w1, rwkv_w2=rwkv_w2, rwkv_a0=rwkv_a0, rwkv_a1=rwkv_a1, rwkv_a2=rwkv_a2, rwkv_g1=rwkv_g1,
               rwkv_g2=rwkv_g2, rwkv_k_k=rwkv_k_k, rwkv_k_a=rwkv_k_a, rwkv_r_k=rwkv_r_k, rwkv_lnx_w=rwkv_lnx_w, rwkv_lnx_b=rwkv_lnx_b,
               rwkv_w_o=rwkv_w_o, rwkv_v0=rwkv_v0, rwkv_v1=rwkv_v1, rwkv_v2=rwkv_v2, ffn_w_gate=ffn_w_gate, ffn_w_up=ffn_w_up, ffn_w_down=ffn_w_down)
    wts = {k: f(v).reshape(_W_SHAPES[k]) for k, v in loc.items()}
    zeros = np.zeros((128, D), np.float32)
    nc = build_fused(S, depth=np.asarray(norm_mix).shape[0])
    maps = []
    for c in range(NCORES):
        b = c % B
        m = {"x": x[b], "positions": pos[b], "zeros": zeros}
        m.update(wts)
        maps.append(m)
    r = _run(nc, maps)
    return np.stack([r[b]["y"] for b in range(B)], axis=0).astype(np.float32)
```
